# Optimizing a Trainium2 kernel written in Bass

```python
import jax, jax.numpy as jnp
from jax import lax
import numpy as np

D_MODEL = 1024
BATCH = 8
SEQ = 4096
DEPTH = 4

GRID_W = 64
CTX_LEN = 256
EPS = 1e-6

CHUNK = 128
A_GROUPS = 4
A_GROUP_DIM = 128
A_WIDTH = A_GROUPS * A_GROUP_DIM
HEAD_DIM = 64
B_Q_HEADS = 8
B_KV_HEADS = 2
B_GQA = B_Q_HEADS // B_KV_HEADS
B_WIDTH = B_Q_HEADS * HEAD_DIM
KV_WIDTH = B_KV_HEADS * HEAD_DIM
WINDOW = 128
BLOCK = 128
ROPE_THETA = 10000.0
AB_SIZES = (A_WIDTH, A_WIDTH, A_WIDTH, B_WIDTH, KV_WIDTH, KV_WIDTH, B_WIDTH)
AB_IN = 3 * A_WIDTH + 2 * B_WIDTH + 2 * KV_WIDTH
AB_MIX = A_WIDTH + B_WIDTH
AB_K_OFF = 3 * A_WIDTH + B_WIDTH

C_WIDTH = D_MODEL
C_HEADS = 4
C_BLOCK = C_WIDTH // C_HEADS
CONV_W = 4
CONV_LEFT = 2
LRU_C = 8.0

N_EVEN = (DEPTH + 1) // 2
N_ODD = DEPTH // 2

kernel_name = "hybrid_gmlp_swa_rglru_prefix_dit"


def _split(p, sizes):
    outs, off = [], 0
    for s in sizes:
        outs.append(p[..., off:off + s])
        off += s
    return outs


def rmsnorm(x, g):
    xf = x.astype(jnp.float32)
    y = xf * lax.rsqrt(jnp.mean(xf * xf, axis=-1, keepdims=True) + EPS)
    return (y * g).astype(x.dtype)


def group_layernorm(v, g, b):
    vf = v.astype(jnp.float32)
    mu = jnp.mean(vf, axis=-1, keepdims=True)
    var = jnp.mean(jnp.square(vf - mu), axis=-1, keepdims=True)
    return ((vf - mu) * lax.rsqrt(var + EPS) * g + b).astype(v.dtype)


def axial_rope(L):
    rows = L // GRID_W
    r, col = jnp.meshgrid(jnp.arange(rows), jnp.arange(GRID_W), indexing="ij")
    r = r.reshape(-1).astype(jnp.float32)
    col = col.reshape(-1).astype(jnp.float32)
    n_freq = HEAD_DIM // 4
    inv_freq = ROPE_THETA ** (-jnp.arange(n_freq, dtype=jnp.float32) / n_freq)
    ang = jnp.concatenate([r[:, None] * inv_freq, col[:, None] * inv_freq], axis=-1)
    return jnp.cos(ang), jnp.sin(ang)


def apply_rope(x, cos, sin):
    half = HEAD_DIM // 2
    x1, x2 = x[..., :half], x[..., half:]
    c, s = cos[None, :, None, :], sin[None, :, None, :]
    return jnp.concatenate([x1 * c - x2 * s, x2 * c + x1 * s], axis=-1).astype(x.dtype)


def chunk_gmlp(u, v, ln_g, ln_b, w_s, b_s):
    bsz, L, _ = u.shape
    n = L // CHUNK
    v = group_layernorm(v.reshape(bsz, L, A_GROUPS, A_GROUP_DIM),
                        ln_g.reshape(A_GROUPS, A_GROUP_DIM), ln_b.reshape(A_GROUPS, A_GROUP_DIM))
    v = v.reshape(bsz, n, CHUNK, A_GROUPS, A_GROUP_DIM)
    sv = jnp.einsum("gpq,bnqgd->bnpgd", w_s, v) + b_s.T[None, None, :, :, None]
    return u * sv.reshape(bsz, L, A_WIDTH)


def _sink_logits(sink, bsz, nq):
    s = sink.reshape(B_KV_HEADS, B_GQA).astype(jnp.float32)[None, :, :, None, None]
    return jnp.broadcast_to(s, (bsz, B_KV_HEADS, B_GQA, nq, 1))


def window_attention(q, k, v, k_c, v_c, sink):
    bsz, L, _, _ = q.shape
    Lc = k_c.shape[1]
    n = L // BLOCK
    q = q.reshape(bsz, L, B_KV_HEADS, B_GQA, HEAD_DIM) * (HEAD_DIM ** -0.5)
    pad = ((0, 0), (BLOCK, BLOCK), (0, 0), (0, 0))
    k_pad = jnp.pad(k, pad)
    v_pad = jnp.pad(v, pad)
    s_sink = _sink_logits(sink, bsz, BLOCK)

    def block(j):
        start = j * BLOCK
        qb = lax.dynamic_slice_in_dim(q, start, BLOCK, axis=1)
        kb = lax.dynamic_slice_in_dim(k_pad, start, 3 * BLOCK, axis=1)
        vb = lax.dynamic_slice_in_dim(v_pad, start, 3 * BLOCK, axis=1)
        s_loc = jnp.einsum("bqkgd,bskd->bkgqs", qb, kb).astype(jnp.float32)
        qpos = start + jnp.arange(BLOCK)
        kpos = start - BLOCK + jnp.arange(3 * BLOCK)
        valid = (jnp.abs(qpos[:, None] - kpos[None, :]) <= WINDOW) & (kpos[None, :] >= 0) & (kpos[None, :] < L)
        s_loc = jnp.where(valid, s_loc, -1e30)
        s_ctx = jnp.einsum("bqkgd,bskd->bkgqs", qb, k_c).astype(jnp.float32)
        p = jax.nn.softmax(jnp.concatenate([s_sink, s_ctx, s_loc], axis=-1), axis=-1)
        p_ctx = p[..., 1:1 + Lc].astype(v.dtype)
        p_loc = p[..., 1 + Lc:].astype(v.dtype)
        return (jnp.einsum("bkgqs,bskd->bqkgd", p_ctx, v_c)
                + jnp.einsum("bkgqs,bskd->bqkgd", p_loc, vb))

    out = lax.map(block, jnp.arange(n))
    return out.transpose(1, 0, 2, 3, 4, 5).reshape(bsz, L, B_WIDTH)


def context_attention(q_c, k_c, v_c, sink):
    bsz, Lc, _, _ = q_c.shape
    q_c = q_c.reshape(bsz, Lc, B_KV_HEADS, B_GQA, HEAD_DIM) * (HEAD_DIM ** -0.5)
    s = jnp.einsum("bqkgd,bskd->bkgqs", q_c, k_c).astype(jnp.float32)
    p = jax.nn.softmax(jnp.concatenate([_sink_logits(sink, bsz, Lc), s], axis=-1), axis=-1)
    out = jnp.einsum("bkgqs,bskd->bqkgd", p[..., 1:].astype(v_c.dtype), v_c)
    return out.reshape(bsz, Lc, B_WIDTH)


def even_mixer(h, hc, w_in, ln_g, ln_b, w_s, b_s, sink, w_out, cos, sin, need_ctx):
    bsz, L, _ = h.shape
    Lc = hc.shape[1]
    u, v, ga, q, k, vv, gb = _split(h @ w_in, AB_SIZES)
    ya = chunk_gmlp(jax.nn.gelu(u), jax.nn.gelu(v), ln_g, ln_b, w_s, b_s) * jax.nn.silu(ga)
    if need_ctx:
        uc, vc, gac, qc, kc, vvc, gbc = _split(hc @ w_in, AB_SIZES)
    else:
        kc, vvc = _split(hc @ w_in[:, AB_K_OFF:AB_K_OFF + 2 * KV_WIDTH], (KV_WIDTH, KV_WIDTH))
    kc = kc.reshape(bsz, Lc, B_KV_HEADS, HEAD_DIM)
    vvc = vvc.reshape(bsz, Lc, B_KV_HEADS, HEAD_DIM)
    q = apply_rope(q.reshape(bsz, L, B_Q_HEADS, HEAD_DIM), cos, sin)
    k = apply_rope(k.reshape(bsz, L, B_KV_HEADS, HEAD_DIM), cos, sin)
    vv = vv.reshape(bsz, L, B_KV_HEADS, HEAD_DIM)
    yb = window_attention(q, k, vv, kc, vvc, sink) * jax.nn.silu(gb)
    y = jnp.concatenate([ya, yb], axis=-1) @ w_out
    if not need_ctx:
        return y, None
    yac = chunk_gmlp(jax.nn.gelu(uc), jax.nn.gelu(vc), ln_g, ln_b, w_s, b_s) * jax.nn.silu(gac)
    ybc = context_attention(qc.reshape(bsz, Lc, B_Q_HEADS, HEAD_DIM), kc, vvc, sink) * jax.nn.silu(gbc)
    yc = jnp.concatenate([yac, ybc], axis=-1) @ w_out
    return y, yc


def depthwise_conv(z, w, b):
    ch = z.shape[-1]
    out = lax.conv_general_dilated(z, w[:, None, :], window_strides=(1,),
                                   padding=[(CONV_LEFT, CONV_W - 1 - CONV_LEFT)],
                                   dimension_numbers=("NWC", "WIO", "NWC"),
                                   feature_group_count=ch)
    return out + b


def rglru_coeffs(z, w_a, b_a, w_i, b_i, lam):
    bsz, L, _ = z.shape
    zb = z.reshape(bsz, L, C_HEADS, C_BLOCK)
    r = jax.nn.sigmoid((jnp.einsum("blhi,hij->blhj", zb, w_a).reshape(bsz, L, C_WIDTH) + b_a).astype(jnp.float32))
    ig = jax.nn.sigmoid((jnp.einsum("blhi,hij->blhj", zb, w_i).reshape(bsz, L, C_WIDTH) + b_i).astype(jnp.float32))
    log_a = -LRU_C * r * jax.nn.softplus(-lam.astype(jnp.float32))
    a = jnp.exp(log_a)
    bx = jnp.sqrt(-jnp.expm1(2.0 * log_a)) * (ig * z.astype(jnp.float32))
    return a, bx


def linear_scan(a, b, h0, reverse):
    if reverse:
        a, b = jnp.flip(a, axis=1), jnp.flip(b, axis=1)

    def combine(e1, e2):
        a1, b1 = e1
        a2, b2 = e2
        return a1 * a2, a2 * b1 + b2

    a_cum, b_cum = lax.associative_scan(combine, (a, b), axis=1)
    h = a_cum * h0[:, None, :] + b_cum
    if reverse:
        h = jnp.flip(h, axis=1)
    return h


def odd_mixer(h, hc, w_in, conv_w, conv_b, w_a, b_a, w_i, b_i, lam, w_out, need_ctx):
    bsz = h.shape[0]
    xr, g = _split(h @ w_in, (C_WIDTH, C_WIDTH))
    if need_ctx:
        xr_c, g_c = _split(hc @ w_in, (C_WIDTH, C_WIDTH))
    else:
        xr_c = hc @ w_in[:, :C_WIDTH]
    z = depthwise_conv(xr, conv_w, conv_b)
    z_c = depthwise_conv(xr_c, conv_w, conv_b)
    h_lat = None
    h_ctx = None
    for d, reverse in ((0, False), (1, True)):
        a_c, b_c = rglru_coeffs(z_c, w_a[d], b_a[d], w_i[d], b_i[d], lam[d])
        s_c = linear_scan(a_c, b_c, jnp.zeros((bsz, C_WIDTH), jnp.float32), reverse)
        h0 = s_c[:, 0] if reverse else s_c[:, -1]
        a_l, b_l = rglru_coeffs(z, w_a[d], b_a[d], w_i[d], b_i[d], lam[d])
        s_l = linear_scan(a_l, b_l, h0, reverse)
        h_lat = s_l if h_lat is None else h_lat + s_l
        if need_ctx:
            h_ctx = s_c if h_ctx is None else h_ctx + s_c
    y = (h_lat.astype(h.dtype) * jax.nn.silu(g)) @ w_out
    if not need_ctx:
        return y, None
    yc = (h_ctx.astype(hc.dtype) * jax.nn.silu(g_c)) @ w_out
    return y, yc


def setup_inputs(seed: int = 0) -> dict:
    key = jax.random.key(seed)
    ks = jax.random.split(key, 32)
    n = jax.random.normal
    f32 = jnp.float32
    d = D_MODEL
    u_lam = jax.random.uniform(ks[22], (N_ODD, 2, C_WIDTH), f32, minval=0.9, maxval=0.999)
    a0 = u_lam ** (1.0 / LRU_C)
    return {
        "x": n(ks[0], (BATCH, SEQ, d), f32),
        "c": n(ks[1], (BATCH, d), f32),
        "ctx": n(ks[2], (BATCH, CTX_LEN, d), f32),
        "c_ctx": n(ks[3], (d,), f32),
        "norm_g": 1.0 + 0.02 * n(ks[4], (DEPTH, d), f32),
        "w_mod": n(ks[5], (DEPTH, d, 3 * d), f32) * (0.5 * d ** -0.5),
        "b_mod": 0.02 * n(ks[6], (DEPTH, 3 * d), f32),
        "ab_w_in": n(ks[7], (N_EVEN, d, AB_IN), f32) * d ** -0.5,
        "a_ln_g": 1.0 + 0.02 * n(ks[8], (N_EVEN, A_WIDTH), f32),
        "a_ln_b": 0.02 * n(ks[9], (N_EVEN, A_WIDTH), f32),
        "a_w_s": n(ks[10], (N_EVEN, A_GROUPS, CHUNK, CHUNK), f32) * CHUNK ** -0.5,
        "a_b_s": 1.0 + 0.02 * n(ks[11], (N_EVEN, A_GROUPS, CHUNK), f32),
        "b_sink": 0.5 * n(ks[12], (N_EVEN, B_Q_HEADS), f32),
        "ab_w_out": n(ks[13], (N_EVEN, AB_MIX, d), f32) * AB_MIX ** -0.5,
        "c_w_in": n(ks[14], (N_ODD, d, 2 * C_WIDTH), f32) * d ** -0.5,
        "c_conv_w": n(ks[15], (N_ODD, CONV_W, C_WIDTH), f32) * CONV_W ** -0.5,
        "c_conv_b": 0.02 * n(ks[16], (N_ODD, C_WIDTH), f32),
        "c_w_a": n(ks[17], (N_ODD, 2, C_HEADS, C_BLOCK, C_BLOCK), f32) * C_BLOCK ** -0.5,
        "c_b_a": 0.02 * n(ks[18], (N_ODD, 2, C_WIDTH), f32),
        "c_w_i": n(ks[19], (N_ODD, 2, C_HEADS, C_BLOCK, C_BLOCK), f32) * C_BLOCK ** -0.5,
        "c_b_i": 0.02 * n(ks[20], (N_ODD, 2, C_WIDTH), f32),
        "c_lam": jnp.log(a0) - jnp.log1p(-a0),
        "c_w_out": n(ks[21], (N_ODD, C_WIDTH, d), f32) * C_WIDTH ** -0.5,
        "final_g": 1.0 + 0.02 * n(ks[23], (d,), f32),
    }


def reference(x, c, ctx, c_ctx, norm_g, w_mod, b_mod, ab_w_in, a_ln_g, a_ln_b, a_w_s, a_b_s, b_sink,
              ab_w_out, c_w_in, c_conv_w, c_conv_b, c_w_a, c_b_a, c_w_i, c_b_i, c_lam, c_w_out, final_g):
    L = x.shape[1]
    cos, sin = axial_rope(L)
    silu_c = jax.nn.silu(c)
    silu_cc = jax.nn.silu(c_ctx)
    xc = ctx
    for layer in range(DEPTH):
        need_ctx = layer < DEPTH - 1
        shift, scale, gate = jnp.split((silu_c @ w_mod[layer] + b_mod[layer])[:, None, :], 3, axis=-1)
        shift_c, scale_c, gate_c = jnp.split(silu_cc @ w_mod[layer] + b_mod[layer], 3, axis=-1)
        h = rmsnorm(x, norm_g[layer]) * (1.0 + scale) + shift
        hc = rmsnorm(xc, norm_g[layer]) * (1.0 + scale_c) + shift_c
        i = layer // 2
        if layer % 2 == 0:
            y, yc = even_mixer(h, hc, ab_w_in[i], a_ln_g[i], a_ln_b[i], a_w_s[i], a_b_s[i], b_sink[i],
                               ab_w_out[i], cos, sin, need_ctx)
        else:
            y, yc = odd_mixer(h, hc, c_w_in[i], c_conv_w[i], c_conv_b[i], c_w_a[i], c_b_a[i], c_w_i[i],
                              c_b_i[i], c_lam[i], c_w_out[i], need_ctx)
        x = x + gate * y
        if need_ctx:
            xc = xc + gate_c * yc
    return rmsnorm(x, final_g)
```

```python
import math
from contextlib import ExitStack

import numpy as np
import concourse.bass as bass
import concourse.mybir as mybir
from concourse.bass_utils import run_bass_kernel_spmd

F32 = mybir.dt.float32
BF16 = mybir.dt.bfloat16
ALU = mybir.AluOpType
AF = mybir.ActivationFunctionType
AX = mybir.AxisListType

D = 1024
L = 4096
LC = 256
NLB = L // 128
NCB = LC // 128
NBLK = NLB + NCB
DEPTH = 4
EPS = 1e-6
AB_IN = 2816
GELU_C = 0.7978845608028654
SQ_044 = math.sqrt(0.044715)

PE, ACT, DVE, POOL, SP = "tensor", "scalar", "vector", "gpsimd", "sync"
ENGS = (PE, ACT, DVE, POOL, SP)


class Buf:
    __slots__ = ("name", "writer", "readers", "sem", "excl")

    def __init__(self, name, excl=False):
        self.name = name
        self.writer = None
        self.readers = {}
        self.sem = None
        self.excl = excl


class Op:
    __slots__ = ("eng", "fn", "deps", "odeps", "signal", "sig", "isdma", "key", "waits", "idx", "dur", "lat", "seg",
                 "nin", "succ", "rt", "fin", "tbl")

    def __init__(self, eng, fn, isdma=False):
        self.eng = eng
        self.fn = fn
        self.deps = []
        self.odeps = []
        self.signal = False
        self.sig = None
        self.isdma = isdma
        self.key = eng
        self.waits = None
        self.idx = 0
        self.dur = 0.3
        self.lat = 0.0
        self.seg = 0
        self.nin = 0
        self.succ = None
        self.rt = 0.0
        self.fin = 0.0
        self.tbl = None


def _b(x):
    return x if isinstance(x, Buf) else x.b


class Tile:
    __slots__ = ("t", "b", "bs")

    def __init__(self, t, b, bs=None):
        self.t = t
        self.b = b
        self.bs = bs


class Prog:
    SCHED = True
    SEM_LAT = 1.0

    def __init__(self, nc, stack):
        self.nc = nc
        self.stack = stack
        self.ops = {e: [] for e in ENGS}
        self.all_ops = []
        self.esem = {e: stack.enter_context(nc.semaphore("es_" + e)) for e in ENGS}
        self.sem_pool = []
        self.sem_live = []
        self.nsem = 0
        self.dma_since_barrier = []
        self.all_dma_sigs = {}
        self.ndma = 0
        self.seg = 0
        self.seg_deps = {0: []}

    def _deps(self, op, reads, writes):
        deps = op.deps
        odeps = op.odeps
        for x in reads:
            b = _b(x)
            w = b.writer
            if w is not None:
                deps.append(w)
            if b.excl:
                for r in b.readers.values():
                    if r.eng != op.eng:
                        deps.append(r)
        for x in writes:
            b = _b(x)
            w = b.writer
            if w is not None:
                if w.isdma or w.eng != op.eng or op.isdma:
                    deps.append(w)
                else:
                    odeps.append(w)
            for r in b.readers.values():
                if r.isdma or r.eng != op.eng or op.isdma:
                    deps.append(r)
                else:
                    odeps.append(r)
        for d in deps:
            d.signal = True
        for x in reads:
            _b(x).readers[op.key] = op
        for x in writes:
            b = _b(x)
            b.writer = op
            b.readers = {}

    def _add(self, o):
        o.idx = len(self.all_ops)
        o.seg = self.seg
        self.all_ops.append(o)

    def op(self, eng, fn, reads=(), writes=(), dur=0.3, tbl=None):
        o = Op(eng, fn)
        o.dur = dur
        o.tbl = tbl
        self._deps(o, reads, writes)
        self._add(o)
        return o

    def dma(self, eng, out, in_, sbuf, reads=(), writes=(), **kw):
        sbuf = _b(sbuf)
        if sbuf.sem is None:
            if self.sem_pool:
                ent = self.sem_pool.pop()
            else:
                ent = [self.stack.enter_context(self.nc.semaphore("ds%d" % self.nsem)), 0, None]
                self.nsem += 1
            sbuf.sem = ent
            self.sem_live.append(sbuf)
        ent = sbuf.sem
        o = Op(eng, lambda e: e.dma_start(out=out, in_=in_, **kw), isdma=True)
        self.ndma += 1
        o.key = ("dma", self.ndma)
        try:
            nbytes = float(out.nbytes())
        except Exception:
            nbytes = 65536.0
        o.dur = 0.6 if eng == POOL else 0.1
        o.lat = 2.0 + nbytes / 120e3
        self._deps(o, reads, writes)
        if ent[2] is not None and ent[2].seg == self.seg:
            o.odeps.append(ent[2])
        ent[2] = o
        ent[1] += 16
        o.sig = (ent[0], ent[1])
        o.signal = True
        self._add(o)
        self.dma_since_barrier.append(o)
        self.all_dma_sigs[id(ent[0])] = o.sig
        return o

    def barrier(self):
        lasts = {}
        for o in self.all_ops:
            if not o.isdma:
                lasts[o.eng] = o
        self._last_hint = lasts
        self.seg += 1
        self.seg_deps[self.seg] = ("BAR", list(self.dma_since_barrier))
        self.dma_since_barrier = []
        for b in self.sem_live:
            self.sem_pool.append(b.sem)
            b.sem = None
        self.sem_live = []

    def _schedule_segment(self, ops):
        seg = ops[0].seg
        for o in ops:
            o.nin = 0
            o.succ = []
            o.rt = 0.0
        for o in ops:
            seen = set()
            for d in o.deps + o.odeps:
                if d.seg == seg and id(d) not in seen:
                    seen.add(id(d))
                    d.succ.append(o)
                    o.nin += 1
        cand = {e: [] for e in ENGS}
        for o in ops:
            if o.nin == 0:
                cand[o.eng].append(o)
        free = {e: 0.0 for e in ENGS}
        order = {e: [] for e in ENGS}
        left = len(ops)
        lat = self.SEM_LAT
        cur_tbl = getattr(self, "_cur_tbl", None)
        TSW = 0.9
        while left:
            best = None
            for e in ENGS:
                c = cand[e]
                if not c:
                    continue
                t = free[e]
                pick = None
                if e == ACT:
                    pk = None
                    for o in c:
                        st_ = o.rt if o.rt > t else t
                        if o.tbl is not None and o.tbl != cur_tbl:
                            st_ += TSW
                        k_ = (st_, o.idx)
                        if pk is None or k_ < pk:
                            pk = k_
                            pick = o
                    start = pk[0]
                else:
                    for o in c:
                        if o.rt <= t:
                            if pick is None or pick.rt > t or o.idx < pick.idx:
                                pick = o
                        elif pick is None or (pick.rt > t and (o.rt < pick.rt or (o.rt == pick.rt and o.idx < pick.idx))):
                            pick = o
                    start = max(t, pick.rt)
                if best is None or start < best[0] or (start == best[0] and pick.idx < best[1].idx):
                    best = (start, pick)
            start, o = best
            e = o.eng
            if e == ACT and o.tbl is not None:
                cur_tbl = o.tbl
            cand[e].remove(o)
            order[e].append(o)
            free[e] = start + o.dur
            o.fin = start + o.dur + o.lat
            left -= 1
            for s_ in o.succ:
                r = o.fin + (lat if (s_.eng != e or o.isdma) else 0.0)
                if r > s_.rt:
                    s_.rt = r
                s_.nin -= 1
                if s_.nin == 0:
                    cand[s_.eng].append(s_)
        self._cur_tbl = cur_tbl
        self.sim_time = self.sim_time + max(free.values()) if hasattr(self, "sim_time") else max(free.values())
        return order

    def emit(self):
        nc = self.nc
        segs = {}
        for o in self.all_ops:
            segs.setdefault(o.seg, []).append(o)
        self.ops = {e: [] for e in ENGS}
        last_compute = {}
        for sg in sorted(segs):
            ops = segs[sg]
            if self.SCHED:
                order = self._schedule_segment(ops)
            else:
                order = {e: [o for o in ops if o.eng == e] for e in ENGS}
            info = self.seg_deps.get(sg)
            if info:
                extra = list(last_compute.values()) + info[1]
                for o in extra:
                    o.signal = True
                for e in ENGS:
                    if order[e]:
                        order[e][0].deps = order[e][0].deps + extra
            for e in ENGS:
                self.ops[e].extend(order[e])
                for o in order[e]:
                    if not o.isdma:
                        last_compute[e] = o
        for e in ENGS:
            c = 0
            for o in self.ops[e]:
                if o.isdma:
                    continue
                if o.signal:
                    c += 1
                    o.sig = (self.esem[e], c)
        fin_waits = list(self.all_dma_sigs.values())
        for e in ENGS:
            known = {}
            for o in self.ops[e]:
                need = {}
                for d in o.deps:
                    s, v = d.sig
                    k = id(s)
                    if known.get(k, 0) >= v:
                        continue
                    if k not in need or need[k][1] < v:
                        need[k] = (s, v)
                for k, (s, v) in need.items():
                    known[k] = v
                o.waits = list(need.values())
        self.stats = dict(
            nops={e: len(self.ops[e]) for e in ENGS},
            nwait={e: sum(len(o.waits) for o in self.ops[e]) for e in ENGS},
            nsig={e: sum(1 for o in self.ops[e] if o.signal) for e in ENGS},
            nsem=self.nsem, sim_us=getattr(self, "sim_time", 0.0),
        )

        def run(e, engobj):
            for o in self.ops[e]:
                for s, v in o.waits:
                    engobj.wait_ge(s, v)
                ins = o.fn(engobj)
                if o.isdma:
                    ins.then_inc(o.sig[0], 16)
                elif o.signal:
                    ins.then_inc(o.sig[0], 1)
            if e == SP:
                for s, v in fin_waits:
                    engobj.wait_ge(s, v)

        with nc.Block() as block:
            @block.tensor
            def _(t):
                run(PE, t)

            @block.scalar
            def _(t):
                run(ACT, t)

            @block.vector
            def _(t):
                run(DVE, t)

            @block.gpsimd
            def _(t):
                run(POOL, t)

            @block.sync
            def _(t):
                run(SP, t)


class ColMap:
    def __init__(self):
        self.off = {}
        self.n = 0

    def add(self, name, ncols):
        self.off[name] = self.n
        self.n += ncols

    def __getitem__(self, name):
        return self.off[name]


def build_colmap():
    cm = ColMap()
    for l in range(DEPTH):
        cm.add("ng%d" % l, 8)
        cm.add("bsh%d" % l, 8)
        cm.add("bsc%d" % l, 8)
    cm.add("c", 8)
    cm.add("cctx", 8)
    for i in range(2):
        for j in range(4):
            cm.add("cw%d_%d" % (i, j), 8)
        cm.add("cb%d" % i, 8)
        for d in range(2):
            cm.add("ba%d_%d" % (i, d), 8)
            cm.add("bi%d_%d" % (i, d), 8)
            cm.add("lam%d_%d" % (i, d), 8)
        cm.add("bs%d" % i, 4)
    return cm


def build_rowmap():
    rm = ColMap()
    for l in range(DEPTH):
        rm.add("bg%d" % l, 1024)
    for i in range(2):
        rm.add("lng%d" % i, 512)
        rm.add("lnb%d" % i, 512)
        rm.add("sink%d" % i, 8)
    rm.add("fg", 1024)
    return rm


CM = build_colmap()
RM = build_rowmap()


def col8(v):
    return np.ascontiguousarray(np.asarray(v, np.float32).reshape(8, 128).T)


def rope_table():
    t = np.arange(L)
    r = (t // 64).astype(np.float64)
    c = (t % 64).astype(np.float64)
    inv = (10000.0 ** (-np.arange(16, dtype=np.float32) / np.float32(16))).astype(np.float32).astype(np.float64)
    ang = np.concatenate([r[:, None] * inv, c[:, None] * inv], axis=-1).astype(np.float32)
    cos = np.cos(ang).astype(np.float32)
    sin = np.sin(ang).astype(np.float32)
    return np.ascontiguousarray(np.concatenate([cos, cos, -sin, sin], axis=-1).astype(np.float32))


class StopBuild(Exception):
    pass


class Builder:
    def __init__(self, layers=(0, 1, 2, 3), do_final=True, dbg=False, stop=None):
        self.stop = stop
        self.stopped = False
        self.cks = []
        self.layers = list(layers)
        self.do_final = do_final
        self.dbg = dbg
        self.nc = bass.Bass("TRN2", target_bir_lowering=False)
        self.uid = 0

    def ck(self, name):
        self.cks.append(name)
        if self.stop is not None and name == self.stop:
            self.stopped = True
        return self.stopped

    def dram_in(self, name, shape, dt=F32):
        return self.nc.dram_tensor(name, list(shape), dt, kind="ExternalInput").ap()

    def dram_out(self, name, shape, dt=F32):
        return self.nc.dram_tensor(name, list(shape), dt, kind="ExternalOutput").ap()

    def dram_tmp(self, name, shape, dt=F32):
        return self.nc.dram_tensor(name, list(shape), dt).ap()

    def sb(self, st, name, shape, dt=F32):
        self.uid += 1
        nm = "%s_%d" % (name, self.uid)
        t = st.enter_context(self.nc.sbuf_tensor(nm, list(shape), dt))
        return Tile(t, Buf(nm))

    def sbc(self, st, name, shape, dt, n):
        T = self.sb(st, name, shape, dt)
        T.bs = [Buf("%s_c%d" % (name, c)) for c in range(n)]
        return T

    def ps(self, st, name, shape, dt=F32):
        self.uid += 1
        nm = "%s_%d" % (name, self.uid)
        t = st.enter_context(self.nc.psum_tensor(nm, list(shape), dt))
        return Tile(t, Buf(nm, excl=True))

    @staticmethod
    def _fs(ap):
        n = 1
        for d in ap.shape[1:]:
            n *= int(d)
        return n

    def act(self, out, in_, func, r, w, scale=1.0, bias=0.0, accum=None):
        kw = {}
        if accum is not None:
            kw["accum_out"] = accum
        tbl = {AF.Exp: "exp", AF.Tanh: "exp", AF.Sqrt: "sqrt", AF.Ln: "ln"}.get(func)
        self.P.op(ACT, lambda e: e.activation(out=out, in_=in_, func=func, bias=bias, scale=scale, **kw), r, w,
                  dur=0.2 + self._fs(out) / 1300.0, tbl=tbl)

    def _vdur(self, eng, out, two_in):
        n = self._fs(out)
        if eng == POOL:
            return 0.25 + n * 0.0022
        return (0.15 + n / 800.0) if two_in else (0.10 + n / 1400.0)

    def tt(self, eng, out, in0, in1, op, r, w):
        self.P.op(eng, lambda e: e.tensor_tensor(out=out, in0=in0, in1=in1, op=op), r, w, dur=self._vdur(eng, out, True))

    def ts(self, eng, out, in0, s1, s2, op0, op1, r, w):
        d = self._vdur(eng, out, False)
        if op1 is None:
            self.P.op(eng, lambda e: e.tensor_scalar(out=out, in0=in0, scalar1=s1, scalar2=None, op0=op0), r, w, dur=d)
        else:
            self.P.op(eng, lambda e: e.tensor_scalar(out=out, in0=in0, scalar1=s1, scalar2=s2, op0=op0, op1=op1), r, w, dur=d)

    def stt(self, eng, out, in0, scalar, in1, op0, op1, r, w):
        assert eng == DVE, "scalar_tensor_tensor is DVE-only"
        self.P.op(eng, lambda e: e.scalar_tensor_tensor(out=out, in0=in0, scalar=scalar, in1=in1, op0=op0, op1=op1), r, w,
                  dur=self._vdur(eng, out, True))

    def cp(self, eng, out, in_, r, w):
        if eng == ACT:
            self.P.op(ACT, lambda e: e.activation(out=out, in_=in_, func=AF.Copy), r, w, dur=0.2 + self._fs(out) / 1300.0)
        else:
            self.P.op(eng, lambda e: e.tensor_copy(out=out, in_=in_), r, w, dur=self._vdur(eng, out, False))

    def mm(self, out, lhsT, rhs, start, stop, r, w):
        self.P.op(PE, lambda e: e.matmul(out, lhsT=lhsT, rhs=rhs, start=start, stop=stop), r, w,
                  dur=0.03 + self._fs(out) / 2000.0)

    def mmx(self, out, lhsT, rhs, start, stop, r, w):
        self.P.op(PE, lambda e: e.matmul(out, lhsT=lhsT, rhs=rhs, start=start, stop=stop, skip_group_check=True), r, w,
                  dur=0.03 + self._fs(out) / 2000.0)

    def tr(self, out, in_, ident, r, w):
        self.P.op(PE, lambda e: e.transpose(out=out, in_=in_, identity=ident), r, w, dur=0.07)

    def memset(self, eng, ap, val, w):
        self.P.op(eng, lambda e: e.memset(ap, val), (), w, dur=self._vdur(eng, ap, False))

    def dma(self, eng, out, in_, sbuf, r=(), w=(), **kw):
        self.P.dma(eng, out, in_, sbuf, r, w, **kw)

    def rsqrt(self, out_ap, out_t, v_ap, v_t, s, q, n):
        sa = s.t[:, 0:n]
        self.act(sa, v_ap, AF.Sqrt, [v_t], [s])
        self.P.op(DVE, lambda e: e.reciprocal(out=out_ap, in_=sa), [s], [out_t])

    @staticmethod
    def run_prop(items):
        st_ = [[g_, 0, float(n_)] for g_, n_ in items]
        while st_:
            st_.sort(key=lambda e_: (e_[1] + 1) / e_[2])
            e_ = st_[0]
            try:
                next(e_[0])
                e_[1] += 1
            except StopIteration:
                st_.remove(e_)

    def build(self):
        nc = self.nc
        with ExitStack() as top:
            self.P = Prog(nc, top)
            self.declare_dram()
            self.setup(top)
            self.ck("setup")
            for l in self.layers:
                if self.stopped:
                    break
                self.P.barrier()
                with ExitStack() as st:
                    self.prologue(st, l)
                    if not self.ck("prologue%d" % l):
                        if l % 2 == 0:
                            self.even_layer(st, l)
                        else:
                            self.odd_layer(st, l)
                    self.P.barrier()
            if self.dbg:
                lastl = self.layers[-1]
                src = self.X[(lastl + 1) % 2]
                db = Buf("dbgout")
                for k in range(0, L + LC, 512):
                    n = min(512, L + LC - k)
                    self.dma(SP, self.dbg_x[k:k + n, :], src[k:k + n, :], db)
            self.P.emit()
        return nc

    def declare_dram(self):
        self.x_in = self.dram_in("x", [L, D])
        self.ctx_in = self.dram_in("ctx", [LC, D])
        self.cols_d = self.dram_in("cols", [128, CM.n])
        self.rows_d = self.dram_in("rows", [128, RM.n])
        self.rope_d = self.dram_in("rope", [L, 128])
        self.w_mod = self.dram_in("w_mod", [DEPTH, D, 3 * D])
        self.ab_w_in = self.dram_in("ab_w_in", [2, D, AB_IN])
        self.ab_w_out = self.dram_in("ab_w_out", [2, D, D])
        self.a_w_sT = self.dram_in("a_w_sT", [2, 128, 4, 128])
        self.c_w_in = self.dram_in("c_w_in", [2, D, 2 * D])
        self.c_w_a = self.dram_in("c_w_a", [2, 2, 4, 256, 256])
        self.c_w_i = self.dram_in("c_w_i", [2, 2, 4, 256, 256])
        self.c_w_out = self.dram_in("c_w_out", [2, D, D])
        self.out_d = self.dram_out("out", [L, D])
        self.X = [self.dram_tmp("XA", [L + LC, D]), self.dram_tmp("XB", [L + LC, D])]
        self.ZS = self.dram_tmp("ZS", [D, L])
        self.SF = self.dram_tmp("SF", [D, L])
        self.SG = self.dram_tmp("SG", [D, L], BF16)
        self.ZB = self.dram_tmp("ZB", [D, L], BF16)
        if self.dbg:
            self.dbg_x = self.dram_out("dbg_x", [L + LC, D])

    def x_src(self, l, blk):
        if l == self.layers[0] and l == 0:
            if blk < NCB:
                return self.ctx_in[blk * 128:(blk + 1) * 128, :]
            return self.x_in[(blk - NCB) * 128:(blk - NCB + 1) * 128, :]
        if l == self.layers[0]:
            return self.dbg_xin[blk * 128:(blk + 1) * 128, :]
        return self.X[l % 2][blk * 128:(blk + 1) * 128, :]

    def x_dst(self, l, blk):
        return self.X[(l + 1) % 2][blk * 128:(blk + 1) * 128, :]

    def setup(self, st):
        nc = self.nc
        self.COLS = self.sb(st, "cols", [128, CM.n])
        self.dma(SP, self.COLS.t[:], self.cols_d, self.COLS, w=[self.COLS])
        self.ck("s_cols")
        self.identf = self.sb(st, "identf", [128, 128])
        self.identb = self.sb(st, "identb", [128, 128], BF16)
        self.memset(POOL, self.identf.t[:], 1.0, [self.identf])
        idf = self.identf
        self.P.op(POOL, lambda e: e.affine_select(out=idf.t[:], in_=idf.t[:], pattern=[[-1, 128]], compare_op=ALU.is_equal,
                                                 fill=0.0, base=0, channel_multiplier=1), [idf], [idf])
        self.cp(DVE, self.identb.t[:], self.identf.t[:], [self.identf], [self.identb])
        self.ck("s_ident")
        mtmp = self.sb(st, "mtmp", [128, 128])
        self.memset(POOL, mtmp.t[:], 1.0, [mtmp])
        self.P.op(POOL, lambda e: e.affine_select(out=mtmp.t[:], in_=mtmp.t[:], pattern=[[-1, 128]], compare_op=ALU.is_ge,
                                                 fill=0.0, base=0, channel_multiplier=1), [mtmp], [mtmp])
        mtmp2 = self.sb(st, "mtmp2", [128, 128])
        self.memset(POOL, mtmp2.t[:], 1.0, [mtmp2])
        self.P.op(POOL, lambda e: e.affine_select(out=mtmp2.t[:], in_=mtmp2.t[:], pattern=[[1, 128]], compare_op=ALU.is_ge,
                                                 fill=0.0, base=0, channel_multiplier=-1), [mtmp2], [mtmp2])
        self.mbprev = self.sb(st, "mbprev", [128, 4, 128], BF16)
        self.mbnext = self.sb(st, "mbnext", [128, 4, 128], BF16)
        for (src_, dst_) in ((mtmp, self.mbprev), (mtmp2, self.mbnext)):
            self.ts(DVE, src_.t[:], src_.t[:], -1.0, 30000.0, ALU.add, ALU.mult, [src_], [src_])
            self.cp(DVE, dst_.t[:], src_.t[:].unsqueeze(1).broadcast_to([128, 4, 128]), [src_], [dst_])
        self.ck("s_masks")
        cc = CM["c"]
        th = self.sb(st, "sc_th", [128, 16])
        sc = self.sb(st, "sc", [128, 16])
        self.act(th.t[:], self.COLS.t[:, cc:cc + 16], AF.Tanh, [self.COLS], [th], scale=0.5)
        self.stt(DVE, sc.t[:], th.t[:], 1.0, self.COLS.t[:, cc:cc + 16], ALU.add, ALU.mult, [th, self.COLS], [sc])
        self.ts(DVE, sc.t[:], sc.t[:], 0.5, None, ALU.mult, None, [sc], [sc])
        self.scT = self.sb(st, "scT", [128, 8, 2], BF16)
        self.cp(DVE, self.scT.t[:, :, 0], sc.t[:, 0:8], [sc], [self.scT])
        self.cp(DVE, self.scT.t[:, :, 1], sc.t[:, 8:16], [sc], [self.scT])
        self.screp = []
        for i in range(2):
            t = self.sb(st, "screp%d" % i, [128, 8, 128], BF16)
            self.cp(DVE, t.t[:], sc.t[:, 8 * i:8 * i + 8].unsqueeze(2).broadcast_to([128, 8, 128]), [sc], [t])
            self.screp.append(t)
        self.ck("s_silu")
        self.SSQ = [self.sb(st, "ssq0", [128, NBLK]), self.sb(st, "ssq1", [128, NBLK])]
        l0 = self.layers[0]
        self.memset(POOL, self.SSQ[l0 % 2].t[:], 0.0, [self.SSQ[l0 % 2]])
        if self.dbg and l0 != 0:
            self.dbg_xin = self.dram_in("dbg_xin", [L + LC, D])
        with ExitStack() as s2:
            xr = [self.sb(s2, "prex%d" % i, [128, D]) for i in range(6)]
            junk = self.sb(s2, "prej", [128, D])
            for blk in range(NBLK):
                xt = xr[blk % 6]
                self.dma(SP, xt.t[:], self.x_src(l0, blk), xt, w=[xt])
                self.act(junk.t[:], xt.t[:], AF.Square, [xt], [junk, self.SSQ[l0 % 2]],
                         accum=self.SSQ[l0 % 2].t[:, blk:blk + 1])
            self.P.barrier()

    def prologue(self, st, l):
        P = self.P
        self.gs = [self.sb(st, "gs%d" % i, [128, 8]) for i in range(2)]
        self.sh = [self.sb(st, "sh%d" % i, [128, 8]) for i in range(2)]
        self.gate = [self.sb(st, "gate%d" % i, [128, D]) for i in range(2)]
        self.rstd = self.sb(st, "rstd", [128, NBLK])
        with ExitStack() as s2:
            bg = self.sb(s2, "bg", [128, D])
            self.dma(SP, bg.t[:], self.rows_d[:, RM["bg%d" % l]:RM["bg%d" % l] + D], bg, w=[bg])
            wm = [self.sb(s2, "wm%d" % i, [128, 8, 512], BF16) for i in range(6)]
            pcols_full = self.ps(s2, "pcols", [128, 512])
            pcols = Tile(pcols_full.t[:, 0:32].rearrange("p (n t) -> p n t", t=2), pcols_full.b)
            pg = [self.ps(s2, "pg%d" % i, [128, 512]) for i in range(2)]
            wsrc = self.w_mod[l].rearrange("(kc p) n -> p kc n", p=128)
            for pi in range(6):
                w = wm[pi]
                self.dma(POOL, w.t[:], wsrc[:, :, pi * 512:(pi + 1) * 512], w, w=[w])
                if pi < 4:
                    for q in range(4):
                        nn = 4 * pi + q
                        for kc in range(8):
                            self.mm(pcols.t[:, nn, :], w.t[:, kc, q * 128:(q + 1) * 128], self.scT.t[:, kc, :],
                                    kc == 0, kc == 7, [w, self.scT], [pcols])
                else:
                    hh = pi - 4
                    for i in range(2):
                        for kc in range(8):
                            self.mm(pg[i].t[:], self.screp[i].t[:, kc, :], w.t[:, kc, :], kc == 0, kc == 7,
                                    [w, self.screp[i]], [pg[i]])
                        self.tt(DVE, self.gate[i].t[:, hh * 512:(hh + 1) * 512], pg[i].t[:], bg.t[:, hh * 512:(hh + 1) * 512],
                                ALU.add, [pg[i], bg], [self.gate[i]])
            modc = self.sb(s2, "modc", [128, 16, 2])
            self.cp(DVE, modc.t[:], pcols.t[:], [pcols], [modc])
            ng, bsh, bsc = CM["ng%d" % l], CM["bsh%d" % l], CM["bsc%d" % l]
            tmp = self.sb(s2, "modtmp", [128, 8])
            for i in range(2):
                self.tt(DVE, self.sh[i].t[:], modc.t[:, 0:8, i], self.COLS.t[:, bsh:bsh + 8], ALU.add,
                        [modc, self.COLS], [self.sh[i]])
                self.tt(DVE, tmp.t[:], modc.t[:, 8:16, i], self.COLS.t[:, bsc:bsc + 8], ALU.add, [modc, self.COLS], [tmp])
                self.stt(DVE, self.gs[i].t[:], tmp.t[:], 1.0, self.COLS.t[:, ng:ng + 8], ALU.add, ALU.mult,
                         [tmp, self.COLS], [self.gs[i]])
            v = self.sb(s2, "nv", [128, NBLK])
            self.ts(DVE, v.t[:], self.SSQ[l % 2].t[:], 1.0 / D, EPS, ALU.mult, ALU.add, [self.SSQ[l % 2]], [v])
            hs = self.sb(s2, "hs", [128, NBLK])
            hq = self.sb(s2, "hq", [128, NBLK])
            self.rsqrt(self.rstd.t[:], self.rstd, v.t[:], v, hs, hq, NBLK)
            self.memset(POOL, self.SSQ[(l + 1) % 2].t[:], 0.0, [self.SSQ[(l + 1) % 2]])
            self.P.barrier()

    def norm_T(self, xt, blk, xn, psT, hT_ap, hT_tile, is_ctx):
        i = 1 if is_ctx else 0
        self.act(xn.t[:], xt.t[:], AF.Copy, [xt, self.rstd], [xn], scale=self.rstd.t[:, blk:blk + 1])
        for c in range(8):
            self.tr(psT.t[:, c, :], xn.t[:, c * 128:(c + 1) * 128], self.identb.t[:], [xn, self.identb], [psT])
        for c in range(8):
            self.ts(DVE, hT_ap[:, c, :], psT.t[:, c, :], self.gs[i].t[:, c:c + 1], self.sh[i].t[:, c:c + 1],
                    ALU.mult, ALU.add, [psT, self.gs[i], self.sh[i]], [hT_tile])

    def gelu2(self, out_t, ps_t, w_t, in_t, n):
        self.act(w_t.t[:, 0:n], ps_t.t[:, 0:n], AF.Square, [ps_t], [w_t], scale=SQ_044)
        self.stt(DVE, in_t.t[:, 0:n], w_t.t[:, 0:n], 1.0, ps_t.t[:, 0:n], ALU.add, ALU.mult, [w_t, ps_t], [in_t])
        self.act(w_t.t[:, 0:n], in_t.t[:, 0:n], AF.Tanh, [in_t], [w_t], scale=GELU_C)
        self.stt(DVE, out_t.t[:, 0:n], w_t.t[:, 0:n], 1.0, ps_t.t[:, 0:n], ALU.add, ALU.mult, [w_t, ps_t], [out_t])

    def silu2(self, out_ap, out_t, ps_ap, ps_t, w_ap, w_t):
        self.act(w_ap, ps_ap, AF.Tanh, [ps_t], [w_t], scale=0.5)
        self.stt(DVE, out_ap, w_ap, 1.0, ps_ap, ALU.add, ALU.mult, [w_t, ps_t], [out_t])

    def even_layer(self, st, l):
        i = l // 2
        W_in = self.sb(st, "Win", [128, 8, AB_IN], BF16)
        W_out = self.sb(st, "Wout", [128, 8, D], BF16)
        W_sT = self.sb(st, "WsT", [128, 4, 128], BF16)
        wsrc = self.ab_w_in[i].rearrange("(kc p) n -> p kc n", p=128)
        Wg = {}
        for (c0, n) in ((512, 512), (0, 512), (2304, 512), (1536, 512), (2048, 256), (1024, 512)):
            Wg[c0] = Buf("Win_%d" % c0)
            for kc in range(0, 8, 4):
                self.dma(POOL, W_in.t[:, kc:kc + 4, c0:c0 + n], wsrc[:, kc:kc + 4, c0:c0 + n], Wg[c0], w=[Wg[c0]])
        self.dma(POOL, W_sT.t[:], self.a_w_sT[i], W_sT, w=[W_sT])
        wosrc = self.ab_w_out[i].rearrange("(kc p) n -> p kc n", p=128)
        for kc in range(0, 8, 2):
            self.dma(POOL, W_out.t[:, kc:kc + 2, :], wosrc[:, kc:kc + 2, :], W_out, w=[W_out])
        lng = self.sb(st, "lng", [128, 512])
        lnb = self.sb(st, "lnb", [128, 512])
        sink = self.sb(st, "sink", [128, 8])
        esink = self.sb(st, "esink", [128, 8])
        self.dma(SP, lng.t[:], self.rows_d[:, RM["lng%d" % i]:RM["lng%d" % i] + 512], lng, w=[lng])
        self.dma(SP, lnb.t[:], self.rows_d[:, RM["lnb%d" % i]:RM["lnb%d" % i] + 512], lnb, w=[lnb])
        self.dma(SP, sink.t[:], self.rows_d[:, RM["sink%d" % i]:RM["sink%d" % i] + 8], sink, w=[sink])
        self.act(esink.t[:], sink.t[:], AF.Exp, [sink], [esink])
        bs0 = CM["bs%d" % i]

        KT = self.sb(st, "KT", [128, 2, NBLK, 128], BF16)
        KTb = [Buf("KT%d" % b) for b in range(NBLK)]
        Vb = self.sb(st, "Vb", [128, NBLK, 2, 128], BF16)
        Vbb = [Buf("Vb%d" % b) for b in range(NBLK)]
        self.memset(POOL, Vb.t[:, :, :, 64:128], 1.0, Vbb)

        ring = lambda name, shape, dt=F32, n=2: [self.sb(st, "%s%d" % (name, k), shape, dt) for k in range(n)]
        xblk = ring("xblk", [128, D], F32, 4)
        hT = ring("hT", [128, 8, 128], BF16, 2)
        ropet = ring("rope", [128, 128], F32, 2)
        qT = ring("qT", [128, 8, 128], BF16, 3)
        sgb2 = ring("sgb2", [128, 512], F32, 3)
        ymix = ring("ymix", [128, D], BF16, 3)
        xn = self.sb(st, "xn", [128, D], BF16)
        wv, iv = self.sb(st, "wv", [128, 512]), self.sb(st, "iv", [128, 512])
        wu, iu = self.sb(st, "wu", [128, 512]), self.sb(st, "iu", [128, 512])
        gu2 = self.sb(st, "gu2", [128, 512])
        sga2 = self.sb(st, "sga2", [128, 512])
        gv = self.sb(st, "gv", [128, 512])
        vln = self.sb(st, "vln", [128, 512], BF16)
        vtmp = self.sb(st, "vtmp", [128, 512])
        lnst = self.sb(st, "lnst", [128, 16])
        lnr = self.sb(st, "lnr", [128, 4])
        lnm = self.sb(st, "lnm", [128, 4])
        lhs_ = self.sb(st, "lhs", [128, 4])
        wg = self.sb(st, "wg", [128, 512])
        r1 = self.sb(st, "r1", [128, 640])
        r2 = self.sb(st, "r2", [128, 640])
        qz = self.sb(st, "qz", [128, 8, 128], BF16)
        self.memset(POOL, qz.t[:], 0.0, [qz])
        kdup = self.sb(st, "kdup", [128, 2, 2, 64], BF16)
        PT = [[self.sb(st, "PT%d_%d" % (kh, k), [128, 512], BF16) for k in range(5)] for kh in range(2)]
        den = self.sb(st, "den", [128, 8])
        rden = self.sb(st, "rden", [128, 8])
        ybt = self.sb(st, "ybt", [128, 512])
        ymixT = self.sb(st, "ymixT", [128, 8, 128], BF16)
        ytmp = self.sb(st, "ytmp", [128, D])

        psT = self.ps(st, "psT", [128, 8, 128], BF16)
        psTb = self.ps(st, "psTb", [128, 8, 128], BF16)
        psA = [self.ps(st, "psA%d" % k, [128, 512]) for k in range(2)]
        psQ = self.ps(st, "psQ", [128, 512])
        psS = [self.ps(st, "psS%d" % k, [128, 512]) for k in range(2)]
        psV = self.ps(st, "psV", [128, 4, 128])
        psY = Tile(psV.t[:].rearrange("p h d -> p (h d)"), psV.b)

        ssq_next = self.SSQ[(l + 1) % 2]

        def proj(ps, hslot, c0, n):
            for kc in range(8):
                self.mm(ps.t[:, 0:n], hT[hslot].t[:, kc, :], W_in.t[:, kc, c0:c0 + n], kc == 0, kc == 7,
                        [hT[hslot], Wg[c0]], [ps])

        def streamN(blk):
            is_ctx = blk < NCB
            mi = 1 if is_ctx else 0
            xt = xblk[blk % 4]
            self.dma(SP, xt.t[:], self.x_src(l, blk), xt, w=[xt])
            yield
            self.act(xn.t[:], xt.t[:], AF.Copy, [xt, self.rstd], [xn], scale=self.rstd.t[:, blk:blk + 1])
            yield
            for c in range(8):
                self.tr(psT.t[:, c, :], xn.t[:, c * 128:(c + 1) * 128], self.identb.t[:], [xn, self.identb], [psT])
            yield
            h = hT[blk % 2]
            for c in range(8):
                self.ts(DVE, h.t[:, c, :], psT.t[:, c, :], self.gs[mi].t[:, c:c + 1], self.sh[mi].t[:, c:c + 1],
                        ALU.mult, ALU.add, [psT, self.gs[mi], self.sh[mi]], [h])
                if c % 4 == 3:
                    yield

        def gelu2_gen(out_t, ps_t, w_t, in_t):
            self.act(w_t.t[:], ps_t.t[:], AF.Square, [ps_t], [w_t], scale=SQ_044)
            yield
            self.stt(DVE, in_t.t[:], w_t.t[:], 1.0, ps_t.t[:], ALU.add, ALU.mult, [w_t, ps_t], [in_t])
            yield
            self.act(w_t.t[:], in_t.t[:], AF.Tanh, [in_t], [w_t], scale=GELU_C)
            yield
            self.stt(DVE, out_t.t[:], w_t.t[:], 1.0, ps_t.t[:], ALU.add, ALU.mult, [w_t, ps_t], [out_t])
            yield

        def streamS1(blk):
            hs = blk % 2
            ym = ymix[blk % 3]
            proj(psA[0], hs, 512, 512)
            yield
            proj(psA[1], hs, 0, 512)
            yield
            yield from gelu2_gen(gv, psA[0], wv, iv)
            self.memset(POOL, lnst.t[:, 4:8], 0.0, [lnst])
            gv3 = gv.t[:].rearrange("p (g d) -> p g d", g=4)
            self.P.op(DVE, lambda e: e.tensor_reduce(out=lnst.t[:, 0:4], in_=gv3, axis=AX.X, op=ALU.add), [gv], [lnst])
            yield
            for g in range(4):
                self.act(vtmp.t[:, g * 128:(g + 1) * 128], gv.t[:, g * 128:(g + 1) * 128], AF.Square, [gv], [vtmp, lnst],
                         accum=lnst.t[:, 4 + g:5 + g])
            yield
            proj(psA[0], hs, 1024, 512)
            yield
            self.ts(DVE, lnst.t[:, 8:12], lnst.t[:, 0:4], 1.0 / 128, None, ALU.mult, None, [lnst], [lnst])
            self.tt(DVE, lnst.t[:, 12:16], lnst.t[:, 8:12], lnst.t[:, 8:12], ALU.mult, [lnst], [lnst])
            self.stt(DVE, lnst.t[:, 12:16], lnst.t[:, 4:8], 1.0 / 128, lnst.t[:, 12:16], ALU.mult, ALU.subtract,
                     [lnst], [lnst])
            self.ts(DVE, lnst.t[:, 12:16], lnst.t[:, 12:16], 4.0 * EPS, None, ALU.add, None, [lnst], [lnst])
            yield
            self.act(lhs_.t[:], lnst.t[:, 12:16], AF.Sqrt, [lnst], [lhs_])
            yield
            self.P.op(DVE, lambda e: e.reciprocal(out=lnr.t[:], in_=lhs_.t[:]), [lhs_], [lnr])
            self.stt(DVE, lnm.t[:], lnst.t[:, 8:12], -1.0, lnr.t[:], ALU.mult, ALU.mult, [lnst, lnr], [lnm])
            yield
            yield from gelu2_gen(gu2, psA[1], wu, iu)
            for g in range(4):
                self.act(vtmp.t[:, g * 128:(g + 1) * 128], gv.t[:, g * 128:(g + 1) * 128], AF.Identity, [gv, lnr, lnm], [vtmp],
                         scale=lnr.t[:, g:g + 1], bias=lnm.t[:, g:g + 1])
            yield
            self.tt(DVE, vtmp.t[:], vtmp.t[:], lng.t[:], ALU.mult, [vtmp, lng], [vtmp])
            yield
            self.tt(DVE, vln.t[:], vtmp.t[:], lnb.t[:], ALU.add, [vtmp, lnb], [vln])
            yield
            self.act(wv.t[:], psA[0].t[:], AF.Tanh, [psA[0]], [wv], scale=0.5)
            yield
            self.stt(DVE, sga2.t[:], wv.t[:], 1.0, psA[0].t[:], ALU.add, ALU.mult, [wv, psA[0]], [sga2])
            yield
            for g in range(4):
                self.mm(psA[1].t[:, g * 128:(g + 1) * 128], W_sT.t[:, g, :], vln.t[:, g * 128:(g + 1) * 128], True, True,
                        [W_sT, vln], [psA[1]])
            yield
            self.stt(DVE, gu2.t[:], gu2.t[:], 0.25, sga2.t[:], ALU.mult, ALU.mult, [gu2, sga2], [gu2])
            yield
            for g in range(4):
                self.stt(DVE, ym.t[:, g * 128:(g + 1) * 128], psA[1].t[:, g * 128:(g + 1) * 128],
                         self.COLS.t[:, bs0 + g:bs0 + g + 1], gu2.t[:, g * 128:(g + 1) * 128], ALU.add, ALU.mult,
                         [psA[1], self.COLS, gu2], [ym])
                if g % 2 == 1:
                    yield

        def streamS2(blk):
            hs = blk % 2
            s3 = blk % 3
            is_ctx = blk < NCB
            rt = ropet[blk % 2]
            if not is_ctx:
                t0 = (blk - NCB) * 128
                self.dma(SP, rt.t[:], self.rope_d[t0:t0 + 128, :], rt, w=[rt])
            proj(psQ, hs, 2304, 512)
            yield
            self.act(wg.t[:], psQ.t[:], AF.Tanh, [psQ], [wg], scale=0.5)
            yield
            self.stt(DVE, sgb2[s3].t[:], wg.t[:], 1.0, psQ.t[:], ALU.add, ALU.mult, [wg, psQ], [sgb2[s3]])
            yield
            proj(psQ, hs, 1536, 512)
            yield
            if is_ctx:
                q3 = psQ.t[:, 0:512].rearrange("p (h d) -> p h d", h=8)
                self.cp(ACT, qz.t[:, 0::2, 0:64], q3[:, 0::2, :], [psQ], [qz])
                self.cp(ACT, qz.t[:, 1::2, 64:128], q3[:, 1::2, :], [psQ], [qz])
                yield
            else:
                src = psQ.t[:, 0:512].rearrange("p (h d) -> p h d", h=8)
                d1 = r1.t[:, 0:512].rearrange("p (h d) -> p h d", h=8)
                d2 = r2.t[:, 0:512].rearrange("p (h d) -> p h d", h=8)
                self.tt(DVE, d1, src, rt.t[:, 0:64].unsqueeze(1).broadcast_to([128, 8, 64]), ALU.mult, [psQ, rt], [r1])
                yield
                self.tt(DVE, d2[:, :, 0:32], src[:, :, 32:64], rt.t[:, 64:96].unsqueeze(1).broadcast_to([128, 8, 32]), ALU.mult,
                        [psQ, rt], [r2])
                self.tt(DVE, d2[:, :, 32:64], src[:, :, 0:32], rt.t[:, 96:128].unsqueeze(1).broadcast_to([128, 8, 32]), ALU.mult,
                        [psQ, rt], [r2])
                yield
                self.tt(POOL, qz.t[:, 0::2, 0:64], d1[:, 0::2, :], d2[:, 0::2, :], ALU.add, [r1, r2], [qz])
                self.tt(POOL, qz.t[:, 1::2, 64:128], d1[:, 1::2, :], d2[:, 1::2, :], ALU.add, [r1, r2], [qz])
                yield
            proj(psQ, hs, 2048, 256)
            yield
            self.cp(ACT, Vb.t[:, blk, :, 0:64], psQ.t[:, 128:256].rearrange("p (h d) -> p h d", h=2), [psQ], [Vbb[blk]])
            k3 = psQ.t[:, 0:128].rearrange("p (h d) -> p h d", h=2)
            if is_ctx:
                for dup in range(2):
                    self.cp(ACT, kdup.t[:, :, dup, :], k3, [psQ], [kdup])
                yield
            else:
                e1 = r1.t[:, 512:640].rearrange("p (h d) -> p h d", h=2)
                e2 = r2.t[:, 512:640].rearrange("p (h d) -> p h d", h=2)
                self.tt(DVE, e1, k3, rt.t[:, 0:64].unsqueeze(1).broadcast_to([128, 2, 64]), ALU.mult, [psQ, rt], [r1])
                self.tt(DVE, e2[:, :, 0:32], k3[:, :, 32:64], rt.t[:, 64:96].unsqueeze(1).broadcast_to([128, 2, 32]), ALU.mult,
                        [psQ, rt], [r2])
                self.tt(DVE, e2[:, :, 32:64], k3[:, :, 0:32], rt.t[:, 96:128].unsqueeze(1).broadcast_to([128, 2, 32]), ALU.mult,
                        [psQ, rt], [r2])
                yield
                for dup in range(2):
                    self.tt(POOL, kdup.t[:, :, dup, :], e1, e2, ALU.add, [r1, r2], [kdup])
                yield
            for h in range(8):
                self.tr(psTb.t[:, h, :], qz.t[:, h, :], self.identb.t[:], [qz, self.identb], [psTb])
            self.cp(ACT, qT[s3].t[:], psTb.t[:], [psTb], [qT[s3]])
            yield
            for kh in range(2):
                self.tr(psTb.t[:, kh, :], kdup.t[:, kh, :, :].rearrange("p a d -> p (a d)"), self.identb.t[:],
                        [kdup, self.identb], [psTb])
            self.cp(ACT, KT.t[:, :, blk, :], psTb.t[:, 0:2, :], [psTb], [KTb[blk]])
            yield

        def streamB(blk):
            s3 = blk % 3
            is_ctx = blk < NCB
            ym = ymix[s3]
            xt = xblk[blk % 4]
            if is_ctx:
                kbs = [(0, None), (1, None)]
            else:
                kbs = [(0, None), (1, None)]
                if blk - 1 >= NCB:
                    kbs.append((blk - 1, self.mbprev))
                kbs.append((blk, None))
                if blk + 1 < NBLK:
                    kbs.append((blk + 1, self.mbnext))
            nk = len(kbs)
            for kh in range(2):
                for ki, (kb, mb) in enumerate(kbs):
                    pss = psS[(kh * 5 + ki) % 2]
                    if mb is not None:
                        self.mmx(pss.t[:], self.identb.t[:], mb.t[:].rearrange("p h q -> p (h q)"), True, False,
                                 [self.identb, mb], [pss])
                    for hl in range(4):
                        h = 4 * kh + hl
                        if mb is not None:
                            self.mmx(pss.t[:, hl * 128:(hl + 1) * 128], KT.t[:, kh, kb, :], qT[s3].t[:, h, :], False, hl == 3,
                                     [KTb[kb], qT[s3]], [pss])
                        else:
                            self.mm(pss.t[:, hl * 128:(hl + 1) * 128], KT.t[:, kh, kb, :], qT[s3].t[:, h, :],
                                    True, True, [KTb[kb], qT[s3]], [pss])
                    yield
                    pt = PT[kh][ki]
                    self.act(pt.t[:], pss.t[:], AF.Exp, [pss], [pt], scale=0.125)
                    yield
                for hl in range(4):
                    for ki, (kb, mb) in enumerate(kbs):
                        self.mm(psV.t[:, hl, 0:65], PT[kh][ki].t[:, hl * 128:(hl + 1) * 128], Vb.t[:, kb, kh, 0:65],
                                ki == 0, ki == nk - 1, [PT[kh][ki], Vbb[kb]], [psV])
                    if hl % 2 == 1:
                        yield
                self.tt(DVE, den.t[:, 4 * kh:4 * kh + 4], psV.t[:, :, 64], esink.t[:, 4 * kh:4 * kh + 4], ALU.add,
                        [psV, esink], [den])
                self.P.op(DVE, lambda e, kh=kh: e.reciprocal(out=rden.t[:, 4 * kh:4 * kh + 4], in_=den.t[:, 4 * kh:4 * kh + 4]),
                          [den], [rden])
                yv = ybt.t[:, kh * 256:(kh + 1) * 256].rearrange("p (h d) -> p h d", h=4)
                self.stt(DVE, yv, psV.t[:, :, 0:64], 0.5, rden.t[:, 4 * kh:4 * kh + 4].unsqueeze(2).broadcast_to([128, 4, 64]),
                         ALU.mult, ALU.mult, [psV, rden], [ybt])
                yield
            self.tt(POOL, ym.t[:, 512:1024], ybt.t[:], sgb2[s3].t[:], ALU.mult, [ybt, sgb2[s3]], [ym])
            yield
            for c in range(8):
                self.tr(psTb.t[:, c, :], ym.t[:, c * 128:(c + 1) * 128], self.identb.t[:], [ym, self.identb], [psTb])
            self.cp(ACT, ymixT.t[:], psTb.t[:], [psTb], [ymixT])
            yield
            g = self.gate[1]
            for n in range(2):
                for kc in range(8):
                    self.mm(psY.t[:], ymixT.t[:, kc, :], W_out.t[:, kc, n * 512:(n + 1) * 512], kc == 0, kc == 7,
                            [ymixT, W_out], [psY])
                yield
                if is_ctx:
                    self.tt(DVE, ytmp.t[:, n * 512:(n + 1) * 512], psY.t[:], g.t[:, n * 512:(n + 1) * 512], ALU.mult, [psY, g], [ytmp])
                else:
                    self.tt(DVE, xt.t[:, n * 512:(n + 1) * 512], psY.t[:], xt.t[:, n * 512:(n + 1) * 512], ALU.add, [psY, xt], [xt])
                yield
            if is_ctx:
                self.tt(POOL, xt.t[:], ytmp.t[:], xt.t[:], ALU.add, [ytmp, xt], [xt])
                yield
            self.act(ytmp.t[:], xt.t[:], AF.Square, [xt], [ytmp, ssq_next], accum=ssq_next.t[:, blk:blk + 1])
            self.dma(SP, self.x_dst(l, blk), xt.t[:], xt, r=[xt])
            yield

        class _Dry:
            def op(self, *a, **k):
                pass

            def dma(self, *a, **k):
                pass

        def count_steps(gen_fn, blk):
            real = self.P
            self.P = _Dry()
            try:
                n = sum(1 for _ in gen_fn(blk))
            finally:
                self.P = real
            return n + 1

        def run_streams(items):
            st_ = [[g_, 0, float(n_)] for g_, n_ in items]
            while st_:
                st_.sort(key=lambda e_: (e_[1] + 1) / e_[2])
                e_ = st_[0]
                try:
                    next(e_[0])
                    e_[1] += 1
                except StopIteration:
                    st_.remove(e_)

        nsteps = {}

        def item(fn, blk):
            key = (fn.__name__, blk < NCB, blk == NCB, blk == NBLK - 1)
            if key not in nsteps:
                nsteps[key] = count_steps(fn, blk)
            return (fn(blk), nsteps[key])

        for t in range(NBLK + 3):
            items = []
            if t - 3 >= 0:
                items.append(item(streamB, t - 3))
            if 0 <= t - 1 < NBLK:
                items.append(item(streamS1, t - 1))
                items.append(item(streamS2, t - 1))
            if t < NBLK:
                items.append(item(streamN, t))
            run_streams(items)
            if t == NCB + 2:
                for kc in range(8):
                    self.tt(POOL if kc % 2 else DVE, W_out.t[:, kc, :], W_out.t[:, kc, :], self.gate[0].t[:], ALU.mult,
                            [W_out, self.gate[0]], [W_out])
            if self.ck("E%d" % t):
                return


    def odd_layer(self, st, l):
        i = l // 2
        need_ctx = l < DEPTH - 1
        last = (l == DEPTH - 1) and self.do_final
        NT = 256
        NSB = L // NT
        W_in = self.sb(st, "cWin", [128, 8, 2 * D], BF16)
        W_a = self.sb(st, "cWa", [128, 2, 4, 2, 256], BF16)
        W_i = self.sb(st, "cWi", [128, 2, 4, 2, 256], BF16)
        W_out = self.sb(st, "cWout", [128, 8, D], BF16)
        wsrc = self.c_w_in[i].rearrange("(kc p) n -> p kc n", p=128)
        Wxr, Wgg = Buf("cWin_xr"), Buf("cWin_g")
        for (c0, bb) in ((0, Wxr), (D, Wgg)):
            for kc in range(0, 8, 4):
                self.dma(POOL, W_in.t[:, kc:kc + 4, c0:c0 + D], wsrc[:, kc:kc + 4, c0:c0 + D], bb, w=[bb])
        for d in range(2):
            self.dma(POOL, W_a.t[:, d], self.c_w_a[i, d].rearrange("h (ic p) j -> p h ic j", p=128), W_a, w=[W_a])
            self.dma(POOL, W_i.t[:, d], self.c_w_i[i, d].rearrange("h (ic p) j -> p h ic j", p=128), W_i, w=[W_i])
        wosrc = self.c_w_out[i].rearrange("(kc p) n -> p kc n", p=128)
        for kc in range(0, 8, 2):
            self.dma(POOL, W_out.t[:, kc:kc + 2, :], wosrc[:, kc:kc + 2, :], W_out, w=[W_out])
        coefh = self.sb(st, "coefh", [128, 2, 8])
        hba = self.sb(st, "hba", [128, 2, 8])
        hbi = self.sb(st, "hbi", [128, 2, 8])
        for d in range(2):
            lam0 = CM["lam%d_%d" % (i, d)]
            self.act(coefh.t[:, d, :], self.COLS.t[:, lam0:lam0 + 8], AF.Exp, [self.COLS], [coefh], scale=-1.0)
            self.act(coefh.t[:, d, :], coefh.t[:, d, :], AF.Ln, [coefh], [coefh], bias=1.0)
            self.ts(DVE, coefh.t[:, d, :], coefh.t[:, d, :], -4.0, None, ALU.mult, None, [coefh], [coefh])
            b0 = CM["ba%d_%d" % (i, d)]
            self.ts(DVE, hba.t[:, d, :], self.COLS.t[:, b0:b0 + 8], 0.5, None, ALU.mult, None, [self.COLS], [hba])
            b0 = CM["bi%d_%d" % (i, d)]
            self.ts(DVE, hbi.t[:, d, :], self.COLS.t[:, b0:b0 + 8], 0.5, None, ALU.mult, None, [self.COLS], [hbi])
        cw0 = [CM["cw%d_%d" % (i, j)] for j in range(4)]
        cb0 = CM["cb%d" % i]
        DG = self.sb(st, "DG", [128, 4, 8, 128], BF16)
        for j in range(4):
            for c in range(8):
                self.ts(DVE, DG.t[:, j, c, :], self.identf.t[:], self.COLS.t[:, cw0[j] + c:cw0[j] + c + 1], None, ALU.mult, None,
                        [self.identf, self.COLS], [DG])
        if last:
            fg = self.sb(st, "fg", [128, D])
            self.dma(SP, fg.t[:], self.rows_d[:, RM["fg"]:RM["fg"] + D], fg, w=[fg])

        xb = [self.sb(st, "xb%d" % k, [128, D]) for k in range(4)]
        xbi = [0]
        xn = self.sb(st, "cxn", [128, D], BF16)
        hTr = [self.sb(st, "chT%d" % k, [128, 8, NT], BF16) for k in range(2)]
        zr = [self.sbc(st, "z0", [128, 8, NT], F32, 8), self.sbc(st, "z1", [128, 8, NT], F32, 8)]
        zbr = [self.sbc(st, "zb0", [128, 8, NT], BF16, 8), self.sbc(st, "zb1", [128, 8, NT], BF16, 8)]
        sgr = [self.sbc(st, "sg0", [128, 8, NT], BF16, 8)]
        gth_ = self.sb(st, "gth", [128, NT])
        gth = [gth_, gth_]
        A = self.sbc(st, "A", [128, 8, NT], F32, 8)
        Wq = self.sbc(st, "Wq", [128, 8, NT], F32, 8)
        TI = self.sbc(st, "TI", [128, 8, NT], F32, 8)
        thr = [self.sb(st, "thr%d" % k, [128, NT]) for k in range(2)]
        S0r = [self.sbc(st, "S00", [128, 8, NT], F32, 8)]
        S1 = self.sbc(st, "S1", [128, 8, NT], F32, 8)
        carry = [self.sbc(st, "carry%d" % k, [128, 8], F32, 8) for k in range(2)]
        ytmp = self.sb(st, "cytmp", [128, D])
        ssq1 = self.sb(st, "ssq1", [128, 2])
        v1 = self.sb(st, "v1", [128, 2])
        r1_ = self.sb(st, "r1", [128, 2])
        hs1 = self.sb(st, "hs1", [128, 2])
        psT = [self.ps(st, "cpsT%d" % k, [128, 8, 128], BF16) for k in range(2)]
        psP = [self.ps(st, "cpsP%d" % k, [128, 512]) for k in range(3)]
        psG = [self.ps(st, "cpsG%d" % k, [128, 512]) for k in range(3)]
        rot = [0, 0]

        def nextP():
            rot[0] += 1
            return psP[rot[0] % 3]

        def nextG():
            rot[1] += 1
            return psG[rot[1] % 3]

        ssq_next = self.SSQ[(l + 1) % 2]
        xbase = self.X[l % 2] if l != self.layers[0] else self.dbg_xin
        ZSb = [Buf("ZS%d" % k) for k in range(NSB)]
        SFb = [Buf("SF%d" % k) for k in range(NSB)]
        SGb = [Buf("SG%d" % k) for k in range(NSB)]
        ZBb = [Buf("ZB%d" % k) for k in range(NSB)]
        wscaled = [False]

        def scale_wout():
            for kc in range(8):
                self.tt(POOL if kc % 2 else DVE, W_out.t[:, kc, :], W_out.t[:, kc, :], self.gate[0].t[:], ALU.mult,
                        [W_out, self.gate[0]], [W_out])
            wscaled[0] = True

        def next_xb():
            xbi[0] += 1
            return xb[xbi[0] % 4]

        def x_load(blk):
            xt = next_xb()
            self.dma(SP, xt.t[:], xbase[blk * 128:(blk + 1) * 128, :], xt, w=[xt])
            return xt

        def g_load_norm(blk0, nb, is_ctx, hT, xts):
            i_ = 1 if is_ctx else 0
            for tb in range(nb):
                xt = xts[tb]
                self.ts(DVE, xn.t[:], xt.t[:], self.rstd.t[:, blk0 + tb:blk0 + tb + 1], None, ALU.mult, None,
                        [xt, self.rstd], [xn])
                yield
                pT = psT[tb % 2]
                for c in range(8):
                    self.tr(pT.t[:, c, :], xn.t[:, c * 128:(c + 1) * 128], self.identb.t[:], [xn, self.identb], [pT])
                yield
                for c in range(8):
                    self.ts(DVE, hT.t[:, c, tb * 128:(tb + 1) * 128], pT.t[:, c, :], self.gs[i_].t[:, c:c + 1],
                            self.sh[i_].t[:, c:c + 1], ALU.mult, ALU.add, [pT, self.gs[i_], self.sh[i_]], [hT])
                    if c % 4 == 3:
                        yield

        def g_project(nt, hT, xr_t, want_g, sg):
            for c in range(8):
                p = nextP()
                for kc in range(8):
                    self.mm(p.t[:, 0:nt], W_in.t[:, kc, c * 128:(c + 1) * 128], hT.t[:, kc, 0:nt], kc == 0, kc == 7,
                            [Wxr, hT], [p])
                yield
                self.cp(DVE, xr_t.t[:, c, 2:2 + nt], p.t[:, 0:nt], [p], [xr_t.bs[c]])
                yield
            if want_g:
                for c in range(8):
                    p = nextP()
                    for kc in range(8):
                        self.mm(p.t[:, 0:nt], W_in.t[:, kc, D + c * 128:D + (c + 1) * 128], hT.t[:, kc, 0:nt], kc == 0, kc == 7,
                                [Wgg, hT], [p])
                    yield
                    w_ = gth[c % 2]
                    self.act(w_.t[:, 0:nt], p.t[:, 0:nt], AF.Tanh, [p], [w_], scale=0.5)
                    yield
                    self.stt(DVE, sg.t[:, c, 0:nt], w_.t[:, 0:nt], 1.0, p.t[:, 0:nt], ALU.add, ALU.mult, [w_, p], [sg.bs[c]])
                    yield

        def g_conv(nt, xr_t, z, zb):
            for c in range(8):
                pz = nextG()
                for j in range(4):
                    self.mm(pz.t[:, 0:nt], DG.t[:, j, c, :], xr_t.t[:, c, j:j + nt], j == 0, j == 3, [DG, xr_t.bs[c]], [pz])
                yield
                self.act(zb.t[:, c, 0:nt], pz.t[:, 0:nt], AF.Identity, [pz, self.COLS], [zb.bs[c]],
                         bias=self.COLS.t[:, cb0 + c:cb0 + c + 1])
                yield
                self.ts(DVE, z.t[:, c, 0:nt], pz.t[:, 0:nt], self.COLS.t[:, cb0 + c:cb0 + c + 1], None, ALU.add, None,
                        [pz, self.COLS], [z.bs[c]])
                yield

        def g_gates_scan(nt, d, use_carry, z, zb, Sout, a2_eng, mid=None):
            for jc in range(8):
                hh = jc // 2
                jl = jc % 2
                pa = nextG()
                for ic in range(2):
                    self.mm(pa.t[:, 0:nt], W_a.t[:, d, hh, ic, jl * 128:(jl + 1) * 128], zb.t[:, 2 * hh + ic, 0:nt],
                            ic == 0, ic == 1, [W_a, zb.bs[2 * hh + ic]], [pa])
                pi_ = nextG()
                for ic in range(2):
                    self.mm(pi_.t[:, 0:nt], W_i.t[:, d, hh, ic, jl * 128:(jl + 1) * 128], zb.t[:, 2 * hh + ic, 0:nt],
                            ic == 0, ic == 1, [W_i, zb.bs[2 * hh + ic]], [pi_])
                yield
                t_ = thr[jc % 2]
                self.act(t_.t[:, 0:nt], pa.t[:, 0:nt], AF.Tanh, [pa, hba], [t_], scale=0.5, bias=hba.t[:, d, jc:jc + 1])
                self.act(A.t[:, jc, 0:nt], t_.t[:, 0:nt], AF.Exp, [t_, coefh], [A.bs[jc]], scale=coefh.t[:, d, jc:jc + 1],
                         bias=coefh.t[:, d, jc:jc + 1])
                yield
                self.act(TI.t[:, jc, 0:nt], pi_.t[:, 0:nt], AF.Tanh, [pi_, hbi], [TI.bs[jc]], scale=0.5,
                         bias=hbi.t[:, d, jc:jc + 1])
                if a2_eng == ACT:
                    self.act(Wq.t[:, jc, 0:nt], A.t[:, jc, 0:nt], AF.Square, [A.bs[jc]], [Wq.bs[jc]])
                else:
                    self.tt(POOL, Wq.t[:, jc, 0:nt], A.t[:, jc, 0:nt], A.t[:, jc, 0:nt], ALU.mult, [A.bs[jc]], [Wq.bs[jc]])
                yield
            if mid is not None:
                yield from mid()
            for jc in range(8):
                self.act(Wq.t[:, jc, 0:nt], Wq.t[:, jc, 0:nt], AF.Sqrt, [Wq.bs[jc]], [Wq.bs[jc]], scale=-0.25, bias=0.25)
                if jc % 2 == 1:
                    yield
            for jc in range(8):
                self.tt(POOL, Wq.t[:, jc, 0:nt], Wq.t[:, jc, 0:nt], z.t[:, jc, 0:nt], ALU.mult, [Wq.bs[jc], z.bs[jc]], [Wq.bs[jc]])
                if jc % 2 == 1:
                    yield
            for jc in range(8):
                self.stt(DVE, TI.t[:, jc, 0:nt], TI.t[:, jc, 0:nt], 1.0, Wq.t[:, jc, 0:nt], ALU.add, ALU.mult,
                         [TI.bs[jc], Wq.bs[jc]], [TI.bs[jc]])
                if use_carry:
                    ini = carry[d].t[:, jc:jc + 1]
                    ideps = [carry[d].bs[jc]]
                else:
                    ini = 0.0
                    ideps = []
                if d == 0:
                    o_, a_, b_ = Sout.t[:, jc, 0:nt], A.t[:, jc, 0:nt], TI.t[:, jc, 0:nt]
                else:
                    o_, a_, b_ = Sout.t[:, jc, 0:nt][:, ::-1], A.t[:, jc, 0:nt][:, ::-1], TI.t[:, jc, 0:nt][:, ::-1]
                self.P.op(DVE, lambda e, o_=o_, a_=a_, b_=b_, ini=ini: e.tensor_tensor_scan(
                    out=o_, data0=a_, data1=b_, initial=ini, op0=ALU.mult, op1=ALU.add),
                    [A.bs[jc], TI.bs[jc]] + ideps, [Sout.bs[jc]], dur=0.2 + nt * (0.0022 if d == 0 else 0.0045))
                col = nt - 1 if d == 0 else 0
                self.cp(DVE, carry[d].t[:, jc:jc + 1], Sout.t[:, jc, col:col + 1], [Sout.bs[jc]], [carry[d].bs[jc]])
                yield

        def g_combine(nt, S0, sg, ymT):
            for c in range(0, 8, 2):
                self.tt(POOL, S0.t[:, c:c + 2, 0:nt], S0.t[:, c:c + 2, 0:nt], S1.t[:, c:c + 2, 0:nt], ALU.add,
                        S0.bs[c:c + 2] + S1.bs[c:c + 2], S0.bs[c:c + 2])
            yield
            self.stt(DVE, ymT.t[:, :, 0:nt], S0.t[:, :, 0:nt], 0.5, sg.t[:, :, 0:nt], ALU.mult, ALU.mult, S0.bs + sg.bs, [ymT])
            yield

        def g_out(tb, blk, gate_t, xt, ymT, final):
            for n in range(2):
                py = nextP()
                for kc in range(8):
                    self.mm(py.t[:], ymT.t[:, kc, tb * 128:(tb + 1) * 128], W_out.t[:, kc, n * 512:(n + 1) * 512],
                            kc == 0, kc == 7, [ymT, W_out], [py])
                yield
                if gate_t is None:
                    self.tt(DVE, xt.t[:, n * 512:(n + 1) * 512], py.t[:], xt.t[:, n * 512:(n + 1) * 512], ALU.add, [py, xt], [xt])
                else:
                    self.tt(DVE, ytmp.t[:, n * 512:(n + 1) * 512], py.t[:], gate_t.t[:, n * 512:(n + 1) * 512], ALU.mult,
                            [py, gate_t], [ytmp])
                yield
            if gate_t is not None:
                self.tt(POOL, xt.t[:], ytmp.t[:], xt.t[:], ALU.add, [ytmp, xt], [xt])
                yield
            if final:
                self.act(ytmp.t[:], xt.t[:], AF.Square, [xt], [ytmp, ssq1], accum=ssq1.t[:, tb:tb + 1])
            else:
                self.act(ytmp.t[:], xt.t[:], AF.Square, [xt], [ytmp, ssq_next], accum=ssq_next.t[:, blk:blk + 1])
                self.dma(SP, self.x_dst(l, blk), xt.t[:], xt, r=[xt])
            yield

        def g_final(blk0, xts):
            self.ts(DVE, v1.t[:], ssq1.t[:], 1.0 / D, EPS, ALU.mult, ALU.add, [ssq1], [v1])
            self.act(hs1.t[:], v1.t[:], AF.Sqrt, [v1], [hs1])
            yield
            self.P.op(DVE, lambda e: e.reciprocal(out=r1_.t[:], in_=hs1.t[:]), [hs1], [r1_])
            self.memset(POOL, ssq1.t[:], 0.0, [ssq1])
            for tb, xt in enumerate(xts):
                self.stt(DVE, xt.t[:], xt.t[:], r1_.t[:, tb:tb + 1], fg.t[:], ALU.mult, ALU.mult, [xt, r1_, fg], [xt])
                blk = blk0 + tb
                self.dma(SP, self.out_d[(blk - NCB) * 128:(blk - NCB + 1) * 128, :], xt.t[:], xt, r=[xt])
                yield

        def run_streams(gens):
            active = list(gens)
            while active:
                for g_ in list(active):
                    try:
                        next(g_)
                    except StopIteration:
                        active.remove(g_)

        def seq(*gens):
            for g_ in gens:
                yield from g_

        z = zr[0]
        sg0 = sgr[0]
        S0 = S0r[0]
        with ExitStack() as p1:
            XR = [self.sbc(p1, "XR%d" % k, [128, 8, NT + 3], BF16, 8) for k in range(3)]
            nt = LC
            self.memset(POOL, XR[0].t[:, :, 0:2], 0.0, XR[0].bs)
            self.memset(POOL, XR[0].t[:, :, 2 + nt:3 + nt], 0.0, XR[0].bs)
            xts = [x_load(tb) for tb in range(NCB)]
            run_streams([seq(g_load_norm(0, NCB, True, hTr[0], xts), g_project(nt, hTr[0], XR[0], need_ctx, sg0),
                             g_conv(nt, XR[0], z, zbr[0]))])
            run_streams([g_gates_scan(nt, 0, False, z, zbr[0], S0, POOL)])
            run_streams([g_gates_scan(nt, 1, False, z, zbr[0], S1, POOL)])
            if need_ctx:
                run_streams([g_combine(nt, S0, sg0, hTr[1])])
                for tb in range(NCB):
                    run_streams([g_out(tb, tb, self.gate[1], xts[tb], hTr[1], False)])
            scale_wout()

            pre = {}

            def NS(it):
                yield from g_load_norm(NCB + it * 2, 2, False, hTr[it % 2], pre.pop(it))

            def PJ(it):
                cur = XR[it % 3]
                prv = XR[(it - 1) % 3]
                hT = hTr[it % 2]
                yield from g_project(NT, hT, cur, True, sg0)
                sgd = self.SG[:, it * NT:(it + 1) * NT].rearrange("(c p) t -> p c t", p=128)
                self.dma(SP, sgd, sg0.t[:], sg0.bs[0], r=sg0.bs, w=[SGb[it]])
                if it == 0:
                    self.memset(POOL, cur.t[:, :, 0:2], 0.0, cur.bs)
                else:
                    self.cp(POOL, cur.t[:, :, 0:2], prv.t[:, :, NT:NT + 2], prv.bs, cur.bs)
                    self.cp(POOL, prv.t[:, :, NT + 2:NT + 3], cur.t[:, :, 2:3], cur.bs, prv.bs)
                if it == NSB - 1:
                    self.memset(POOL, cur.t[:, :, NT + 2:NT + 3], 0.0, cur.bs)
                yield

            def FWD(sb_):
                xr_t = XR[sb_ % 3]
                z = zr[sb_ % 2]
                zb = zbr[sb_ % 2]
                yield from g_conv(NT, xr_t, z, zb)
                zd = self.ZS[:, sb_ * NT:(sb_ + 1) * NT].rearrange("(c p) t -> p c t", p=128)
                self.dma(SP, zd, z.t[:], z.bs[0], r=z.bs, w=[ZSb[sb_]])
                zbd = self.ZB[:, sb_ * NT:(sb_ + 1) * NT].rearrange("(c p) t -> p c t", p=128)
                self.dma(SP, zbd, zb.t[:], zb.bs[0], r=zb.bs, w=[ZBb[sb_]])
                yield
                yield from g_gates_scan(NT, 0, True, z, zb, S0, POOL)
                sfd = self.SF[:, sb_ * NT:(sb_ + 1) * NT].rearrange("(c p) t -> p c t", p=128)
                self.dma(SP, sfd, S0.t[:], S0.bs[0], r=S0.bs, w=[SFb[sb_]])
                yield

            pre[0] = [x_load(NCB), x_load(NCB + 1)]
            self.run_prop([(NS(0), 8)])
            for it in range(NSB + 2):
                if it + 1 < NSB:
                    pre[it + 1] = [x_load(NCB + (it + 1) * 2), x_load(NCB + (it + 1) * 2 + 1)]
                items = []
                if it - 2 >= 0:
                    items.append((FWD(it - 2), 66))
                if it < NSB:
                    items.append((PJ(it), 41))
                if it + 1 < NSB:
                    items.append((NS(it + 1), 8))
                self.run_prop(items)
            self.P.barrier()

        with ExitStack() as p2:
            sgr.append(self.sbc(p2, "sg1", [128, 8, NT], BF16, 8))
            S0r.append(self.sbc(p2, "S01", [128, 8, NT], F32, 8))

            def loads(sb_):
                k = sb_ % 2
                zd = self.ZS[:, sb_ * NT:(sb_ + 1) * NT].rearrange("(c p) t -> p c t", p=128)
                self.dma(SP, zr[k].t[:], zd, zr[k].bs[0], r=[ZSb[sb_]], w=zr[k].bs)
                sfd = self.SF[:, sb_ * NT:(sb_ + 1) * NT].rearrange("(c p) t -> p c t", p=128)
                self.dma(SP, S0r[k].t[:], sfd, S0r[k].bs[0], r=[SFb[sb_]], w=S0r[k].bs)
                sgd = self.SG[:, sb_ * NT:(sb_ + 1) * NT].rearrange("(c p) t -> p c t", p=128)
                self.dma(SP, sgr[k].t[:], sgd, sgr[k].bs[0], r=[SGb[sb_]], w=sgr[k].bs)
                zbd = self.ZB[:, sb_ * NT:(sb_ + 1) * NT].rearrange("(c p) t -> p c t", p=128)
                self.dma(SP, zbr[k].t[:], zbd, zbr[k].bs[0], r=[ZBb[sb_]], w=zbr[k].bs)

            def G(sb_):
                k = sb_ % 2
                if sb_ - 1 >= 0:
                    loads(sb_ - 1)
                yield
                yield from g_gates_scan(NT, 1, True, zr[k], zbr[k], S1, POOL)
                yield from g_combine(NT, S0r[k], sgr[k], hTr[k])

            def O(sb_):
                k = sb_ % 2
                blk0 = NCB + sb_ * 2
                xts = [x_load(blk0 + tb) for tb in range(2)]
                yield
                for tb in range(2):
                    yield from g_out(tb, blk0 + tb, None, xts[tb], hTr[k], last)
                if last:
                    yield from g_final(blk0, xts)

            self.memset(POOL, ssq1.t[:], 0.0, [ssq1])
            loads(NSB - 1)
            for sb_ in range(NSB - 1, -2, -1):
                items = []
                if sb_ >= 0:
                    g_ = G(sb_)
                    for _ in range(25):
                        next(g_)
                    items.append((g_, 22))
                if sb_ + 1 < NSB:
                    items.append((O(sb_ + 1), 15))
                self.run_prop(items)
            self.P.barrier()


_CACHE = {}


def make_in_maps(inp, n_cores=8):
    f = lambda a: np.ascontiguousarray(np.asarray(a, np.float32))
    rows = np.zeros((128, RM.n), np.float32)
    for l in range(DEPTH):
        rows[:, RM["bg%d" % l]:RM["bg%d" % l] + D] = f(inp["b_mod"])[l, 2 * D:3 * D][None, :]
    for i in range(2):
        rows[:, RM["lng%d" % i]:RM["lng%d" % i] + 512] = f(inp["a_ln_g"])[i][None, :]
        rows[:, RM["lnb%d" % i]:RM["lnb%d" % i] + 512] = f(inp["a_ln_b"])[i][None, :]
        rows[:, RM["sink%d" % i]:RM["sink%d" % i] + 8] = f(inp["b_sink"])[i][None, :]
    rows[:, RM["fg"]:RM["fg"] + D] = f(inp["final_g"])[None, :]
    rope = rope_table()
    a_w_sT = np.ascontiguousarray(np.transpose(f(inp["a_w_s"]), (0, 3, 1, 2)))
    shared = {
        "rows": rows, "rope": rope, "w_mod": f(inp["w_mod"]), "ab_w_in": f(inp["ab_w_in"]),
        "ab_w_out": f(inp["ab_w_out"]), "a_w_sT": a_w_sT, "c_w_in": f(inp["c_w_in"]),
        "c_w_a": f(inp["c_w_a"]), "c_w_i": f(inp["c_w_i"]), "c_w_out": f(inp["c_w_out"]),
    }
    maps = []
    for b in range(n_cores):
        cols = np.zeros((128, CM.n), np.float32)
        for l in range(DEPTH):
            cols[:, CM["ng%d" % l]:CM["ng%d" % l] + 8] = col8(inp["norm_g"][l])
            cols[:, CM["bsh%d" % l]:CM["bsh%d" % l] + 8] = col8(inp["b_mod"][l, 0:D])
            cols[:, CM["bsc%d" % l]:CM["bsc%d" % l] + 8] = col8(inp["b_mod"][l, D:2 * D])
        cols[:, CM["c"]:CM["c"] + 8] = col8(inp["c"][b])
        cols[:, CM["cctx"]:CM["cctx"] + 8] = col8(inp["c_ctx"])
        for i in range(2):
            for j in range(4):
                cols[:, CM["cw%d_%d" % (i, j)]:CM["cw%d_%d" % (i, j)] + 8] = col8(inp["c_conv_w"][i, j])
            cols[:, CM["cb%d" % i]:CM["cb%d" % i] + 8] = col8(inp["c_conv_b"][i])
            for d in range(2):
                cols[:, CM["ba%d_%d" % (i, d)]:CM["ba%d_%d" % (i, d)] + 8] = col8(inp["c_b_a"][i, d])
                cols[:, CM["bi%d_%d" % (i, d)]:CM["bi%d_%d" % (i, d)] + 8] = col8(inp["c_b_i"][i, d])
                cols[:, CM["lam%d_%d" % (i, d)]:CM["lam%d_%d" % (i, d)] + 8] = col8(inp["c_lam"][i, d])
            cols[:, CM["bs%d" % i]:CM["bs%d" % i] + 4] = np.asarray(inp["a_b_s"][i], np.float32).T
        m = dict(shared)
        m["cols"] = cols
        m["x"] = f(inp["x"][b])
        m["ctx"] = f(inp["ctx"][b])
        maps.append(m)
    return maps


def kernel(**inputs):
    if "nc" not in _CACHE:
        _CACHE["nc"] = Builder().build()
    nc = _CACHE["nc"]
    maps = make_in_maps(inputs, 8)
    res = run_bass_kernel_spmd(nc, maps, core_ids=list(range(8)))
    out = np.stack([np.asarray(r["out"], np.float32) for r in res.results], axis=0)
    return out
```

```python
import math
from contextlib import ExitStack

import numpy as np
import concourse.bass as bass
import concourse.mybir as mybir
from concourse.bass_utils import run_bass_kernel_spmd

F32 = mybir.dt.float32
BF16 = mybir.dt.bfloat16
ALU = mybir.AluOpType
AF = mybir.ActivationFunctionType
AX = mybir.AxisListType

D = 1024
L = 4096
LC = 256
NLB = L // 128
NCB = LC // 128
NBLK = NLB + NCB
DEPTH = 4
EPS = 1e-6
AB_IN = 2816
GELU_C = 0.7978845608028654
SQ_044 = math.sqrt(0.044715)

PE, ACT, DVE, POOL, SP = "tensor", "scalar", "vector", "gpsimd", "sync"
ENGS = (PE, ACT, DVE, POOL, SP)


class Buf:
    __slots__ = ("name", "writer", "readers", "sem", "excl")

    def __init__(self, name, excl=False):
        self.name = name
        self.writer = None
        self.readers = {}
        self.sem = None
        self.excl = excl


class Op:
    __slots__ = ("eng", "fn", "deps", "odeps", "signal", "sig", "isdma", "key", "waits", "idx", "dur", "lat", "seg",
                 "nin", "succ", "rt", "fin", "tbl")

    def __init__(self, eng, fn, isdma=False):
        self.eng = eng
        self.fn = fn
        self.deps = []
        self.odeps = []
        self.signal = False
        self.sig = None
        self.isdma = isdma
        self.key = eng
        self.waits = None
        self.idx = 0
        self.dur = 0.3
        self.lat = 0.0
        self.seg = 0
        self.nin = 0
        self.succ = None
        self.rt = 0.0
        self.fin = 0.0
        self.tbl = None


def _b(x):
    return x if isinstance(x, Buf) else x.b


class Tile:
    __slots__ = ("t", "b", "bs")

    def __init__(self, t, b, bs=None):
        self.t = t
        self.b = b
        self.bs = bs


class Prog:
    SCHED = True
    SEM_LAT = 1.1

    def __init__(self, nc, stack):
        self.nc = nc
        self.stack = stack
        self.ops = {e: [] for e in ENGS}
        self.all_ops = []
        self.esem = {e: stack.enter_context(nc.semaphore("es_" + e)) for e in ENGS}
        self.sem_pool = []
        self.sem_live = []
        self.nsem = 0
        self.dma_since_barrier = []
        self.all_dma_sigs = {}
        self.ndma = 0
        self.seg = 0
        self.seg_deps = {0: []}

    def _deps(self, op, reads, writes):
        deps = op.deps
        odeps = op.odeps
        for x in reads:
            b = _b(x)
            w = b.writer
            if w is not None:
                deps.append(w)
            if b.excl:
                for r in b.readers.values():
                    if r.eng != op.eng:
                        deps.append(r)
        for x in writes:
            b = _b(x)
            w = b.writer
            if w is not None:
                if w.isdma or w.eng != op.eng or op.isdma:
                    deps.append(w)
                else:
                    odeps.append(w)
            for r in b.readers.values():
                if r.isdma or r.eng != op.eng or op.isdma:
                    deps.append(r)
                else:
                    odeps.append(r)
        for d in deps:
            d.signal = True
        for x in reads:
            _b(x).readers[op.key] = op
        for x in writes:
            b = _b(x)
            b.writer = op
            b.readers = {}

    def _add(self, o):
        o.idx = len(self.all_ops)
        o.seg = self.seg
        self.all_ops.append(o)

    def op(self, eng, fn, reads=(), writes=(), dur=0.3, tbl=None):
        o = Op(eng, fn)
        o.dur = dur
        o.tbl = tbl
        self._deps(o, reads, writes)
        self._add(o)
        return o

    def dma(self, eng, out, in_, sbuf, reads=(), writes=(), **kw):
        sbuf = _b(sbuf)
        if sbuf.sem is None:
            if self.sem_pool:
                ent = self.sem_pool.pop()
            else:
                ent = [self.stack.enter_context(self.nc.semaphore("ds%d" % self.nsem)), 0, None]
                self.nsem += 1
            sbuf.sem = ent
            self.sem_live.append(sbuf)
        ent = sbuf.sem
        o = Op(eng, lambda e: e.dma_start(out=out, in_=in_, **kw), isdma=True)
        self.ndma += 1
        o.key = ("dma", self.ndma)
        try:
            nbytes = float(out.nbytes())
        except Exception:
            nbytes = 65536.0
        o.dur = 0.6 if eng == POOL else 0.1
        o.lat = 2.0 + nbytes / 120e3
        self._deps(o, reads, writes)
        if ent[2] is not None and ent[2].seg == self.seg:
            o.odeps.append(ent[2])
        ent[2] = o
        ent[1] += 16
        o.sig = (ent[0], ent[1])
        o.signal = True
        self._add(o)
        self.dma_since_barrier.append(o)
        self.all_dma_sigs[id(ent[0])] = o.sig
        return o

    def barrier(self):
        lasts = {}
        for o in self.all_ops:
            if not o.isdma:
                lasts[o.eng] = o
        self._last_hint = lasts
        self.seg += 1
        self.seg_deps[self.seg] = ("BAR", list(self.dma_since_barrier))
        self.dma_since_barrier = []
        for b in self.sem_live:
            self.sem_pool.append(b.sem)
            b.sem = None
        self.sem_live = []

    def _schedule_segment(self, ops):
        seg = ops[0].seg
        for o in ops:
            o.nin = 0
            o.succ = []
            o.rt = 0.0
        for o in ops:
            seen = set()
            for d in o.deps + o.odeps:
                if d.seg == seg and id(d) not in seen:
                    seen.add(id(d))
                    d.succ.append(o)
                    o.nin += 1
        cand = {e: [] for e in ENGS}
        for o in ops:
            if o.nin == 0:
                cand[o.eng].append(o)
        free = {e: 0.0 for e in ENGS}
        order = {e: [] for e in ENGS}
        left = len(ops)
        lat = self.SEM_LAT
        cur_tbl = getattr(self, "_cur_tbl", None)
        TSW = 1.0
        while left:
            best = None
            for e in ENGS:
                c = cand[e]
                if not c:
                    continue
                t = free[e]
                pick = None
                if e == ACT:
                    pk = None
                    for o in c:
                        st_ = o.rt if o.rt > t else t
                        if o.tbl is not None and o.tbl != cur_tbl:
                            st_ += TSW
                        k_ = (st_, o.idx)
                        if pk is None or k_ < pk:
                            pk = k_
                            pick = o
                    start = pk[0]
                else:
                    for o in c:
                        if o.rt <= t:
                            if pick is None or pick.rt > t or o.idx < pick.idx:
                                pick = o
                        elif pick is None or (pick.rt > t and (o.rt < pick.rt or (o.rt == pick.rt and o.idx < pick.idx))):
                            pick = o
                    start = max(t, pick.rt)
                if best is None or start < best[0] or (start == best[0] and pick.idx < best[1].idx):
                    best = (start, pick)
            start, o = best
            e = o.eng
            if e == ACT and o.tbl is not None:
                cur_tbl = o.tbl
            cand[e].remove(o)
            order[e].append(o)
            free[e] = start + o.dur
            o.fin = start + o.dur + o.lat
            left -= 1
            for s_ in o.succ:
                r = o.fin + (lat if (s_.eng != e or o.isdma) else 0.0)
                if r > s_.rt:
                    s_.rt = r
                s_.nin -= 1
                if s_.nin == 0:
                    cand[s_.eng].append(s_)
        self._cur_tbl = cur_tbl
        self.sim_time = self.sim_time + max(free.values()) if hasattr(self, "sim_time") else max(free.values())
        return order

    def emit(self):
        nc = self.nc
        segs = {}
        for o in self.all_ops:
            segs.setdefault(o.seg, []).append(o)
        self.ops = {e: [] for e in ENGS}
        last_compute = {}
        for sg in sorted(segs):
            ops = segs[sg]
            if self.SCHED:
                order = self._schedule_segment(ops)
            else:
                order = {e: [o for o in ops if o.eng == e] for e in ENGS}
            info = self.seg_deps.get(sg)
            if info:
                extra = list(last_compute.values()) + info[1]
                for o in extra:
                    o.signal = True
                for e in ENGS:
                    if order[e]:
                        order[e][0].deps = order[e][0].deps + extra
            for e in ENGS:
                self.ops[e].extend(order[e])
                for o in order[e]:
                    if not o.isdma:
                        last_compute[e] = o
        for e in ENGS:
            c = 0
            for o in self.ops[e]:
                if o.isdma:
                    continue
                if o.signal:
                    c += 1
                    o.sig = (self.esem[e], c)
        fin_waits = list(self.all_dma_sigs.values())
        for e in ENGS:
            known = {}
            for o in self.ops[e]:
                need = {}
                for d in o.deps:
                    s, v = d.sig
                    k = id(s)
                    if known.get(k, 0) >= v:
                        continue
                    if k not in need or need[k][1] < v:
                        need[k] = (s, v)
                for k, (s, v) in need.items():
                    known[k] = v
                o.waits = list(need.values())
        self.stats = dict(
            nops={e: len(self.ops[e]) for e in ENGS},
            nwait={e: sum(len(o.waits) for o in self.ops[e]) for e in ENGS},
            nsig={e: sum(1 for o in self.ops[e] if o.signal) for e in ENGS},
            nsem=self.nsem, sim_us=getattr(self, "sim_time", 0.0),
        )

        def run(e, engobj):
            for o in self.ops[e]:
                for s, v in o.waits:
                    engobj.wait_ge(s, v)
                ins = o.fn(engobj)
                if o.isdma:
                    ins.then_inc(o.sig[0], 16)
                elif o.signal:
                    ins.then_inc(o.sig[0], 1)
            if e == SP:
                for s, v in fin_waits:
                    engobj.wait_ge(s, v)

        with nc.Block() as block:
            @block.tensor
            def _(t):
                run(PE, t)

            @block.scalar
            def _(t):
                run(ACT, t)

            @block.vector
            def _(t):
                run(DVE, t)

            @block.gpsimd
            def _(t):
                run(POOL, t)

            @block.sync
            def _(t):
                run(SP, t)


class ColMap:
    def __init__(self):
        self.off = {}
        self.n = 0

    def add(self, name, ncols):
        self.off[name] = self.n
        self.n += ncols

    def __getitem__(self, name):
        return self.off[name]


def build_colmap():
    cm = ColMap()
    for l in range(DEPTH):
        cm.add("ng%d" % l, 8)
        cm.add("bsh%d" % l, 8)
        cm.add("bsc%d" % l, 8)
    cm.add("c", 8)
    cm.add("cctx", 8)
    for i in range(2):
        for j in range(4):
            cm.add("cw%d_%d" % (i, j), 8)
        cm.add("cb%d" % i, 8)
        for d in range(2):
            cm.add("ba%d_%d" % (i, d), 8)
            cm.add("bi%d_%d" % (i, d), 8)
            cm.add("lam%d_%d" % (i, d), 8)
        cm.add("bs%d" % i, 4)
    return cm


def build_rowmap():
    rm = ColMap()
    for l in range(DEPTH):
        rm.add("bg%d" % l, 1024)
    for i in range(2):
        rm.add("lng%d" % i, 512)
        rm.add("lnb%d" % i, 512)
        rm.add("sink%d" % i, 8)
    rm.add("fg", 1024)
    return rm


CM = build_colmap()
RM = build_rowmap()


def col8(v):
    return np.ascontiguousarray(np.asarray(v, np.float32).reshape(8, 128).T)


def rope_table():
    t = np.arange(L)
    r = (t // 64).astype(np.float64)
    c = (t % 64).astype(np.float64)
    inv = (10000.0 ** (-np.arange(16, dtype=np.float32) / np.float32(16))).astype(np.float32).astype(np.float64)
    ang = np.concatenate([r[:, None] * inv, c[:, None] * inv], axis=-1).astype(np.float32)
    cos = np.cos(ang).astype(np.float32)
    sin = np.sin(ang).astype(np.float32)
    return np.ascontiguousarray(np.concatenate([cos, cos, -sin, sin], axis=-1).astype(np.float32))


class StopBuild(Exception):
    pass


class Builder:
    def __init__(self, layers=(0, 1, 2, 3), do_final=True, dbg=False, stop=None):
        self.stop = stop
        self.stopped = False
        self.cks = []
        self.layers = list(layers)
        self.do_final = do_final
        self.dbg = dbg
        self.nc = bass.Bass("TRN2", target_bir_lowering=False)
        self.uid = 0

    def ck(self, name):
        self.cks.append(name)
        if self.stop is not None and name == self.stop:
            self.stopped = True
        return self.stopped

    def dram_in(self, name, shape, dt=F32):
        return self.nc.dram_tensor(name, list(shape), dt, kind="ExternalInput").ap()

    def dram_out(self, name, shape, dt=F32):
        return self.nc.dram_tensor(name, list(shape), dt, kind="ExternalOutput").ap()

    def dram_tmp(self, name, shape, dt=F32):
        return self.nc.dram_tensor(name, list(shape), dt).ap()

    def sb(self, st, name, shape, dt=F32):
        self.uid += 1
        nm = "%s_%d" % (name, self.uid)
        t = st.enter_context(self.nc.sbuf_tensor(nm, list(shape), dt))
        return Tile(t, Buf(nm))

    def sbc(self, st, name, shape, dt, n):
        T = self.sb(st, name, shape, dt)
        T.bs = [Buf("%s_c%d" % (name, c)) for c in range(n)]
        return T

    def ps(self, st, name, shape, dt=F32):
        self.uid += 1
        nm = "%s_%d" % (name, self.uid)
        t = st.enter_context(self.nc.psum_tensor(nm, list(shape), dt))
        return Tile(t, Buf(nm, excl=True))

    @staticmethod
    def _fs(ap):
        n = 1
        for d in ap.shape[1:]:
            n *= int(d)
        return n

    def act(self, out, in_, func, r, w, scale=1.0, bias=0.0, accum=None):
        kw = {}
        if accum is not None:
            kw["accum_out"] = accum
        tbl = {AF.Exp: "exp", AF.Tanh: "exp", AF.Sqrt: "sqrt", AF.Ln: "ln"}.get(func)
        self.P.op(ACT, lambda e: e.activation(out=out, in_=in_, func=func, bias=bias, scale=scale, **kw), r, w,
                  dur=0.2 + self._fs(out) / 1300.0, tbl=tbl)

    def _vdur(self, eng, out, two_in):
        n = self._fs(out)
        if eng == POOL:
            return 0.25 + n * 0.0022
        return (0.15 + n / 800.0) if two_in else (0.10 + n / 1400.0)

    def tt(self, eng, out, in0, in1, op, r, w):
        self.P.op(eng, lambda e: e.tensor_tensor(out=out, in0=in0, in1=in1, op=op), r, w, dur=self._vdur(eng, out, True))

    def ts(self, eng, out, in0, s1, s2, op0, op1, r, w):
        d = self._vdur(eng, out, False)
        if op1 is None:
            self.P.op(eng, lambda e: e.tensor_scalar(out=out, in0=in0, scalar1=s1, scalar2=None, op0=op0), r, w, dur=d)
        else:
            self.P.op(eng, lambda e: e.tensor_scalar(out=out, in0=in0, scalar1=s1, scalar2=s2, op0=op0, op1=op1), r, w, dur=d)

    def stt(self, eng, out, in0, scalar, in1, op0, op1, r, w):
        assert eng == DVE, "scalar_tensor_tensor is DVE-only"
        self.P.op(eng, lambda e: e.scalar_tensor_tensor(out=out, in0=in0, scalar=scalar, in1=in1, op0=op0, op1=op1), r, w,
                  dur=self._vdur(eng, out, True))

    def cp(self, eng, out, in_, r, w):
        if eng == ACT:
            self.P.op(ACT, lambda e: e.activation(out=out, in_=in_, func=AF.Copy), r, w, dur=0.2 + self._fs(out) / 1300.0)
        else:
            self.P.op(eng, lambda e: e.tensor_copy(out=out, in_=in_), r, w, dur=self._vdur(eng, out, False))

    def mm(self, out, lhsT, rhs, start, stop, r, w):
        self.P.op(PE, lambda e: e.matmul(out, lhsT=lhsT, rhs=rhs, start=start, stop=stop), r, w,
                  dur=0.02 + self._fs(out) / 2200.0)

    def mmx(self, out, lhsT, rhs, start, stop, r, w):
        self.P.op(PE, lambda e: e.matmul(out, lhsT=lhsT, rhs=rhs, start=start, stop=stop, skip_group_check=True), r, w,
                  dur=0.02 + self._fs(out) / 2200.0)

    def tr(self, out, in_, ident, r, w):
        self.P.op(PE, lambda e: e.transpose(out=out, in_=in_, identity=ident), r, w, dur=0.07)

    def memset(self, eng, ap, val, w):
        self.P.op(eng, lambda e: e.memset(ap, val), (), w, dur=self._vdur(eng, ap, False))

    def dma(self, eng, out, in_, sbuf, r=(), w=(), **kw):
        self.P.dma(eng, out, in_, sbuf, r, w, **kw)

    def rsqrt(self, out_ap, out_t, v_ap, v_t, s, q, n):
        sa = s.t[:, 0:n]
        self.act(sa, v_ap, AF.Sqrt, [v_t], [s])
        self.P.op(DVE, lambda e: e.reciprocal(out=out_ap, in_=sa), [s], [out_t])

    @staticmethod
    def run_prop(items):
        st_ = [[g_, 0, float(n_)] for g_, n_ in items]
        while st_:
            st_.sort(key=lambda e_: (e_[1] + 1) / e_[2])
            e_ = st_[0]
            try:
                next(e_[0])
                e_[1] += 1
            except StopIteration:
                st_.remove(e_)

    def build(self):
        nc = self.nc
        with ExitStack() as top:
            self.P = Prog(nc, top)
            self.declare_dram()
            self.setup(top)
            self.ck("setup")
            for l in self.layers:
                if self.stopped:
                    break
                self.P.barrier()
                with ExitStack() as st:
                    self.prologue(st, l)
                    if not self.ck("prologue%d" % l):
                        if l % 2 == 0:
                            self.even_layer(st, l)
                        else:
                            self.odd_layer(st, l)
                    self.P.barrier()
            if self.dbg:
                lastl = self.layers[-1]
                src = self.X[(lastl + 1) % 2]
                db = Buf("dbgout")
                for k in range(0, L + LC, 512):
                    n = min(512, L + LC - k)
                    self.dma(SP, self.dbg_x[k:k + n, :], src[k:k + n, :], db)
            self.P.emit()
        return nc

    def declare_dram(self):
        self.x_in = self.dram_in("x", [L, D])
        self.ctx_in = self.dram_in("ctx", [LC, D])
        self.cols_d = self.dram_in("cols", [128, CM.n])
        self.rows_d = self.dram_in("rows", [128, RM.n])
        self.rope_d = self.dram_in("rope", [L, 128])
        self.w_mod = self.dram_in("w_mod", [DEPTH, D, 3 * D])
        self.ab_w_in = self.dram_in("ab_w_in", [2, D, AB_IN])
        self.ab_w_out = self.dram_in("ab_w_out", [2, D, D])
        self.a_w_sT = self.dram_in("a_w_sT", [2, 128, 4, 128])
        self.c_w_in = self.dram_in("c_w_in", [2, D, 2 * D])
        self.c_w_a = self.dram_in("c_w_a", [2, 2, 4, 256, 256])
        self.c_w_i = self.dram_in("c_w_i", [2, 2, 4, 256, 256])
        self.c_w_out = self.dram_in("c_w_out", [2, D, D])
        self.out_d = self.dram_out("out", [L, D])
        self.X = [self.dram_tmp("XA", [L + LC, D]), self.dram_tmp("XB", [L + LC, D])]
        self.ZS = self.dram_tmp("ZS", [D, L])
        self.SF = self.dram_tmp("SF", [D, L])
        self.SG = self.dram_tmp("SG", [D, L], BF16)
        self.ZB = self.dram_tmp("ZB", [D, L], BF16)
        if self.dbg:
            self.dbg_x = self.dram_out("dbg_x", [L + LC, D])

    def x_src(self, l, blk):
        if l == self.layers[0] and l == 0:
            if blk < NCB:
                return self.ctx_in[blk * 128:(blk + 1) * 128, :]
            return self.x_in[(blk - NCB) * 128:(blk - NCB + 1) * 128, :]
        if l == self.layers[0]:
            return self.dbg_xin[blk * 128:(blk + 1) * 128, :]
        return self.X[l % 2][blk * 128:(blk + 1) * 128, :]

    def x_dst(self, l, blk):
        return self.X[(l + 1) % 2][blk * 128:(blk + 1) * 128, :]

    def setup(self, st):
        nc = self.nc
        self.COLS = self.sb(st, "cols", [128, CM.n])
        self.dma(SP, self.COLS.t[:], self.cols_d, self.COLS, w=[self.COLS])
        self.ck("s_cols")
        self.identf = self.sb(st, "identf", [128, 128])
        self.identb = self.sb(st, "identb", [128, 128], BF16)
        self.memset(POOL, self.identf.t[:], 1.0, [self.identf])
        idf = self.identf
        self.P.op(POOL, lambda e: e.affine_select(out=idf.t[:], in_=idf.t[:], pattern=[[-1, 128]], compare_op=ALU.is_equal,
                                                 fill=0.0, base=0, channel_multiplier=1), [idf], [idf])
        self.cp(DVE, self.identb.t[:], self.identf.t[:], [self.identf], [self.identb])
        self.ck("s_ident")
        mtmp = self.sb(st, "mtmp", [128, 128])
        self.memset(POOL, mtmp.t[:], 1.0, [mtmp])
        self.P.op(POOL, lambda e: e.affine_select(out=mtmp.t[:], in_=mtmp.t[:], pattern=[[-1, 128]], compare_op=ALU.is_ge,
                                                 fill=0.0, base=0, channel_multiplier=1), [mtmp], [mtmp])
        mtmp2 = self.sb(st, "mtmp2", [128, 128])
        self.memset(POOL, mtmp2.t[:], 1.0, [mtmp2])
        self.P.op(POOL, lambda e: e.affine_select(out=mtmp2.t[:], in_=mtmp2.t[:], pattern=[[1, 128]], compare_op=ALU.is_ge,
                                                 fill=0.0, base=0, channel_multiplier=-1), [mtmp2], [mtmp2])
        self.mbprev = self.sb(st, "mbprev", [128, 4, 128], BF16)
        self.mbnext = self.sb(st, "mbnext", [128, 4, 128], BF16)
        for (src_, dst_) in ((mtmp, self.mbprev), (mtmp2, self.mbnext)):
            self.ts(DVE, src_.t[:], src_.t[:], -1.0, 30000.0, ALU.add, ALU.mult, [src_], [src_])
            self.cp(DVE, dst_.t[:], src_.t[:].unsqueeze(1).broadcast_to([128, 4, 128]), [src_], [dst_])
        self.ck("s_masks")
        cc = CM["c"]
        th = self.sb(st, "sc_th", [128, 16])
        sc = self.sb(st, "sc", [128, 16])
        self.act(th.t[:], self.COLS.t[:, cc:cc + 16], AF.Tanh, [self.COLS], [th], scale=0.5)
        self.stt(DVE, sc.t[:], th.t[:], 1.0, self.COLS.t[:, cc:cc + 16], ALU.add, ALU.mult, [th, self.COLS], [sc])
        self.ts(DVE, sc.t[:], sc.t[:], 0.5, None, ALU.mult, None, [sc], [sc])
        self.scT = self.sb(st, "scT", [128, 8, 2], BF16)
        self.cp(DVE, self.scT.t[:, :, 0], sc.t[:, 0:8], [sc], [self.scT])
        self.cp(DVE, self.scT.t[:, :, 1], sc.t[:, 8:16], [sc], [self.scT])
        self.screp = []
        for i in range(2):
            t = self.sb(st, "screp%d" % i, [128, 8, 128], BF16)
            self.cp(DVE, t.t[:], sc.t[:, 8 * i:8 * i + 8].unsqueeze(2).broadcast_to([128, 8, 128]), [sc], [t])
            self.screp.append(t)
        self.ck("s_silu")
        self.SSQ = [self.sb(st, "ssq0", [128, NBLK]), self.sb(st, "ssq1", [128, NBLK])]
        l0 = self.layers[0]
        self.memset(POOL, self.SSQ[l0 % 2].t[:], 0.0, [self.SSQ[l0 % 2]])
        if self.dbg and l0 != 0:
            self.dbg_xin = self.dram_in("dbg_xin", [L + LC, D])
        with ExitStack() as s2:
            xr = [self.sb(s2, "prex%d" % i, [128, D]) for i in range(6)]
            junk = self.sb(s2, "prej", [128, D])
            for blk in range(NBLK):
                xt = xr[blk % 6]
                self.dma(SP, xt.t[:], self.x_src(l0, blk), xt, w=[xt])
                self.act(junk.t[:], xt.t[:], AF.Square, [xt], [junk, self.SSQ[l0 % 2]],
                         accum=self.SSQ[l0 % 2].t[:, blk:blk + 1])
            self.P.barrier()

    def prologue(self, st, l):
        P = self.P
        self.gs = [self.sb(st, "gs%d" % i, [128, 8]) for i in range(2)]
        self.sh = [self.sb(st, "sh%d" % i, [128, 8]) for i in range(2)]
        self.gate = [self.sb(st, "gate%d" % i, [128, D]) for i in range(2)]
        self.rstd = self.sb(st, "rstd", [128, NBLK])
        with ExitStack() as s2:
            bg = self.sb(s2, "bg", [128, D])
            self.dma(SP, bg.t[:], self.rows_d[:, RM["bg%d" % l]:RM["bg%d" % l] + D], bg, w=[bg])
            wm = [self.sb(s2, "wm%d" % i, [128, 8, 512], BF16) for i in range(6)]
            pcols_full = self.ps(s2, "pcols", [128, 512])
            pcols = Tile(pcols_full.t[:, 0:32].rearrange("p (n t) -> p n t", t=2), pcols_full.b)
            pg = [self.ps(s2, "pg%d" % i, [128, 512]) for i in range(2)]
            wsrc = self.w_mod[l].rearrange("(kc p) n -> p kc n", p=128)
            for pi in range(6):
                w = wm[pi]
                self.dma(POOL, w.t[:], wsrc[:, :, pi * 512:(pi + 1) * 512], w, w=[w])
                if pi < 4:
                    for q in range(4):
                        nn = 4 * pi + q
                        for kc in range(8):
                            self.mm(pcols.t[:, nn, :], w.t[:, kc, q * 128:(q + 1) * 128], self.scT.t[:, kc, :],
                                    kc == 0, kc == 7, [w, self.scT], [pcols])
                else:
                    hh = pi - 4
                    for i in range(2):
                        for kc in range(8):
                            self.mm(pg[i].t[:], self.screp[i].t[:, kc, :], w.t[:, kc, :], kc == 0, kc == 7,
                                    [w, self.screp[i]], [pg[i]])
                        self.tt(DVE, self.gate[i].t[:, hh * 512:(hh + 1) * 512], pg[i].t[:], bg.t[:, hh * 512:(hh + 1) * 512],
                                ALU.add, [pg[i], bg], [self.gate[i]])
            modc = self.sb(s2, "modc", [128, 16, 2])
            self.cp(DVE, modc.t[:], pcols.t[:], [pcols], [modc])
            ng, bsh, bsc = CM["ng%d" % l], CM["bsh%d" % l], CM["bsc%d" % l]
            tmp = self.sb(s2, "modtmp", [128, 8])
            for i in range(2):
                self.tt(DVE, self.sh[i].t[:], modc.t[:, 0:8, i], self.COLS.t[:, bsh:bsh + 8], ALU.add,
                        [modc, self.COLS], [self.sh[i]])
                self.tt(DVE, tmp.t[:], modc.t[:, 8:16, i], self.COLS.t[:, bsc:bsc + 8], ALU.add, [modc, self.COLS], [tmp])
                self.stt(DVE, self.gs[i].t[:], tmp.t[:], 1.0, self.COLS.t[:, ng:ng + 8], ALU.add, ALU.mult,
                         [tmp, self.COLS], [self.gs[i]])
            v = self.sb(s2, "nv", [128, NBLK])
            self.ts(DVE, v.t[:], self.SSQ[l % 2].t[:], 1.0 / D, EPS, ALU.mult, ALU.add, [self.SSQ[l % 2]], [v])
            hs = self.sb(s2, "hs", [128, NBLK])
            hq = self.sb(s2, "hq", [128, NBLK])
            self.rsqrt(self.rstd.t[:], self.rstd, v.t[:], v, hs, hq, NBLK)
            self.memset(POOL, self.SSQ[(l + 1) % 2].t[:], 0.0, [self.SSQ[(l + 1) % 2]])
            self.P.barrier()

    def norm_T(self, xt, blk, xn, psT, hT_ap, hT_tile, is_ctx):
        i = 1 if is_ctx else 0
        self.act(xn.t[:], xt.t[:], AF.Copy, [xt, self.rstd], [xn], scale=self.rstd.t[:, blk:blk + 1])
        for c in range(8):
            self.tr(psT.t[:, c, :], xn.t[:, c * 128:(c + 1) * 128], self.identb.t[:], [xn, self.identb], [psT])
        for c in range(8):
            self.ts(DVE, hT_ap[:, c, :], psT.t[:, c, :], self.gs[i].t[:, c:c + 1], self.sh[i].t[:, c:c + 1],
                    ALU.mult, ALU.add, [psT, self.gs[i], self.sh[i]], [hT_tile])

    def gelu2(self, out_t, ps_t, w_t, in_t, n):
        self.act(w_t.t[:, 0:n], ps_t.t[:, 0:n], AF.Square, [ps_t], [w_t], scale=SQ_044)
        self.stt(DVE, in_t.t[:, 0:n], w_t.t[:, 0:n], 1.0, ps_t.t[:, 0:n], ALU.add, ALU.mult, [w_t, ps_t], [in_t])
        self.act(w_t.t[:, 0:n], in_t.t[:, 0:n], AF.Tanh, [in_t], [w_t], scale=GELU_C)
        self.stt(DVE, out_t.t[:, 0:n], w_t.t[:, 0:n], 1.0, ps_t.t[:, 0:n], ALU.add, ALU.mult, [w_t, ps_t], [out_t])

    def silu2(self, out_ap, out_t, ps_ap, ps_t, w_ap, w_t):
        self.act(w_ap, ps_ap, AF.Tanh, [ps_t], [w_t], scale=0.5)
        self.stt(DVE, out_ap, w_ap, 1.0, ps_ap, ALU.add, ALU.mult, [w_t, ps_t], [out_t])

    def even_layer(self, st, l):
        i = l // 2
        W_in = self.sb(st, "Win", [128, 8, AB_IN], BF16)
        W_out = self.sb(st, "Wout", [128, 8, D], BF16)
        W_sT = self.sb(st, "WsT", [128, 4, 128], BF16)
        wsrc = self.ab_w_in[i].rearrange("(kc p) n -> p kc n", p=128)
        Wg = {}
        for (c0, n) in ((512, 512), (0, 512), (2304, 512), (1536, 512), (2048, 256), (1024, 512)):
            Wg[c0] = Buf("Win_%d" % c0)
            for kc in range(0, 8, 4):
                self.dma(POOL, W_in.t[:, kc:kc + 4, c0:c0 + n], wsrc[:, kc:kc + 4, c0:c0 + n], Wg[c0], w=[Wg[c0]])
        self.dma(POOL, W_sT.t[:], self.a_w_sT[i], W_sT, w=[W_sT])
        wosrc = self.ab_w_out[i].rearrange("(kc p) n -> p kc n", p=128)
        for kc in range(0, 8, 2):
            self.dma(POOL, W_out.t[:, kc:kc + 2, :], wosrc[:, kc:kc + 2, :], W_out, w=[W_out])
        lng = self.sb(st, "lng", [128, 512])
        lnb = self.sb(st, "lnb", [128, 512])
        sink = self.sb(st, "sink", [128, 8])
        esink = self.sb(st, "esink", [128, 8])
        self.dma(SP, lng.t[:], self.rows_d[:, RM["lng%d" % i]:RM["lng%d" % i] + 512], lng, w=[lng])
        self.dma(SP, lnb.t[:], self.rows_d[:, RM["lnb%d" % i]:RM["lnb%d" % i] + 512], lnb, w=[lnb])
        self.dma(SP, sink.t[:], self.rows_d[:, RM["sink%d" % i]:RM["sink%d" % i] + 8], sink, w=[sink])
        self.act(esink.t[:], sink.t[:], AF.Exp, [sink], [esink])
        bs0 = CM["bs%d" % i]

        KT = self.sb(st, "KT", [128, 2, NBLK, 128], BF16)
        KTb = [Buf("KT%d" % b) for b in range(NBLK)]
        Vb = self.sb(st, "Vb", [128, NBLK, 2, 128], BF16)
        Vbb = [Buf("Vb%d" % b) for b in range(NBLK)]
        self.memset(POOL, Vb.t[:, :, :, 64:128], 1.0, Vbb)

        ring = lambda name, shape, dt=F32, n=2: [self.sb(st, "%s%d" % (name, k), shape, dt) for k in range(n)]
        xblk = ring("xblk", [128, D], F32, 4)
        hT = ring("hT", [128, 8, 128], BF16, 2)
        ropet = ring("rope", [128, 128], F32, 2)
        qT = ring("qT", [128, 8, 128], BF16, 3)
        sgb2 = ring("sgb2", [128, 512], F32, 3)
        ymix = ring("ymix", [128, D], BF16, 3)
        xn = self.sb(st, "xn", [128, D], BF16)
        wv, iv = self.sb(st, "wv", [128, 512]), self.sb(st, "iv", [128, 512])
        wu, iu = self.sb(st, "wu", [128, 512]), self.sb(st, "iu", [128, 512])
        gu2 = self.sb(st, "gu2", [128, 512])
        sga2 = self.sb(st, "sga2", [128, 512])
        gv = self.sb(st, "gv", [128, 512])
        vln = self.sb(st, "vln", [128, 512], BF16)
        vtmp = self.sb(st, "vtmp", [128, 512])
        lnst = self.sb(st, "lnst", [128, 16])
        lnr = self.sb(st, "lnr", [128, 4])
        lnm = self.sb(st, "lnm", [128, 4])
        lhs_ = self.sb(st, "lhs", [128, 4])
        wg = self.sb(st, "wg", [128, 512])
        r1 = self.sb(st, "r1", [128, 640])
        r2 = self.sb(st, "r2", [128, 640])
        qz = self.sb(st, "qz", [128, 8, 128], BF16)
        self.memset(POOL, qz.t[:], 0.0, [qz])
        kdup = self.sb(st, "kdup", [128, 2, 2, 64], BF16)
        PT = [[self.sb(st, "PT%d_%d" % (kh, k), [128, 512], BF16) for k in range(5)] for kh in range(2)]
        den = self.sb(st, "den", [128, 8])
        rden = self.sb(st, "rden", [128, 8])
        ybt = self.sb(st, "ybt", [128, 512])
        ymixT = self.sb(st, "ymixT", [128, 8, 128], BF16)
        ytmp = self.sb(st, "ytmp", [128, D])

        psT = self.ps(st, "psT", [128, 8, 128], BF16)
        psTb = self.ps(st, "psTb", [128, 8, 128], BF16)
        psA = [self.ps(st, "psA%d" % k, [128, 512]) for k in range(2)]
        psQ = self.ps(st, "psQ", [128, 512])
        psS = [self.ps(st, "psS%d" % k, [128, 512]) for k in range(2)]
        psV = self.ps(st, "psV", [128, 4, 128])
        psY = Tile(psV.t[:].rearrange("p h d -> p (h d)"), psV.b)

        ssq_next = self.SSQ[(l + 1) % 2]

        def proj(ps, hslot, c0, n):
            for kc in range(8):
                self.mm(ps.t[:, 0:n], hT[hslot].t[:, kc, :], W_in.t[:, kc, c0:c0 + n], kc == 0, kc == 7,
                        [hT[hslot], Wg[c0]], [ps])

        def streamN(blk):
            is_ctx = blk < NCB
            mi = 1 if is_ctx else 0
            xt = xblk[blk % 4]
            self.dma(SP, xt.t[:], self.x_src(l, blk), xt, w=[xt])
            yield
            self.act(xn.t[:], xt.t[:], AF.Copy, [xt, self.rstd], [xn], scale=self.rstd.t[:, blk:blk + 1])
            yield
            for c in range(8):
                self.tr(psT.t[:, c, :], xn.t[:, c * 128:(c + 1) * 128], self.identb.t[:], [xn, self.identb], [psT])
            yield
            h = hT[blk % 2]
            for c in range(8):
                self.ts(DVE, h.t[:, c, :], psT.t[:, c, :], self.gs[mi].t[:, c:c + 1], self.sh[mi].t[:, c:c + 1],
                        ALU.mult, ALU.add, [psT, self.gs[mi], self.sh[mi]], [h])
                if c % 4 == 3:
                    yield

        def gelu2_gen(out_t, ps_t, w_t, in_t):
            self.act(w_t.t[:], ps_t.t[:], AF.Square, [ps_t], [w_t], scale=SQ_044)
            yield
            self.stt(DVE, in_t.t[:], w_t.t[:], 1.0, ps_t.t[:], ALU.add, ALU.mult, [w_t, ps_t], [in_t])
            yield
            self.act(w_t.t[:], in_t.t[:], AF.Tanh, [in_t], [w_t], scale=GELU_C)
            yield
            self.stt(DVE, out_t.t[:], w_t.t[:], 1.0, ps_t.t[:], ALU.add, ALU.mult, [w_t, ps_t], [out_t])
            yield

        def streamS1(blk):
            hs = blk % 2
            ym = ymix[blk % 3]
            proj(psA[0], hs, 512, 512)
            yield
            proj(psA[1], hs, 0, 512)
            yield
            yield from gelu2_gen(gv, psA[0], wv, iv)
            self.memset(POOL, lnst.t[:, 4:8], 0.0, [lnst])
            gv3 = gv.t[:].rearrange("p (g d) -> p g d", g=4)
            self.P.op(DVE, lambda e: e.tensor_reduce(out=lnst.t[:, 0:4], in_=gv3, axis=AX.X, op=ALU.add), [gv], [lnst])
            yield
            for g in range(4):
                self.act(vtmp.t[:, g * 128:(g + 1) * 128], gv.t[:, g * 128:(g + 1) * 128], AF.Square, [gv], [vtmp, lnst],
                         accum=lnst.t[:, 4 + g:5 + g])
            yield
            proj(psA[0], hs, 1024, 512)
            yield
            self.ts(DVE, lnst.t[:, 8:12], lnst.t[:, 0:4], 1.0 / 128, None, ALU.mult, None, [lnst], [lnst])
            self.tt(DVE, lnst.t[:, 12:16], lnst.t[:, 8:12], lnst.t[:, 8:12], ALU.mult, [lnst], [lnst])
            self.stt(DVE, lnst.t[:, 12:16], lnst.t[:, 4:8], 1.0 / 128, lnst.t[:, 12:16], ALU.mult, ALU.subtract,
                     [lnst], [lnst])
            self.ts(DVE, lnst.t[:, 12:16], lnst.t[:, 12:16], 4.0 * EPS, None, ALU.add, None, [lnst], [lnst])
            yield
            self.act(lhs_.t[:], lnst.t[:, 12:16], AF.Sqrt, [lnst], [lhs_])
            yield
            self.P.op(DVE, lambda e: e.reciprocal(out=lnr.t[:], in_=lhs_.t[:]), [lhs_], [lnr])
            self.stt(DVE, lnm.t[:], lnst.t[:, 8:12], -1.0, lnr.t[:], ALU.mult, ALU.mult, [lnst, lnr], [lnm])
            yield
            yield from gelu2_gen(gu2, psA[1], wu, iu)
            for g in range(4):
                self.act(vtmp.t[:, g * 128:(g + 1) * 128], gv.t[:, g * 128:(g + 1) * 128], AF.Identity, [gv, lnr, lnm], [vtmp],
                         scale=lnr.t[:, g:g + 1], bias=lnm.t[:, g:g + 1])
            yield
            self.tt(DVE, vtmp.t[:], vtmp.t[:], lng.t[:], ALU.mult, [vtmp, lng], [vtmp])
            yield
            self.tt(DVE, vln.t[:], vtmp.t[:], lnb.t[:], ALU.add, [vtmp, lnb], [vln])
            yield
            self.act(wv.t[:], psA[0].t[:], AF.Tanh, [psA[0]], [wv], scale=0.5)
            yield
            self.stt(DVE, sga2.t[:], wv.t[:], 1.0, psA[0].t[:], ALU.add, ALU.mult, [wv, psA[0]], [sga2])
            yield
            for g in range(4):
                self.mm(psA[1].t[:, g * 128:(g + 1) * 128], W_sT.t[:, g, :], vln.t[:, g * 128:(g + 1) * 128], True, True,
                        [W_sT, vln], [psA[1]])
            yield
            self.stt(DVE, gu2.t[:], gu2.t[:], 0.25, sga2.t[:], ALU.mult, ALU.mult, [gu2, sga2], [gu2])
            yield
            for g in range(4):
                self.stt(DVE, ym.t[:, g * 128:(g + 1) * 128], psA[1].t[:, g * 128:(g + 1) * 128],
                         self.COLS.t[:, bs0 + g:bs0 + g + 1], gu2.t[:, g * 128:(g + 1) * 128], ALU.add, ALU.mult,
                         [psA[1], self.COLS, gu2], [ym])
                if g % 2 == 1:
                    yield

        def streamS2(blk):
            hs = blk % 2
            s3 = blk % 3
            is_ctx = blk < NCB
            rt = ropet[blk % 2]
            if not is_ctx:
                t0 = (blk - NCB) * 128
                self.dma(SP, rt.t[:], self.rope_d[t0:t0 + 128, :], rt, w=[rt])
            proj(psQ, hs, 2304, 512)
            yield
            self.act(wg.t[:], psQ.t[:], AF.Tanh, [psQ], [wg], scale=0.5)
            yield
            self.stt(DVE, sgb2[s3].t[:], wg.t[:], 1.0, psQ.t[:], ALU.add, ALU.mult, [wg, psQ], [sgb2[s3]])
            yield
            proj(psQ, hs, 1536, 512)
            yield
            if is_ctx:
                q3 = psQ.t[:, 0:512].rearrange("p (h d) -> p h d", h=8)
                self.cp(ACT, qz.t[:, 0::2, 0:64], q3[:, 0::2, :], [psQ], [qz])
                self.cp(ACT, qz.t[:, 1::2, 64:128], q3[:, 1::2, :], [psQ], [qz])
                yield
            else:
                src = psQ.t[:, 0:512].rearrange("p (h d) -> p h d", h=8)
                d1 = r1.t[:, 0:512].rearrange("p (h d) -> p h d", h=8)
                d2 = r2.t[:, 0:512].rearrange("p (h d) -> p h d", h=8)
                self.tt(DVE, d1, src, rt.t[:, 0:64].unsqueeze(1).broadcast_to([128, 8, 64]), ALU.mult, [psQ, rt], [r1])
                yield
                self.tt(DVE, d2[:, :, 0:32], src[:, :, 32:64], rt.t[:, 64:96].unsqueeze(1).broadcast_to([128, 8, 32]), ALU.mult,
                        [psQ, rt], [r2])
                self.tt(DVE, d2[:, :, 32:64], src[:, :, 0:32], rt.t[:, 96:128].unsqueeze(1).broadcast_to([128, 8, 32]), ALU.mult,
                        [psQ, rt], [r2])
                yield
                self.tt(POOL, qz.t[:, 0::2, 0:64], d1[:, 0::2, :], d2[:, 0::2, :], ALU.add, [r1, r2], [qz])
                self.tt(POOL, qz.t[:, 1::2, 64:128], d1[:, 1::2, :], d2[:, 1::2, :], ALU.add, [r1, r2], [qz])
                yield
            proj(psQ, hs, 2048, 256)
            yield
            self.cp(ACT, Vb.t[:, blk, :, 0:64], psQ.t[:, 128:256].rearrange("p (h d) -> p h d", h=2), [psQ], [Vbb[blk]])
            k3 = psQ.t[:, 0:128].rearrange("p (h d) -> p h d", h=2)
            if is_ctx:
                for dup in range(2):
                    self.cp(ACT, kdup.t[:, :, dup, :], k3, [psQ], [kdup])
                yield
            else:
                e1 = r1.t[:, 512:640].rearrange("p (h d) -> p h d", h=2)
                e2 = r2.t[:, 512:640].rearrange("p (h d) -> p h d", h=2)
                self.tt(DVE, e1, k3, rt.t[:, 0:64].unsqueeze(1).broadcast_to([128, 2, 64]), ALU.mult, [psQ, rt], [r1])
                self.tt(DVE, e2[:, :, 0:32], k3[:, :, 32:64], rt.t[:, 64:96].unsqueeze(1).broadcast_to([128, 2, 32]), ALU.mult,
                        [psQ, rt], [r2])
                self.tt(DVE, e2[:, :, 32:64], k3[:, :, 0:32], rt.t[:, 96:128].unsqueeze(1).broadcast_to([128, 2, 32]), ALU.mult,
                        [psQ, rt], [r2])
                yield
                for dup in range(2):
                    self.tt(POOL, kdup.t[:, :, dup, :], e1, e2, ALU.add, [r1, r2], [kdup])
                yield
            for h in range(8):
                self.tr(psTb.t[:, h, :], qz.t[:, h, :], self.identb.t[:], [qz, self.identb], [psTb])
            self.cp(ACT, qT[s3].t[:], psTb.t[:], [psTb], [qT[s3]])
            yield
            for kh in range(2):
                self.tr(psTb.t[:, kh, :], kdup.t[:, kh, :, :].rearrange("p a d -> p (a d)"), self.identb.t[:],
                        [kdup, self.identb], [psTb])
            self.cp(ACT, KT.t[:, :, blk, :], psTb.t[:, 0:2, :], [psTb], [KTb[blk]])
            yield

        def streamB(blk):
            s3 = blk % 3
            is_ctx = blk < NCB
            ym = ymix[s3]
            xt = xblk[blk % 4]
            if is_ctx:
                kbs = [(0, None), (1, None)]
            else:
                kbs = [(0, None), (1, None)]
                if blk - 1 >= NCB:
                    kbs.append((blk - 1, self.mbprev))
                kbs.append((blk, None))
                if blk + 1 < NBLK:
                    kbs.append((blk + 1, self.mbnext))
            nk = len(kbs)
            for kh in range(2):
                for ki, (kb, mb) in enumerate(kbs):
                    pss = psS[(kh * 5 + ki) % 2]
                    if mb is not None:
                        self.mmx(pss.t[:], self.identb.t[:], mb.t[:].rearrange("p h q -> p (h q)"), True, False,
                                 [self.identb, mb], [pss])
                    for hl in range(4):
                        h = 4 * kh + hl
                        if mb is not None:
                            self.mmx(pss.t[:, hl * 128:(hl + 1) * 128], KT.t[:, kh, kb, :], qT[s3].t[:, h, :], False, hl == 3,
                                     [KTb[kb], qT[s3]], [pss])
                        else:
                            self.mm(pss.t[:, hl * 128:(hl + 1) * 128], KT.t[:, kh, kb, :], qT[s3].t[:, h, :],
                                    True, True, [KTb[kb], qT[s3]], [pss])
                    yield
                    pt = PT[kh][ki]
                    self.act(pt.t[:], pss.t[:], AF.Exp, [pss], [pt], scale=0.125)
                    yield
                for hl in range(4):
                    for ki, (kb, mb) in enumerate(kbs):
                        self.mm(psV.t[:, hl, 0:65], PT[kh][ki].t[:, hl * 128:(hl + 1) * 128], Vb.t[:, kb, kh, 0:65],
                                ki == 0, ki == nk - 1, [PT[kh][ki], Vbb[kb]], [psV])
                    if hl % 2 == 1:
                        yield
                self.tt(DVE, den.t[:, 4 * kh:4 * kh + 4], psV.t[:, :, 64], esink.t[:, 4 * kh:4 * kh + 4], ALU.add,
                        [psV, esink], [den])
                self.P.op(DVE, lambda e, kh=kh: e.reciprocal(out=rden.t[:, 4 * kh:4 * kh + 4], in_=den.t[:, 4 * kh:4 * kh + 4]),
                          [den], [rden])
                yv = ybt.t[:, kh * 256:(kh + 1) * 256].rearrange("p (h d) -> p h d", h=4)
                self.stt(DVE, yv, psV.t[:, :, 0:64], 0.5, rden.t[:, 4 * kh:4 * kh + 4].unsqueeze(2).broadcast_to([128, 4, 64]),
                         ALU.mult, ALU.mult, [psV, rden], [ybt])
                yield
            self.tt(POOL, ym.t[:, 512:1024], ybt.t[:], sgb2[s3].t[:], ALU.mult, [ybt, sgb2[s3]], [ym])
            yield
            for c in range(8):
                self.tr(psTb.t[:, c, :], ym.t[:, c * 128:(c + 1) * 128], self.identb.t[:], [ym, self.identb], [psTb])
            self.cp(ACT, ymixT.t[:], psTb.t[:], [psTb], [ymixT])
            yield
            g = self.gate[1]
            for n in range(2):
                for kc in range(8):
                    self.mm(psY.t[:], ymixT.t[:, kc, :], W_out.t[:, kc, n * 512:(n + 1) * 512], kc == 0, kc == 7,
                            [ymixT, W_out], [psY])
                yield
                if is_ctx:
                    self.tt(DVE, ytmp.t[:, n * 512:(n + 1) * 512], psY.t[:], g.t[:, n * 512:(n + 1) * 512], ALU.mult, [psY, g], [ytmp])
                else:
                    self.tt(DVE, xt.t[:, n * 512:(n + 1) * 512], psY.t[:], xt.t[:, n * 512:(n + 1) * 512], ALU.add, [psY, xt], [xt])
                yield
            if is_ctx:
                self.tt(POOL, xt.t[:], ytmp.t[:], xt.t[:], ALU.add, [ytmp, xt], [xt])
                yield
            self.act(ytmp.t[:], xt.t[:], AF.Square, [xt], [ytmp, ssq_next], accum=ssq_next.t[:, blk:blk + 1])
            self.dma(SP, self.x_dst(l, blk), xt.t[:], xt, r=[xt])
            yield

        class _Dry:
            def op(self, *a, **k):
                pass

            def dma(self, *a, **k):
                pass

        def count_steps(gen_fn, blk):
            real = self.P
            self.P = _Dry()
            try:
                n = sum(1 for _ in gen_fn(blk))
            finally:
                self.P = real
            return n + 1

        def run_streams(items):
            st_ = [[g_, 0, float(n_)] for g_, n_ in items]
            while st_:
                st_.sort(key=lambda e_: (e_[1] + 1) / e_[2])
                e_ = st_[0]
                try:
                    next(e_[0])
                    e_[1] += 1
                except StopIteration:
                    st_.remove(e_)

        nsteps = {}

        def item(fn, blk):
            key = (fn.__name__, blk < NCB, blk == NCB, blk == NBLK - 1)
            if key not in nsteps:
                nsteps[key] = count_steps(fn, blk)
            return (fn(blk), nsteps[key])

        for t in range(NBLK + 3):
            items = []
            if t - 3 >= 0:
                items.append(item(streamB, t - 3))
            if 0 <= t - 1 < NBLK:
                items.append(item(streamS1, t - 1))
                items.append(item(streamS2, t - 1))
            if t < NBLK:
                items.append(item(streamN, t))
            run_streams(items)
            if t == NCB + 2:
                for kc in range(8):
                    self.tt(POOL if kc % 2 else DVE, W_out.t[:, kc, :], W_out.t[:, kc, :], self.gate[0].t[:], ALU.mult,
                            [W_out, self.gate[0]], [W_out])
            if self.ck("E%d" % t):
                return


    def odd_layer(self, st, l):
        i = l // 2
        need_ctx = l < DEPTH - 1
        last = (l == DEPTH - 1) and self.do_final
        NT = 256
        NSB = L // NT
        W_in = self.sb(st, "cWin", [128, 8, 2 * D], BF16)
        W_a = self.sb(st, "cWa", [128, 2, 4, 2, 256], BF16)
        W_i = self.sb(st, "cWi", [128, 2, 4, 2, 256], BF16)
        W_out = self.sb(st, "cWout", [128, 8, D], BF16)
        wsrc = self.c_w_in[i].rearrange("(kc p) n -> p kc n", p=128)
        Wxr, Wgg = Buf("cWin_xr"), Buf("cWin_g")
        for (c0, bb) in ((0, Wxr), (D, Wgg)):
            for kc in range(0, 8, 4):
                self.dma(POOL, W_in.t[:, kc:kc + 4, c0:c0 + D], wsrc[:, kc:kc + 4, c0:c0 + D], bb, w=[bb])
        for d in range(2):
            self.dma(POOL, W_a.t[:, d], self.c_w_a[i, d].rearrange("h (ic p) j -> p h ic j", p=128), W_a, w=[W_a])
            self.dma(POOL, W_i.t[:, d], self.c_w_i[i, d].rearrange("h (ic p) j -> p h ic j", p=128), W_i, w=[W_i])
        wosrc = self.c_w_out[i].rearrange("(kc p) n -> p kc n", p=128)
        for kc in range(0, 8, 2):
            self.dma(POOL, W_out.t[:, kc:kc + 2, :], wosrc[:, kc:kc + 2, :], W_out, w=[W_out])
        coefh = self.sb(st, "coefh", [128, 2, 8])
        hba = self.sb(st, "hba", [128, 2, 8])
        hbi = self.sb(st, "hbi", [128, 2, 8])
        for d in range(2):
            lam0 = CM["lam%d_%d" % (i, d)]
            self.act(coefh.t[:, d, :], self.COLS.t[:, lam0:lam0 + 8], AF.Exp, [self.COLS], [coefh], scale=-1.0)
            self.act(coefh.t[:, d, :], coefh.t[:, d, :], AF.Ln, [coefh], [coefh], bias=1.0)
            self.ts(DVE, coefh.t[:, d, :], coefh.t[:, d, :], -4.0, None, ALU.mult, None, [coefh], [coefh])
            b0 = CM["ba%d_%d" % (i, d)]
            self.ts(DVE, hba.t[:, d, :], self.COLS.t[:, b0:b0 + 8], 0.5, None, ALU.mult, None, [self.COLS], [hba])
            b0 = CM["bi%d_%d" % (i, d)]
            self.ts(DVE, hbi.t[:, d, :], self.COLS.t[:, b0:b0 + 8], 0.5, None, ALU.mult, None, [self.COLS], [hbi])
        cw0 = [CM["cw%d_%d" % (i, j)] for j in range(4)]
        cb0 = CM["cb%d" % i]
        DG = self.sb(st, "DG", [128, 4, 8, 128], BF16)
        for j in range(4):
            for c in range(8):
                self.ts(DVE, DG.t[:, j, c, :], self.identf.t[:], self.COLS.t[:, cw0[j] + c:cw0[j] + c + 1], None, ALU.mult, None,
                        [self.identf, self.COLS], [DG])
        if last:
            fg = self.sb(st, "fg", [128, D])
            self.dma(SP, fg.t[:], self.rows_d[:, RM["fg"]:RM["fg"] + D], fg, w=[fg])

        xb = [self.sb(st, "xb%d" % k, [128, D]) for k in range(4)]
        xbi = [0]
        xn = self.sb(st, "cxn", [128, D], BF16)
        hTr = [self.sb(st, "chT%d" % k, [128, 8, NT], BF16) for k in range(2)]
        zr = [self.sbc(st, "z0", [128, 8, NT], F32, 8), self.sbc(st, "z1", [128, 8, NT], F32, 8)]
        zbr = [self.sbc(st, "zb0", [128, 8, NT], BF16, 8), self.sbc(st, "zb1", [128, 8, NT], BF16, 8)]
        sgr = [self.sbc(st, "sg0", [128, 8, NT], BF16, 8)]
        gth_ = self.sb(st, "gth", [128, NT])
        gth = [gth_, gth_]
        A = self.sbc(st, "A", [128, 8, NT], F32, 8)
        Wq = self.sbc(st, "Wq", [128, 8, NT], F32, 8)
        TI = self.sbc(st, "TI", [128, 8, NT], F32, 8)
        thr = [self.sb(st, "thr%d" % k, [128, NT]) for k in range(2)]
        S0r = [self.sbc(st, "S00", [128, 8, NT], F32, 8)]
        S1 = self.sbc(st, "S1", [128, 8, NT], F32, 8)
        carry = [self.sbc(st, "carry%d" % k, [128, 8], F32, 8) for k in range(2)]
        ytmp = self.sb(st, "cytmp", [128, D])
        ssq1 = self.sb(st, "ssq1", [128, 2])
        v1 = self.sb(st, "v1", [128, 2])
        r1_ = self.sb(st, "r1", [128, 2])
        hs1 = self.sb(st, "hs1", [128, 2])
        psT = [self.ps(st, "cpsT%d" % k, [128, 8, 128], BF16) for k in range(2)]
        psP = [self.ps(st, "cpsP%d" % k, [128, 512]) for k in range(3)]
        psG = [self.ps(st, "cpsG%d" % k, [128, 512]) for k in range(3)]
        rot = [0, 0]

        def nextP():
            rot[0] += 1
            return psP[rot[0] % 3]

        def nextG():
            rot[1] += 1
            return psG[rot[1] % 3]

        ssq_next = self.SSQ[(l + 1) % 2]
        xbase = self.X[l % 2] if l != self.layers[0] else self.dbg_xin
        ZSb = [Buf("ZS%d" % k) for k in range(NSB)]
        SFb = [Buf("SF%d" % k) for k in range(NSB)]
        SGb = [Buf("SG%d" % k) for k in range(NSB)]
        ZBb = [Buf("ZB%d" % k) for k in range(NSB)]
        wscaled = [False]

        def scale_wout():
            for kc in range(8):
                self.tt(POOL if kc % 2 else DVE, W_out.t[:, kc, :], W_out.t[:, kc, :], self.gate[0].t[:], ALU.mult,
                        [W_out, self.gate[0]], [W_out])
            wscaled[0] = True

        def next_xb():
            xbi[0] += 1
            return xb[xbi[0] % 4]

        def x_load(blk):
            xt = next_xb()
            self.dma(SP, xt.t[:], xbase[blk * 128:(blk + 1) * 128, :], xt, w=[xt])
            return xt

        def g_load_norm(blk0, nb, is_ctx, hT, xts):
            i_ = 1 if is_ctx else 0
            for tb in range(nb):
                xt = xts[tb]
                self.ts(DVE, xn.t[:], xt.t[:], self.rstd.t[:, blk0 + tb:blk0 + tb + 1], None, ALU.mult, None,
                        [xt, self.rstd], [xn])
                yield
                pT = psT[tb % 2]
                for c in range(8):
                    self.tr(pT.t[:, c, :], xn.t[:, c * 128:(c + 1) * 128], self.identb.t[:], [xn, self.identb], [pT])
                yield
                for c in range(8):
                    self.ts(DVE, hT.t[:, c, tb * 128:(tb + 1) * 128], pT.t[:, c, :], self.gs[i_].t[:, c:c + 1],
                            self.sh[i_].t[:, c:c + 1], ALU.mult, ALU.add, [pT, self.gs[i_], self.sh[i_]], [hT])
                    if c % 4 == 3:
                        yield

        def g_project(nt, hT, xr_t, want_g, sg):
            for c in range(8):
                p = nextP()
                for kc in range(8):
                    self.mm(p.t[:, 0:nt], W_in.t[:, kc, c * 128:(c + 1) * 128], hT.t[:, kc, 0:nt], kc == 0, kc == 7,
                            [Wxr, hT], [p])
                yield
                self.cp(DVE, xr_t.t[:, c, 2:2 + nt], p.t[:, 0:nt], [p], [xr_t.bs[c]])
                yield
            if want_g:
                for c in range(8):
                    p = nextP()
                    for kc in range(8):
                        self.mm(p.t[:, 0:nt], W_in.t[:, kc, D + c * 128:D + (c + 1) * 128], hT.t[:, kc, 0:nt], kc == 0, kc == 7,
                                [Wgg, hT], [p])
                    yield
                    w_ = gth[c % 2]
                    self.act(w_.t[:, 0:nt], p.t[:, 0:nt], AF.Tanh, [p], [w_], scale=0.5)
                    yield
                    self.stt(DVE, sg.t[:, c, 0:nt], w_.t[:, 0:nt], 1.0, p.t[:, 0:nt], ALU.add, ALU.mult, [w_, p], [sg.bs[c]])
                    yield

        def g_conv(nt, xr_t, z, zb):
            for c in range(8):
                pz = nextG()
                for j in range(4):
                    self.mm(pz.t[:, 0:nt], DG.t[:, j, c, :], xr_t.t[:, c, j:j + nt], j == 0, j == 3, [DG, xr_t.bs[c]], [pz])
                yield
                self.act(zb.t[:, c, 0:nt], pz.t[:, 0:nt], AF.Identity, [pz, self.COLS], [zb.bs[c]],
                         bias=self.COLS.t[:, cb0 + c:cb0 + c + 1])
                yield
                self.ts(DVE, z.t[:, c, 0:nt], pz.t[:, 0:nt], self.COLS.t[:, cb0 + c:cb0 + c + 1], None, ALU.add, None,
                        [pz, self.COLS], [z.bs[c]])
                yield

        def g_gates_scan(nt, d, use_carry, z, zb, Sout, a2_eng, mid=None):
            for jc in range(8):
                hh = jc // 2
                jl = jc % 2
                pa = nextG()
                for ic in range(2):
                    self.mm(pa.t[:, 0:nt], W_a.t[:, d, hh, ic, jl * 128:(jl + 1) * 128], zb.t[:, 2 * hh + ic, 0:nt],
                            ic == 0, ic == 1, [W_a, zb.bs[2 * hh + ic]], [pa])
                pi_ = nextG()
                for ic in range(2):
                    self.mm(pi_.t[:, 0:nt], W_i.t[:, d, hh, ic, jl * 128:(jl + 1) * 128], zb.t[:, 2 * hh + ic, 0:nt],
                            ic == 0, ic == 1, [W_i, zb.bs[2 * hh + ic]], [pi_])
                yield
                t_ = thr[jc % 2]
                self.act(t_.t[:, 0:nt], pa.t[:, 0:nt], AF.Tanh, [pa, hba], [t_], scale=0.5, bias=hba.t[:, d, jc:jc + 1])
                self.act(A.t[:, jc, 0:nt], t_.t[:, 0:nt], AF.Exp, [t_, coefh], [A.bs[jc]], scale=coefh.t[:, d, jc:jc + 1],
                         bias=coefh.t[:, d, jc:jc + 1])
                yield
                self.act(TI.t[:, jc, 0:nt], pi_.t[:, 0:nt], AF.Tanh, [pi_, hbi], [TI.bs[jc]], scale=0.5,
                         bias=hbi.t[:, d, jc:jc + 1])
                if a2_eng == ACT:
                    self.act(Wq.t[:, jc, 0:nt], A.t[:, jc, 0:nt], AF.Square, [A.bs[jc]], [Wq.bs[jc]])
                else:
                    self.tt(POOL, Wq.t[:, jc, 0:nt], A.t[:, jc, 0:nt], A.t[:, jc, 0:nt], ALU.mult, [A.bs[jc]], [Wq.bs[jc]])
                yield
            if mid is not None:
                yield from mid()
            for jc in range(8):
                self.act(Wq.t[:, jc, 0:nt], Wq.t[:, jc, 0:nt], AF.Sqrt, [Wq.bs[jc]], [Wq.bs[jc]], scale=-0.25, bias=0.25)
                if jc % 2 == 1:
                    yield
            for jc in range(8):
                self.tt(POOL, Wq.t[:, jc, 0:nt], Wq.t[:, jc, 0:nt], z.t[:, jc, 0:nt], ALU.mult, [Wq.bs[jc], z.bs[jc]], [Wq.bs[jc]])
                if jc % 2 == 1:
                    yield
            for jc in range(8):
                self.stt(DVE, TI.t[:, jc, 0:nt], TI.t[:, jc, 0:nt], 1.0, Wq.t[:, jc, 0:nt], ALU.add, ALU.mult,
                         [TI.bs[jc], Wq.bs[jc]], [TI.bs[jc]])
                if use_carry:
                    ini = carry[d].t[:, jc:jc + 1]
                    ideps = [carry[d].bs[jc]]
                else:
                    ini = 0.0
                    ideps = []
                if d == 0:
                    o_, a_, b_ = Sout.t[:, jc, 0:nt], A.t[:, jc, 0:nt], TI.t[:, jc, 0:nt]
                else:
                    o_, a_, b_ = Sout.t[:, jc, 0:nt][:, ::-1], A.t[:, jc, 0:nt][:, ::-1], TI.t[:, jc, 0:nt][:, ::-1]
                self.P.op(DVE, lambda e, o_=o_, a_=a_, b_=b_, ini=ini: e.tensor_tensor_scan(
                    out=o_, data0=a_, data1=b_, initial=ini, op0=ALU.mult, op1=ALU.add),
                    [A.bs[jc], TI.bs[jc]] + ideps, [Sout.bs[jc]], dur=0.2 + nt * (0.0022 if d == 0 else 0.0045))
                col = nt - 1 if d == 0 else 0
                self.cp(DVE, carry[d].t[:, jc:jc + 1], Sout.t[:, jc, col:col + 1], [Sout.bs[jc]], [carry[d].bs[jc]])
                yield

        def g_combine(nt, S0, sg, ymT):
            for c in range(0, 8, 2):
                self.tt(POOL, S0.t[:, c:c + 2, 0:nt], S0.t[:, c:c + 2, 0:nt], S1.t[:, c:c + 2, 0:nt], ALU.add,
                        S0.bs[c:c + 2] + S1.bs[c:c + 2], S0.bs[c:c + 2])
            yield
            self.stt(DVE, ymT.t[:, :, 0:nt], S0.t[:, :, 0:nt], 0.5, sg.t[:, :, 0:nt], ALU.mult, ALU.mult, S0.bs + sg.bs, [ymT])
            yield

        def g_out(tb, blk, gate_t, xt, ymT, final):
            for n in range(2):
                py = nextP()
                for kc in range(8):
                    self.mm(py.t[:], ymT.t[:, kc, tb * 128:(tb + 1) * 128], W_out.t[:, kc, n * 512:(n + 1) * 512],
                            kc == 0, kc == 7, [ymT, W_out], [py])
                yield
                if gate_t is None:
                    self.tt(DVE, xt.t[:, n * 512:(n + 1) * 512], py.t[:], xt.t[:, n * 512:(n + 1) * 512], ALU.add, [py, xt], [xt])
                else:
                    self.tt(DVE, ytmp.t[:, n * 512:(n + 1) * 512], py.t[:], gate_t.t[:, n * 512:(n + 1) * 512], ALU.mult,
                            [py, gate_t], [ytmp])
                yield
            if gate_t is not None:
                self.tt(POOL, xt.t[:], ytmp.t[:], xt.t[:], ALU.add, [ytmp, xt], [xt])
                yield
            if final:
                self.act(ytmp.t[:], xt.t[:], AF.Square, [xt], [ytmp, ssq1], accum=ssq1.t[:, tb:tb + 1])
            else:
                self.act(ytmp.t[:], xt.t[:], AF.Square, [xt], [ytmp, ssq_next], accum=ssq_next.t[:, blk:blk + 1])
                self.dma(SP, self.x_dst(l, blk), xt.t[:], xt, r=[xt])
            yield

        def g_final(blk0, xts):
            self.ts(DVE, v1.t[:], ssq1.t[:], 1.0 / D, EPS, ALU.mult, ALU.add, [ssq1], [v1])
            self.act(hs1.t[:], v1.t[:], AF.Sqrt, [v1], [hs1])
            yield
            self.P.op(DVE, lambda e: e.reciprocal(out=r1_.t[:], in_=hs1.t[:]), [hs1], [r1_])
            self.memset(POOL, ssq1.t[:], 0.0, [ssq1])
            for tb, xt in enumerate(xts):
                self.stt(DVE, xt.t[:], xt.t[:], r1_.t[:, tb:tb + 1], fg.t[:], ALU.mult, ALU.mult, [xt, r1_, fg], [xt])
                blk = blk0 + tb
                self.dma(SP, self.out_d[(blk - NCB) * 128:(blk - NCB + 1) * 128, :], xt.t[:], xt, r=[xt])
                yield

        def run_streams(gens):
            active = list(gens)
            while active:
                for g_ in list(active):
                    try:
                        next(g_)
                    except StopIteration:
                        active.remove(g_)

        def seq(*gens):
            for g_ in gens:
                yield from g_

        z = zr[0]
        sg0 = sgr[0]
        S0 = S0r[0]
        with ExitStack() as p1:
            XR = [self.sbc(p1, "XR%d" % k, [128, 8, NT + 3], BF16, 8) for k in range(3)]
            nt = LC
            self.memset(POOL, XR[0].t[:, :, 0:2], 0.0, XR[0].bs)
            self.memset(POOL, XR[0].t[:, :, 2 + nt:3 + nt], 0.0, XR[0].bs)
            xts = [x_load(tb) for tb in range(NCB)]
            run_streams([seq(g_load_norm(0, NCB, True, hTr[0], xts), g_project(nt, hTr[0], XR[0], need_ctx, sg0),
                             g_conv(nt, XR[0], z, zbr[0]))])
            run_streams([g_gates_scan(nt, 0, False, z, zbr[0], S0, POOL)])
            run_streams([g_gates_scan(nt, 1, False, z, zbr[0], S1, POOL)])
            if need_ctx:
                run_streams([g_combine(nt, S0, sg0, hTr[1])])
                for tb in range(NCB):
                    run_streams([g_out(tb, tb, self.gate[1], xts[tb], hTr[1], False)])
            scale_wout()

            pre = {}

            def NS(it):
                yield from g_load_norm(NCB + it * 2, 2, False, hTr[it % 2], pre.pop(it))

            def PJ(it):
                cur = XR[it % 3]
                prv = XR[(it - 1) % 3]
                hT = hTr[it % 2]
                yield from g_project(NT, hT, cur, True, sg0)
                sgd = self.SG[:, it * NT:(it + 1) * NT].rearrange("(c p) t -> p c t", p=128)
                self.dma(SP, sgd, sg0.t[:], sg0.bs[0], r=sg0.bs, w=[SGb[it]])
                if it == 0:
                    self.memset(POOL, cur.t[:, :, 0:2], 0.0, cur.bs)
                else:
                    self.cp(POOL, cur.t[:, :, 0:2], prv.t[:, :, NT:NT + 2], prv.bs, cur.bs)
                    self.cp(POOL, prv.t[:, :, NT + 2:NT + 3], cur.t[:, :, 2:3], cur.bs, prv.bs)
                if it == NSB - 1:
                    self.memset(POOL, cur.t[:, :, NT + 2:NT + 3], 0.0, cur.bs)
                yield

            def FWD(sb_):
                xr_t = XR[sb_ % 3]
                z = zr[sb_ % 2]
                zb = zbr[sb_ % 2]
                yield from g_conv(NT, xr_t, z, zb)
                zd = self.ZS[:, sb_ * NT:(sb_ + 1) * NT].rearrange("(c p) t -> p c t", p=128)
                self.dma(SP, zd, z.t[:], z.bs[0], r=z.bs, w=[ZSb[sb_]])
                zbd = self.ZB[:, sb_ * NT:(sb_ + 1) * NT].rearrange("(c p) t -> p c t", p=128)
                self.dma(SP, zbd, zb.t[:], zb.bs[0], r=zb.bs, w=[ZBb[sb_]])
                yield
                yield from g_gates_scan(NT, 0, True, z, zb, S0, POOL)
                sfd = self.SF[:, sb_ * NT:(sb_ + 1) * NT].rearrange("(c p) t -> p c t", p=128)
                self.dma(SP, sfd, S0.t[:], S0.bs[0], r=S0.bs, w=[SFb[sb_]])
                yield

            pre[0] = [x_load(NCB), x_load(NCB + 1)]
            self.run_prop([(NS(0), 8)])
            for it in range(NSB + 2):
                if it + 1 < NSB:
                    pre[it + 1] = [x_load(NCB + (it + 1) * 2), x_load(NCB + (it + 1) * 2 + 1)]
                items = []
                if it - 2 >= 0:
                    items.append((FWD(it - 2), 66))
                if it < NSB:
                    items.append((PJ(it), 41))
                if it + 1 < NSB:
                    items.append((NS(it + 1), 8))
                self.run_prop(items)
            self.P.barrier()

        with ExitStack() as p2:
            sgr.append(self.sbc(p2, "sg1", [128, 8, NT], BF16, 8))
            S0r.append(self.sbc(p2, "S01", [128, 8, NT], F32, 8))

            def loads(sb_):
                k = sb_ % 2
                zd = self.ZS[:, sb_ * NT:(sb_ + 1) * NT].rearrange("(c p) t -> p c t", p=128)
                self.dma(SP, zr[k].t[:], zd, zr[k].bs[0], r=[ZSb[sb_]], w=zr[k].bs)
                sfd = self.SF[:, sb_ * NT:(sb_ + 1) * NT].rearrange("(c p) t -> p c t", p=128)
                self.dma(SP, S0r[k].t[:], sfd, S0r[k].bs[0], r=[SFb[sb_]], w=S0r[k].bs)
                sgd = self.SG[:, sb_ * NT:(sb_ + 1) * NT].rearrange("(c p) t -> p c t", p=128)
                self.dma(SP, sgr[k].t[:], sgd, sgr[k].bs[0], r=[SGb[sb_]], w=sgr[k].bs)
                zbd = self.ZB[:, sb_ * NT:(sb_ + 1) * NT].rearrange("(c p) t -> p c t", p=128)
                self.dma(SP, zbr[k].t[:], zbd, zbr[k].bs[0], r=[ZBb[sb_]], w=zbr[k].bs)

            def G(sb_):
                k = sb_ % 2
                if sb_ - 1 >= 0:
                    loads(sb_ - 1)
                yield
                yield from g_gates_scan(NT, 1, True, zr[k], zbr[k], S1, POOL)
                yield from g_combine(NT, S0r[k], sgr[k], hTr[k])

            def O(sb_):
                k = sb_ % 2
                blk0 = NCB + sb_ * 2
                xts = [x_load(blk0 + tb) for tb in range(2)]
                yield
                for tb in range(2):
                    yield from g_out(tb, blk0 + tb, None, xts[tb], hTr[k], last)
                if last:
                    yield from g_final(blk0, xts)

            self.memset(POOL, ssq1.t[:], 0.0, [ssq1])
            loads(NSB - 1)
            for sb_ in range(NSB - 1, -2, -1):
                items = []
                if sb_ >= 0:
                    g_ = G(sb_)
                    for _ in range(25):
                        next(g_)
                    items.append((g_, 22))
                if sb_ + 1 < NSB:
                    items.append((O(sb_ + 1), 15))
                self.run_prop(items)
            self.P.barrier()


_CACHE = {}


def make_in_maps(inp, n_cores=8):
    f = lambda a: np.ascontiguousarray(np.asarray(a, np.float32))
    rows = np.zeros((128, RM.n), np.float32)
    for l in range(DEPTH):
        rows[:, RM["bg%d" % l]:RM["bg%d" % l] + D] = f(inp["b_mod"])[l, 2 * D:3 * D][None, :]
    for i in range(2):
        rows[:, RM["lng%d" % i]:RM["lng%d" % i] + 512] = f(inp["a_ln_g"])[i][None, :]
        rows[:, RM["lnb%d" % i]:RM["lnb%d" % i] + 512] = f(inp["a_ln_b"])[i][None, :]
        rows[:, RM["sink%d" % i]:RM["sink%d" % i] + 8] = f(inp["b_sink"])[i][None, :]
    rows[:, RM["fg"]:RM["fg"] + D] = f(inp["final_g"])[None, :]
    rope = rope_table()
    a_w_sT = np.ascontiguousarray(np.transpose(f(inp["a_w_s"]), (0, 3, 1, 2)))
    shared = {
        "rows": rows, "rope": rope, "w_mod": f(inp["w_mod"]), "ab_w_in": f(inp["ab_w_in"]),
        "ab_w_out": f(inp["ab_w_out"]), "a_w_sT": a_w_sT, "c_w_in": f(inp["c_w_in"]),
        "c_w_a": f(inp["c_w_a"]), "c_w_i": f(inp["c_w_i"]), "c_w_out": f(inp["c_w_out"]),
    }
    maps = []
    for b in range(n_cores):
        cols = np.zeros((128, CM.n), np.float32)
        for l in range(DEPTH):
            cols[:, CM["ng%d" % l]:CM["ng%d" % l] + 8] = col8(inp["norm_g"][l])
            cols[:, CM["bsh%d" % l]:CM["bsh%d" % l] + 8] = col8(inp["b_mod"][l, 0:D])
            cols[:, CM["bsc%d" % l]:CM["bsc%d" % l] + 8] = col8(inp["b_mod"][l, D:2 * D])
        cols[:, CM["c"]:CM["c"] + 8] = col8(inp["c"][b])
        cols[:, CM["cctx"]:CM["cctx"] + 8] = col8(inp["c_ctx"])
        for i in range(2):
            for j in range(4):
                cols[:, CM["cw%d_%d" % (i, j)]:CM["cw%d_%d" % (i, j)] + 8] = col8(inp["c_conv_w"][i, j])
            cols[:, CM["cb%d" % i]:CM["cb%d" % i] + 8] = col8(inp["c_conv_b"][i])
            for d in range(2):
                cols[:, CM["ba%d_%d" % (i, d)]:CM["ba%d_%d" % (i, d)] + 8] = col8(inp["c_b_a"][i, d])
                cols[:, CM["bi%d_%d" % (i, d)]:CM["bi%d_%d" % (i, d)] + 8] = col8(inp["c_b_i"][i, d])
                cols[:, CM["lam%d_%d" % (i, d)]:CM["lam%d_%d" % (i, d)] + 8] = col8(inp["c_lam"][i, d])
            cols[:, CM["bs%d" % i]:CM["bs%d" % i] + 4] = np.asarray(inp["a_b_s"][i], np.float32).T
        m = dict(shared)
        m["cols"] = cols
        m["x"] = f(inp["x"][b])
        m["ctx"] = f(inp["ctx"][b])
        maps.append(m)
    return maps


def kernel(**inputs):
    if "nc" not in _CACHE:
        _CACHE["nc"] = Builder().build()
    nc = _CACHE["nc"]
    maps = make_in_maps(inputs, 8)
    res = run_bass_kernel_spmd(nc, maps, core_ids=list(range(8)))
    out = np.stack([np.asarray(r["out"], np.float32) for r in res.results], axis=0)
    return out
```

```python
import math
from contextlib import ExitStack

import numpy as np
import concourse.bass as bass
import concourse.mybir as mybir
from concourse.bass_utils import run_bass_kernel_spmd

F32 = mybir.dt.float32
BF16 = mybir.dt.bfloat16
ALU = mybir.AluOpType
AF = mybir.ActivationFunctionType
AX = mybir.AxisListType

D = 1024
L = 4096
LC = 256
NLB = L // 128
NCB = LC // 128
NBLK = NLB + NCB
DEPTH = 4
EPS = 1e-6
AB_IN = 2816
GELU_C = 0.7978845608028654
SQ_044 = math.sqrt(0.044715)

PE, ACT, DVE, POOL, SP = "tensor", "scalar", "vector", "gpsimd", "sync"
ENGS = (PE, ACT, DVE, POOL, SP)


class Buf:
    __slots__ = ("name", "writer", "readers", "sem", "excl")

    def __init__(self, name, excl=False):
        self.name = name
        self.writer = None
        self.readers = {}
        self.sem = None
        self.excl = excl


class Op:
    __slots__ = ("eng", "fn", "deps", "odeps", "signal", "sig", "isdma", "key", "waits", "idx", "dur", "lat", "seg",
                 "nin", "succ", "rt", "fin", "tbl", "bl")

    def __init__(self, eng, fn, isdma=False):
        self.eng = eng
        self.fn = fn
        self.deps = []
        self.odeps = []
        self.signal = False
        self.sig = None
        self.isdma = isdma
        self.key = eng
        self.waits = None
        self.idx = 0
        self.dur = 0.3
        self.lat = 0.0
        self.seg = 0
        self.nin = 0
        self.succ = None
        self.rt = 0.0
        self.fin = 0.0
        self.tbl = None
        self.bl = 0.0


def _b(x):
    return x if isinstance(x, Buf) else x.b


class Tile:
    __slots__ = ("t", "b", "bs")

    def __init__(self, t, b, bs=None):
        self.t = t
        self.b = b
        self.bs = bs


class Prog:
    SCHED = True
    SEM_LAT = 1.0

    def __init__(self, nc, stack):
        self.nc = nc
        self.stack = stack
        self.ops = {e: [] for e in ENGS}
        self.all_ops = []
        self.esem = {e: stack.enter_context(nc.semaphore("es_" + e)) for e in ENGS}
        self.sem_pool = []
        self.sem_live = []
        self.nsem = 0
        self.dma_since_barrier = []
        self.all_dma_sigs = {}
        self.ndma = 0
        self.seg = 0
        self.seg_deps = {0: []}

    def _deps(self, op, reads, writes):
        deps = op.deps
        odeps = op.odeps
        for x in reads:
            b = _b(x)
            w = b.writer
            if w is not None:
                deps.append(w)
            if b.excl:
                for r in b.readers.values():
                    if r.eng != op.eng:
                        deps.append(r)
        for x in writes:
            b = _b(x)
            w = b.writer
            if w is not None:
                if w.isdma or w.eng != op.eng or op.isdma:
                    deps.append(w)
                else:
                    odeps.append(w)
            for r in b.readers.values():
                if r.isdma or r.eng != op.eng or op.isdma:
                    deps.append(r)
                else:
                    odeps.append(r)
        for d in deps:
            d.signal = True
        for x in reads:
            _b(x).readers[op.key] = op
        for x in writes:
            b = _b(x)
            b.writer = op
            b.readers = {}

    def _add(self, o):
        o.idx = len(self.all_ops)
        o.seg = self.seg
        self.all_ops.append(o)

    def op(self, eng, fn, reads=(), writes=(), dur=0.3, tbl=None):
        o = Op(eng, fn)
        o.dur = dur
        o.tbl = tbl
        self._deps(o, reads, writes)
        self._add(o)
        return o

    def dma(self, eng, out, in_, sbuf, reads=(), writes=(), **kw):
        sbuf = _b(sbuf)
        if sbuf.sem is None:
            if self.sem_pool:
                ent = self.sem_pool.pop()
            else:
                ent = [self.stack.enter_context(self.nc.semaphore("ds%d" % self.nsem)), 0, None]
                self.nsem += 1
            sbuf.sem = ent
            self.sem_live.append(sbuf)
        ent = sbuf.sem
        o = Op(eng, lambda e: e.dma_start(out=out, in_=in_, **kw), isdma=True)
        self.ndma += 1
        o.key = ("dma", self.ndma)
        try:
            nbytes = float(out.nbytes())
        except Exception:
            nbytes = 65536.0
        o.dur = 0.6 if eng == POOL else 0.1
        o.lat = 2.0 + nbytes / 120e3
        self._deps(o, reads, writes)
        if ent[2] is not None and ent[2].seg == self.seg:
            o.odeps.append(ent[2])
        ent[2] = o
        ent[1] += 16
        o.sig = (ent[0], ent[1])
        o.signal = True
        self._add(o)
        self.dma_since_barrier.append(o)
        self.all_dma_sigs[id(ent[0])] = o.sig
        return o

    def barrier(self):
        lasts = {}
        for o in self.all_ops:
            if not o.isdma:
                lasts[o.eng] = o
        self._last_hint = lasts
        self.seg += 1
        self.seg_deps[self.seg] = ("BAR", list(self.dma_since_barrier))
        self.dma_since_barrier = []
        for b in self.sem_live:
            self.sem_pool.append(b.sem)
            b.sem = None
        self.sem_live = []

    def _schedule_segment(self, ops):
        seg = ops[0].seg
        for o in ops:
            o.nin = 0
            o.succ = []
            o.rt = 0.0
        for o in ops:
            seen = set()
            for d in o.deps + o.odeps:
                if d.seg == seg and id(d) not in seen:
                    seen.add(id(d))
                    d.succ.append(o)
                    o.nin += 1
        for o in reversed(ops):
            m = 0.0
            for s_ in o.succ:
                v = s_.bl + (self.SEM_LAT if s_.eng != o.eng else 0.0)
                if v > m:
                    m = v
            o.bl = o.dur + o.lat + m
        BLE = ('tensor',)
        cand = {e: [] for e in ENGS}
        for o in ops:
            if o.nin == 0:
                cand[o.eng].append(o)
        free = {e: 0.0 for e in ENGS}
        order = {e: [] for e in ENGS}
        left = len(ops)
        lat = self.SEM_LAT
        cur_tbl = getattr(self, "_cur_tbl", None)
        TSW = 0.9
        while left:
            best = None
            for e in ENGS:
                c = cand[e]
                if not c:
                    continue
                t = free[e]
                pick = None
                if e == ACT:
                    pk = None
                    for o in c:
                        st_ = o.rt if o.rt > t else t
                        if o.tbl is not None and o.tbl != cur_tbl:
                            st_ += TSW
                        k_ = (st_, o.idx)
                        if pk is None or k_ < pk:
                            pk = k_
                            pick = o
                    start = pk[0]
                else:
                    ubl = e in BLE
                    for o in c:
                        if o.rt <= t:
                            if pick is None or pick.rt > t or ((o.bl, -o.idx) > (pick.bl, -pick.idx) if ubl else o.idx < pick.idx):
                                pick = o
                        elif pick is None or (pick.rt > t and (o.rt < pick.rt or (o.rt == pick.rt and o.idx < pick.idx))):
                            pick = o
                    start = max(t, pick.rt)
                if best is None or start < best[0] or (start == best[0] and pick.idx < best[1].idx):
                    best = (start, pick)
            start, o = best
            e = o.eng
            if e == ACT and o.tbl is not None:
                cur_tbl = o.tbl
            cand[e].remove(o)
            order[e].append(o)
            free[e] = start + o.dur
            o.fin = start + o.dur + o.lat
            left -= 1
            for s_ in o.succ:
                r = o.fin + (lat if (s_.eng != e or o.isdma) else 0.0)
                if r > s_.rt:
                    s_.rt = r
                s_.nin -= 1
                if s_.nin == 0:
                    cand[s_.eng].append(s_)
        self._cur_tbl = cur_tbl
        self.sim_time = self.sim_time + max(free.values()) if hasattr(self, "sim_time") else max(free.values())
        return order

    def emit(self):
        nc = self.nc
        segs = {}
        for o in self.all_ops:
            segs.setdefault(o.seg, []).append(o)
        self.ops = {e: [] for e in ENGS}
        last_compute = {}
        for sg in sorted(segs):
            ops = segs[sg]
            if self.SCHED:
                order = self._schedule_segment(ops)
            else:
                order = {e: [o for o in ops if o.eng == e] for e in ENGS}
            info = self.seg_deps.get(sg)
            if info:
                extra = list(last_compute.values()) + info[1]
                for o in extra:
                    o.signal = True
                for e in ENGS:
                    if order[e]:
                        order[e][0].deps = order[e][0].deps + extra
            for e in ENGS:
                self.ops[e].extend(order[e])
                for o in order[e]:
                    if not o.isdma:
                        last_compute[e] = o
        for e in ENGS:
            c = 0
            for o in self.ops[e]:
                if o.isdma:
                    continue
                if o.signal:
                    c += 1
                    o.sig = (self.esem[e], c)
        fin_waits = list(self.all_dma_sigs.values())
        for e in ENGS:
            known = {}
            for o in self.ops[e]:
                need = {}
                for d in o.deps:
                    s, v = d.sig
                    k = id(s)
                    if known.get(k, 0) >= v:
                        continue
                    if k not in need or need[k][1] < v:
                        need[k] = (s, v)
                for k, (s, v) in need.items():
                    known[k] = v
                o.waits = list(need.values())
        self.stats = dict(
            nops={e: len(self.ops[e]) for e in ENGS},
            nwait={e: sum(len(o.waits) for o in self.ops[e]) for e in ENGS},
            nsig={e: sum(1 for o in self.ops[e] if o.signal) for e in ENGS},
            nsem=self.nsem, sim_us=getattr(self, "sim_time", 0.0),
        )

        def run(e, engobj):
            for o in self.ops[e]:
                for s, v in o.waits:
                    engobj.wait_ge(s, v)
                ins = o.fn(engobj)
                if o.isdma:
                    ins.then_inc(o.sig[0], 16)
                elif o.signal:
                    ins.then_inc(o.sig[0], 1)
            if e == SP:
                for s, v in fin_waits:
                    engobj.wait_ge(s, v)

        with nc.Block() as block:
            @block.tensor
            def _(t):
                run(PE, t)

            @block.scalar
            def _(t):
                run(ACT, t)

            @block.vector
            def _(t):
                run(DVE, t)

            @block.gpsimd
            def _(t):
                run(POOL, t)

            @block.sync
            def _(t):
                run(SP, t)


class ColMap:
    def __init__(self):
        self.off = {}
        self.n = 0

    def add(self, name, ncols):
        self.off[name] = self.n
        self.n += ncols

    def __getitem__(self, name):
        return self.off[name]


def build_colmap():
    cm = ColMap()
    for l in range(DEPTH):
        cm.add("ng%d" % l, 8)
        cm.add("bsh%d" % l, 8)
        cm.add("bsc%d" % l, 8)
    cm.add("c", 8)
    cm.add("cctx", 8)
    for i in range(2):
        for j in range(4):
            cm.add("cw%d_%d" % (i, j), 8)
        cm.add("cb%d" % i, 8)
        for d in range(2):
            cm.add("ba%d_%d" % (i, d), 8)
            cm.add("bi%d_%d" % (i, d), 8)
            cm.add("lam%d_%d" % (i, d), 8)
        cm.add("bs%d" % i, 4)
    return cm


def build_rowmap():
    rm = ColMap()
    for l in range(DEPTH):
        rm.add("bg%d" % l, 1024)
    for i in range(2):
        rm.add("lng%d" % i, 512)
        rm.add("lnb%d" % i, 512)
        rm.add("sink%d" % i, 8)
    rm.add("fg", 1024)
    return rm


CM = build_colmap()
RM = build_rowmap()


def col8(v):
    return np.ascontiguousarray(np.asarray(v, np.float32).reshape(8, 128).T)


def rope_table():
    t = np.arange(L)
    r = (t // 64).astype(np.float64)
    c = (t % 64).astype(np.float64)
    inv = (10000.0 ** (-np.arange(16, dtype=np.float32) / np.float32(16))).astype(np.float32).astype(np.float64)
    ang = np.concatenate([r[:, None] * inv, c[:, None] * inv], axis=-1).astype(np.float32)
    cos = np.cos(ang).astype(np.float32)
    sin = np.sin(ang).astype(np.float32)
    return np.ascontiguousarray(np.concatenate([cos, cos, -sin, sin], axis=-1).astype(np.float32))


class StopBuild(Exception):
    pass


class Builder:
    def __init__(self, layers=(0, 1, 2, 3), do_final=True, dbg=False, stop=None):
        self.stop = stop
        self.stopped = False
        self.cks = []
        self.layers = list(layers)
        self.do_final = do_final
        self.dbg = dbg
        self.nc = bass.Bass("TRN2", target_bir_lowering=False)
        self.uid = 0

    def ck(self, name):
        self.cks.append(name)
        if self.stop is not None and name == self.stop:
            self.stopped = True
        return self.stopped

    def dram_in(self, name, shape, dt=F32):
        return self.nc.dram_tensor(name, list(shape), dt, kind="ExternalInput").ap()

    def dram_out(self, name, shape, dt=F32):
        return self.nc.dram_tensor(name, list(shape), dt, kind="ExternalOutput").ap()

    def dram_tmp(self, name, shape, dt=F32):
        return self.nc.dram_tensor(name, list(shape), dt).ap()

    def sb(self, st, name, shape, dt=F32):
        self.uid += 1
        nm = "%s_%d" % (name, self.uid)
        t = st.enter_context(self.nc.sbuf_tensor(nm, list(shape), dt))
        return Tile(t, Buf(nm))

    def sbc(self, st, name, shape, dt, n):
        T = self.sb(st, name, shape, dt)
        T.bs = [Buf("%s_c%d" % (name, c)) for c in range(n)]
        return T

    def ps(self, st, name, shape, dt=F32):
        self.uid += 1
        nm = "%s_%d" % (name, self.uid)
        t = st.enter_context(self.nc.psum_tensor(nm, list(shape), dt))
        return Tile(t, Buf(nm, excl=True))

    @staticmethod
    def _fs(ap):
        n = 1
        for d in ap.shape[1:]:
            n *= int(d)
        return n

    def act(self, out, in_, func, r, w, scale=1.0, bias=0.0, accum=None):
        kw = {}
        if accum is not None:
            kw["accum_out"] = accum
        tbl = {AF.Exp: "exp", AF.Tanh: "exp", AF.Sqrt: "sqrt", AF.Ln: "ln"}.get(func)
        self.P.op(ACT, lambda e: e.activation(out=out, in_=in_, func=func, bias=bias, scale=scale, **kw), r, w,
                  dur=0.2 + self._fs(out) / 1300.0, tbl=tbl)

    def _vdur(self, eng, out, two_in):
        n = self._fs(out)
        if eng == POOL:
            return 0.25 + n * 0.0022
        return (0.15 + n / 800.0) if two_in else (0.10 + n / 1400.0)

    def tt(self, eng, out, in0, in1, op, r, w):
        self.P.op(eng, lambda e: e.tensor_tensor(out=out, in0=in0, in1=in1, op=op), r, w, dur=self._vdur(eng, out, True))

    def ts(self, eng, out, in0, s1, s2, op0, op1, r, w):
        d = self._vdur(eng, out, False)
        if op1 is None:
            self.P.op(eng, lambda e: e.tensor_scalar(out=out, in0=in0, scalar1=s1, scalar2=None, op0=op0), r, w, dur=d)
        else:
            self.P.op(eng, lambda e: e.tensor_scalar(out=out, in0=in0, scalar1=s1, scalar2=s2, op0=op0, op1=op1), r, w, dur=d)

    def stt(self, eng, out, in0, scalar, in1, op0, op1, r, w):
        assert eng == DVE, "scalar_tensor_tensor is DVE-only"
        self.P.op(eng, lambda e: e.scalar_tensor_tensor(out=out, in0=in0, scalar=scalar, in1=in1, op0=op0, op1=op1), r, w,
                  dur=self._vdur(eng, out, True))

    def cp(self, eng, out, in_, r, w):
        if eng == ACT:
            self.P.op(ACT, lambda e: e.activation(out=out, in_=in_, func=AF.Copy), r, w, dur=0.2 + self._fs(out) / 1300.0)
        else:
            self.P.op(eng, lambda e: e.tensor_copy(out=out, in_=in_), r, w, dur=self._vdur(eng, out, False))

    def mm(self, out, lhsT, rhs, start, stop, r, w):
        self.P.op(PE, lambda e: e.matmul(out, lhsT=lhsT, rhs=rhs, start=start, stop=stop), r, w,
                  dur=0.02 + self._fs(out) / 2200.0)

    def mmx(self, out, lhsT, rhs, start, stop, r, w):
        self.P.op(PE, lambda e: e.matmul(out, lhsT=lhsT, rhs=rhs, start=start, stop=stop, skip_group_check=True), r, w,
                  dur=0.02 + self._fs(out) / 2200.0)

    def tr(self, out, in_, ident, r, w):
        self.P.op(PE, lambda e: e.transpose(out=out, in_=in_, identity=ident), r, w, dur=0.07)

    def memset(self, eng, ap, val, w):
        self.P.op(eng, lambda e: e.memset(ap, val), (), w, dur=self._vdur(eng, ap, False))

    def dma(self, eng, out, in_, sbuf, r=(), w=(), **kw):
        self.P.dma(eng, out, in_, sbuf, r, w, **kw)

    def rsqrt(self, out_ap, out_t, v_ap, v_t, s, q, n):
        sa = s.t[:, 0:n]
        self.act(sa, v_ap, AF.Sqrt, [v_t], [s])
        self.P.op(DVE, lambda e: e.reciprocal(out=out_ap, in_=sa), [s], [out_t])

    @staticmethod
    def run_prop(items):
        st_ = [[g_, 0, float(n_)] for g_, n_ in items]
        while st_:
            st_.sort(key=lambda e_: (e_[1] + 1) / e_[2])
            e_ = st_[0]
            try:
                next(e_[0])
                e_[1] += 1
            except StopIteration:
                st_.remove(e_)

    def build(self):
        nc = self.nc
        with ExitStack() as top:
            self.P = Prog(nc, top)
            self.declare_dram()
            self.setup(top)
            self.ck("setup")
            for l in self.layers:
                if self.stopped:
                    break
                self.P.barrier()
                with ExitStack() as st:
                    self.prologue(st, l)
                    if not self.ck("prologue%d" % l):
                        if l % 2 == 0:
                            self.even_layer(st, l)
                        else:
                            self.odd_layer(st, l)
                    self.P.barrier()
            if self.dbg:
                lastl = self.layers[-1]
                src = self.X[(lastl + 1) % 2]
                db = Buf("dbgout")
                for k in range(0, L + LC, 512):
                    n = min(512, L + LC - k)
                    self.dma(SP, self.dbg_x[k:k + n, :], src[k:k + n, :], db)
            self.P.emit()
        return nc

    def declare_dram(self):
        self.x_in = self.dram_in("x", [L, D])
        self.ctx_in = self.dram_in("ctx", [LC, D])
        self.cols_d = self.dram_in("cols", [128, CM.n])
        self.rows_d = self.dram_in("rows", [128, RM.n])
        self.rope_d = self.dram_in("rope", [L, 128])
        self.w_mod = self.dram_in("w_mod", [DEPTH, D, 3 * D])
        self.ab_w_in = self.dram_in("ab_w_in", [2, D, AB_IN])
        self.ab_w_out = self.dram_in("ab_w_out", [2, D, D])
        self.a_w_sT = self.dram_in("a_w_sT", [2, 128, 4, 128])
        self.c_w_in = self.dram_in("c_w_in", [2, D, 2 * D])
        self.c_w_a = self.dram_in("c_w_a", [2, 2, 4, 256, 256])
        self.c_w_i = self.dram_in("c_w_i", [2, 2, 4, 256, 256])
        self.c_w_out = self.dram_in("c_w_out", [2, D, D])
        self.out_d = self.dram_out("out", [L, D])
        self.X = [self.dram_tmp("XA", [L + LC, D]), self.dram_tmp("XB", [L + LC, D])]
        self.ZS = self.dram_tmp("ZS", [D, L])
        self.SF = self.dram_tmp("SF", [D, L])
        self.SG = self.dram_tmp("SG", [D, L], BF16)
        self.ZB = self.dram_tmp("ZB", [D, L], BF16)
        if self.dbg:
            self.dbg_x = self.dram_out("dbg_x", [L + LC, D])

    def x_src(self, l, blk):
        if l == self.layers[0] and l == 0:
            if blk < NCB:
                return self.ctx_in[blk * 128:(blk + 1) * 128, :]
            return self.x_in[(blk - NCB) * 128:(blk - NCB + 1) * 128, :]
        if l == self.layers[0]:
            return self.dbg_xin[blk * 128:(blk + 1) * 128, :]
        return self.X[l % 2][blk * 128:(blk + 1) * 128, :]

    def x_dst(self, l, blk):
        return self.X[(l + 1) % 2][blk * 128:(blk + 1) * 128, :]

    def setup(self, st):
        nc = self.nc
        self.COLS = self.sb(st, "cols", [128, CM.n])
        self.dma(SP, self.COLS.t[:], self.cols_d, self.COLS, w=[self.COLS])
        self.ck("s_cols")
        self.identf = self.sb(st, "identf", [128, 128])
        self.identb = self.sb(st, "identb", [128, 128], BF16)
        self.memset(POOL, self.identf.t[:], 1.0, [self.identf])
        idf = self.identf
        self.P.op(POOL, lambda e: e.affine_select(out=idf.t[:], in_=idf.t[:], pattern=[[-1, 128]], compare_op=ALU.is_equal,
                                                 fill=0.0, base=0, channel_multiplier=1), [idf], [idf])
        self.cp(DVE, self.identb.t[:], self.identf.t[:], [self.identf], [self.identb])
        self.ck("s_ident")
        mtmp = self.sb(st, "mtmp", [128, 128])
        self.memset(POOL, mtmp.t[:], 1.0, [mtmp])
        self.P.op(POOL, lambda e: e.affine_select(out=mtmp.t[:], in_=mtmp.t[:], pattern=[[-1, 128]], compare_op=ALU.is_ge,
                                                 fill=0.0, base=0, channel_multiplier=1), [mtmp], [mtmp])
        mtmp2 = self.sb(st, "mtmp2", [128, 128])
        self.memset(POOL, mtmp2.t[:], 1.0, [mtmp2])
        self.P.op(POOL, lambda e: e.affine_select(out=mtmp2.t[:], in_=mtmp2.t[:], pattern=[[1, 128]], compare_op=ALU.is_ge,
                                                 fill=0.0, base=0, channel_multiplier=-1), [mtmp2], [mtmp2])
        self.mbprev = self.sb(st, "mbprev", [128, 4, 128], BF16)
        self.mbnext = self.sb(st, "mbnext", [128, 4, 128], BF16)
        for (src_, dst_) in ((mtmp, self.mbprev), (mtmp2, self.mbnext)):
            self.ts(DVE, src_.t[:], src_.t[:], -1.0, 30000.0, ALU.add, ALU.mult, [src_], [src_])
            self.cp(DVE, dst_.t[:], src_.t[:].unsqueeze(1).broadcast_to([128, 4, 128]), [src_], [dst_])
        self.ck("s_masks")
        cc = CM["c"]
        th = self.sb(st, "sc_th", [128, 16])
        sc = self.sb(st, "sc", [128, 16])
        self.act(th.t[:], self.COLS.t[:, cc:cc + 16], AF.Tanh, [self.COLS], [th], scale=0.5)
        self.stt(DVE, sc.t[:], th.t[:], 1.0, self.COLS.t[:, cc:cc + 16], ALU.add, ALU.mult, [th, self.COLS], [sc])
        self.ts(DVE, sc.t[:], sc.t[:], 0.5, None, ALU.mult, None, [sc], [sc])
        self.scT = self.sb(st, "scT", [128, 8, 2], BF16)
        self.cp(DVE, self.scT.t[:, :, 0], sc.t[:, 0:8], [sc], [self.scT])
        self.cp(DVE, self.scT.t[:, :, 1], sc.t[:, 8:16], [sc], [self.scT])
        self.screp = []
        for i in range(2):
            t = self.sb(st, "screp%d" % i, [128, 8, 128], BF16)
            self.cp(DVE, t.t[:], sc.t[:, 8 * i:8 * i + 8].unsqueeze(2).broadcast_to([128, 8, 128]), [sc], [t])
            self.screp.append(t)
        self.ck("s_silu")
        self.SSQ = [self.sb(st, "ssq0", [128, NBLK]), self.sb(st, "ssq1", [128, NBLK])]
        l0 = self.layers[0]
        self.memset(POOL, self.SSQ[l0 % 2].t[:], 0.0, [self.SSQ[l0 % 2]])
        if self.dbg and l0 != 0:
            self.dbg_xin = self.dram_in("dbg_xin", [L + LC, D])
        with ExitStack() as s2:
            xr = [self.sb(s2, "prex%d" % i, [128, D]) for i in range(6)]
            junk = self.sb(s2, "prej", [128, D])
            for blk in range(NBLK):
                xt = xr[blk % 6]
                self.dma(SP, xt.t[:], self.x_src(l0, blk), xt, w=[xt])
                self.act(junk.t[:], xt.t[:], AF.Square, [xt], [junk, self.SSQ[l0 % 2]],
                         accum=self.SSQ[l0 % 2].t[:, blk:blk + 1])
            self.P.barrier()

    def prologue(self, st, l):
        P = self.P
        self.gs = [self.sb(st, "gs%d" % i, [128, 8]) for i in range(2)]
        self.sh = [self.sb(st, "sh%d" % i, [128, 8]) for i in range(2)]
        self.gate = [self.sb(st, "gate%d" % i, [128, D]) for i in range(2)]
        self.rstd = self.sb(st, "rstd", [128, NBLK])
        with ExitStack() as s2:
            bg = self.sb(s2, "bg", [128, D])
            self.dma(SP, bg.t[:], self.rows_d[:, RM["bg%d" % l]:RM["bg%d" % l] + D], bg, w=[bg])
            wm = [self.sb(s2, "wm%d" % i, [128, 8, 512], BF16) for i in range(6)]
            pcols_full = self.ps(s2, "pcols", [128, 512])
            pcols = Tile(pcols_full.t[:, 0:32].rearrange("p (n t) -> p n t", t=2), pcols_full.b)
            pg = [self.ps(s2, "pg%d" % i, [128, 512]) for i in range(2)]
            wsrc = self.w_mod[l].rearrange("(kc p) n -> p kc n", p=128)
            for pi in range(6):
                w = wm[pi]
                self.dma(POOL, w.t[:], wsrc[:, :, pi * 512:(pi + 1) * 512], w, w=[w])
                if pi < 4:
                    for q in range(4):
                        nn = 4 * pi + q
                        for kc in range(8):
                            self.mm(pcols.t[:, nn, :], w.t[:, kc, q * 128:(q + 1) * 128], self.scT.t[:, kc, :],
                                    kc == 0, kc == 7, [w, self.scT], [pcols])
                else:
                    hh = pi - 4
                    for i in range(2):
                        for kc in range(8):
                            self.mm(pg[i].t[:], self.screp[i].t[:, kc, :], w.t[:, kc, :], kc == 0, kc == 7,
                                    [w, self.screp[i]], [pg[i]])
                        self.tt(DVE, self.gate[i].t[:, hh * 512:(hh + 1) * 512], pg[i].t[:], bg.t[:, hh * 512:(hh + 1) * 512],
                                ALU.add, [pg[i], bg], [self.gate[i]])
            modc = self.sb(s2, "modc", [128, 16, 2])
            self.cp(DVE, modc.t[:], pcols.t[:], [pcols], [modc])
            ng, bsh, bsc = CM["ng%d" % l], CM["bsh%d" % l], CM["bsc%d" % l]
            tmp = self.sb(s2, "modtmp", [128, 8])
            for i in range(2):
                self.tt(DVE, self.sh[i].t[:], modc.t[:, 0:8, i], self.COLS.t[:, bsh:bsh + 8], ALU.add,
                        [modc, self.COLS], [self.sh[i]])
                self.tt(DVE, tmp.t[:], modc.t[:, 8:16, i], self.COLS.t[:, bsc:bsc + 8], ALU.add, [modc, self.COLS], [tmp])
                self.stt(DVE, self.gs[i].t[:], tmp.t[:], 1.0, self.COLS.t[:, ng:ng + 8], ALU.add, ALU.mult,
                         [tmp, self.COLS], [self.gs[i]])
            v = self.sb(s2, "nv", [128, NBLK])
            self.ts(DVE, v.t[:], self.SSQ[l % 2].t[:], 1.0 / D, EPS, ALU.mult, ALU.add, [self.SSQ[l % 2]], [v])
            hs = self.sb(s2, "hs", [128, NBLK])
            hq = self.sb(s2, "hq", [128, NBLK])
            self.rsqrt(self.rstd.t[:], self.rstd, v.t[:], v, hs, hq, NBLK)
            self.memset(POOL, self.SSQ[(l + 1) % 2].t[:], 0.0, [self.SSQ[(l + 1) % 2]])
            self.P.barrier()

    def norm_T(self, xt, blk, xn, psT, hT_ap, hT_tile, is_ctx):
        i = 1 if is_ctx else 0
        self.act(xn.t[:], xt.t[:], AF.Copy, [xt, self.rstd], [xn], scale=self.rstd.t[:, blk:blk + 1])
        for c in range(8):
            self.tr(psT.t[:, c, :], xn.t[:, c * 128:(c + 1) * 128], self.identb.t[:], [xn, self.identb], [psT])
        for c in range(8):
            self.ts(DVE, hT_ap[:, c, :], psT.t[:, c, :], self.gs[i].t[:, c:c + 1], self.sh[i].t[:, c:c + 1],
                    ALU.mult, ALU.add, [psT, self.gs[i], self.sh[i]], [hT_tile])

    def gelu2(self, out_t, ps_t, w_t, in_t, n):
        self.act(w_t.t[:, 0:n], ps_t.t[:, 0:n], AF.Square, [ps_t], [w_t], scale=SQ_044)
        self.stt(DVE, in_t.t[:, 0:n], w_t.t[:, 0:n], 1.0, ps_t.t[:, 0:n], ALU.add, ALU.mult, [w_t, ps_t], [in_t])
        self.act(w_t.t[:, 0:n], in_t.t[:, 0:n], AF.Tanh, [in_t], [w_t], scale=GELU_C)
        self.stt(DVE, out_t.t[:, 0:n], w_t.t[:, 0:n], 1.0, ps_t.t[:, 0:n], ALU.add, ALU.mult, [w_t, ps_t], [out_t])

    def silu2(self, out_ap, out_t, ps_ap, ps_t, w_ap, w_t):
        self.act(w_ap, ps_ap, AF.Tanh, [ps_t], [w_t], scale=0.5)
        self.stt(DVE, out_ap, w_ap, 1.0, ps_ap, ALU.add, ALU.mult, [w_t, ps_t], [out_t])

    def even_layer(self, st, l):
        i = l // 2
        W_in = self.sb(st, "Win", [128, 8, AB_IN], BF16)
        W_out = self.sb(st, "Wout", [128, 8, D], BF16)
        W_sT = self.sb(st, "WsT", [128, 4, 128], BF16)
        wsrc = self.ab_w_in[i].rearrange("(kc p) n -> p kc n", p=128)
        Wg = {}
        for (c0, n) in ((512, 512), (0, 512), (2304, 512), (1536, 512), (2048, 256), (1024, 512)):
            Wg[c0] = Buf("Win_%d" % c0)
            for kc in range(0, 8, 4):
                self.dma(POOL, W_in.t[:, kc:kc + 4, c0:c0 + n], wsrc[:, kc:kc + 4, c0:c0 + n], Wg[c0], w=[Wg[c0]])
        self.dma(POOL, W_sT.t[:], self.a_w_sT[i], W_sT, w=[W_sT])
        wosrc = self.ab_w_out[i].rearrange("(kc p) n -> p kc n", p=128)
        for kc in range(0, 8, 2):
            self.dma(POOL, W_out.t[:, kc:kc + 2, :], wosrc[:, kc:kc + 2, :], W_out, w=[W_out])
        lng = self.sb(st, "lng", [128, 512])
        lnb = self.sb(st, "lnb", [128, 512])
        sink = self.sb(st, "sink", [128, 8])
        esink = self.sb(st, "esink", [128, 8])
        self.dma(SP, lng.t[:], self.rows_d[:, RM["lng%d" % i]:RM["lng%d" % i] + 512], lng, w=[lng])
        self.dma(SP, lnb.t[:], self.rows_d[:, RM["lnb%d" % i]:RM["lnb%d" % i] + 512], lnb, w=[lnb])
        self.dma(SP, sink.t[:], self.rows_d[:, RM["sink%d" % i]:RM["sink%d" % i] + 8], sink, w=[sink])
        self.act(esink.t[:], sink.t[:], AF.Exp, [sink], [esink])
        bs0 = CM["bs%d" % i]

        KT = self.sb(st, "KT", [128, 2, NBLK, 128], BF16)
        KTb = [Buf("KT%d" % b) for b in range(NBLK)]
        Vb = self.sb(st, "Vb", [128, NBLK, 2, 128], BF16)
        Vbb = [Buf("Vb%d" % b) for b in range(NBLK)]
        self.memset(POOL, Vb.t[:, :, :, 64:128], 1.0, Vbb)

        ring = lambda name, shape, dt=F32, n=2: [self.sb(st, "%s%d" % (name, k), shape, dt) for k in range(n)]
        xblk = ring("xblk", [128, D], F32, 4)
        hT = ring("hT", [128, 8, 128], BF16, 2)
        ropet = ring("rope", [128, 128], F32, 2)
        qT = ring("qT", [128, 8, 128], BF16, 3)
        sgb2 = ring("sgb2", [128, 512], F32, 3)
        ymix = ring("ymix", [128, D], BF16, 3)
        xn = self.sb(st, "xn", [128, D], BF16)
        wv, iv = self.sb(st, "wv", [128, 512]), self.sb(st, "iv", [128, 512])
        wu, iu = self.sb(st, "wu", [128, 512]), self.sb(st, "iu", [128, 512])
        gu2 = self.sb(st, "gu2", [128, 512])
        sga2 = self.sb(st, "sga2", [128, 512])
        gv = self.sb(st, "gv", [128, 512])
        vln = self.sb(st, "vln", [128, 512], BF16)
        vtmp = self.sb(st, "vtmp", [128, 512])
        lnst = self.sb(st, "lnst", [128, 16])
        lnr = self.sb(st, "lnr", [128, 4])
        lnm = self.sb(st, "lnm", [128, 4])
        lhs_ = self.sb(st, "lhs", [128, 4])
        wg = self.sb(st, "wg", [128, 512])
        r1 = self.sb(st, "r1", [128, 640])
        r2 = self.sb(st, "r2", [128, 640])
        qz = self.sb(st, "qz", [128, 8, 128], BF16)
        self.memset(POOL, qz.t[:], 0.0, [qz])
        kdup = self.sb(st, "kdup", [128, 2, 2, 64], BF16)
        PT = [[self.sb(st, "PT%d_%d" % (kh, k), [128, 512], BF16) for k in range(5)] for kh in range(2)]
        den = self.sb(st, "den", [128, 8])
        rden = self.sb(st, "rden", [128, 8])
        ybt = self.sb(st, "ybt", [128, 512])
        ymixT = self.sb(st, "ymixT", [128, 8, 128], BF16)
        ytmp = self.sb(st, "ytmp", [128, D])

        psT = self.ps(st, "psT", [128, 8, 128], BF16)
        psTb = self.ps(st, "psTb", [128, 8, 128], BF16)
        psA = [self.ps(st, "psA%d" % k, [128, 512]) for k in range(2)]
        psQ = self.ps(st, "psQ", [128, 512])
        psS = [self.ps(st, "psS%d" % k, [128, 512]) for k in range(2)]
        psV = self.ps(st, "psV", [128, 4, 128])
        psY = Tile(psV.t[:].rearrange("p h d -> p (h d)"), psV.b)

        ssq_next = self.SSQ[(l + 1) % 2]

        def proj(ps, hslot, c0, n):
            for kc in range(8):
                self.mm(ps.t[:, 0:n], hT[hslot].t[:, kc, :], W_in.t[:, kc, c0:c0 + n], kc == 0, kc == 7,
                        [hT[hslot], Wg[c0]], [ps])

        def streamN(blk):
            is_ctx = blk < NCB
            mi = 1 if is_ctx else 0
            xt = xblk[blk % 4]
            self.dma(SP, xt.t[:], self.x_src(l, blk), xt, w=[xt])
            yield
            self.act(xn.t[:], xt.t[:], AF.Copy, [xt, self.rstd], [xn], scale=self.rstd.t[:, blk:blk + 1])
            yield
            for c in range(8):
                self.tr(psT.t[:, c, :], xn.t[:, c * 128:(c + 1) * 128], self.identb.t[:], [xn, self.identb], [psT])
            yield
            h = hT[blk % 2]
            for c in range(8):
                self.ts(DVE, h.t[:, c, :], psT.t[:, c, :], self.gs[mi].t[:, c:c + 1], self.sh[mi].t[:, c:c + 1],
                        ALU.mult, ALU.add, [psT, self.gs[mi], self.sh[mi]], [h])
                if c % 4 == 3:
                    yield

        def gelu2_gen(out_t, ps_t, w_t, in_t):
            self.act(w_t.t[:], ps_t.t[:], AF.Square, [ps_t], [w_t], scale=SQ_044)
            yield
            self.stt(DVE, in_t.t[:], w_t.t[:], 1.0, ps_t.t[:], ALU.add, ALU.mult, [w_t, ps_t], [in_t])
            yield
            self.act(w_t.t[:], in_t.t[:], AF.Tanh, [in_t], [w_t], scale=GELU_C)
            yield
            self.stt(DVE, out_t.t[:], w_t.t[:], 1.0, ps_t.t[:], ALU.add, ALU.mult, [w_t, ps_t], [out_t])
            yield

        def streamS1(blk):
            hs = blk % 2
            ym = ymix[blk % 3]
            proj(psA[0], hs, 512, 512)
            yield
            proj(psA[1], hs, 0, 512)
            yield
            yield from gelu2_gen(gv, psA[0], wv, iv)
            self.memset(POOL, lnst.t[:, 4:8], 0.0, [lnst])
            gv3 = gv.t[:].rearrange("p (g d) -> p g d", g=4)
            self.P.op(DVE, lambda e: e.tensor_reduce(out=lnst.t[:, 0:4], in_=gv3, axis=AX.X, op=ALU.add), [gv], [lnst])
            yield
            for g in range(4):
                self.act(vtmp.t[:, g * 128:(g + 1) * 128], gv.t[:, g * 128:(g + 1) * 128], AF.Square, [gv], [vtmp, lnst],
                         accum=lnst.t[:, 4 + g:5 + g])
            yield
            proj(psA[0], hs, 1024, 512)
            yield
            self.ts(DVE, lnst.t[:, 8:12], lnst.t[:, 0:4], 1.0 / 128, None, ALU.mult, None, [lnst], [lnst])
            self.tt(DVE, lnst.t[:, 12:16], lnst.t[:, 8:12], lnst.t[:, 8:12], ALU.mult, [lnst], [lnst])
            self.stt(DVE, lnst.t[:, 12:16], lnst.t[:, 4:8], 1.0 / 128, lnst.t[:, 12:16], ALU.mult, ALU.subtract,
                     [lnst], [lnst])
            self.ts(DVE, lnst.t[:, 12:16], lnst.t[:, 12:16], 4.0 * EPS, None, ALU.add, None, [lnst], [lnst])
            yield
            self.act(lhs_.t[:], lnst.t[:, 12:16], AF.Sqrt, [lnst], [lhs_])
            yield
            self.P.op(DVE, lambda e: e.reciprocal(out=lnr.t[:], in_=lhs_.t[:]), [lhs_], [lnr])
            self.stt(DVE, lnm.t[:], lnst.t[:, 8:12], -1.0, lnr.t[:], ALU.mult, ALU.mult, [lnst, lnr], [lnm])
            yield
            yield from gelu2_gen(gu2, psA[1], wu, iu)
            for g in range(4):
                self.act(vtmp.t[:, g * 128:(g + 1) * 128], gv.t[:, g * 128:(g + 1) * 128], AF.Identity, [gv, lnr, lnm], [vtmp],
                         scale=lnr.t[:, g:g + 1], bias=lnm.t[:, g:g + 1])
            yield
            self.tt(DVE, vtmp.t[:], vtmp.t[:], lng.t[:], ALU.mult, [vtmp, lng], [vtmp])
            yield
            self.tt(DVE, vln.t[:], vtmp.t[:], lnb.t[:], ALU.add, [vtmp, lnb], [vln])
            yield
            self.act(wv.t[:], psA[0].t[:], AF.Tanh, [psA[0]], [wv], scale=0.5)
            yield
            self.stt(DVE, sga2.t[:], wv.t[:], 1.0, psA[0].t[:], ALU.add, ALU.mult, [wv, psA[0]], [sga2])
            yield
            for g in range(4):
                self.mm(psA[1].t[:, g * 128:(g + 1) * 128], W_sT.t[:, g, :], vln.t[:, g * 128:(g + 1) * 128], True, True,
                        [W_sT, vln], [psA[1]])
            yield
            self.stt(DVE, gu2.t[:], gu2.t[:], 0.25, sga2.t[:], ALU.mult, ALU.mult, [gu2, sga2], [gu2])
            yield
            for g in range(4):
                self.stt(DVE, ym.t[:, g * 128:(g + 1) * 128], psA[1].t[:, g * 128:(g + 1) * 128],
                         self.COLS.t[:, bs0 + g:bs0 + g + 1], gu2.t[:, g * 128:(g + 1) * 128], ALU.add, ALU.mult,
                         [psA[1], self.COLS, gu2], [ym])
                if g % 2 == 1:
                    yield

        def streamS2(blk):
            hs = blk % 2
            s3 = blk % 3
            is_ctx = blk < NCB
            rt = ropet[blk % 2]
            if not is_ctx:
                t0 = (blk - NCB) * 128
                self.dma(SP, rt.t[:], self.rope_d[t0:t0 + 128, :], rt, w=[rt])
            proj(psQ, hs, 2304, 512)
            yield
            self.act(wg.t[:], psQ.t[:], AF.Tanh, [psQ], [wg], scale=0.5)
            yield
            self.stt(DVE, sgb2[s3].t[:], wg.t[:], 1.0, psQ.t[:], ALU.add, ALU.mult, [wg, psQ], [sgb2[s3]])
            yield
            proj(psQ, hs, 1536, 512)
            yield
            if is_ctx:
                q3 = psQ.t[:, 0:512].rearrange("p (h d) -> p h d", h=8)
                self.cp(ACT, qz.t[:, 0::2, 0:64], q3[:, 0::2, :], [psQ], [qz])
                self.cp(ACT, qz.t[:, 1::2, 64:128], q3[:, 1::2, :], [psQ], [qz])
                yield
            else:
                src = psQ.t[:, 0:512].rearrange("p (h d) -> p h d", h=8)
                d1 = r1.t[:, 0:512].rearrange("p (h d) -> p h d", h=8)
                d2 = r2.t[:, 0:512].rearrange("p (h d) -> p h d", h=8)
                self.tt(DVE, d1, src, rt.t[:, 0:64].unsqueeze(1).broadcast_to([128, 8, 64]), ALU.mult, [psQ, rt], [r1])
                yield
                self.tt(DVE, d2[:, :, 0:32], src[:, :, 32:64], rt.t[:, 64:96].unsqueeze(1).broadcast_to([128, 8, 32]), ALU.mult,
                        [psQ, rt], [r2])
                self.tt(DVE, d2[:, :, 32:64], src[:, :, 0:32], rt.t[:, 96:128].unsqueeze(1).broadcast_to([128, 8, 32]), ALU.mult,
                        [psQ, rt], [r2])
                yield
                self.tt(POOL, qz.t[:, 0::2, 0:64], d1[:, 0::2, :], d2[:, 0::2, :], ALU.add, [r1, r2], [qz])
                self.tt(POOL, qz.t[:, 1::2, 64:128], d1[:, 1::2, :], d2[:, 1::2, :], ALU.add, [r1, r2], [qz])
                yield
            proj(psQ, hs, 2048, 256)
            yield
            self.cp(ACT, Vb.t[:, blk, :, 0:64], psQ.t[:, 128:256].rearrange("p (h d) -> p h d", h=2), [psQ], [Vbb[blk]])
            k3 = psQ.t[:, 0:128].rearrange("p (h d) -> p h d", h=2)
            if is_ctx:
                for dup in range(2):
                    self.cp(ACT, kdup.t[:, :, dup, :], k3, [psQ], [kdup])
                yield
            else:
                e1 = r1.t[:, 512:640].rearrange("p (h d) -> p h d", h=2)
                e2 = r2.t[:, 512:640].rearrange("p (h d) -> p h d", h=2)
                self.tt(DVE, e1, k3, rt.t[:, 0:64].unsqueeze(1).broadcast_to([128, 2, 64]), ALU.mult, [psQ, rt], [r1])
                self.tt(DVE, e2[:, :, 0:32], k3[:, :, 32:64], rt.t[:, 64:96].unsqueeze(1).broadcast_to([128, 2, 32]), ALU.mult,
                        [psQ, rt], [r2])
                self.tt(DVE, e2[:, :, 32:64], k3[:, :, 0:32], rt.t[:, 96:128].unsqueeze(1).broadcast_to([128, 2, 32]), ALU.mult,
                        [psQ, rt], [r2])
                yield
                for dup in range(2):
                    self.tt(POOL, kdup.t[:, :, dup, :], e1, e2, ALU.add, [r1, r2], [kdup])
                yield
            for h in range(8):
                self.tr(psTb.t[:, h, :], qz.t[:, h, :], self.identb.t[:], [qz, self.identb], [psTb])
            self.cp(ACT, qT[s3].t[:], psTb.t[:], [psTb], [qT[s3]])
            yield
            for kh in range(2):
                self.tr(psTb.t[:, kh, :], kdup.t[:, kh, :, :].rearrange("p a d -> p (a d)"), self.identb.t[:],
                        [kdup, self.identb], [psTb])
            self.cp(ACT, KT.t[:, :, blk, :], psTb.t[:, 0:2, :], [psTb], [KTb[blk]])
            yield

        def streamB(blk):
            s3 = blk % 3
            is_ctx = blk < NCB
            ym = ymix[s3]
            xt = xblk[blk % 4]
            if is_ctx:
                kbs = [(0, None), (1, None)]
            else:
                kbs = [(0, None), (1, None)]
                if blk - 1 >= NCB:
                    kbs.append((blk - 1, self.mbprev))
                kbs.append((blk, None))
                if blk + 1 < NBLK:
                    kbs.append((blk + 1, self.mbnext))
            nk = len(kbs)
            for kh in range(2):
                for ki, (kb, mb) in enumerate(kbs):
                    pss = psS[(kh * 5 + ki) % 2]
                    if mb is not None:
                        self.mmx(pss.t[:], self.identb.t[:], mb.t[:].rearrange("p h q -> p (h q)"), True, False,
                                 [self.identb, mb], [pss])
                    for hl in range(4):
                        h = 4 * kh + hl
                        if mb is not None:
                            self.mmx(pss.t[:, hl * 128:(hl + 1) * 128], KT.t[:, kh, kb, :], qT[s3].t[:, h, :], False, hl == 3,
                                     [KTb[kb], qT[s3]], [pss])
                        else:
                            self.mm(pss.t[:, hl * 128:(hl + 1) * 128], KT.t[:, kh, kb, :], qT[s3].t[:, h, :],
                                    True, True, [KTb[kb], qT[s3]], [pss])
                    yield
                    pt = PT[kh][ki]
                    self.act(pt.t[:], pss.t[:], AF.Exp, [pss], [pt], scale=0.125)
                    yield
                for hl in range(4):
                    for ki, (kb, mb) in enumerate(kbs):
                        self.mm(psV.t[:, hl, 0:65], PT[kh][ki].t[:, hl * 128:(hl + 1) * 128], Vb.t[:, kb, kh, 0:65],
                                ki == 0, ki == nk - 1, [PT[kh][ki], Vbb[kb]], [psV])
                    if hl % 2 == 1:
                        yield
                self.tt(DVE, den.t[:, 4 * kh:4 * kh + 4], psV.t[:, :, 64], esink.t[:, 4 * kh:4 * kh + 4], ALU.add,
                        [psV, esink], [den])
                self.P.op(DVE, lambda e, kh=kh: e.reciprocal(out=rden.t[:, 4 * kh:4 * kh + 4], in_=den.t[:, 4 * kh:4 * kh + 4]),
                          [den], [rden])
                yv = ybt.t[:, kh * 256:(kh + 1) * 256].rearrange("p (h d) -> p h d", h=4)
                self.stt(DVE, yv, psV.t[:, :, 0:64], 0.5, rden.t[:, 4 * kh:4 * kh + 4].unsqueeze(2).broadcast_to([128, 4, 64]),
                         ALU.mult, ALU.mult, [psV, rden], [ybt])
                yield
            self.tt(POOL, ym.t[:, 512:1024], ybt.t[:], sgb2[s3].t[:], ALU.mult, [ybt, sgb2[s3]], [ym])
            yield
            for c in range(8):
                self.tr(psTb.t[:, c, :], ym.t[:, c * 128:(c + 1) * 128], self.identb.t[:], [ym, self.identb], [psTb])
            self.cp(ACT, ymixT.t[:], psTb.t[:], [psTb], [ymixT])
            yield
            g = self.gate[1]
            for n in range(2):
                for kc in range(8):
                    self.mm(psY.t[:], ymixT.t[:, kc, :], W_out.t[:, kc, n * 512:(n + 1) * 512], kc == 0, kc == 7,
                            [ymixT, W_out], [psY])
                yield
                if is_ctx:
                    self.tt(DVE, ytmp.t[:, n * 512:(n + 1) * 512], psY.t[:], g.t[:, n * 512:(n + 1) * 512], ALU.mult, [psY, g], [ytmp])
                else:
                    self.tt(DVE, xt.t[:, n * 512:(n + 1) * 512], psY.t[:], xt.t[:, n * 512:(n + 1) * 512], ALU.add, [psY, xt], [xt])
                yield
            if is_ctx:
                self.tt(POOL, xt.t[:], ytmp.t[:], xt.t[:], ALU.add, [ytmp, xt], [xt])
                yield
            self.act(ytmp.t[:], xt.t[:], AF.Square, [xt], [ytmp, ssq_next], accum=ssq_next.t[:, blk:blk + 1])
            self.dma(SP, self.x_dst(l, blk), xt.t[:], xt, r=[xt])
            yield

        class _Dry:
            def op(self, *a, **k):
                pass

            def dma(self, *a, **k):
                pass

        def count_steps(gen_fn, blk):
            real = self.P
            self.P = _Dry()
            try:
                n = sum(1 for _ in gen_fn(blk))
            finally:
                self.P = real
            return n + 1

        def run_streams(items):
            st_ = [[g_, 0, float(n_)] for g_, n_ in items]
            while st_:
                st_.sort(key=lambda e_: (e_[1] + 1) / e_[2])
                e_ = st_[0]
                try:
                    next(e_[0])
                    e_[1] += 1
                except StopIteration:
                    st_.remove(e_)

        nsteps = {}

        def item(fn, blk):
            key = (fn.__name__, blk < NCB, blk == NCB, blk == NBLK - 1)
            if key not in nsteps:
                nsteps[key] = count_steps(fn, blk)
            return (fn(blk), nsteps[key])

        for t in range(NBLK + 3):
            items = []
            if t - 3 >= 0:
                items.append(item(streamB, t - 3))
            if 0 <= t - 1 < NBLK:
                items.append(item(streamS1, t - 1))
                items.append(item(streamS2, t - 1))
            if t < NBLK:
                items.append(item(streamN, t))
            run_streams(items)
            if t == NCB + 2:
                for kc in range(8):
                    self.tt(POOL if kc % 2 else DVE, W_out.t[:, kc, :], W_out.t[:, kc, :], self.gate[0].t[:], ALU.mult,
                            [W_out, self.gate[0]], [W_out])
            if self.ck("E%d" % t):
                return


    def odd_layer(self, st, l):
        i = l // 2
        need_ctx = l < DEPTH - 1
        last = (l == DEPTH - 1) and self.do_final
        NT = 256
        NSB = L // NT
        W_in = self.sb(st, "cWin", [128, 8, 2 * D], BF16)
        W_a = self.sb(st, "cWa", [128, 2, 4, 2, 256], BF16)
        W_i = self.sb(st, "cWi", [128, 2, 4, 2, 256], BF16)
        W_out = self.sb(st, "cWout", [128, 8, D], BF16)
        wsrc = self.c_w_in[i].rearrange("(kc p) n -> p kc n", p=128)
        Wxr, Wgg = Buf("cWin_xr"), Buf("cWin_g")
        for (c0, bb) in ((0, Wxr), (D, Wgg)):
            for kc in range(0, 8, 4):
                self.dma(POOL, W_in.t[:, kc:kc + 4, c0:c0 + D], wsrc[:, kc:kc + 4, c0:c0 + D], bb, w=[bb])
        for d in range(2):
            self.dma(POOL, W_a.t[:, d], self.c_w_a[i, d].rearrange("h (ic p) j -> p h ic j", p=128), W_a, w=[W_a])
            self.dma(POOL, W_i.t[:, d], self.c_w_i[i, d].rearrange("h (ic p) j -> p h ic j", p=128), W_i, w=[W_i])
        wosrc = self.c_w_out[i].rearrange("(kc p) n -> p kc n", p=128)
        for kc in range(0, 8, 2):
            self.dma(POOL, W_out.t[:, kc:kc + 2, :], wosrc[:, kc:kc + 2, :], W_out, w=[W_out])
        coefh = self.sb(st, "coefh", [128, 2, 8])
        hba = self.sb(st, "hba", [128, 2, 8])
        hbi = self.sb(st, "hbi", [128, 2, 8])
        for d in range(2):
            lam0 = CM["lam%d_%d" % (i, d)]
            self.act(coefh.t[:, d, :], self.COLS.t[:, lam0:lam0 + 8], AF.Exp, [self.COLS], [coefh], scale=-1.0)
            self.act(coefh.t[:, d, :], coefh.t[:, d, :], AF.Ln, [coefh], [coefh], bias=1.0)
            self.ts(DVE, coefh.t[:, d, :], coefh.t[:, d, :], -4.0, None, ALU.mult, None, [coefh], [coefh])
            b0 = CM["ba%d_%d" % (i, d)]
            self.ts(DVE, hba.t[:, d, :], self.COLS.t[:, b0:b0 + 8], 0.5, None, ALU.mult, None, [self.COLS], [hba])
            b0 = CM["bi%d_%d" % (i, d)]
            self.ts(DVE, hbi.t[:, d, :], self.COLS.t[:, b0:b0 + 8], 0.5, None, ALU.mult, None, [self.COLS], [hbi])
        cw0 = [CM["cw%d_%d" % (i, j)] for j in range(4)]
        cb0 = CM["cb%d" % i]
        DG = self.sb(st, "DG", [128, 4, 8, 128], BF16)
        for j in range(4):
            for c in range(8):
                self.ts(DVE, DG.t[:, j, c, :], self.identf.t[:], self.COLS.t[:, cw0[j] + c:cw0[j] + c + 1], None, ALU.mult, None,
                        [self.identf, self.COLS], [DG])
        if last:
            fg = self.sb(st, "fg", [128, D])
            self.dma(SP, fg.t[:], self.rows_d[:, RM["fg"]:RM["fg"] + D], fg, w=[fg])

        xb = [self.sb(st, "xb%d" % k, [128, D]) for k in range(4)]
        xbi = [0]
        xn = self.sb(st, "cxn", [128, D], BF16)
        hTr = [self.sb(st, "chT%d" % k, [128, 8, NT], BF16) for k in range(2)]
        zr = [self.sbc(st, "z0", [128, 8, NT], F32, 8), self.sbc(st, "z1", [128, 8, NT], F32, 8)]
        zbr = [self.sbc(st, "zb0", [128, 8, NT], BF16, 8), self.sbc(st, "zb1", [128, 8, NT], BF16, 8)]
        sgr = [self.sbc(st, "sg0", [128, 8, NT], BF16, 8)]
        gth_ = self.sb(st, "gth", [128, NT])
        gth = [gth_, gth_]
        A = self.sbc(st, "A", [128, 8, NT], F32, 8)
        Wq = self.sbc(st, "Wq", [128, 8, NT], F32, 8)
        TI = self.sbc(st, "TI", [128, 8, NT], F32, 8)
        thr = [self.sb(st, "thr%d" % k, [128, NT]) for k in range(2)]
        S0r = [self.sbc(st, "S00", [128, 8, NT], F32, 8)]
        S1 = self.sbc(st, "S1", [128, 8, NT], F32, 8)
        carry = [self.sbc(st, "carry%d" % k, [128, 8], F32, 8) for k in range(2)]
        ytmp = self.sb(st, "cytmp", [128, D])
        ssq1 = self.sb(st, "ssq1", [128, 2])
        v1 = self.sb(st, "v1", [128, 2])
        r1_ = self.sb(st, "r1", [128, 2])
        hs1 = self.sb(st, "hs1", [128, 2])
        psT = [self.ps(st, "cpsT%d" % k, [128, 8, 128], BF16) for k in range(2)]
        psP = [self.ps(st, "cpsP%d" % k, [128, 512]) for k in range(3)]
        psG = [self.ps(st, "cpsG%d" % k, [128, 512]) for k in range(3)]
        rot = [0, 0]

        def nextP():
            rot[0] += 1
            return psP[rot[0] % 3]

        def nextG():
            rot[1] += 1
            return psG[rot[1] % 3]

        ssq_next = self.SSQ[(l + 1) % 2]
        xbase = self.X[l % 2] if l != self.layers[0] else self.dbg_xin
        ZSb = [Buf("ZS%d" % k) for k in range(NSB)]
        SFb = [Buf("SF%d" % k) for k in range(NSB)]
        SGb = [Buf("SG%d" % k) for k in range(NSB)]
        ZBb = [Buf("ZB%d" % k) for k in range(NSB)]
        wscaled = [False]

        def scale_wout():
            for kc in range(8):
                self.tt(POOL if kc % 2 else DVE, W_out.t[:, kc, :], W_out.t[:, kc, :], self.gate[0].t[:], ALU.mult,
                        [W_out, self.gate[0]], [W_out])
            wscaled[0] = True

        def next_xb():
            xbi[0] += 1
            return xb[xbi[0] % 4]

        def x_load(blk):
            xt = next_xb()
            self.dma(SP, xt.t[:], xbase[blk * 128:(blk + 1) * 128, :], xt, w=[xt])
            return xt

        def g_load_norm(blk0, nb, is_ctx, hT, xts):
            i_ = 1 if is_ctx else 0
            for tb in range(nb):
                xt = xts[tb]
                self.ts(DVE, xn.t[:], xt.t[:], self.rstd.t[:, blk0 + tb:blk0 + tb + 1], None, ALU.mult, None,
                        [xt, self.rstd], [xn])
                yield
                pT = psT[tb % 2]
                for c in range(8):
                    self.tr(pT.t[:, c, :], xn.t[:, c * 128:(c + 1) * 128], self.identb.t[:], [xn, self.identb], [pT])
                yield
                for c in range(8):
                    self.ts(DVE, hT.t[:, c, tb * 128:(tb + 1) * 128], pT.t[:, c, :], self.gs[i_].t[:, c:c + 1],
                            self.sh[i_].t[:, c:c + 1], ALU.mult, ALU.add, [pT, self.gs[i_], self.sh[i_]], [hT])
                    if c % 4 == 3:
                        yield

        def g_project(nt, hT, xr_t, want_g, sg):
            for c in range(8):
                p = nextP()
                for kc in range(8):
                    self.mm(p.t[:, 0:nt], W_in.t[:, kc, c * 128:(c + 1) * 128], hT.t[:, kc, 0:nt], kc == 0, kc == 7,
                            [Wxr, hT], [p])
                yield
                self.cp(DVE, xr_t.t[:, c, 2:2 + nt], p.t[:, 0:nt], [p], [xr_t.bs[c]])
                yield
            if want_g:
                for c in range(8):
                    p = nextP()
                    for kc in range(8):
                        self.mm(p.t[:, 0:nt], W_in.t[:, kc, D + c * 128:D + (c + 1) * 128], hT.t[:, kc, 0:nt], kc == 0, kc == 7,
                                [Wgg, hT], [p])
                    yield
                    w_ = gth[c % 2]
                    self.act(w_.t[:, 0:nt], p.t[:, 0:nt], AF.Tanh, [p], [w_], scale=0.5)
                    yield
                    self.stt(DVE, sg.t[:, c, 0:nt], w_.t[:, 0:nt], 1.0, p.t[:, 0:nt], ALU.add, ALU.mult, [w_, p], [sg.bs[c]])
                    yield

        def g_conv(nt, xr_t, z, zb):
            for c in range(8):
                pz = nextG()
                for j in range(4):
                    self.mm(pz.t[:, 0:nt], DG.t[:, j, c, :], xr_t.t[:, c, j:j + nt], j == 0, j == 3, [DG, xr_t.bs[c]], [pz])
                yield
                self.act(zb.t[:, c, 0:nt], pz.t[:, 0:nt], AF.Identity, [pz, self.COLS], [zb.bs[c]],
                         bias=self.COLS.t[:, cb0 + c:cb0 + c + 1])
                yield
                self.ts(DVE, z.t[:, c, 0:nt], pz.t[:, 0:nt], self.COLS.t[:, cb0 + c:cb0 + c + 1], None, ALU.add, None,
                        [pz, self.COLS], [z.bs[c]])
                yield

        def g_gates_scan(nt, d, use_carry, z, zb, Sout, a2_eng, mid=None):
            for jc in range(8):
                hh = jc // 2
                jl = jc % 2
                pa = nextG()
                for ic in range(2):
                    self.mm(pa.t[:, 0:nt], W_a.t[:, d, hh, ic, jl * 128:(jl + 1) * 128], zb.t[:, 2 * hh + ic, 0:nt],
                            ic == 0, ic == 1, [W_a, zb.bs[2 * hh + ic]], [pa])
                pi_ = nextG()
                for ic in range(2):
                    self.mm(pi_.t[:, 0:nt], W_i.t[:, d, hh, ic, jl * 128:(jl + 1) * 128], zb.t[:, 2 * hh + ic, 0:nt],
                            ic == 0, ic == 1, [W_i, zb.bs[2 * hh + ic]], [pi_])
                yield
                t_ = thr[jc % 2]
                self.act(t_.t[:, 0:nt], pa.t[:, 0:nt], AF.Tanh, [pa, hba], [t_], scale=0.5, bias=hba.t[:, d, jc:jc + 1])
                self.act(A.t[:, jc, 0:nt], t_.t[:, 0:nt], AF.Exp, [t_, coefh], [A.bs[jc]], scale=coefh.t[:, d, jc:jc + 1],
                         bias=coefh.t[:, d, jc:jc + 1])
                yield
                self.act(TI.t[:, jc, 0:nt], pi_.t[:, 0:nt], AF.Tanh, [pi_, hbi], [TI.bs[jc]], scale=0.5,
                         bias=hbi.t[:, d, jc:jc + 1])
                if a2_eng == ACT:
                    self.act(Wq.t[:, jc, 0:nt], A.t[:, jc, 0:nt], AF.Square, [A.bs[jc]], [Wq.bs[jc]])
                else:
                    self.tt(POOL, Wq.t[:, jc, 0:nt], A.t[:, jc, 0:nt], A.t[:, jc, 0:nt], ALU.mult, [A.bs[jc]], [Wq.bs[jc]])
                yield
            if mid is not None:
                yield from mid()
            for jc in range(8):
                self.act(Wq.t[:, jc, 0:nt], Wq.t[:, jc, 0:nt], AF.Sqrt, [Wq.bs[jc]], [Wq.bs[jc]], scale=-0.25, bias=0.25)
                if jc % 2 == 1:
                    yield
            for jc in range(8):
                self.tt(POOL, Wq.t[:, jc, 0:nt], Wq.t[:, jc, 0:nt], z.t[:, jc, 0:nt], ALU.mult, [Wq.bs[jc], z.bs[jc]], [Wq.bs[jc]])
                if jc % 2 == 1:
                    yield
            for jc in range(8):
                self.stt(DVE, TI.t[:, jc, 0:nt], TI.t[:, jc, 0:nt], 1.0, Wq.t[:, jc, 0:nt], ALU.add, ALU.mult,
                         [TI.bs[jc], Wq.bs[jc]], [TI.bs[jc]])
                if use_carry:
                    ini = carry[d].t[:, jc:jc + 1]
                    ideps = [carry[d].bs[jc]]
                else:
                    ini = 0.0
                    ideps = []
                if d == 0:
                    o_, a_, b_ = Sout.t[:, jc, 0:nt], A.t[:, jc, 0:nt], TI.t[:, jc, 0:nt]
                else:
                    o_, a_, b_ = Sout.t[:, jc, 0:nt][:, ::-1], A.t[:, jc, 0:nt][:, ::-1], TI.t[:, jc, 0:nt][:, ::-1]
                self.P.op(DVE, lambda e, o_=o_, a_=a_, b_=b_, ini=ini: e.tensor_tensor_scan(
                    out=o_, data0=a_, data1=b_, initial=ini, op0=ALU.mult, op1=ALU.add),
                    [A.bs[jc], TI.bs[jc]] + ideps, [Sout.bs[jc]], dur=0.2 + nt * (0.0022 if d == 0 else 0.0045))
                col = nt - 1 if d == 0 else 0
                self.cp(DVE, carry[d].t[:, jc:jc + 1], Sout.t[:, jc, col:col + 1], [Sout.bs[jc]], [carry[d].bs[jc]])
                yield

        def g_combine(nt, S0, sg, ymT):
            for c in range(0, 8, 2):
                self.tt(POOL, S0.t[:, c:c + 2, 0:nt], S0.t[:, c:c + 2, 0:nt], S1.t[:, c:c + 2, 0:nt], ALU.add,
                        S0.bs[c:c + 2] + S1.bs[c:c + 2], S0.bs[c:c + 2])
            yield
            self.stt(DVE, ymT.t[:, :, 0:nt], S0.t[:, :, 0:nt], 0.5, sg.t[:, :, 0:nt], ALU.mult, ALU.mult, S0.bs + sg.bs, [ymT])
            yield

        def g_out(tb, blk, gate_t, xt, ymT, final):
            for n in range(2):
                py = nextP()
                for kc in range(8):
                    self.mm(py.t[:], ymT.t[:, kc, tb * 128:(tb + 1) * 128], W_out.t[:, kc, n * 512:(n + 1) * 512],
                            kc == 0, kc == 7, [ymT, W_out], [py])
                yield
                if gate_t is None:
                    self.tt(DVE, xt.t[:, n * 512:(n + 1) * 512], py.t[:], xt.t[:, n * 512:(n + 1) * 512], ALU.add, [py, xt], [xt])
                else:
                    self.tt(DVE, ytmp.t[:, n * 512:(n + 1) * 512], py.t[:], gate_t.t[:, n * 512:(n + 1) * 512], ALU.mult,
                            [py, gate_t], [ytmp])
                yield
            if gate_t is not None:
                self.tt(POOL, xt.t[:], ytmp.t[:], xt.t[:], ALU.add, [ytmp, xt], [xt])
                yield
            if final:
                self.act(ytmp.t[:], xt.t[:], AF.Square, [xt], [ytmp, ssq1], accum=ssq1.t[:, tb:tb + 1])
            else:
                self.act(ytmp.t[:], xt.t[:], AF.Square, [xt], [ytmp, ssq_next], accum=ssq_next.t[:, blk:blk + 1])
                self.dma(SP, self.x_dst(l, blk), xt.t[:], xt, r=[xt])
            yield

        def g_final(blk0, xts):
            self.ts(DVE, v1.t[:], ssq1.t[:], 1.0 / D, EPS, ALU.mult, ALU.add, [ssq1], [v1])
            self.act(hs1.t[:], v1.t[:], AF.Sqrt, [v1], [hs1])
            yield
            self.P.op(DVE, lambda e: e.reciprocal(out=r1_.t[:], in_=hs1.t[:]), [hs1], [r1_])
            self.memset(POOL, ssq1.t[:], 0.0, [ssq1])
            for tb, xt in enumerate(xts):
                self.stt(DVE, xt.t[:], xt.t[:], r1_.t[:, tb:tb + 1], fg.t[:], ALU.mult, ALU.mult, [xt, r1_, fg], [xt])
                blk = blk0 + tb
                self.dma(SP, self.out_d[(blk - NCB) * 128:(blk - NCB + 1) * 128, :], xt.t[:], xt, r=[xt])
                yield

        def run_streams(gens):
            active = list(gens)
            while active:
                for g_ in list(active):
                    try:
                        next(g_)
                    except StopIteration:
                        active.remove(g_)

        def seq(*gens):
            for g_ in gens:
                yield from g_

        z = zr[0]
        sg0 = sgr[0]
        S0 = S0r[0]
        with ExitStack() as p1:
            XR = [self.sbc(p1, "XR%d" % k, [128, 8, NT + 3], BF16, 8) for k in range(3)]
            nt = LC
            self.memset(POOL, XR[0].t[:, :, 0:2], 0.0, XR[0].bs)
            self.memset(POOL, XR[0].t[:, :, 2 + nt:3 + nt], 0.0, XR[0].bs)
            xts = [x_load(tb) for tb in range(NCB)]
            run_streams([seq(g_load_norm(0, NCB, True, hTr[0], xts), g_project(nt, hTr[0], XR[0], need_ctx, sg0),
                             g_conv(nt, XR[0], z, zbr[0]))])
            run_streams([g_gates_scan(nt, 0, False, z, zbr[0], S0, POOL)])
            run_streams([g_gates_scan(nt, 1, False, z, zbr[0], S1, POOL)])
            if need_ctx:
                run_streams([g_combine(nt, S0, sg0, hTr[1])])
                for tb in range(NCB):
                    run_streams([g_out(tb, tb, self.gate[1], xts[tb], hTr[1], False)])
            scale_wout()

            pre = {}

            def NS(it):
                yield from g_load_norm(NCB + it * 2, 2, False, hTr[it % 2], pre.pop(it))

            def PJ(it):
                cur = XR[it % 3]
                prv = XR[(it - 1) % 3]
                hT = hTr[it % 2]
                yield from g_project(NT, hT, cur, True, sg0)
                sgd = self.SG[:, it * NT:(it + 1) * NT].rearrange("(c p) t -> p c t", p=128)
                self.dma(SP, sgd, sg0.t[:], sg0.bs[0], r=sg0.bs, w=[SGb[it]])
                if it == 0:
                    self.memset(POOL, cur.t[:, :, 0:2], 0.0, cur.bs)
                else:
                    self.cp(POOL, cur.t[:, :, 0:2], prv.t[:, :, NT:NT + 2], prv.bs, cur.bs)
                    self.cp(POOL, prv.t[:, :, NT + 2:NT + 3], cur.t[:, :, 2:3], cur.bs, prv.bs)
                if it == NSB - 1:
                    self.memset(POOL, cur.t[:, :, NT + 2:NT + 3], 0.0, cur.bs)
                yield

            def FWD(sb_):
                xr_t = XR[sb_ % 3]
                z = zr[sb_ % 2]
                zb = zbr[sb_ % 2]
                yield from g_conv(NT, xr_t, z, zb)
                zd = self.ZS[:, sb_ * NT:(sb_ + 1) * NT].rearrange("(c p) t -> p c t", p=128)
                self.dma(SP, zd, z.t[:], z.bs[0], r=z.bs, w=[ZSb[sb_]])
                zbd = self.ZB[:, sb_ * NT:(sb_ + 1) * NT].rearrange("(c p) t -> p c t", p=128)
                self.dma(SP, zbd, zb.t[:], zb.bs[0], r=zb.bs, w=[ZBb[sb_]])
                yield
                yield from g_gates_scan(NT, 0, True, z, zb, S0, POOL)
                sfd = self.SF[:, sb_ * NT:(sb_ + 1) * NT].rearrange("(c p) t -> p c t", p=128)
                self.dma(SP, sfd, S0.t[:], S0.bs[0], r=S0.bs, w=[SFb[sb_]])
                yield

            pre[0] = [x_load(NCB), x_load(NCB + 1)]
            self.run_prop([(NS(0), 8)])
            for it in range(NSB + 2):
                if it + 1 < NSB:
                    pre[it + 1] = [x_load(NCB + (it + 1) * 2), x_load(NCB + (it + 1) * 2 + 1)]
                items = []
                if it - 2 >= 0:
                    items.append((FWD(it - 2), 66))
                if it < NSB:
                    items.append((PJ(it), 41))
                if it + 1 < NSB:
                    items.append((NS(it + 1), 8))
                self.run_prop(items)
            self.P.barrier()

        with ExitStack() as p2:
            sgr.append(self.sbc(p2, "sg1", [128, 8, NT], BF16, 8))
            S0r.append(self.sbc(p2, "S01", [128, 8, NT], F32, 8))

            def loads(sb_):
                k = sb_ % 2
                zd = self.ZS[:, sb_ * NT:(sb_ + 1) * NT].rearrange("(c p) t -> p c t", p=128)
                self.dma(SP, zr[k].t[:], zd, zr[k].bs[0], r=[ZSb[sb_]], w=zr[k].bs)
                sfd = self.SF[:, sb_ * NT:(sb_ + 1) * NT].rearrange("(c p) t -> p c t", p=128)
                self.dma(SP, S0r[k].t[:], sfd, S0r[k].bs[0], r=[SFb[sb_]], w=S0r[k].bs)
                sgd = self.SG[:, sb_ * NT:(sb_ + 1) * NT].rearrange("(c p) t -> p c t", p=128)
                self.dma(SP, sgr[k].t[:], sgd, sgr[k].bs[0], r=[SGb[sb_]], w=sgr[k].bs)
                zbd = self.ZB[:, sb_ * NT:(sb_ + 1) * NT].rearrange("(c p) t -> p c t", p=128)
                self.dma(SP, zbr[k].t[:], zbd, zbr[k].bs[0], r=[ZBb[sb_]], w=zbr[k].bs)

            def G(sb_):
                k = sb_ % 2
                if sb_ - 1 >= 0:
                    loads(sb_ - 1)
                yield
                yield from g_gates_scan(NT, 1, True, zr[k], zbr[k], S1, POOL)
                yield from g_combine(NT, S0r[k], sgr[k], hTr[k])

            def O(sb_):
                k = sb_ % 2
                blk0 = NCB + sb_ * 2
                xts = [x_load(blk0 + tb) for tb in range(2)]
                yield
                for tb in range(2):
                    yield from g_out(tb, blk0 + tb, None, xts[tb], hTr[k], last)
                if last:
                    yield from g_final(blk0, xts)

            self.memset(POOL, ssq1.t[:], 0.0, [ssq1])
            loads(NSB - 1)
            for sb_ in range(NSB - 1, -2, -1):
                items = []
                if sb_ >= 0:
                    g_ = G(sb_)
                    for _ in range(25):
                        next(g_)
                    items.append((g_, 22))
                if sb_ + 1 < NSB:
                    items.append((O(sb_ + 1), 15))
                self.run_prop(items)
            self.P.barrier()


_CACHE = {}


def make_in_maps(inp, n_cores=8):
    f = lambda a: np.ascontiguousarray(np.asarray(a, np.float32))
    rows = np.zeros((128, RM.n), np.float32)
    for l in range(DEPTH):
        rows[:, RM["bg%d" % l]:RM["bg%d" % l] + D] = f(inp["b_mod"])[l, 2 * D:3 * D][None, :]
    for i in range(2):
        rows[:, RM["lng%d" % i]:RM["lng%d" % i] + 512] = f(inp["a_ln_g"])[i][None, :]
        rows[:, RM["lnb%d" % i]:RM["lnb%d" % i] + 512] = f(inp["a_ln_b"])[i][None, :]
        rows[:, RM["sink%d" % i]:RM["sink%d" % i] + 8] = f(inp["b_sink"])[i][None, :]
    rows[:, RM["fg"]:RM["fg"] + D] = f(inp["final_g"])[None, :]
    rope = rope_table()
    a_w_sT = np.ascontiguousarray(np.transpose(f(inp["a_w_s"]), (0, 3, 1, 2)))
    shared = {
        "rows": rows, "rope": rope, "w_mod": f(inp["w_mod"]), "ab_w_in": f(inp["ab_w_in"]),
        "ab_w_out": f(inp["ab_w_out"]), "a_w_sT": a_w_sT, "c_w_in": f(inp["c_w_in"]),
        "c_w_a": f(inp["c_w_a"]), "c_w_i": f(inp["c_w_i"]), "c_w_out": f(inp["c_w_out"]),
    }
    maps = []
    for b in range(n_cores):
        cols = np.zeros((128, CM.n), np.float32)
        for l in range(DEPTH):
            cols[:, CM["ng%d" % l]:CM["ng%d" % l] + 8] = col8(inp["norm_g"][l])
            cols[:, CM["bsh%d" % l]:CM["bsh%d" % l] + 8] = col8(inp["b_mod"][l, 0:D])
            cols[:, CM["bsc%d" % l]:CM["bsc%d" % l] + 8] = col8(inp["b_mod"][l, D:2 * D])
        cols[:, CM["c"]:CM["c"] + 8] = col8(inp["c"][b])
        cols[:, CM["cctx"]:CM["cctx"] + 8] = col8(inp["c_ctx"])
        for i in range(2):
            for j in range(4):
                cols[:, CM["cw%d_%d" % (i, j)]:CM["cw%d_%d" % (i, j)] + 8] = col8(inp["c_conv_w"][i, j])
            cols[:, CM["cb%d" % i]:CM["cb%d" % i] + 8] = col8(inp["c_conv_b"][i])
            for d in range(2):
                cols[:, CM["ba%d_%d" % (i, d)]:CM["ba%d_%d" % (i, d)] + 8] = col8(inp["c_b_a"][i, d])
                cols[:, CM["bi%d_%d" % (i, d)]:CM["bi%d_%d" % (i, d)] + 8] = col8(inp["c_b_i"][i, d])
                cols[:, CM["lam%d_%d" % (i, d)]:CM["lam%d_%d" % (i, d)] + 8] = col8(inp["c_lam"][i, d])
            cols[:, CM["bs%d" % i]:CM["bs%d" % i] + 4] = np.asarray(inp["a_b_s"][i], np.float32).T
        m = dict(shared)
        m["cols"] = cols
        m["x"] = f(inp["x"][b])
        m["ctx"] = f(inp["ctx"][b])
        maps.append(m)
    return maps


def kernel(**inputs):
    if "nc" not in _CACHE:
        _CACHE["nc"] = Builder().build()
    nc = _CACHE["nc"]
    maps = make_in_maps(inputs, 8)
    res = run_bass_kernel_spmd(nc, maps, core_ids=list(range(8)))
    out = np.stack([np.asarray(r["out"], np.float32) for r in res.results], axis=0)
    return out
```

```python
import math
from contextlib import ExitStack

import numpy as np
import concourse.bass as bass
import concourse.mybir as mybir
from concourse.bass_utils import run_bass_kernel_spmd

F32 = mybir.dt.float32
BF16 = mybir.dt.bfloat16
ALU = mybir.AluOpType
AF = mybir.ActivationFunctionType
AX = mybir.AxisListType

D = 1024
L = 4096
LC = 256
NLB = L // 128
NCB = LC // 128
NBLK = NLB + NCB
DEPTH = 4
EPS = 1e-6
AB_IN = 2816
GELU_C = 0.7978845608028654
SQ_044 = math.sqrt(0.044715)

PE, ACT, DVE, POOL, SP = "tensor", "scalar", "vector", "gpsimd", "sync"
ENGS = (PE, ACT, DVE, POOL, SP)


class Buf:
    __slots__ = ("name", "writer", "readers", "sem", "excl")

    def __init__(self, name, excl=False):
        self.name = name
        self.writer = None
        self.readers = {}
        self.sem = None
        self.excl = excl


class Op:
    __slots__ = ("eng", "fn", "deps", "odeps", "signal", "sig", "isdma", "key", "waits", "idx", "dur", "lat", "seg",
                 "nin", "succ", "rt", "fin", "tbl", "bl")

    def __init__(self, eng, fn, isdma=False):
        self.eng = eng
        self.fn = fn
        self.deps = []
        self.odeps = []
        self.signal = False
        self.sig = None
        self.isdma = isdma
        self.key = eng
        self.waits = None
        self.idx = 0
        self.dur = 0.3
        self.lat = 0.0
        self.seg = 0
        self.nin = 0
        self.succ = None
        self.rt = 0.0
        self.fin = 0.0
        self.tbl = None
        self.bl = 0.0


def _b(x):
    return x if isinstance(x, Buf) else x.b


class Tile:
    __slots__ = ("t", "b", "bs")

    def __init__(self, t, b, bs=None):
        self.t = t
        self.b = b
        self.bs = bs


class Prog:
    SCHED = True
    SEM_LAT = 1.0

    def __init__(self, nc, stack):
        self.nc = nc
        self.stack = stack
        self.ops = {e: [] for e in ENGS}
        self.all_ops = []
        self.esem = {e: stack.enter_context(nc.semaphore("es_" + e)) for e in ENGS}
        self.sem_pool = []
        self.sem_live = []
        self.nsem = 0
        self.dma_since_barrier = []
        self.all_dma_sigs = {}
        self.ndma = 0
        self.seg = 0
        self.seg_deps = {0: []}

    def _deps(self, op, reads, writes):
        deps = op.deps
        odeps = op.odeps
        for x in reads:
            b = _b(x)
            w = b.writer
            if w is not None:
                deps.append(w)
            if b.excl:
                for r in b.readers.values():
                    if r.eng != op.eng:
                        deps.append(r)
        for x in writes:
            b = _b(x)
            w = b.writer
            if w is not None:
                if w.isdma or w.eng != op.eng or op.isdma:
                    deps.append(w)
                else:
                    odeps.append(w)
            for r in b.readers.values():
                if r.isdma or r.eng != op.eng or op.isdma:
                    deps.append(r)
                else:
                    odeps.append(r)
        for d in deps:
            d.signal = True
        for x in reads:
            _b(x).readers[id(op)] = op
        for x in writes:
            b = _b(x)
            b.writer = op
            b.readers = {}

    def _add(self, o):
        o.idx = len(self.all_ops)
        o.seg = self.seg
        self.all_ops.append(o)

    def op(self, eng, fn, reads=(), writes=(), dur=0.3, tbl=None):
        o = Op(eng, fn)
        o.dur = dur
        o.tbl = tbl
        self._deps(o, reads, writes)
        self._add(o)
        return o

    def dma(self, eng, out, in_, sbuf, reads=(), writes=(), **kw):
        sbuf = _b(sbuf)
        if sbuf.sem is None:
            if self.sem_pool:
                ent = self.sem_pool.pop()
            else:
                ent = [self.stack.enter_context(self.nc.semaphore("ds%d" % self.nsem)), 0, None]
                self.nsem += 1
            sbuf.sem = ent
            self.sem_live.append(sbuf)
        ent = sbuf.sem
        o = Op(eng, lambda e: e.dma_start(out=out, in_=in_, **kw), isdma=True)
        self.ndma += 1
        o.key = ("dma", self.ndma)
        try:
            nbytes = float(out.nbytes())
        except Exception:
            nbytes = 65536.0
        o.dur = 0.6 if eng == POOL else 0.1
        o.lat = 2.0 + nbytes / 120e3
        self._deps(o, reads, writes)
        if ent[2] is not None and ent[2].seg == self.seg:
            o.odeps.append(ent[2])
        ent[2] = o
        ent[1] += 16
        o.sig = (ent[0], ent[1])
        o.signal = True
        self._add(o)
        self.dma_since_barrier.append(o)
        self.all_dma_sigs[id(ent[0])] = o.sig
        return o

    def barrier(self):
        lasts = {}
        for o in self.all_ops:
            if not o.isdma:
                lasts[o.eng] = o
        self._last_hint = lasts
        self.seg += 1
        self.seg_deps[self.seg] = ("BAR", list(self.dma_since_barrier))
        self.dma_since_barrier = []
        for b in self.sem_live:
            self.sem_pool.append(b.sem)
            b.sem = None
        self.sem_live = []

    def _schedule_segment(self, ops):
        seg = ops[0].seg
        for o in ops:
            o.nin = 0
            o.succ = []
            o.rt = 0.0
        for o in ops:
            seen = set()
            for d in o.deps + o.odeps:
                if d.seg == seg and id(d) not in seen:
                    seen.add(id(d))
                    d.succ.append(o)
                    o.nin += 1
        for o in reversed(ops):
            m = 0.0
            for s_ in o.succ:
                v = s_.bl + (self.SEM_LAT if s_.eng != o.eng else 0.0)
                if v > m:
                    m = v
            o.bl = o.dur + o.lat + m
        BLE = ('tensor', 'vector', 'gpsimd', 'sync')
        cand = {e: [] for e in ENGS}
        for o in ops:
            if o.nin == 0:
                cand[o.eng].append(o)
        free = {e: 0.0 for e in ENGS}
        order = {e: [] for e in ENGS}
        left = len(ops)
        lat = self.SEM_LAT
        cur_tbl = getattr(self, "_cur_tbl", None)
        TSW = 0.9
        while left:
            best = None
            for e in ENGS:
                c = cand[e]
                if not c:
                    continue
                t = free[e]
                pick = None
                if e == ACT:
                    pk = None
                    for o in c:
                        st_ = o.rt if o.rt > t else t
                        if o.tbl is not None and o.tbl != cur_tbl:
                            st_ += TSW
                        k_ = (st_, o.idx)
                        if pk is None or k_ < pk:
                            pk = k_
                            pick = o
                    start = pk[0]
                else:
                    ubl = e in BLE
                    for o in c:
                        if o.rt <= t:
                            if pick is None or pick.rt > t or ((o.bl, -o.idx) > (pick.bl, -pick.idx) if ubl else o.idx < pick.idx):
                                pick = o
                        elif pick is None or (pick.rt > t and (o.rt < pick.rt or (o.rt == pick.rt and o.idx < pick.idx))):
                            pick = o
                    start = max(t, pick.rt)
                if best is None or start < best[0] or (start == best[0] and pick.idx < best[1].idx):
                    best = (start, pick)
            start, o = best
            e = o.eng
            if e == ACT and o.tbl is not None:
                cur_tbl = o.tbl
            cand[e].remove(o)
            order[e].append(o)
            free[e] = start + o.dur
            o.fin = start + o.dur + o.lat
            left -= 1
            for s_ in o.succ:
                r = o.fin + (lat if (s_.eng != e or o.isdma) else 0.0)
                if r > s_.rt:
                    s_.rt = r
                s_.nin -= 1
                if s_.nin == 0:
                    cand[s_.eng].append(s_)
        self._cur_tbl = cur_tbl
        self.sim_time = self.sim_time + max(free.values()) if hasattr(self, "sim_time") else max(free.values())
        return order

    def emit(self):
        nc = self.nc
        segs = {}
        for o in self.all_ops:
            segs.setdefault(o.seg, []).append(o)
        self.ops = {e: [] for e in ENGS}
        last_compute = {}
        for sg in sorted(segs):
            ops = segs[sg]
            if self.SCHED:
                order = self._schedule_segment(ops)
            else:
                order = {e: [o for o in ops if o.eng == e] for e in ENGS}
            info = self.seg_deps.get(sg)
            if info:
                extra = list(last_compute.values()) + info[1]
                for o in extra:
                    o.signal = True
                for e in ENGS:
                    if order[e]:
                        order[e][0].deps = order[e][0].deps + extra
            for e in ENGS:
                self.ops[e].extend(order[e])
                for o in order[e]:
                    if not o.isdma:
                        last_compute[e] = o
        for e in ENGS:
            c = 0
            for o in self.ops[e]:
                if o.isdma:
                    continue
                if o.signal:
                    c += 1
                    o.sig = (self.esem[e], c)
        fin_waits = list(self.all_dma_sigs.values())
        for e in ENGS:
            known = {}
            for o in self.ops[e]:
                need = {}
                for d in o.deps:
                    s, v = d.sig
                    k = id(s)
                    if known.get(k, 0) >= v:
                        continue
                    if k not in need or need[k][1] < v:
                        need[k] = (s, v)
                for k, (s, v) in need.items():
                    known[k] = v
                o.waits = list(need.values())
        self.stats = dict(
            nops={e: len(self.ops[e]) for e in ENGS},
            nwait={e: sum(len(o.waits) for o in self.ops[e]) for e in ENGS},
            nsig={e: sum(1 for o in self.ops[e] if o.signal) for e in ENGS},
            nsem=self.nsem, sim_us=getattr(self, "sim_time", 0.0),
        )

        def run(e, engobj):
            for o in self.ops[e]:
                for s, v in o.waits:
                    engobj.wait_ge(s, v)
                ins = o.fn(engobj)
                if o.isdma:
                    ins.then_inc(o.sig[0], 16)
                elif o.signal:
                    ins.then_inc(o.sig[0], 1)
            if e == SP:
                for s, v in fin_waits:
                    engobj.wait_ge(s, v)

        with nc.Block() as block:
            @block.tensor
            def _(t):
                run(PE, t)

            @block.scalar
            def _(t):
                run(ACT, t)

            @block.vector
            def _(t):
                run(DVE, t)

            @block.gpsimd
            def _(t):
                run(POOL, t)

            @block.sync
            def _(t):
                run(SP, t)


class ColMap:
    def __init__(self):
        self.off = {}
        self.n = 0

    def add(self, name, ncols):
        self.off[name] = self.n
        self.n += ncols

    def __getitem__(self, name):
        return self.off[name]


def build_colmap():
    cm = ColMap()
    for l in range(DEPTH):
        cm.add("ng%d" % l, 8)
        cm.add("bsh%d" % l, 8)
        cm.add("bsc%d" % l, 8)
    cm.add("c", 8)
    cm.add("cctx", 8)
    for i in range(2):
        for j in range(4):
            cm.add("cw%d_%d" % (i, j), 8)
        cm.add("cb%d" % i, 8)
        for d in range(2):
            cm.add("ba%d_%d" % (i, d), 8)
            cm.add("bi%d_%d" % (i, d), 8)
            cm.add("lam%d_%d" % (i, d), 8)
        cm.add("bs%d" % i, 4)
    return cm


def build_rowmap():
    rm = ColMap()
    for l in range(DEPTH):
        rm.add("bg%d" % l, 1024)
    for i in range(2):
        rm.add("lng%d" % i, 512)
        rm.add("lnb%d" % i, 512)
        rm.add("sink%d" % i, 8)
    rm.add("fg", 1024)
    return rm


CM = build_colmap()
RM = build_rowmap()


def col8(v):
    return np.ascontiguousarray(np.asarray(v, np.float32).reshape(8, 128).T)


def rope_table():
    t = np.arange(L)
    r = (t // 64).astype(np.float64)
    c = (t % 64).astype(np.float64)
    inv = (10000.0 ** (-np.arange(16, dtype=np.float32) / np.float32(16))).astype(np.float32).astype(np.float64)
    ang = np.concatenate([r[:, None] * inv, c[:, None] * inv], axis=-1).astype(np.float32)
    cos = np.cos(ang).astype(np.float32)
    sin = np.sin(ang).astype(np.float32)
    return np.ascontiguousarray(np.concatenate([cos, cos, -sin, sin], axis=-1).astype(np.float32))


class StopBuild(Exception):
    pass


class Builder:
    def __init__(self, layers=(0, 1, 2, 3), do_final=True, dbg=False, stop=None):
        self.stop = stop
        self.stopped = False
        self.cks = []
        self.layers = list(layers)
        self.do_final = do_final
        self.dbg = dbg
        self.nc = bass.Bass("TRN2", target_bir_lowering=False)
        self.uid = 0

    def ck(self, name):
        self.cks.append(name)
        if self.stop is not None and name == self.stop:
            self.stopped = True
        return self.stopped

    def dram_in(self, name, shape, dt=F32):
        return self.nc.dram_tensor(name, list(shape), dt, kind="ExternalInput").ap()

    def dram_out(self, name, shape, dt=F32):
        return self.nc.dram_tensor(name, list(shape), dt, kind="ExternalOutput").ap()

    def dram_tmp(self, name, shape, dt=F32):
        return self.nc.dram_tensor(name, list(shape), dt).ap()

    def sb(self, st, name, shape, dt=F32):
        self.uid += 1
        nm = "%s_%d" % (name, self.uid)
        t = st.enter_context(self.nc.sbuf_tensor(nm, list(shape), dt))
        return Tile(t, Buf(nm))

    def sbc(self, st, name, shape, dt, n):
        T = self.sb(st, name, shape, dt)
        T.bs = [Buf("%s_c%d" % (name, c)) for c in range(n)]
        return T

    def ps(self, st, name, shape, dt=F32):
        self.uid += 1
        nm = "%s_%d" % (name, self.uid)
        t = st.enter_context(self.nc.psum_tensor(nm, list(shape), dt))
        return Tile(t, Buf(nm, excl=True))

    @staticmethod
    def _fs(ap):
        n = 1
        for d in ap.shape[1:]:
            n *= int(d)
        return n

    def act(self, out, in_, func, r, w, scale=1.0, bias=0.0, accum=None):
        kw = {}
        if accum is not None:
            kw["accum_out"] = accum
        tbl = {AF.Exp: "exp", AF.Tanh: "exp", AF.Sqrt: "sqrt", AF.Ln: "ln"}.get(func)
        self.P.op(ACT, lambda e: e.activation(out=out, in_=in_, func=func, bias=bias, scale=scale, **kw), r, w,
                  dur=0.2 + self._fs(out) / 1300.0, tbl=tbl)

    def _vdur(self, eng, out, two_in):
        n = self._fs(out)
        if eng == POOL:
            return 0.25 + n * 0.0022
        return (0.15 + n / 800.0) if two_in else (0.10 + n / 1400.0)

    def tt(self, eng, out, in0, in1, op, r, w):
        self.P.op(eng, lambda e: e.tensor_tensor(out=out, in0=in0, in1=in1, op=op), r, w, dur=self._vdur(eng, out, True))

    def ts(self, eng, out, in0, s1, s2, op0, op1, r, w):
        d = self._vdur(eng, out, False)
        if op1 is None:
            self.P.op(eng, lambda e: e.tensor_scalar(out=out, in0=in0, scalar1=s1, scalar2=None, op0=op0), r, w, dur=d)
        else:
            self.P.op(eng, lambda e: e.tensor_scalar(out=out, in0=in0, scalar1=s1, scalar2=s2, op0=op0, op1=op1), r, w, dur=d)

    def stt(self, eng, out, in0, scalar, in1, op0, op1, r, w):
        assert eng == DVE, "scalar_tensor_tensor is DVE-only"
        self.P.op(eng, lambda e: e.scalar_tensor_tensor(out=out, in0=in0, scalar=scalar, in1=in1, op0=op0, op1=op1), r, w,
                  dur=self._vdur(eng, out, True))

    def cp(self, eng, out, in_, r, w):
        if eng == ACT:
            self.P.op(ACT, lambda e: e.activation(out=out, in_=in_, func=AF.Copy), r, w, dur=0.2 + self._fs(out) / 1300.0)
        else:
            self.P.op(eng, lambda e: e.tensor_copy(out=out, in_=in_), r, w, dur=self._vdur(eng, out, False))

    def mm(self, out, lhsT, rhs, start, stop, r, w):
        self.P.op(PE, lambda e: e.matmul(out, lhsT=lhsT, rhs=rhs, start=start, stop=stop), r, w,
                  dur=0.02 + self._fs(out) / 2200.0)

    def mmx(self, out, lhsT, rhs, start, stop, r, w):
        self.P.op(PE, lambda e: e.matmul(out, lhsT=lhsT, rhs=rhs, start=start, stop=stop, skip_group_check=True), r, w,
                  dur=0.02 + self._fs(out) / 2200.0)

    def tr(self, out, in_, ident, r, w):
        self.P.op(PE, lambda e: e.transpose(out=out, in_=in_, identity=ident), r, w, dur=0.07)

    def memset(self, eng, ap, val, w):
        self.P.op(eng, lambda e: e.memset(ap, val), (), w, dur=self._vdur(eng, ap, False))

    def dma(self, eng, out, in_, sbuf, r=(), w=(), **kw):
        self.P.dma(eng, out, in_, sbuf, r, w, **kw)

    def rsqrt(self, out_ap, out_t, v_ap, v_t, s, q, n):
        sa = s.t[:, 0:n]
        self.act(sa, v_ap, AF.Sqrt, [v_t], [s])
        self.P.op(DVE, lambda e: e.reciprocal(out=out_ap, in_=sa), [s], [out_t])

    @staticmethod
    def run_prop(items):
        st_ = [[g_, 0, float(n_)] for g_, n_ in items]
        while st_:
            st_.sort(key=lambda e_: (e_[1] + 1) / e_[2])
            e_ = st_[0]
            try:
                next(e_[0])
                e_[1] += 1
            except StopIteration:
                st_.remove(e_)

    def build(self):
        nc = self.nc
        with ExitStack() as top:
            self.P = Prog(nc, top)
            self.declare_dram()
            self.setup(top)
            self.ck("setup")
            for l in self.layers:
                if self.stopped:
                    break
                self.P.barrier()
                with ExitStack() as st:
                    self.prologue(st, l)
                    if not self.ck("prologue%d" % l):
                        if l % 2 == 0:
                            self.even_layer(st, l)
                        else:
                            self.odd_layer(st, l)
                    self.P.barrier()
            if self.dbg:
                lastl = self.layers[-1]
                src = self.X[(lastl + 1) % 2]
                db = Buf("dbgout")
                for k in range(0, L + LC, 512):
                    n = min(512, L + LC - k)
                    self.dma(SP, self.dbg_x[k:k + n, :], src[k:k + n, :], db)
            self.P.emit()
        return nc

    def declare_dram(self):
        self.x_in = self.dram_in("x", [L, D])
        self.ctx_in = self.dram_in("ctx", [LC, D])
        self.cols_d = self.dram_in("cols", [128, CM.n])
        self.rows_d = self.dram_in("rows", [128, RM.n])
        self.rope_d = self.dram_in("rope", [L, 128])
        self.w_mod = self.dram_in("w_mod", [DEPTH, D, 3 * D])
        self.ab_w_in = self.dram_in("ab_w_in", [2, D, AB_IN])
        self.ab_w_out = self.dram_in("ab_w_out", [2, D, D])
        self.a_w_sT = self.dram_in("a_w_sT", [2, 128, 4, 128])
        self.c_w_in = self.dram_in("c_w_in", [2, D, 2 * D])
        self.c_w_a = self.dram_in("c_w_a", [2, 2, 4, 256, 256])
        self.c_w_i = self.dram_in("c_w_i", [2, 2, 4, 256, 256])
        self.c_w_out = self.dram_in("c_w_out", [2, D, D])
        self.out_d = self.dram_out("out", [L, D])
        self.X = [self.dram_tmp("XA", [L + LC, D]), self.dram_tmp("XB", [L + LC, D])]
        self.ZS = self.dram_tmp("ZS", [D, L])
        self.SF = self.dram_tmp("SF", [D, L])
        self.SG = self.dram_tmp("SG", [D, L], BF16)
        self.ZB = self.dram_tmp("ZB", [D, L], BF16)
        if self.dbg:
            self.dbg_x = self.dram_out("dbg_x", [L + LC, D])

    def x_src(self, l, blk):
        if l == self.layers[0] and l == 0:
            if blk < NCB:
                return self.ctx_in[blk * 128:(blk + 1) * 128, :]
            return self.x_in[(blk - NCB) * 128:(blk - NCB + 1) * 128, :]
        if l == self.layers[0]:
            return self.dbg_xin[blk * 128:(blk + 1) * 128, :]
        return self.X[l % 2][blk * 128:(blk + 1) * 128, :]

    def x_dst(self, l, blk):
        return self.X[(l + 1) % 2][blk * 128:(blk + 1) * 128, :]

    def setup(self, st):
        nc = self.nc
        self.COLS = self.sb(st, "cols", [128, CM.n])
        self.dma(SP, self.COLS.t[:], self.cols_d, self.COLS, w=[self.COLS])
        self.ck("s_cols")
        self.identf = self.sb(st, "identf", [128, 128])
        self.identb = self.sb(st, "identb", [128, 128], BF16)
        self.memset(POOL, self.identf.t[:], 1.0, [self.identf])
        idf = self.identf
        self.P.op(POOL, lambda e: e.affine_select(out=idf.t[:], in_=idf.t[:], pattern=[[-1, 128]], compare_op=ALU.is_equal,
                                                 fill=0.0, base=0, channel_multiplier=1), [idf], [idf])
        self.cp(DVE, self.identb.t[:], self.identf.t[:], [self.identf], [self.identb])
        self.ck("s_ident")
        mtmp = self.sb(st, "mtmp", [128, 128])
        self.memset(POOL, mtmp.t[:], 1.0, [mtmp])
        self.P.op(POOL, lambda e: e.affine_select(out=mtmp.t[:], in_=mtmp.t[:], pattern=[[-1, 128]], compare_op=ALU.is_ge,
                                                 fill=0.0, base=0, channel_multiplier=1), [mtmp], [mtmp])
        mtmp2 = self.sb(st, "mtmp2", [128, 128])
        self.memset(POOL, mtmp2.t[:], 1.0, [mtmp2])
        self.P.op(POOL, lambda e: e.affine_select(out=mtmp2.t[:], in_=mtmp2.t[:], pattern=[[1, 128]], compare_op=ALU.is_ge,
                                                 fill=0.0, base=0, channel_multiplier=-1), [mtmp2], [mtmp2])
        self.mbprev = self.sb(st, "mbprev", [128, 4, 128], BF16)
        self.mbnext = self.sb(st, "mbnext", [128, 4, 128], BF16)
        for (src_, dst_) in ((mtmp, self.mbprev), (mtmp2, self.mbnext)):
            self.ts(DVE, src_.t[:], src_.t[:], -1.0, 30000.0, ALU.add, ALU.mult, [src_], [src_])
            self.cp(DVE, dst_.t[:], src_.t[:].unsqueeze(1).broadcast_to([128, 4, 128]), [src_], [dst_])
        self.ck("s_masks")
        cc = CM["c"]
        th = self.sb(st, "sc_th", [128, 16])
        sc = self.sb(st, "sc", [128, 16])
        self.act(th.t[:], self.COLS.t[:, cc:cc + 16], AF.Tanh, [self.COLS], [th], scale=0.5)
        self.stt(DVE, sc.t[:], th.t[:], 1.0, self.COLS.t[:, cc:cc + 16], ALU.add, ALU.mult, [th, self.COLS], [sc])
        self.ts(DVE, sc.t[:], sc.t[:], 0.5, None, ALU.mult, None, [sc], [sc])
        self.scT = self.sb(st, "scT", [128, 8, 2], BF16)
        self.cp(DVE, self.scT.t[:, :, 0], sc.t[:, 0:8], [sc], [self.scT])
        self.cp(DVE, self.scT.t[:, :, 1], sc.t[:, 8:16], [sc], [self.scT])
        self.screp = []
        for i in range(2):
            t = self.sb(st, "screp%d" % i, [128, 8, 128], BF16)
            self.cp(DVE, t.t[:], sc.t[:, 8 * i:8 * i + 8].unsqueeze(2).broadcast_to([128, 8, 128]), [sc], [t])
            self.screp.append(t)
        self.ck("s_silu")
        self.SSQ = [self.sb(st, "ssq0", [128, NBLK]), self.sb(st, "ssq1", [128, NBLK])]
        l0 = self.layers[0]
        self.memset(POOL, self.SSQ[l0 % 2].t[:], 0.0, [self.SSQ[l0 % 2]])
        if self.dbg and l0 != 0:
            self.dbg_xin = self.dram_in("dbg_xin", [L + LC, D])
        with ExitStack() as s2:
            xr = [self.sb(s2, "prex%d" % i, [128, D]) for i in range(6)]
            junk = self.sb(s2, "prej", [128, D])
            for blk in range(NBLK):
                xt = xr[blk % 6]
                self.dma(SP, xt.t[:], self.x_src(l0, blk), xt, w=[xt])
                self.act(junk.t[:], xt.t[:], AF.Square, [xt], [junk, self.SSQ[l0 % 2]],
                         accum=self.SSQ[l0 % 2].t[:, blk:blk + 1])
            self.P.barrier()

    def prologue(self, st, l):
        P = self.P
        self.gs = [self.sb(st, "gs%d" % i, [128, 8]) for i in range(2)]
        self.sh = [self.sb(st, "sh%d" % i, [128, 8]) for i in range(2)]
        self.gate = [self.sb(st, "gate%d" % i, [128, D]) for i in range(2)]
        self.rstd = self.sb(st, "rstd", [128, NBLK])
        with ExitStack() as s2:
            bg = self.sb(s2, "bg", [128, D])
            self.dma(SP, bg.t[:], self.rows_d[:, RM["bg%d" % l]:RM["bg%d" % l] + D], bg, w=[bg])
            wm = [self.sb(s2, "wm%d" % i, [128, 8, 512], BF16) for i in range(6)]
            pcols_full = self.ps(s2, "pcols", [128, 512])
            pcols = Tile(pcols_full.t[:, 0:32].rearrange("p (n t) -> p n t", t=2), pcols_full.b)
            pg = [self.ps(s2, "pg%d" % i, [128, 512]) for i in range(2)]
            wsrc = self.w_mod[l].rearrange("(kc p) n -> p kc n", p=128)
            for pi in range(6):
                w = wm[pi]
                self.dma(POOL, w.t[:], wsrc[:, :, pi * 512:(pi + 1) * 512], w, w=[w])
                if pi < 4:
                    for q in range(4):
                        nn = 4 * pi + q
                        for kc in range(8):
                            self.mm(pcols.t[:, nn, :], w.t[:, kc, q * 128:(q + 1) * 128], self.scT.t[:, kc, :],
                                    kc == 0, kc == 7, [w, self.scT], [pcols])
                else:
                    hh = pi - 4
                    for i in range(2):
                        for kc in range(8):
                            self.mm(pg[i].t[:], self.screp[i].t[:, kc, :], w.t[:, kc, :], kc == 0, kc == 7,
                                    [w, self.screp[i]], [pg[i]])
                        self.tt(DVE, self.gate[i].t[:, hh * 512:(hh + 1) * 512], pg[i].t[:], bg.t[:, hh * 512:(hh + 1) * 512],
                                ALU.add, [pg[i], bg], [self.gate[i]])
            modc = self.sb(s2, "modc", [128, 16, 2])
            self.cp(DVE, modc.t[:], pcols.t[:], [pcols], [modc])
            ng, bsh, bsc = CM["ng%d" % l], CM["bsh%d" % l], CM["bsc%d" % l]
            tmp = self.sb(s2, "modtmp", [128, 8])
            for i in range(2):
                self.tt(DVE, self.sh[i].t[:], modc.t[:, 0:8, i], self.COLS.t[:, bsh:bsh + 8], ALU.add,
                        [modc, self.COLS], [self.sh[i]])
                self.tt(DVE, tmp.t[:], modc.t[:, 8:16, i], self.COLS.t[:, bsc:bsc + 8], ALU.add, [modc, self.COLS], [tmp])
                self.stt(DVE, self.gs[i].t[:], tmp.t[:], 1.0, self.COLS.t[:, ng:ng + 8], ALU.add, ALU.mult,
                         [tmp, self.COLS], [self.gs[i]])
            v = self.sb(s2, "nv", [128, NBLK])
            self.ts(DVE, v.t[:], self.SSQ[l % 2].t[:], 1.0 / D, EPS, ALU.mult, ALU.add, [self.SSQ[l % 2]], [v])
            hs = self.sb(s2, "hs", [128, NBLK])
            hq = self.sb(s2, "hq", [128, NBLK])
            self.rsqrt(self.rstd.t[:], self.rstd, v.t[:], v, hs, hq, NBLK)
            self.memset(POOL, self.SSQ[(l + 1) % 2].t[:], 0.0, [self.SSQ[(l + 1) % 2]])
            self.P.barrier()

    def norm_T(self, xt, blk, xn, psT, hT_ap, hT_tile, is_ctx):
        i = 1 if is_ctx else 0
        self.act(xn.t[:], xt.t[:], AF.Copy, [xt, self.rstd], [xn], scale=self.rstd.t[:, blk:blk + 1])
        for c in range(8):
            self.tr(psT.t[:, c, :], xn.t[:, c * 128:(c + 1) * 128], self.identb.t[:], [xn, self.identb], [psT])
        for c in range(8):
            self.ts(DVE, hT_ap[:, c, :], psT.t[:, c, :], self.gs[i].t[:, c:c + 1], self.sh[i].t[:, c:c + 1],
                    ALU.mult, ALU.add, [psT, self.gs[i], self.sh[i]], [hT_tile])

    def gelu2(self, out_t, ps_t, w_t, in_t, n):
        self.act(w_t.t[:, 0:n], ps_t.t[:, 0:n], AF.Square, [ps_t], [w_t], scale=SQ_044)
        self.stt(DVE, in_t.t[:, 0:n], w_t.t[:, 0:n], 1.0, ps_t.t[:, 0:n], ALU.add, ALU.mult, [w_t, ps_t], [in_t])
        self.act(w_t.t[:, 0:n], in_t.t[:, 0:n], AF.Tanh, [in_t], [w_t], scale=GELU_C)
        self.stt(DVE, out_t.t[:, 0:n], w_t.t[:, 0:n], 1.0, ps_t.t[:, 0:n], ALU.add, ALU.mult, [w_t, ps_t], [out_t])

    def silu2(self, out_ap, out_t, ps_ap, ps_t, w_ap, w_t):
        self.act(w_ap, ps_ap, AF.Tanh, [ps_t], [w_t], scale=0.5)
        self.stt(DVE, out_ap, w_ap, 1.0, ps_ap, ALU.add, ALU.mult, [w_t, ps_t], [out_t])

    def even_layer(self, st, l):
        i = l // 2
        W_in = self.sb(st, "Win", [128, 8, AB_IN], BF16)
        W_out = self.sb(st, "Wout", [128, 8, D], BF16)
        W_sT = self.sb(st, "WsT", [128, 4, 128], BF16)
        wsrc = self.ab_w_in[i].rearrange("(kc p) n -> p kc n", p=128)
        Wg = {}
        for (c0, n) in ((512, 512), (0, 512), (2304, 512), (1536, 512), (2048, 256), (1024, 512)):
            Wg[c0] = Buf("Win_%d" % c0)
            for kc in range(0, 8, 4):
                self.dma(POOL, W_in.t[:, kc:kc + 4, c0:c0 + n], wsrc[:, kc:kc + 4, c0:c0 + n], Wg[c0], w=[Wg[c0]])
        self.dma(POOL, W_sT.t[:], self.a_w_sT[i], W_sT, w=[W_sT])
        wosrc = self.ab_w_out[i].rearrange("(kc p) n -> p kc n", p=128)
        for kc in range(0, 8, 2):
            self.dma(POOL, W_out.t[:, kc:kc + 2, :], wosrc[:, kc:kc + 2, :], W_out, w=[W_out])
        lng = self.sb(st, "lng", [128, 512])
        lnb = self.sb(st, "lnb", [128, 512])
        sink = self.sb(st, "sink", [128, 8])
        esink = self.sb(st, "esink", [128, 8])
        self.dma(SP, lng.t[:], self.rows_d[:, RM["lng%d" % i]:RM["lng%d" % i] + 512], lng, w=[lng])
        self.dma(SP, lnb.t[:], self.rows_d[:, RM["lnb%d" % i]:RM["lnb%d" % i] + 512], lnb, w=[lnb])
        self.dma(SP, sink.t[:], self.rows_d[:, RM["sink%d" % i]:RM["sink%d" % i] + 8], sink, w=[sink])
        self.act(esink.t[:], sink.t[:], AF.Exp, [sink], [esink])
        bs0 = CM["bs%d" % i]

        KT = self.sb(st, "KT", [128, 2, NBLK, 128], BF16)
        KTb = [Buf("KT%d" % b) for b in range(NBLK)]
        Vb = self.sb(st, "Vb", [128, NBLK, 2, 128], BF16)
        Vbb = [Buf("Vb%d" % b) for b in range(NBLK)]
        self.memset(POOL, Vb.t[:, :, :, 64:128], 1.0, Vbb)

        ring = lambda name, shape, dt=F32, n=2: [self.sb(st, "%s%d" % (name, k), shape, dt) for k in range(n)]
        xblk = ring("xblk", [128, D], F32, 4)
        hT = ring("hT", [128, 8, 128], BF16, 2)
        ropet = ring("rope", [128, 128], F32, 2)
        qT = ring("qT", [128, 8, 128], BF16, 3)
        sgb2 = ring("sgb2", [128, 512], F32, 3)
        ymix = ring("ymix", [128, D], BF16, 3)
        xn = self.sb(st, "xn", [128, D], BF16)
        wv, iv = self.sb(st, "wv", [128, 512]), self.sb(st, "iv", [128, 512])
        wu, iu = self.sb(st, "wu", [128, 512]), self.sb(st, "iu", [128, 512])
        gu2 = self.sb(st, "gu2", [128, 512])
        sga2 = self.sb(st, "sga2", [128, 512])
        gv = self.sb(st, "gv", [128, 512])
        vln = self.sb(st, "vln", [128, 512], BF16)
        vtmp = self.sb(st, "vtmp", [128, 512])
        lnst = self.sb(st, "lnst", [128, 16])
        lnr = self.sb(st, "lnr", [128, 4])
        lnm = self.sb(st, "lnm", [128, 4])
        lhs_ = self.sb(st, "lhs", [128, 4])
        wg = self.sb(st, "wg", [128, 512])
        r1 = self.sb(st, "r1", [128, 640])
        r2 = self.sb(st, "r2", [128, 640])
        qz = self.sb(st, "qz", [128, 8, 128], BF16)
        self.memset(POOL, qz.t[:], 0.0, [qz])
        kdup = self.sb(st, "kdup", [128, 2, 2, 64], BF16)
        PT = [[self.sb(st, "PT%d_%d" % (kh, k), [128, 512], BF16) for k in range(5)] for kh in range(2)]
        den = self.sb(st, "den", [128, 8])
        rden = self.sb(st, "rden", [128, 8])
        ybt = self.sb(st, "ybt", [128, 512])
        ymixT = self.sb(st, "ymixT", [128, 8, 128], BF16)
        ytmp = self.sb(st, "ytmp", [128, D])

        psT = self.ps(st, "psT", [128, 8, 128], BF16)
        psTb = self.ps(st, "psTb", [128, 8, 128], BF16)
        psA = [self.ps(st, "psA%d" % k, [128, 512]) for k in range(2)]
        psQ = self.ps(st, "psQ", [128, 512])
        psS = [self.ps(st, "psS%d" % k, [128, 512]) for k in range(2)]
        psV = self.ps(st, "psV", [128, 4, 128])
        psY = Tile(psV.t[:].rearrange("p h d -> p (h d)"), psV.b)

        ssq_next = self.SSQ[(l + 1) % 2]

        def proj(ps, hslot, c0, n):
            for kc in range(8):
                self.mm(ps.t[:, 0:n], hT[hslot].t[:, kc, :], W_in.t[:, kc, c0:c0 + n], kc == 0, kc == 7,
                        [hT[hslot], Wg[c0]], [ps])

        def streamN(blk):
            is_ctx = blk < NCB
            mi = 1 if is_ctx else 0
            xt = xblk[blk % 4]
            self.dma(SP, xt.t[:], self.x_src(l, blk), xt, w=[xt])
            yield
            self.act(xn.t[:], xt.t[:], AF.Copy, [xt, self.rstd], [xn], scale=self.rstd.t[:, blk:blk + 1])
            yield
            for c in range(8):
                self.tr(psT.t[:, c, :], xn.t[:, c * 128:(c + 1) * 128], self.identb.t[:], [xn, self.identb], [psT])
            yield
            h = hT[blk % 2]
            for c in range(8):
                self.ts(DVE, h.t[:, c, :], psT.t[:, c, :], self.gs[mi].t[:, c:c + 1], self.sh[mi].t[:, c:c + 1],
                        ALU.mult, ALU.add, [psT, self.gs[mi], self.sh[mi]], [h])
                if c % 4 == 3:
                    yield

        def gelu2_gen(out_t, ps_t, w_t, in_t):
            self.act(w_t.t[:], ps_t.t[:], AF.Square, [ps_t], [w_t], scale=SQ_044)
            yield
            self.stt(DVE, in_t.t[:], w_t.t[:], 1.0, ps_t.t[:], ALU.add, ALU.mult, [w_t, ps_t], [in_t])
            yield
            self.act(w_t.t[:], in_t.t[:], AF.Tanh, [in_t], [w_t], scale=GELU_C)
            yield
            self.stt(DVE, out_t.t[:], w_t.t[:], 1.0, ps_t.t[:], ALU.add, ALU.mult, [w_t, ps_t], [out_t])
            yield

        def streamS1(blk):
            hs = blk % 2
            ym = ymix[blk % 3]
            proj(psA[0], hs, 512, 512)
            yield
            proj(psA[1], hs, 0, 512)
            yield
            yield from gelu2_gen(gv, psA[0], wv, iv)
            self.memset(POOL, lnst.t[:, 4:8], 0.0, [lnst])
            gv3 = gv.t[:].rearrange("p (g d) -> p g d", g=4)
            self.P.op(DVE, lambda e: e.tensor_reduce(out=lnst.t[:, 0:4], in_=gv3, axis=AX.X, op=ALU.add), [gv], [lnst])
            yield
            for g in range(4):
                self.act(vtmp.t[:, g * 128:(g + 1) * 128], gv.t[:, g * 128:(g + 1) * 128], AF.Square, [gv], [vtmp, lnst],
                         accum=lnst.t[:, 4 + g:5 + g])
            yield
            proj(psA[0], hs, 1024, 512)
            yield
            self.ts(DVE, lnst.t[:, 8:12], lnst.t[:, 0:4], 1.0 / 128, None, ALU.mult, None, [lnst], [lnst])
            self.tt(DVE, lnst.t[:, 12:16], lnst.t[:, 8:12], lnst.t[:, 8:12], ALU.mult, [lnst], [lnst])
            self.stt(DVE, lnst.t[:, 12:16], lnst.t[:, 4:8], 1.0 / 128, lnst.t[:, 12:16], ALU.mult, ALU.subtract,
                     [lnst], [lnst])
            self.ts(DVE, lnst.t[:, 12:16], lnst.t[:, 12:16], 4.0 * EPS, None, ALU.add, None, [lnst], [lnst])
            yield
            self.act(lhs_.t[:], lnst.t[:, 12:16], AF.Sqrt, [lnst], [lhs_])
            yield
            self.P.op(DVE, lambda e: e.reciprocal(out=lnr.t[:], in_=lhs_.t[:]), [lhs_], [lnr])
            self.stt(DVE, lnm.t[:], lnst.t[:, 8:12], -1.0, lnr.t[:], ALU.mult, ALU.mult, [lnst, lnr], [lnm])
            yield
            yield from gelu2_gen(gu2, psA[1], wu, iu)
            for g in range(4):
                self.act(vtmp.t[:, g * 128:(g + 1) * 128], gv.t[:, g * 128:(g + 1) * 128], AF.Identity, [gv, lnr, lnm], [vtmp],
                         scale=lnr.t[:, g:g + 1], bias=lnm.t[:, g:g + 1])
            yield
            self.tt(DVE, vtmp.t[:], vtmp.t[:], lng.t[:], ALU.mult, [vtmp, lng], [vtmp])
            yield
            self.tt(DVE, vln.t[:], vtmp.t[:], lnb.t[:], ALU.add, [vtmp, lnb], [vln])
            yield
            self.act(wv.t[:], psA[0].t[:], AF.Tanh, [psA[0]], [wv], scale=0.5)
            yield
            self.stt(DVE, sga2.t[:], wv.t[:], 1.0, psA[0].t[:], ALU.add, ALU.mult, [wv, psA[0]], [sga2])
            yield
            for g in range(4):
                self.mm(psA[1].t[:, g * 128:(g + 1) * 128], W_sT.t[:, g, :], vln.t[:, g * 128:(g + 1) * 128], True, True,
                        [W_sT, vln], [psA[1]])
            yield
            self.stt(DVE, gu2.t[:], gu2.t[:], 0.25, sga2.t[:], ALU.mult, ALU.mult, [gu2, sga2], [gu2])
            yield
            for g in range(4):
                self.stt(DVE, ym.t[:, g * 128:(g + 1) * 128], psA[1].t[:, g * 128:(g + 1) * 128],
                         self.COLS.t[:, bs0 + g:bs0 + g + 1], gu2.t[:, g * 128:(g + 1) * 128], ALU.add, ALU.mult,
                         [psA[1], self.COLS, gu2], [ym])
                if g % 2 == 1:
                    yield

        def streamS2(blk):
            hs = blk % 2
            s3 = blk % 3
            is_ctx = blk < NCB
            rt = ropet[blk % 2]
            if not is_ctx:
                t0 = (blk - NCB) * 128
                self.dma(SP, rt.t[:], self.rope_d[t0:t0 + 128, :], rt, w=[rt])
            proj(psQ, hs, 2304, 512)
            yield
            self.act(wg.t[:], psQ.t[:], AF.Tanh, [psQ], [wg], scale=0.5)
            yield
            self.stt(DVE, sgb2[s3].t[:], wg.t[:], 1.0, psQ.t[:], ALU.add, ALU.mult, [wg, psQ], [sgb2[s3]])
            yield
            proj(psQ, hs, 1536, 512)
            yield
            if is_ctx:
                q3 = psQ.t[:, 0:512].rearrange("p (h d) -> p h d", h=8)
                self.cp(ACT, qz.t[:, 0::2, 0:64], q3[:, 0::2, :], [psQ], [qz])
                self.cp(ACT, qz.t[:, 1::2, 64:128], q3[:, 1::2, :], [psQ], [qz])
                yield
            else:
                src = psQ.t[:, 0:512].rearrange("p (h d) -> p h d", h=8)
                d1 = r1.t[:, 0:512].rearrange("p (h d) -> p h d", h=8)
                d2 = r2.t[:, 0:512].rearrange("p (h d) -> p h d", h=8)
                self.tt(DVE, d1, src, rt.t[:, 0:64].unsqueeze(1).broadcast_to([128, 8, 64]), ALU.mult, [psQ, rt], [r1])
                yield
                self.tt(DVE, d2[:, :, 0:32], src[:, :, 32:64], rt.t[:, 64:96].unsqueeze(1).broadcast_to([128, 8, 32]), ALU.mult,
                        [psQ, rt], [r2])
                self.tt(DVE, d2[:, :, 32:64], src[:, :, 0:32], rt.t[:, 96:128].unsqueeze(1).broadcast_to([128, 8, 32]), ALU.mult,
                        [psQ, rt], [r2])
                yield
                self.tt(POOL, qz.t[:, 0::2, 0:64], d1[:, 0::2, :], d2[:, 0::2, :], ALU.add, [r1, r2], [qz])
                self.tt(POOL, qz.t[:, 1::2, 64:128], d1[:, 1::2, :], d2[:, 1::2, :], ALU.add, [r1, r2], [qz])
                yield
            proj(psQ, hs, 2048, 256)
            yield
            self.cp(ACT, Vb.t[:, blk, :, 0:64], psQ.t[:, 128:256].rearrange("p (h d) -> p h d", h=2), [psQ], [Vbb[blk]])
            k3 = psQ.t[:, 0:128].rearrange("p (h d) -> p h d", h=2)
            if is_ctx:
                for dup in range(2):
                    self.cp(ACT, kdup.t[:, :, dup, :], k3, [psQ], [kdup])
                yield
            else:
                e1 = r1.t[:, 512:640].rearrange("p (h d) -> p h d", h=2)
                e2 = r2.t[:, 512:640].rearrange("p (h d) -> p h d", h=2)
                self.tt(DVE, e1, k3, rt.t[:, 0:64].unsqueeze(1).broadcast_to([128, 2, 64]), ALU.mult, [psQ, rt], [r1])
                self.tt(DVE, e2[:, :, 0:32], k3[:, :, 32:64], rt.t[:, 64:96].unsqueeze(1).broadcast_to([128, 2, 32]), ALU.mult,
                        [psQ, rt], [r2])
                self.tt(DVE, e2[:, :, 32:64], k3[:, :, 0:32], rt.t[:, 96:128].unsqueeze(1).broadcast_to([128, 2, 32]), ALU.mult,
                        [psQ, rt], [r2])
                yield
                for dup in range(2):
                    self.tt(POOL, kdup.t[:, :, dup, :], e1, e2, ALU.add, [r1, r2], [kdup])
                yield
            for h in range(8):
                self.tr(psTb.t[:, h, :], qz.t[:, h, :], self.identb.t[:], [qz, self.identb], [psTb])
            self.cp(ACT, qT[s3].t[:], psTb.t[:], [psTb], [qT[s3]])
            yield
            for kh in range(2):
                self.tr(psTb.t[:, kh, :], kdup.t[:, kh, :, :].rearrange("p a d -> p (a d)"), self.identb.t[:],
                        [kdup, self.identb], [psTb])
            self.cp(ACT, KT.t[:, :, blk, :], psTb.t[:, 0:2, :], [psTb], [KTb[blk]])
            yield

        def streamB(blk):
            s3 = blk % 3
            is_ctx = blk < NCB
            ym = ymix[s3]
            xt = xblk[blk % 4]
            if is_ctx:
                kbs = [(0, None), (1, None)]
            else:
                kbs = [(0, None), (1, None)]
                if blk - 1 >= NCB:
                    kbs.append((blk - 1, self.mbprev))
                kbs.append((blk, None))
                if blk + 1 < NBLK:
                    kbs.append((blk + 1, self.mbnext))
            nk = len(kbs)
            for kh in range(2):
                for ki, (kb, mb) in enumerate(kbs):
                    pss = psS[(kh * 5 + ki) % 2]
                    if mb is not None:
                        self.mmx(pss.t[:], self.identb.t[:], mb.t[:].rearrange("p h q -> p (h q)"), True, False,
                                 [self.identb, mb], [pss])
                    for hl in range(4):
                        h = 4 * kh + hl
                        if mb is not None:
                            self.mmx(pss.t[:, hl * 128:(hl + 1) * 128], KT.t[:, kh, kb, :], qT[s3].t[:, h, :], False, hl == 3,
                                     [KTb[kb], qT[s3]], [pss])
                        else:
                            self.mm(pss.t[:, hl * 128:(hl + 1) * 128], KT.t[:, kh, kb, :], qT[s3].t[:, h, :],
                                    True, True, [KTb[kb], qT[s3]], [pss])
                    yield
                    pt = PT[kh][ki]
                    self.act(pt.t[:], pss.t[:], AF.Exp, [pss], [pt], scale=0.125)
                    yield
                for hl in range(4):
                    for ki, (kb, mb) in enumerate(kbs):
                        self.mm(psV.t[:, hl, 0:65], PT[kh][ki].t[:, hl * 128:(hl + 1) * 128], Vb.t[:, kb, kh, 0:65],
                                ki == 0, ki == nk - 1, [PT[kh][ki], Vbb[kb]], [psV])
                    if hl % 2 == 1:
                        yield
                self.tt(DVE, den.t[:, 4 * kh:4 * kh + 4], psV.t[:, :, 64], esink.t[:, 4 * kh:4 * kh + 4], ALU.add,
                        [psV, esink], [den])
                self.P.op(DVE, lambda e, kh=kh: e.reciprocal(out=rden.t[:, 4 * kh:4 * kh + 4], in_=den.t[:, 4 * kh:4 * kh + 4]),
                          [den], [rden])
                yv = ybt.t[:, kh * 256:(kh + 1) * 256].rearrange("p (h d) -> p h d", h=4)
                self.stt(DVE, yv, psV.t[:, :, 0:64], 0.5, rden.t[:, 4 * kh:4 * kh + 4].unsqueeze(2).broadcast_to([128, 4, 64]),
                         ALU.mult, ALU.mult, [psV, rden], [ybt])
                yield
            self.tt(POOL, ym.t[:, 512:1024], ybt.t[:], sgb2[s3].t[:], ALU.mult, [ybt, sgb2[s3]], [ym])
            yield
            for c in range(8):
                self.tr(psTb.t[:, c, :], ym.t[:, c * 128:(c + 1) * 128], self.identb.t[:], [ym, self.identb], [psTb])
            self.cp(ACT, ymixT.t[:], psTb.t[:], [psTb], [ymixT])
            yield
            g = self.gate[1]
            for n in range(2):
                for kc in range(8):
                    self.mm(psY.t[:], ymixT.t[:, kc, :], W_out.t[:, kc, n * 512:(n + 1) * 512], kc == 0, kc == 7,
                            [ymixT, W_out], [psY])
                yield
                if is_ctx:
                    self.tt(DVE, ytmp.t[:, n * 512:(n + 1) * 512], psY.t[:], g.t[:, n * 512:(n + 1) * 512], ALU.mult, [psY, g], [ytmp])
                else:
                    self.tt(DVE, xt.t[:, n * 512:(n + 1) * 512], psY.t[:], xt.t[:, n * 512:(n + 1) * 512], ALU.add, [psY, xt], [xt])
                yield
            if is_ctx:
                self.tt(POOL, xt.t[:], ytmp.t[:], xt.t[:], ALU.add, [ytmp, xt], [xt])
                yield
            self.act(ytmp.t[:], xt.t[:], AF.Square, [xt], [ytmp, ssq_next], accum=ssq_next.t[:, blk:blk + 1])
            self.dma(SP, self.x_dst(l, blk), xt.t[:], xt, r=[xt])
            yield

        class _Dry:
            def op(self, *a, **k):
                pass

            def dma(self, *a, **k):
                pass

        def count_steps(gen_fn, blk):
            real = self.P
            self.P = _Dry()
            try:
                n = sum(1 for _ in gen_fn(blk))
            finally:
                self.P = real
            return n + 1

        def run_streams(items):
            st_ = [[g_, 0, float(n_)] for g_, n_ in items]
            while st_:
                st_.sort(key=lambda e_: (e_[1] + 1) / e_[2])
                e_ = st_[0]
                try:
                    next(e_[0])
                    e_[1] += 1
                except StopIteration:
                    st_.remove(e_)

        nsteps = {}

        def item(fn, blk):
            key = (fn.__name__, blk < NCB, blk == NCB, blk == NBLK - 1)
            if key not in nsteps:
                nsteps[key] = count_steps(fn, blk)
            return (fn(blk), nsteps[key])

        for t in range(NBLK + 3):
            items = []
            if t - 3 >= 0:
                items.append(item(streamB, t - 3))
            if 0 <= t - 1 < NBLK:
                items.append(item(streamS1, t - 1))
                items.append(item(streamS2, t - 1))
            if t < NBLK:
                items.append(item(streamN, t))
            run_streams(items)
            if t == NCB + 2:
                for kc in range(8):
                    self.tt(POOL if kc % 2 else DVE, W_out.t[:, kc, :], W_out.t[:, kc, :], self.gate[0].t[:], ALU.mult,
                            [W_out, self.gate[0]], [W_out])
            if self.ck("E%d" % t):
                return


    def odd_layer(self, st, l):
        i = l // 2
        need_ctx = l < DEPTH - 1
        last = (l == DEPTH - 1) and self.do_final
        NT = 256
        NSB = L // NT
        W_in = self.sb(st, "cWin", [128, 8, 2 * D], BF16)
        W_a = self.sb(st, "cWa", [128, 2, 4, 2, 256], BF16)
        W_i = self.sb(st, "cWi", [128, 2, 4, 2, 256], BF16)
        W_out = self.sb(st, "cWout", [128, 8, D], BF16)
        wsrc = self.c_w_in[i].rearrange("(kc p) n -> p kc n", p=128)
        Wxr, Wgg = Buf("cWin_xr"), Buf("cWin_g")
        for (c0, bb) in ((0, Wxr), (D, Wgg)):
            for kc in range(0, 8, 4):
                self.dma(POOL, W_in.t[:, kc:kc + 4, c0:c0 + D], wsrc[:, kc:kc + 4, c0:c0 + D], bb, w=[bb])
        for d in range(2):
            self.dma(POOL, W_a.t[:, d], self.c_w_a[i, d].rearrange("h (ic p) j -> p h ic j", p=128), W_a, w=[W_a])
            self.dma(POOL, W_i.t[:, d], self.c_w_i[i, d].rearrange("h (ic p) j -> p h ic j", p=128), W_i, w=[W_i])
        wosrc = self.c_w_out[i].rearrange("(kc p) n -> p kc n", p=128)
        for kc in range(0, 8, 2):
            self.dma(POOL, W_out.t[:, kc:kc + 2, :], wosrc[:, kc:kc + 2, :], W_out, w=[W_out])
        coefh = self.sb(st, "coefh", [128, 2, 8])
        hba = self.sb(st, "hba", [128, 2, 8])
        hbi = self.sb(st, "hbi", [128, 2, 8])
        for d in range(2):
            lam0 = CM["lam%d_%d" % (i, d)]
            self.act(coefh.t[:, d, :], self.COLS.t[:, lam0:lam0 + 8], AF.Exp, [self.COLS], [coefh], scale=-1.0)
            self.act(coefh.t[:, d, :], coefh.t[:, d, :], AF.Ln, [coefh], [coefh], bias=1.0)
            self.ts(DVE, coefh.t[:, d, :], coefh.t[:, d, :], -4.0, None, ALU.mult, None, [coefh], [coefh])
            b0 = CM["ba%d_%d" % (i, d)]
            self.ts(DVE, hba.t[:, d, :], self.COLS.t[:, b0:b0 + 8], 0.5, None, ALU.mult, None, [self.COLS], [hba])
            b0 = CM["bi%d_%d" % (i, d)]
            self.ts(DVE, hbi.t[:, d, :], self.COLS.t[:, b0:b0 + 8], 0.5, None, ALU.mult, None, [self.COLS], [hbi])
        cw0 = [CM["cw%d_%d" % (i, j)] for j in range(4)]
        cb0 = CM["cb%d" % i]
        DG = self.sb(st, "DG", [128, 4, 8, 128], BF16)
        for j in range(4):
            for c in range(8):
                self.ts(DVE, DG.t[:, j, c, :], self.identf.t[:], self.COLS.t[:, cw0[j] + c:cw0[j] + c + 1], None, ALU.mult, None,
                        [self.identf, self.COLS], [DG])
        if last:
            fg = self.sb(st, "fg", [128, D])
            self.dma(SP, fg.t[:], self.rows_d[:, RM["fg"]:RM["fg"] + D], fg, w=[fg])

        xb = [self.sb(st, "xb%d" % k, [128, D]) for k in range(4)]
        xbi = [0]
        xn = self.sb(st, "cxn", [128, D], BF16)
        hTr = [self.sb(st, "chT%d" % k, [128, 8, NT], BF16) for k in range(2)]
        zr = [self.sbc(st, "z0", [128, 8, NT], F32, 8), self.sbc(st, "z1", [128, 8, NT], F32, 8)]
        zbr = [self.sbc(st, "zb0", [128, 8, NT], BF16, 8), self.sbc(st, "zb1", [128, 8, NT], BF16, 8)]
        sgr = [self.sbc(st, "sg0", [128, 8, NT], BF16, 8)]
        gth_ = self.sb(st, "gth", [128, NT])
        gth = [gth_, gth_]
        A = self.sbc(st, "A", [128, 8, NT], F32, 8)
        Wq = self.sbc(st, "Wq", [128, 8, NT], F32, 8)
        TI = self.sbc(st, "TI", [128, 8, NT], F32, 8)
        thr = [self.sb(st, "thr%d" % k, [128, NT]) for k in range(2)]
        S0r = [self.sbc(st, "S00", [128, 8, NT], F32, 8)]
        S1 = self.sbc(st, "S1", [128, 8, NT], F32, 8)
        carry = [self.sbc(st, "carry%d" % k, [128, 8], F32, 8) for k in range(2)]
        ytmp = self.sb(st, "cytmp", [128, D])
        ssq1 = self.sb(st, "ssq1", [128, 2])
        v1 = self.sb(st, "v1", [128, 2])
        r1_ = self.sb(st, "r1", [128, 2])
        hs1 = self.sb(st, "hs1", [128, 2])
        psT = [self.ps(st, "cpsT%d" % k, [128, 8, 128], BF16) for k in range(2)]
        psP = [self.ps(st, "cpsP%d" % k, [128, 512]) for k in range(3)]
        psG = [self.ps(st, "cpsG%d" % k, [128, 512]) for k in range(3)]
        rot = [0, 0]

        def nextP():
            rot[0] += 1
            return psP[rot[0] % 3]

        def nextG():
            rot[1] += 1
            return psG[rot[1] % 3]

        ssq_next = self.SSQ[(l + 1) % 2]
        xbase = self.X[l % 2] if l != self.layers[0] else self.dbg_xin
        ZSb = [Buf("ZS%d" % k) for k in range(NSB)]
        SFb = [Buf("SF%d" % k) for k in range(NSB)]
        SGb = [Buf("SG%d" % k) for k in range(NSB)]
        ZBb = [Buf("ZB%d" % k) for k in range(NSB)]
        wscaled = [False]

        def scale_wout():
            for kc in range(8):
                self.tt(POOL if kc % 2 else DVE, W_out.t[:, kc, :], W_out.t[:, kc, :], self.gate[0].t[:], ALU.mult,
                        [W_out, self.gate[0]], [W_out])
            wscaled[0] = True

        def next_xb():
            xbi[0] += 1
            return xb[xbi[0] % 4]

        def x_load(blk):
            xt = next_xb()
            self.dma(SP, xt.t[:], xbase[blk * 128:(blk + 1) * 128, :], xt, w=[xt])
            return xt

        def g_load_norm(blk0, nb, is_ctx, hT, xts):
            i_ = 1 if is_ctx else 0
            for tb in range(nb):
                xt = xts[tb]
                self.ts(DVE, xn.t[:], xt.t[:], self.rstd.t[:, blk0 + tb:blk0 + tb + 1], None, ALU.mult, None,
                        [xt, self.rstd], [xn])
                yield
                pT = psT[tb % 2]
                for c in range(8):
                    self.tr(pT.t[:, c, :], xn.t[:, c * 128:(c + 1) * 128], self.identb.t[:], [xn, self.identb], [pT])
                yield
                for c in range(8):
                    self.ts(DVE, hT.t[:, c, tb * 128:(tb + 1) * 128], pT.t[:, c, :], self.gs[i_].t[:, c:c + 1],
                            self.sh[i_].t[:, c:c + 1], ALU.mult, ALU.add, [pT, self.gs[i_], self.sh[i_]], [hT])
                    if c % 4 == 3:
                        yield

        def g_project(nt, hT, xr_t, want_g, sg):
            for c in range(8):
                p = nextP()
                for kc in range(8):
                    self.mm(p.t[:, 0:nt], W_in.t[:, kc, c * 128:(c + 1) * 128], hT.t[:, kc, 0:nt], kc == 0, kc == 7,
                            [Wxr, hT], [p])
                yield
                self.cp(DVE, xr_t.t[:, c, 2:2 + nt], p.t[:, 0:nt], [p], [xr_t.bs[c]])
                yield
            if want_g:
                for c in range(8):
                    p = nextP()
                    for kc in range(8):
                        self.mm(p.t[:, 0:nt], W_in.t[:, kc, D + c * 128:D + (c + 1) * 128], hT.t[:, kc, 0:nt], kc == 0, kc == 7,
                                [Wgg, hT], [p])
                    yield
                    w_ = gth[c % 2]
                    self.act(w_.t[:, 0:nt], p.t[:, 0:nt], AF.Tanh, [p], [w_], scale=0.5)
                    yield
                    self.stt(DVE, sg.t[:, c, 0:nt], w_.t[:, 0:nt], 1.0, p.t[:, 0:nt], ALU.add, ALU.mult, [w_, p], [sg.bs[c]])
                    yield

        def g_conv(nt, xr_t, z, zb):
            for c in range(8):
                pz = nextG()
                for j in range(4):
                    self.mm(pz.t[:, 0:nt], DG.t[:, j, c, :], xr_t.t[:, c, j:j + nt], j == 0, j == 3, [DG, xr_t.bs[c]], [pz])
                yield
                self.act(zb.t[:, c, 0:nt], pz.t[:, 0:nt], AF.Identity, [pz, self.COLS], [zb.bs[c]],
                         bias=self.COLS.t[:, cb0 + c:cb0 + c + 1])
                yield
                self.ts(DVE, z.t[:, c, 0:nt], pz.t[:, 0:nt], self.COLS.t[:, cb0 + c:cb0 + c + 1], None, ALU.add, None,
                        [pz, self.COLS], [z.bs[c]])
                yield

        def g_gates_scan(nt, d, use_carry, z, zb, Sout, a2_eng, mid=None):
            for jc in range(8):
                hh = jc // 2
                jl = jc % 2
                pa = nextG()
                for ic in range(2):
                    self.mm(pa.t[:, 0:nt], W_a.t[:, d, hh, ic, jl * 128:(jl + 1) * 128], zb.t[:, 2 * hh + ic, 0:nt],
                            ic == 0, ic == 1, [W_a, zb.bs[2 * hh + ic]], [pa])
                pi_ = nextG()
                for ic in range(2):
                    self.mm(pi_.t[:, 0:nt], W_i.t[:, d, hh, ic, jl * 128:(jl + 1) * 128], zb.t[:, 2 * hh + ic, 0:nt],
                            ic == 0, ic == 1, [W_i, zb.bs[2 * hh + ic]], [pi_])
                yield
                t_ = thr[jc % 2]
                self.act(t_.t[:, 0:nt], pa.t[:, 0:nt], AF.Tanh, [pa, hba], [t_], scale=0.5, bias=hba.t[:, d, jc:jc + 1])
                self.act(A.t[:, jc, 0:nt], t_.t[:, 0:nt], AF.Exp, [t_, coefh], [A.bs[jc]], scale=coefh.t[:, d, jc:jc + 1],
                         bias=coefh.t[:, d, jc:jc + 1])
                yield
                self.act(TI.t[:, jc, 0:nt], pi_.t[:, 0:nt], AF.Tanh, [pi_, hbi], [TI.bs[jc]], scale=0.5,
                         bias=hbi.t[:, d, jc:jc + 1])
                if a2_eng == ACT:
                    self.act(Wq.t[:, jc, 0:nt], A.t[:, jc, 0:nt], AF.Square, [A.bs[jc]], [Wq.bs[jc]])
                else:
                    self.tt(POOL, Wq.t[:, jc, 0:nt], A.t[:, jc, 0:nt], A.t[:, jc, 0:nt], ALU.mult, [A.bs[jc]], [Wq.bs[jc]])
                yield
            if mid is not None:
                yield from mid()
            for jc in range(8):
                self.act(Wq.t[:, jc, 0:nt], Wq.t[:, jc, 0:nt], AF.Sqrt, [Wq.bs[jc]], [Wq.bs[jc]], scale=-0.25, bias=0.25)
                if jc % 2 == 1:
                    yield
            for jc in range(8):
                self.tt(POOL, Wq.t[:, jc, 0:nt], Wq.t[:, jc, 0:nt], z.t[:, jc, 0:nt], ALU.mult, [Wq.bs[jc], z.bs[jc]], [Wq.bs[jc]])
                if jc % 2 == 1:
                    yield
            for jc in range(8):
                self.stt(DVE, TI.t[:, jc, 0:nt], TI.t[:, jc, 0:nt], 1.0, Wq.t[:, jc, 0:nt], ALU.add, ALU.mult,
                         [TI.bs[jc], Wq.bs[jc]], [TI.bs[jc]])
                if use_carry:
                    ini = carry[d].t[:, jc:jc + 1]
                    ideps = [carry[d].bs[jc]]
                else:
                    ini = 0.0
                    ideps = []
                if d == 0:
                    o_, a_, b_ = Sout.t[:, jc, 0:nt], A.t[:, jc, 0:nt], TI.t[:, jc, 0:nt]
                else:
                    o_, a_, b_ = Sout.t[:, jc, 0:nt][:, ::-1], A.t[:, jc, 0:nt][:, ::-1], TI.t[:, jc, 0:nt][:, ::-1]
                self.P.op(DVE, lambda e, o_=o_, a_=a_, b_=b_, ini=ini: e.tensor_tensor_scan(
                    out=o_, data0=a_, data1=b_, initial=ini, op0=ALU.mult, op1=ALU.add),
                    [A.bs[jc], TI.bs[jc]] + ideps, [Sout.bs[jc]], dur=0.2 + nt * (0.0022 if d == 0 else 0.0045))
                col = nt - 1 if d == 0 else 0
                self.cp(DVE, carry[d].t[:, jc:jc + 1], Sout.t[:, jc, col:col + 1], [Sout.bs[jc]], [carry[d].bs[jc]])
                yield

        def g_combine(nt, S0, sg, ymT):
            for c in range(0, 8, 2):
                self.tt(POOL, S0.t[:, c:c + 2, 0:nt], S0.t[:, c:c + 2, 0:nt], S1.t[:, c:c + 2, 0:nt], ALU.add,
                        S0.bs[c:c + 2] + S1.bs[c:c + 2], S0.bs[c:c + 2])
            yield
            self.stt(DVE, ymT.t[:, :, 0:nt], S0.t[:, :, 0:nt], 0.5, sg.t[:, :, 0:nt], ALU.mult, ALU.mult, S0.bs + sg.bs, [ymT])
            yield

        def g_out(tb, blk, gate_t, xt, ymT, final):
            for n in range(2):
                py = nextP()
                for kc in range(8):
                    self.mm(py.t[:], ymT.t[:, kc, tb * 128:(tb + 1) * 128], W_out.t[:, kc, n * 512:(n + 1) * 512],
                            kc == 0, kc == 7, [ymT, W_out], [py])
                yield
                if gate_t is None:
                    self.tt(DVE, xt.t[:, n * 512:(n + 1) * 512], py.t[:], xt.t[:, n * 512:(n + 1) * 512], ALU.add, [py, xt], [xt])
                else:
                    self.tt(DVE, ytmp.t[:, n * 512:(n + 1) * 512], py.t[:], gate_t.t[:, n * 512:(n + 1) * 512], ALU.mult,
                            [py, gate_t], [ytmp])
                yield
            if gate_t is not None:
                self.tt(POOL, xt.t[:], ytmp.t[:], xt.t[:], ALU.add, [ytmp, xt], [xt])
                yield
            if final:
                self.act(ytmp.t[:], xt.t[:], AF.Square, [xt], [ytmp, ssq1], accum=ssq1.t[:, tb:tb + 1])
            else:
                self.act(ytmp.t[:], xt.t[:], AF.Square, [xt], [ytmp, ssq_next], accum=ssq_next.t[:, blk:blk + 1])
                self.dma(SP, self.x_dst(l, blk), xt.t[:], xt, r=[xt])
            yield

        def g_final(blk0, xts):
            self.ts(DVE, v1.t[:], ssq1.t[:], 1.0 / D, EPS, ALU.mult, ALU.add, [ssq1], [v1])
            self.act(hs1.t[:], v1.t[:], AF.Sqrt, [v1], [hs1])
            yield
            self.P.op(DVE, lambda e: e.reciprocal(out=r1_.t[:], in_=hs1.t[:]), [hs1], [r1_])
            self.memset(POOL, ssq1.t[:], 0.0, [ssq1])
            for tb, xt in enumerate(xts):
                self.stt(DVE, xt.t[:], xt.t[:], r1_.t[:, tb:tb + 1], fg.t[:], ALU.mult, ALU.mult, [xt, r1_, fg], [xt])
                blk = blk0 + tb
                self.dma(SP, self.out_d[(blk - NCB) * 128:(blk - NCB + 1) * 128, :], xt.t[:], xt, r=[xt])
                yield

        def run_streams(gens):
            active = list(gens)
            while active:
                for g_ in list(active):
                    try:
                        next(g_)
                    except StopIteration:
                        active.remove(g_)

        def seq(*gens):
            for g_ in gens:
                yield from g_

        z = zr[0]
        sg0 = sgr[0]
        S0 = S0r[0]
        with ExitStack() as p1:
            XR = [self.sbc(p1, "XR%d" % k, [128, 8, NT + 3], BF16, 8) for k in range(3)]
            nt = LC
            self.memset(POOL, XR[0].t[:, :, 0:2], 0.0, XR[0].bs)
            self.memset(POOL, XR[0].t[:, :, 2 + nt:3 + nt], 0.0, XR[0].bs)
            xts = [x_load(tb) for tb in range(NCB)]
            run_streams([seq(g_load_norm(0, NCB, True, hTr[0], xts), g_project(nt, hTr[0], XR[0], need_ctx, sg0),
                             g_conv(nt, XR[0], z, zbr[0]))])
            run_streams([g_gates_scan(nt, 0, False, z, zbr[0], S0, POOL)])
            run_streams([g_gates_scan(nt, 1, False, z, zbr[0], S1, POOL)])
            if need_ctx:
                run_streams([g_combine(nt, S0, sg0, hTr[1])])
                for tb in range(NCB):
                    run_streams([g_out(tb, tb, self.gate[1], xts[tb], hTr[1], False)])
            scale_wout()

            pre = {}

            def NS(it):
                yield from g_load_norm(NCB + it * 2, 2, False, hTr[it % 2], pre.pop(it))

            def PJ(it):
                cur = XR[it % 3]
                prv = XR[(it - 1) % 3]
                hT = hTr[it % 2]
                yield from g_project(NT, hT, cur, True, sg0)
                sgd = self.SG[:, it * NT:(it + 1) * NT].rearrange("(c p) t -> p c t", p=128)
                self.dma(SP, sgd, sg0.t[:], sg0.bs[0], r=sg0.bs, w=[SGb[it]])
                if it == 0:
                    self.memset(POOL, cur.t[:, :, 0:2], 0.0, cur.bs)
                else:
                    self.cp(POOL, cur.t[:, :, 0:2], prv.t[:, :, NT:NT + 2], prv.bs, cur.bs)
                    self.cp(POOL, prv.t[:, :, NT + 2:NT + 3], cur.t[:, :, 2:3], cur.bs, prv.bs)
                if it == NSB - 1:
                    self.memset(POOL, cur.t[:, :, NT + 2:NT + 3], 0.0, cur.bs)
                yield

            def FWD(sb_):
                xr_t = XR[sb_ % 3]
                z = zr[sb_ % 2]
                zb = zbr[sb_ % 2]
                yield from g_conv(NT, xr_t, z, zb)
                zd = self.ZS[:, sb_ * NT:(sb_ + 1) * NT].rearrange("(c p) t -> p c t", p=128)
                self.dma(SP, zd, z.t[:], z.bs[0], r=z.bs, w=[ZSb[sb_]])
                zbd = self.ZB[:, sb_ * NT:(sb_ + 1) * NT].rearrange("(c p) t -> p c t", p=128)
                self.dma(SP, zbd, zb.t[:], zb.bs[0], r=zb.bs, w=[ZBb[sb_]])
                yield
                yield from g_gates_scan(NT, 0, True, z, zb, S0, POOL)
                sfd = self.SF[:, sb_ * NT:(sb_ + 1) * NT].rearrange("(c p) t -> p c t", p=128)
                self.dma(SP, sfd, S0.t[:], S0.bs[0], r=S0.bs, w=[SFb[sb_]])
                yield

            pre[0] = [x_load(NCB), x_load(NCB + 1)]
            self.run_prop([(NS(0), 8)])
            for it in range(NSB + 2):
                if it + 1 < NSB:
                    pre[it + 1] = [x_load(NCB + (it + 1) * 2), x_load(NCB + (it + 1) * 2 + 1)]
                items = []
                if it - 2 >= 0:
                    items.append((FWD(it - 2), 66))
                if it < NSB:
                    items.append((PJ(it), 41))
                if it + 1 < NSB:
                    items.append((NS(it + 1), 8))
                self.run_prop(items)
            self.P.barrier()

        with ExitStack() as p2:
            sgr.append(self.sbc(p2, "sg1", [128, 8, NT], BF16, 8))
            S0r.append(self.sbc(p2, "S01", [128, 8, NT], F32, 8))

            def loads(sb_):
                k = sb_ % 2
                zd = self.ZS[:, sb_ * NT:(sb_ + 1) * NT].rearrange("(c p) t -> p c t", p=128)
                self.dma(SP, zr[k].t[:], zd, zr[k].bs[0], r=[ZSb[sb_]], w=zr[k].bs)
                sfd = self.SF[:, sb_ * NT:(sb_ + 1) * NT].rearrange("(c p) t -> p c t", p=128)
                self.dma(SP, S0r[k].t[:], sfd, S0r[k].bs[0], r=[SFb[sb_]], w=S0r[k].bs)
                sgd = self.SG[:, sb_ * NT:(sb_ + 1) * NT].rearrange("(c p) t -> p c t", p=128)
                self.dma(SP, sgr[k].t[:], sgd, sgr[k].bs[0], r=[SGb[sb_]], w=sgr[k].bs)
                zbd = self.ZB[:, sb_ * NT:(sb_ + 1) * NT].rearrange("(c p) t -> p c t", p=128)
                self.dma(SP, zbr[k].t[:], zbd, zbr[k].bs[0], r=[ZBb[sb_]], w=zbr[k].bs)

            def G(sb_):
                k = sb_ % 2
                if sb_ - 1 >= 0:
                    loads(sb_ - 1)
                yield
                yield from g_gates_scan(NT, 1, True, zr[k], zbr[k], S1, POOL)
                yield from g_combine(NT, S0r[k], sgr[k], hTr[k])

            def O(sb_):
                k = sb_ % 2
                blk0 = NCB + sb_ * 2
                xts = [x_load(blk0 + tb) for tb in range(2)]
                yield
                for tb in range(2):
                    yield from g_out(tb, blk0 + tb, None, xts[tb], hTr[k], last)
                if last:
                    yield from g_final(blk0, xts)

            self.memset(POOL, ssq1.t[:], 0.0, [ssq1])
            loads(NSB - 1)
            for sb_ in range(NSB - 1, -2, -1):
                items = []
                if sb_ >= 0:
                    g_ = G(sb_)
                    for _ in range(25):
                        next(g_)
                    items.append((g_, 22))
                if sb_ + 1 < NSB:
                    items.append((O(sb_ + 1), 15))
                self.run_prop(items)
            self.P.barrier()


_CACHE = {}


def make_in_maps(inp, n_cores=8):
    f = lambda a: np.ascontiguousarray(np.asarray(a, np.float32))
    rows = np.zeros((128, RM.n), np.float32)
    for l in range(DEPTH):
        rows[:, RM["bg%d" % l]:RM["bg%d" % l] + D] = f(inp["b_mod"])[l, 2 * D:3 * D][None, :]
    for i in range(2):
        rows[:, RM["lng%d" % i]:RM["lng%d" % i] + 512] = f(inp["a_ln_g"])[i][None, :]
        rows[:, RM["lnb%d" % i]:RM["lnb%d" % i] + 512] = f(inp["a_ln_b"])[i][None, :]
        rows[:, RM["sink%d" % i]:RM["sink%d" % i] + 8] = f(inp["b_sink"])[i][None, :]
    rows[:, RM["fg"]:RM["fg"] + D] = f(inp["final_g"])[None, :]
    rope = rope_table()
    a_w_sT = np.ascontiguousarray(np.transpose(f(inp["a_w_s"]), (0, 3, 1, 2)))
    shared = {
        "rows": rows, "rope": rope, "w_mod": f(inp["w_mod"]), "ab_w_in": f(inp["ab_w_in"]),
        "ab_w_out": f(inp["ab_w_out"]), "a_w_sT": a_w_sT, "c_w_in": f(inp["c_w_in"]),
        "c_w_a": f(inp["c_w_a"]), "c_w_i": f(inp["c_w_i"]), "c_w_out": f(inp["c_w_out"]),
    }
    maps = []
    for b in range(n_cores):
        cols = np.zeros((128, CM.n), np.float32)
        for l in range(DEPTH):
            cols[:, CM["ng%d" % l]:CM["ng%d" % l] + 8] = col8(inp["norm_g"][l])
            cols[:, CM["bsh%d" % l]:CM["bsh%d" % l] + 8] = col8(inp["b_mod"][l, 0:D])
            cols[:, CM["bsc%d" % l]:CM["bsc%d" % l] + 8] = col8(inp["b_mod"][l, D:2 * D])
        cols[:, CM["c"]:CM["c"] + 8] = col8(inp["c"][b])
        cols[:, CM["cctx"]:CM["cctx"] + 8] = col8(inp["c_ctx"])
        for i in range(2):
            for j in range(4):
                cols[:, CM["cw%d_%d" % (i, j)]:CM["cw%d_%d" % (i, j)] + 8] = col8(inp["c_conv_w"][i, j])
            cols[:, CM["cb%d" % i]:CM["cb%d" % i] + 8] = col8(inp["c_conv_b"][i])
            for d in range(2):
                cols[:, CM["ba%d_%d" % (i, d)]:CM["ba%d_%d" % (i, d)] + 8] = col8(inp["c_b_a"][i, d])
                cols[:, CM["bi%d_%d" % (i, d)]:CM["bi%d_%d" % (i, d)] + 8] = col8(inp["c_b_i"][i, d])
                cols[:, CM["lam%d_%d" % (i, d)]:CM["lam%d_%d" % (i, d)] + 8] = col8(inp["c_lam"][i, d])
            cols[:, CM["bs%d" % i]:CM["bs%d" % i] + 4] = np.asarray(inp["a_b_s"][i], np.float32).T
        m = dict(shared)
        m["cols"] = cols
        m["x"] = f(inp["x"][b])
        m["ctx"] = f(inp["ctx"][b])
        maps.append(m)
    return maps


def kernel(**inputs):
    if "nc" not in _CACHE:
        _CACHE["nc"] = Builder().build()
    nc = _CACHE["nc"]
    maps = make_in_maps(inputs, 8)
    res = run_bass_kernel_spmd(nc, maps, core_ids=list(range(8)))
    out = np.stack([np.asarray(r["out"], np.float32) for r in res.results], axis=0)
    return out
```

```python
import math
from contextlib import ExitStack

import numpy as np
import concourse.bass as bass
import concourse.mybir as mybir
from concourse.bass_utils import run_bass_kernel_spmd

F32 = mybir.dt.float32
BF16 = mybir.dt.bfloat16
ALU = mybir.AluOpType
AF = mybir.ActivationFunctionType
AX = mybir.AxisListType

D = 1024
L = 4096
LC = 256
NLB = L // 128
NCB = LC // 128
NBLK = NLB + NCB
DEPTH = 4
EPS = 1e-6
AB_IN = 2816
GELU_C = 0.7978845608028654
SQ_044 = math.sqrt(0.044715)

PE, ACT, DVE, POOL, SP = "tensor", "scalar", "vector", "gpsimd", "sync"
ENGS = (PE, ACT, DVE, POOL, SP)


class Buf:
    __slots__ = ("name", "writer", "readers", "sem", "excl")

    def __init__(self, name, excl=False):
        self.name = name
        self.writer = None
        self.readers = {}
        self.sem = None
        self.excl = excl


class Op:
    __slots__ = ("eng", "fn", "deps", "odeps", "signal", "sig", "isdma", "key", "waits", "idx", "dur", "lat", "seg",
                 "nin", "succ", "rt", "fin", "tbl", "bl")

    def __init__(self, eng, fn, isdma=False):
        self.eng = eng
        self.fn = fn
        self.deps = []
        self.odeps = []
        self.signal = False
        self.sig = None
        self.isdma = isdma
        self.key = eng
        self.waits = None
        self.idx = 0
        self.dur = 0.3
        self.lat = 0.0
        self.seg = 0
        self.nin = 0
        self.succ = None
        self.rt = 0.0
        self.fin = 0.0
        self.tbl = None
        self.bl = 0.0


def _b(x):
    return x if isinstance(x, Buf) else x.b


class Tile:
    __slots__ = ("t", "b", "bs")

    def __init__(self, t, b, bs=None):
        self.t = t
        self.b = b
        self.bs = bs


class Prog:
    SCHED = True
    SEM_LAT = 1.0

    def __init__(self, nc, stack):
        self.nc = nc
        self.stack = stack
        self.ops = {e: [] for e in ENGS}
        self.all_ops = []
        self.esem = {e: stack.enter_context(nc.semaphore("es_" + e)) for e in ENGS}
        self.sem_pool = []
        self.sem_live = []
        self.nsem = 0
        self.dma_since_barrier = []
        self.all_dma_sigs = {}
        self.ndma = 0
        self.seg = 0
        self.seg_deps = {0: []}

    def _deps(self, op, reads, writes):
        deps = op.deps
        odeps = op.odeps
        for x in reads:
            b = _b(x)
            w = b.writer
            if w is not None:
                deps.append(w)
            if b.excl:
                for r in b.readers.values():
                    if r.eng != op.eng:
                        deps.append(r)
        for x in writes:
            b = _b(x)
            w = b.writer
            if w is not None:
                if w.isdma or w.eng != op.eng or op.isdma:
                    deps.append(w)
                else:
                    odeps.append(w)
            for r in b.readers.values():
                if r.isdma or r.eng != op.eng or op.isdma:
                    deps.append(r)
                else:
                    odeps.append(r)
        for d in deps:
            d.signal = True
        for x in reads:
            _b(x).readers[id(op)] = op
        for x in writes:
            b = _b(x)
            b.writer = op
            b.readers = {}

    def _add(self, o):
        o.idx = len(self.all_ops)
        o.seg = self.seg
        self.all_ops.append(o)

    def op(self, eng, fn, reads=(), writes=(), dur=0.3, tbl=None):
        o = Op(eng, fn)
        o.dur = dur
        o.tbl = tbl
        self._deps(o, reads, writes)
        self._add(o)
        return o

    def dma(self, eng, out, in_, sbuf, reads=(), writes=(), **kw):
        sbuf = _b(sbuf)
        if sbuf.sem is None:
            if self.sem_pool:
                ent = self.sem_pool.pop()
            else:
                ent = [self.stack.enter_context(self.nc.semaphore("ds%d" % self.nsem)), 0, None]
                self.nsem += 1
            sbuf.sem = ent
            self.sem_live.append(sbuf)
        ent = sbuf.sem
        o = Op(eng, lambda e: e.dma_start(out=out, in_=in_, **kw), isdma=True)
        self.ndma += 1
        o.key = ("dma", self.ndma)
        try:
            nbytes = float(out.nbytes())
        except Exception:
            nbytes = 65536.0
        o.dur = 0.6 if eng == POOL else 0.1
        o.lat = 2.0 + nbytes / 120e3
        self._deps(o, reads, writes)
        if ent[2] is not None and ent[2].seg == self.seg:
            o.odeps.append(ent[2])
        ent[2] = o
        ent[1] += 16
        o.sig = (ent[0], ent[1])
        o.signal = True
        self._add(o)
        self.dma_since_barrier.append(o)
        self.all_dma_sigs[id(ent[0])] = o.sig
        return o

    def barrier(self):
        lasts = {}
        for o in self.all_ops:
            if not o.isdma:
                lasts[o.eng] = o
        self._last_hint = lasts
        self.seg += 1
        self.seg_deps[self.seg] = ("BAR", list(self.dma_since_barrier))
        self.dma_since_barrier = []
        for b in self.sem_live:
            self.sem_pool.append(b.sem)
            b.sem = None
        self.sem_live = []

    def _schedule_segment(self, ops):
        seg = ops[0].seg
        for o in ops:
            o.nin = 0
            o.succ = []
            o.rt = 0.0
        for o in ops:
            seen = set()
            for d in o.deps + o.odeps:
                if d.seg == seg and id(d) not in seen:
                    seen.add(id(d))
                    d.succ.append(o)
                    o.nin += 1
        for o in reversed(ops):
            m = 0.0
            for s_ in o.succ:
                v = s_.bl + (self.SEM_LAT if s_.eng != o.eng else 0.0)
                if v > m:
                    m = v
            o.bl = o.dur + o.lat + m
        BLE = ('tensor', 'vector', 'gpsimd', 'sync')
        cand = {e: [] for e in ENGS}
        for o in ops:
            if o.nin == 0:
                cand[o.eng].append(o)
        free = {e: 0.0 for e in ENGS}
        order = {e: [] for e in ENGS}
        left = len(ops)
        lat = self.SEM_LAT
        cur_tbl = getattr(self, "_cur_tbl", None)
        TSW = 1.3
        while left:
            best = None
            for e in ENGS:
                c = cand[e]
                if not c:
                    continue
                t = free[e]
                pick = None
                if e == ACT:
                    pk = None
                    for o in c:
                        st_ = o.rt if o.rt > t else t
                        if o.tbl is not None and o.tbl != cur_tbl:
                            st_ += TSW
                        k_ = (st_, o.idx)
                        if pk is None or k_ < pk:
                            pk = k_
                            pick = o
                    start = pk[0]
                else:
                    ubl = e in BLE
                    for o in c:
                        if o.rt <= t:
                            if pick is None or pick.rt > t or ((o.bl, -o.idx) > (pick.bl, -pick.idx) if ubl else o.idx < pick.idx):
                                pick = o
                        elif pick is None or (pick.rt > t and (o.rt < pick.rt or (o.rt == pick.rt and o.idx < pick.idx))):
                            pick = o
                    start = max(t, pick.rt)
                if best is None or start < best[0] or (start == best[0] and pick.idx < best[1].idx):
                    best = (start, pick)
            start, o = best
            e = o.eng
            if e == ACT and o.tbl is not None:
                cur_tbl = o.tbl
            cand[e].remove(o)
            order[e].append(o)
            free[e] = start + o.dur
            o.fin = start + o.dur + o.lat
            left -= 1
            for s_ in o.succ:
                r = o.fin + (lat if (s_.eng != e or o.isdma) else 0.0)
                if r > s_.rt:
                    s_.rt = r
                s_.nin -= 1
                if s_.nin == 0:
                    cand[s_.eng].append(s_)
        self._cur_tbl = cur_tbl
        self.sim_time = self.sim_time + max(free.values()) if hasattr(self, "sim_time") else max(free.values())
        return order

    def emit(self):
        nc = self.nc
        segs = {}
        for o in self.all_ops:
            segs.setdefault(o.seg, []).append(o)
        self.ops = {e: [] for e in ENGS}
        last_compute = {}
        for sg in sorted(segs):
            ops = segs[sg]
            if self.SCHED:
                order = self._schedule_segment(ops)
            else:
                order = {e: [o for o in ops if o.eng == e] for e in ENGS}
            info = self.seg_deps.get(sg)
            if info:
                extra = list(last_compute.values()) + info[1]
                for o in extra:
                    o.signal = True
                for e in ENGS:
                    if order[e]:
                        order[e][0].deps = order[e][0].deps + extra
            for e in ENGS:
                self.ops[e].extend(order[e])
                for o in order[e]:
                    if not o.isdma:
                        last_compute[e] = o
        for e in ENGS:
            c = 0
            for o in self.ops[e]:
                if o.isdma:
                    continue
                if o.signal:
                    c += 1
                    o.sig = (self.esem[e], c)
        fin_waits = list(self.all_dma_sigs.values())
        for e in ENGS:
            known = {}
            for o in self.ops[e]:
                need = {}
                for d in o.deps:
                    s, v = d.sig
                    k = id(s)
                    if known.get(k, 0) >= v:
                        continue
                    if k not in need or need[k][1] < v:
                        need[k] = (s, v)
                for k, (s, v) in need.items():
                    known[k] = v
                o.waits = list(need.values())
        self.stats = dict(
            nops={e: len(self.ops[e]) for e in ENGS},
            nwait={e: sum(len(o.waits) for o in self.ops[e]) for e in ENGS},
            nsig={e: sum(1 for o in self.ops[e] if o.signal) for e in ENGS},
            nsem=self.nsem, sim_us=getattr(self, "sim_time", 0.0),
        )

        def run(e, engobj):
            for o in self.ops[e]:
                for s, v in o.waits:
                    engobj.wait_ge(s, v)
                ins = o.fn(engobj)
                if o.isdma:
                    ins.then_inc(o.sig[0], 16)
                elif o.signal:
                    ins.then_inc(o.sig[0], 1)
            if e == SP:
                for s, v in fin_waits:
                    engobj.wait_ge(s, v)

        with nc.Block() as block:
            @block.tensor
            def _(t):
                run(PE, t)

            @block.scalar
            def _(t):
                run(ACT, t)

            @block.vector
            def _(t):
                run(DVE, t)

            @block.gpsimd
            def _(t):
                run(POOL, t)

            @block.sync
            def _(t):
                run(SP, t)


class ColMap:
    def __init__(self):
        self.off = {}
        self.n = 0

    def add(self, name, ncols):
        self.off[name] = self.n
        self.n += ncols

    def __getitem__(self, name):
        return self.off[name]


def build_colmap():
    cm = ColMap()
    for l in range(DEPTH):
        cm.add("ng%d" % l, 8)
        cm.add("bsh%d" % l, 8)
        cm.add("bsc%d" % l, 8)
    cm.add("c", 8)
    cm.add("cctx", 8)
    for i in range(2):
        for j in range(4):
            cm.add("cw%d_%d" % (i, j), 8)
        cm.add("cb%d" % i, 8)
        for d in range(2):
            cm.add("ba%d_%d" % (i, d), 8)
            cm.add("bi%d_%d" % (i, d), 8)
            cm.add("lam%d_%d" % (i, d), 8)
        cm.add("bs%d" % i, 4)
    return cm


def build_rowmap():
    rm = ColMap()
    for l in range(DEPTH):
        rm.add("bg%d" % l, 1024)
    for i in range(2):
        rm.add("lng%d" % i, 512)
        rm.add("lnb%d" % i, 512)
        rm.add("sink%d" % i, 8)
    rm.add("fg", 1024)
    return rm


CM = build_colmap()
RM = build_rowmap()


def col8(v):
    return np.ascontiguousarray(np.asarray(v, np.float32).reshape(8, 128).T)


def rope_table():
    t = np.arange(L)
    r = (t // 64).astype(np.float64)
    c = (t % 64).astype(np.float64)
    inv = (10000.0 ** (-np.arange(16, dtype=np.float32) / np.float32(16))).astype(np.float32).astype(np.float64)
    ang = np.concatenate([r[:, None] * inv, c[:, None] * inv], axis=-1).astype(np.float32)
    cos = np.cos(ang).astype(np.float32)
    sin = np.sin(ang).astype(np.float32)
    return np.ascontiguousarray(np.concatenate([cos, cos, -sin, sin], axis=-1).astype(np.float32))


class StopBuild(Exception):
    pass


class Builder:
    def __init__(self, layers=(0, 1, 2, 3), do_final=True, dbg=False, stop=None):
        self.stop = stop
        self.stopped = False
        self.cks = []
        self.layers = list(layers)
        self.do_final = do_final
        self.dbg = dbg
        self.nc = bass.Bass("TRN2", target_bir_lowering=False)
        self.uid = 0

    def ck(self, name):
        self.cks.append(name)
        if self.stop is not None and name == self.stop:
            self.stopped = True
        return self.stopped

    def dram_in(self, name, shape, dt=F32):
        return self.nc.dram_tensor(name, list(shape), dt, kind="ExternalInput").ap()

    def dram_out(self, name, shape, dt=F32):
        return self.nc.dram_tensor(name, list(shape), dt, kind="ExternalOutput").ap()

    def dram_tmp(self, name, shape, dt=F32):
        return self.nc.dram_tensor(name, list(shape), dt).ap()

    def sb(self, st, name, shape, dt=F32):
        self.uid += 1
        nm = "%s_%d" % (name, self.uid)
        t = st.enter_context(self.nc.sbuf_tensor(nm, list(shape), dt))
        return Tile(t, Buf(nm))

    def sbc(self, st, name, shape, dt, n):
        T = self.sb(st, name, shape, dt)
        T.bs = [Buf("%s_c%d" % (name, c)) for c in range(n)]
        return T

    def ps(self, st, name, shape, dt=F32):
        self.uid += 1
        nm = "%s_%d" % (name, self.uid)
        t = st.enter_context(self.nc.psum_tensor(nm, list(shape), dt))
        return Tile(t, Buf(nm, excl=True))

    @staticmethod
    def _fs(ap):
        n = 1
        for d in ap.shape[1:]:
            n *= int(d)
        return n

    def act(self, out, in_, func, r, w, scale=1.0, bias=0.0, accum=None):
        kw = {}
        if accum is not None:
            kw["accum_out"] = accum
        tbl = {AF.Exp: "exp", AF.Tanh: "exp", AF.Sqrt: "sqrt", AF.Ln: "ln"}.get(func)
        self.P.op(ACT, lambda e: e.activation(out=out, in_=in_, func=func, bias=bias, scale=scale, **kw), r, w,
                  dur=0.2 + self._fs(out) / 1300.0, tbl=tbl)

    def _vdur(self, eng, out, two_in):
        n = self._fs(out)
        if eng == POOL:
            return 0.25 + n * 0.0022
        return (0.15 + n / 800.0) if two_in else (0.10 + n / 1400.0)

    def tt(self, eng, out, in0, in1, op, r, w):
        self.P.op(eng, lambda e: e.tensor_tensor(out=out, in0=in0, in1=in1, op=op), r, w, dur=self._vdur(eng, out, True))

    def ts(self, eng, out, in0, s1, s2, op0, op1, r, w):
        d = self._vdur(eng, out, False)
        if op1 is None:
            self.P.op(eng, lambda e: e.tensor_scalar(out=out, in0=in0, scalar1=s1, scalar2=None, op0=op0), r, w, dur=d)
        else:
            self.P.op(eng, lambda e: e.tensor_scalar(out=out, in0=in0, scalar1=s1, scalar2=s2, op0=op0, op1=op1), r, w, dur=d)

    def stt(self, eng, out, in0, scalar, in1, op0, op1, r, w):
        assert eng == DVE, "scalar_tensor_tensor is DVE-only"
        self.P.op(eng, lambda e: e.scalar_tensor_tensor(out=out, in0=in0, scalar=scalar, in1=in1, op0=op0, op1=op1), r, w,
                  dur=self._vdur(eng, out, True))

    def cp(self, eng, out, in_, r, w):
        if eng == ACT:
            self.P.op(ACT, lambda e: e.activation(out=out, in_=in_, func=AF.Copy), r, w, dur=0.2 + self._fs(out) / 1300.0)
        else:
            self.P.op(eng, lambda e: e.tensor_copy(out=out, in_=in_), r, w, dur=self._vdur(eng, out, False))

    def mm(self, out, lhsT, rhs, start, stop, r, w):
        self.P.op(PE, lambda e: e.matmul(out, lhsT=lhsT, rhs=rhs, start=start, stop=stop), r, w,
                  dur=0.02 + self._fs(out) / 2200.0)

    def mmx(self, out, lhsT, rhs, start, stop, r, w):
        self.P.op(PE, lambda e: e.matmul(out, lhsT=lhsT, rhs=rhs, start=start, stop=stop, skip_group_check=True), r, w,
                  dur=0.02 + self._fs(out) / 2200.0)

    def tr(self, out, in_, ident, r, w):
        self.P.op(PE, lambda e: e.transpose(out=out, in_=in_, identity=ident), r, w, dur=0.07)

    def memset(self, eng, ap, val, w):
        self.P.op(eng, lambda e: e.memset(ap, val), (), w, dur=self._vdur(eng, ap, False))

    def dma(self, eng, out, in_, sbuf, r=(), w=(), **kw):
        self.P.dma(eng, out, in_, sbuf, r, w, **kw)

    def rsqrt(self, out_ap, out_t, v_ap, v_t, s, q, n):
        sa = s.t[:, 0:n]
        self.act(sa, v_ap, AF.Sqrt, [v_t], [s])
        self.P.op(DVE, lambda e: e.reciprocal(out=out_ap, in_=sa), [s], [out_t])

    @staticmethod
    def run_prop(items):
        st_ = [[g_, 0, float(n_)] for g_, n_ in items]
        while st_:
            st_.sort(key=lambda e_: (e_[1] + 1) / e_[2])
            e_ = st_[0]
            try:
                next(e_[0])
                e_[1] += 1
            except StopIteration:
                st_.remove(e_)

    def build(self):
        nc = self.nc
        with ExitStack() as top:
            self.P = Prog(nc, top)
            self.declare_dram()
            self.setup(top)
            self.ck("setup")
            for l in self.layers:
                if self.stopped:
                    break
                self.P.barrier()
                with ExitStack() as st:
                    self.prologue(st, l)
                    if not self.ck("prologue%d" % l):
                        if l % 2 == 0:
                            self.even_layer(st, l)
                        else:
                            self.odd_layer(st, l)
                    self.P.barrier()
            if self.dbg:
                lastl = self.layers[-1]
                src = self.X[(lastl + 1) % 2]
                db = Buf("dbgout")
                for k in range(0, L + LC, 512):
                    n = min(512, L + LC - k)
                    self.dma(SP, self.dbg_x[k:k + n, :], src[k:k + n, :], db)
            self.P.emit()
        return nc

    def declare_dram(self):
        self.x_in = self.dram_in("x", [L, D])
        self.ctx_in = self.dram_in("ctx", [LC, D])
        self.cols_d = self.dram_in("cols", [128, CM.n])
        self.rows_d = self.dram_in("rows", [128, RM.n])
        self.rope_d = self.dram_in("rope", [L, 128])
        self.w_mod = self.dram_in("w_mod", [DEPTH, D, 3 * D])
        self.ab_w_in = self.dram_in("ab_w_in", [2, D, AB_IN])
        self.ab_w_out = self.dram_in("ab_w_out", [2, D, D])
        self.a_w_sT = self.dram_in("a_w_sT", [2, 128, 4, 128])
        self.c_w_in = self.dram_in("c_w_in", [2, D, 2 * D])
        self.c_w_a = self.dram_in("c_w_a", [2, 2, 4, 256, 256])
        self.c_w_i = self.dram_in("c_w_i", [2, 2, 4, 256, 256])
        self.c_w_out = self.dram_in("c_w_out", [2, D, D])
        self.out_d = self.dram_out("out", [L, D])
        self.X = [self.dram_tmp("XA", [L + LC, D]), self.dram_tmp("XB", [L + LC, D])]
        self.ZS = self.dram_tmp("ZS", [D, L])
        self.SF = self.dram_tmp("SF", [D, L])
        self.SG = self.dram_tmp("SG", [D, L], BF16)
        self.ZB = self.dram_tmp("ZB", [D, L], BF16)
        if self.dbg:
            self.dbg_x = self.dram_out("dbg_x", [L + LC, D])

    def x_src(self, l, blk):
        if l == self.layers[0] and l == 0:
            if blk < NCB:
                return self.ctx_in[blk * 128:(blk + 1) * 128, :]
            return self.x_in[(blk - NCB) * 128:(blk - NCB + 1) * 128, :]
        if l == self.layers[0]:
            return self.dbg_xin[blk * 128:(blk + 1) * 128, :]
        return self.X[l % 2][blk * 128:(blk + 1) * 128, :]

    def x_dst(self, l, blk):
        return self.X[(l + 1) % 2][blk * 128:(blk + 1) * 128, :]

    def setup(self, st):
        nc = self.nc
        self.COLS = self.sb(st, "cols", [128, CM.n])
        self.dma(SP, self.COLS.t[:], self.cols_d, self.COLS, w=[self.COLS])
        self.ck("s_cols")
        self.identf = self.sb(st, "identf", [128, 128])
        self.identb = self.sb(st, "identb", [128, 128], BF16)
        self.memset(POOL, self.identf.t[:], 1.0, [self.identf])
        idf = self.identf
        self.P.op(POOL, lambda e: e.affine_select(out=idf.t[:], in_=idf.t[:], pattern=[[-1, 128]], compare_op=ALU.is_equal,
                                                 fill=0.0, base=0, channel_multiplier=1), [idf], [idf])
        self.cp(DVE, self.identb.t[:], self.identf.t[:], [self.identf], [self.identb])
        self.ck("s_ident")
        mtmp = self.sb(st, "mtmp", [128, 128])
        self.memset(POOL, mtmp.t[:], 1.0, [mtmp])
        self.P.op(POOL, lambda e: e.affine_select(out=mtmp.t[:], in_=mtmp.t[:], pattern=[[-1, 128]], compare_op=ALU.is_ge,
                                                 fill=0.0, base=0, channel_multiplier=1), [mtmp], [mtmp])
        mtmp2 = self.sb(st, "mtmp2", [128, 128])
        self.memset(POOL, mtmp2.t[:], 1.0, [mtmp2])
        self.P.op(POOL, lambda e: e.affine_select(out=mtmp2.t[:], in_=mtmp2.t[:], pattern=[[1, 128]], compare_op=ALU.is_ge,
                                                 fill=0.0, base=0, channel_multiplier=-1), [mtmp2], [mtmp2])
        self.mbprev = self.sb(st, "mbprev", [128, 4, 128], BF16)
        self.mbnext = self.sb(st, "mbnext", [128, 4, 128], BF16)
        for (src_, dst_) in ((mtmp, self.mbprev), (mtmp2, self.mbnext)):
            self.ts(DVE, src_.t[:], src_.t[:], -1.0, 30000.0, ALU.add, ALU.mult, [src_], [src_])
            self.cp(DVE, dst_.t[:], src_.t[:].unsqueeze(1).broadcast_to([128, 4, 128]), [src_], [dst_])
        self.ck("s_masks")
        cc = CM["c"]
        th = self.sb(st, "sc_th", [128, 16])
        sc = self.sb(st, "sc", [128, 16])
        self.act(th.t[:], self.COLS.t[:, cc:cc + 16], AF.Tanh, [self.COLS], [th], scale=0.5)
        self.stt(DVE, sc.t[:], th.t[:], 1.0, self.COLS.t[:, cc:cc + 16], ALU.add, ALU.mult, [th, self.COLS], [sc])
        self.ts(DVE, sc.t[:], sc.t[:], 0.5, None, ALU.mult, None, [sc], [sc])
        self.scT = self.sb(st, "scT", [128, 8, 2], BF16)
        self.cp(DVE, self.scT.t[:, :, 0], sc.t[:, 0:8], [sc], [self.scT])
        self.cp(DVE, self.scT.t[:, :, 1], sc.t[:, 8:16], [sc], [self.scT])
        self.screp = []
        for i in range(2):
            t = self.sb(st, "screp%d" % i, [128, 8, 128], BF16)
            self.cp(DVE, t.t[:], sc.t[:, 8 * i:8 * i + 8].unsqueeze(2).broadcast_to([128, 8, 128]), [sc], [t])
            self.screp.append(t)
        self.ck("s_silu")
        self.SSQ = [self.sb(st, "ssq0", [128, NBLK]), self.sb(st, "ssq1", [128, NBLK])]
        l0 = self.layers[0]
        self.memset(POOL, self.SSQ[l0 % 2].t[:], 0.0, [self.SSQ[l0 % 2]])
        if self.dbg and l0 != 0:
            self.dbg_xin = self.dram_in("dbg_xin", [L + LC, D])
        with ExitStack() as s2:
            xr = [self.sb(s2, "prex%d" % i, [128, D]) for i in range(6)]
            junk = self.sb(s2, "prej", [128, D])
            for blk in range(NBLK):
                xt = xr[blk % 6]
                self.dma(SP, xt.t[:], self.x_src(l0, blk), xt, w=[xt])
                self.act(junk.t[:], xt.t[:], AF.Square, [xt], [junk, self.SSQ[l0 % 2]],
                         accum=self.SSQ[l0 % 2].t[:, blk:blk + 1])
            self.P.barrier()

    def prologue(self, st, l):
        P = self.P
        self.gs = [self.sb(st, "gs%d" % i, [128, 8]) for i in range(2)]
        self.sh = [self.sb(st, "sh%d" % i, [128, 8]) for i in range(2)]
        self.gate = [self.sb(st, "gate%d" % i, [128, D]) for i in range(2)]
        self.rstd = self.sb(st, "rstd", [128, NBLK])
        with ExitStack() as s2:
            bg = self.sb(s2, "bg", [128, D])
            self.dma(SP, bg.t[:], self.rows_d[:, RM["bg%d" % l]:RM["bg%d" % l] + D], bg, w=[bg])
            wm = [self.sb(s2, "wm%d" % i, [128, 8, 512], BF16) for i in range(6)]
            pcols_full = self.ps(s2, "pcols", [128, 512])
            pcols = Tile(pcols_full.t[:, 0:32].rearrange("p (n t) -> p n t", t=2), pcols_full.b)
            pg = [self.ps(s2, "pg%d" % i, [128, 512]) for i in range(2)]
            wsrc = self.w_mod[l].rearrange("(kc p) n -> p kc n", p=128)
            for pi in range(6):
                w = wm[pi]
                self.dma(POOL, w.t[:], wsrc[:, :, pi * 512:(pi + 1) * 512], w, w=[w])
                if pi < 4:
                    for q in range(4):
                        nn = 4 * pi + q
                        for kc in range(8):
                            self.mm(pcols.t[:, nn, :], w.t[:, kc, q * 128:(q + 1) * 128], self.scT.t[:, kc, :],
                                    kc == 0, kc == 7, [w, self.scT], [pcols])
                else:
                    hh = pi - 4
                    for i in range(2):
                        for kc in range(8):
                            self.mm(pg[i].t[:], self.screp[i].t[:, kc, :], w.t[:, kc, :], kc == 0, kc == 7,
                                    [w, self.screp[i]], [pg[i]])
                        self.tt(DVE, self.gate[i].t[:, hh * 512:(hh + 1) * 512], pg[i].t[:], bg.t[:, hh * 512:(hh + 1) * 512],
                                ALU.add, [pg[i], bg], [self.gate[i]])
            modc = self.sb(s2, "modc", [128, 16, 2])
            self.cp(DVE, modc.t[:], pcols.t[:], [pcols], [modc])
            ng, bsh, bsc = CM["ng%d" % l], CM["bsh%d" % l], CM["bsc%d" % l]
            tmp = self.sb(s2, "modtmp", [128, 8])
            for i in range(2):
                self.tt(DVE, self.sh[i].t[:], modc.t[:, 0:8, i], self.COLS.t[:, bsh:bsh + 8], ALU.add,
                        [modc, self.COLS], [self.sh[i]])
                self.tt(DVE, tmp.t[:], modc.t[:, 8:16, i], self.COLS.t[:, bsc:bsc + 8], ALU.add, [modc, self.COLS], [tmp])
                self.stt(DVE, self.gs[i].t[:], tmp.t[:], 1.0, self.COLS.t[:, ng:ng + 8], ALU.add, ALU.mult,
                         [tmp, self.COLS], [self.gs[i]])
            v = self.sb(s2, "nv", [128, NBLK])
            self.ts(DVE, v.t[:], self.SSQ[l % 2].t[:], 1.0 / D, EPS, ALU.mult, ALU.add, [self.SSQ[l % 2]], [v])
            hs = self.sb(s2, "hs", [128, NBLK])
            hq = self.sb(s2, "hq", [128, NBLK])
            self.rsqrt(self.rstd.t[:], self.rstd, v.t[:], v, hs, hq, NBLK)
            self.memset(POOL, self.SSQ[(l + 1) % 2].t[:], 0.0, [self.SSQ[(l + 1) % 2]])
            self.P.barrier()

    def norm_T(self, xt, blk, xn, psT, hT_ap, hT_tile, is_ctx):
        i = 1 if is_ctx else 0
        self.act(xn.t[:], xt.t[:], AF.Copy, [xt, self.rstd], [xn], scale=self.rstd.t[:, blk:blk + 1])
        for c in range(8):
            self.tr(psT.t[:, c, :], xn.t[:, c * 128:(c + 1) * 128], self.identb.t[:], [xn, self.identb], [psT])
        for c in range(8):
            self.ts(DVE, hT_ap[:, c, :], psT.t[:, c, :], self.gs[i].t[:, c:c + 1], self.sh[i].t[:, c:c + 1],
                    ALU.mult, ALU.add, [psT, self.gs[i], self.sh[i]], [hT_tile])

    def gelu2(self, out_t, ps_t, w_t, in_t, n):
        self.act(w_t.t[:, 0:n], ps_t.t[:, 0:n], AF.Square, [ps_t], [w_t], scale=SQ_044)
        self.stt(DVE, in_t.t[:, 0:n], w_t.t[:, 0:n], 1.0, ps_t.t[:, 0:n], ALU.add, ALU.mult, [w_t, ps_t], [in_t])
        self.act(w_t.t[:, 0:n], in_t.t[:, 0:n], AF.Tanh, [in_t], [w_t], scale=GELU_C)
        self.stt(DVE, out_t.t[:, 0:n], w_t.t[:, 0:n], 1.0, ps_t.t[:, 0:n], ALU.add, ALU.mult, [w_t, ps_t], [out_t])

    def silu2(self, out_ap, out_t, ps_ap, ps_t, w_ap, w_t):
        self.act(w_ap, ps_ap, AF.Tanh, [ps_t], [w_t], scale=0.5)
        self.stt(DVE, out_ap, w_ap, 1.0, ps_ap, ALU.add, ALU.mult, [w_t, ps_t], [out_t])

    def even_layer(self, st, l):
        i = l // 2
        W_in = self.sb(st, "Win", [128, 8, AB_IN], BF16)
        W_out = self.sb(st, "Wout", [128, 8, D], BF16)
        W_sT = self.sb(st, "WsT", [128, 4, 128], BF16)
        wsrc = self.ab_w_in[i].rearrange("(kc p) n -> p kc n", p=128)
        Wg = {}
        for (c0, n) in ((512, 512), (0, 512), (2304, 512), (1536, 512), (2048, 256), (1024, 512)):
            Wg[c0] = Buf("Win_%d" % c0)
            for kc in range(0, 8, 4):
                self.dma(POOL, W_in.t[:, kc:kc + 4, c0:c0 + n], wsrc[:, kc:kc + 4, c0:c0 + n], Wg[c0], w=[Wg[c0]])
        self.dma(POOL, W_sT.t[:], self.a_w_sT[i], W_sT, w=[W_sT])
        wosrc = self.ab_w_out[i].rearrange("(kc p) n -> p kc n", p=128)
        for kc in range(0, 8, 2):
            self.dma(POOL, W_out.t[:, kc:kc + 2, :], wosrc[:, kc:kc + 2, :], W_out, w=[W_out])
        lng = self.sb(st, "lng", [128, 512])
        lnb = self.sb(st, "lnb", [128, 512])
        sink = self.sb(st, "sink", [128, 8])
        esink = self.sb(st, "esink", [128, 8])
        self.dma(SP, lng.t[:], self.rows_d[:, RM["lng%d" % i]:RM["lng%d" % i] + 512], lng, w=[lng])
        self.dma(SP, lnb.t[:], self.rows_d[:, RM["lnb%d" % i]:RM["lnb%d" % i] + 512], lnb, w=[lnb])
        self.dma(SP, sink.t[:], self.rows_d[:, RM["sink%d" % i]:RM["sink%d" % i] + 8], sink, w=[sink])
        self.act(esink.t[:], sink.t[:], AF.Exp, [sink], [esink])
        bs0 = CM["bs%d" % i]

        KT = self.sb(st, "KT", [128, 2, NBLK, 128], BF16)
        KTb = [Buf("KT%d" % b) for b in range(NBLK)]
        Vb = self.sb(st, "Vb", [128, NBLK, 2, 128], BF16)
        Vbb = [Buf("Vb%d" % b) for b in range(NBLK)]
        self.memset(POOL, Vb.t[:, :, :, 64:128], 1.0, Vbb)

        ring = lambda name, shape, dt=F32, n=2: [self.sb(st, "%s%d" % (name, k), shape, dt) for k in range(n)]
        xblk = ring("xblk", [128, D], F32, 4)
        hT = ring("hT", [128, 8, 128], BF16, 2)
        ropet = ring("rope", [128, 128], F32, 2)
        qT = ring("qT", [128, 8, 128], BF16, 3)
        sgb2 = ring("sgb2", [128, 512], F32, 3)
        ymix = ring("ymix", [128, D], BF16, 3)
        xn = self.sb(st, "xn", [128, D], BF16)
        wv, iv = self.sb(st, "wv", [128, 512]), self.sb(st, "iv", [128, 512])
        wu, iu = self.sb(st, "wu", [128, 512]), self.sb(st, "iu", [128, 512])
        gu2 = self.sb(st, "gu2", [128, 512])
        sga2 = self.sb(st, "sga2", [128, 512])
        gv = self.sb(st, "gv", [128, 512])
        vln = self.sb(st, "vln", [128, 512], BF16)
        vtmp = self.sb(st, "vtmp", [128, 512])
        lnst = self.sb(st, "lnst", [128, 16])
        lnr = self.sb(st, "lnr", [128, 4])
        lnm = self.sb(st, "lnm", [128, 4])
        lhs_ = self.sb(st, "lhs", [128, 4])
        wg = self.sb(st, "wg", [128, 512])
        r1 = self.sb(st, "r1", [128, 640])
        r2 = self.sb(st, "r2", [128, 640])
        qz = self.sb(st, "qz", [128, 8, 128], BF16)
        self.memset(POOL, qz.t[:], 0.0, [qz])
        kdup = self.sb(st, "kdup", [128, 2, 2, 64], BF16)
        PT = [[self.sb(st, "PT%d_%d" % (kh, k), [128, 512], BF16) for k in range(5)] for kh in range(2)]
        den = self.sb(st, "den", [128, 8])
        rden = self.sb(st, "rden", [128, 8])
        ybt = self.sb(st, "ybt", [128, 512])
        ymixT = self.sb(st, "ymixT", [128, 8, 128], BF16)
        ytmp = self.sb(st, "ytmp", [128, D])

        psT = self.ps(st, "psT", [128, 8, 128], BF16)
        psTb = self.ps(st, "psTb", [128, 8, 128], BF16)
        psA = [self.ps(st, "psA%d" % k, [128, 512]) for k in range(2)]
        psQ = self.ps(st, "psQ", [128, 512])
        psS = [self.ps(st, "psS%d" % k, [128, 512]) for k in range(2)]
        psV = self.ps(st, "psV", [128, 4, 128])
        psY = Tile(psV.t[:].rearrange("p h d -> p (h d)"), psV.b)

        ssq_next = self.SSQ[(l + 1) % 2]

        def proj(ps, hslot, c0, n):
            for kc in range(8):
                self.mm(ps.t[:, 0:n], hT[hslot].t[:, kc, :], W_in.t[:, kc, c0:c0 + n], kc == 0, kc == 7,
                        [hT[hslot], Wg[c0]], [ps])

        def streamN(blk):
            is_ctx = blk < NCB
            mi = 1 if is_ctx else 0
            xt = xblk[blk % 4]
            self.dma(SP, xt.t[:], self.x_src(l, blk), xt, w=[xt])
            yield
            self.act(xn.t[:], xt.t[:], AF.Copy, [xt, self.rstd], [xn], scale=self.rstd.t[:, blk:blk + 1])
            yield
            for c in range(8):
                self.tr(psT.t[:, c, :], xn.t[:, c * 128:(c + 1) * 128], self.identb.t[:], [xn, self.identb], [psT])
            yield
            h = hT[blk % 2]
            for c in range(8):
                self.ts(DVE, h.t[:, c, :], psT.t[:, c, :], self.gs[mi].t[:, c:c + 1], self.sh[mi].t[:, c:c + 1],
                        ALU.mult, ALU.add, [psT, self.gs[mi], self.sh[mi]], [h])
                if c % 4 == 3:
                    yield

        def gelu2_gen(out_t, ps_t, w_t, in_t):
            self.act(w_t.t[:], ps_t.t[:], AF.Square, [ps_t], [w_t], scale=SQ_044)
            yield
            self.stt(DVE, in_t.t[:], w_t.t[:], 1.0, ps_t.t[:], ALU.add, ALU.mult, [w_t, ps_t], [in_t])
            yield
            self.act(w_t.t[:], in_t.t[:], AF.Tanh, [in_t], [w_t], scale=GELU_C)
            yield
            self.stt(DVE, out_t.t[:], w_t.t[:], 1.0, ps_t.t[:], ALU.add, ALU.mult, [w_t, ps_t], [out_t])
            yield

        def streamS1(blk):
            hs = blk % 2
            ym = ymix[blk % 3]
            proj(psA[0], hs, 512, 512)
            yield
            proj(psA[1], hs, 0, 512)
            yield
            yield from gelu2_gen(gv, psA[0], wv, iv)
            self.memset(POOL, lnst.t[:, 4:8], 0.0, [lnst])
            gv3 = gv.t[:].rearrange("p (g d) -> p g d", g=4)
            self.P.op(DVE, lambda e: e.tensor_reduce(out=lnst.t[:, 0:4], in_=gv3, axis=AX.X, op=ALU.add), [gv], [lnst])
            yield
            for g in range(4):
                self.act(vtmp.t[:, g * 128:(g + 1) * 128], gv.t[:, g * 128:(g + 1) * 128], AF.Square, [gv], [vtmp, lnst],
                         accum=lnst.t[:, 4 + g:5 + g])
            yield
            proj(psA[0], hs, 1024, 512)
            yield
            self.ts(DVE, lnst.t[:, 8:12], lnst.t[:, 0:4], 1.0 / 128, None, ALU.mult, None, [lnst], [lnst])
            self.tt(DVE, lnst.t[:, 12:16], lnst.t[:, 8:12], lnst.t[:, 8:12], ALU.mult, [lnst], [lnst])
            self.stt(DVE, lnst.t[:, 12:16], lnst.t[:, 4:8], 1.0 / 128, lnst.t[:, 12:16], ALU.mult, ALU.subtract,
                     [lnst], [lnst])
            self.ts(DVE, lnst.t[:, 12:16], lnst.t[:, 12:16], 4.0 * EPS, None, ALU.add, None, [lnst], [lnst])
            yield
            self.act(lhs_.t[:], lnst.t[:, 12:16], AF.Sqrt, [lnst], [lhs_])
            yield
            self.P.op(DVE, lambda e: e.reciprocal(out=lnr.t[:], in_=lhs_.t[:]), [lhs_], [lnr])
            self.stt(DVE, lnm.t[:], lnst.t[:, 8:12], -1.0, lnr.t[:], ALU.mult, ALU.mult, [lnst, lnr], [lnm])
            yield
            yield from gelu2_gen(gu2, psA[1], wu, iu)
            for g in range(4):
                self.act(vtmp.t[:, g * 128:(g + 1) * 128], gv.t[:, g * 128:(g + 1) * 128], AF.Identity, [gv, lnr, lnm], [vtmp],
                         scale=lnr.t[:, g:g + 1], bias=lnm.t[:, g:g + 1])
            yield
            self.tt(DVE, vtmp.t[:], vtmp.t[:], lng.t[:], ALU.mult, [vtmp, lng], [vtmp])
            yield
            self.tt(DVE, vln.t[:], vtmp.t[:], lnb.t[:], ALU.add, [vtmp, lnb], [vln])
            yield
            self.act(wv.t[:], psA[0].t[:], AF.Tanh, [psA[0]], [wv], scale=0.5)
            yield
            self.stt(DVE, sga2.t[:], wv.t[:], 1.0, psA[0].t[:], ALU.add, ALU.mult, [wv, psA[0]], [sga2])
            yield
            for g in range(4):
                self.mm(psA[1].t[:, g * 128:(g + 1) * 128], W_sT.t[:, g, :], vln.t[:, g * 128:(g + 1) * 128], True, True,
                        [W_sT, vln], [psA[1]])
            yield
            self.stt(DVE, gu2.t[:], gu2.t[:], 0.25, sga2.t[:], ALU.mult, ALU.mult, [gu2, sga2], [gu2])
            yield
            for g in range(4):
                self.stt(DVE, ym.t[:, g * 128:(g + 1) * 128], psA[1].t[:, g * 128:(g + 1) * 128],
                         self.COLS.t[:, bs0 + g:bs0 + g + 1], gu2.t[:, g * 128:(g + 1) * 128], ALU.add, ALU.mult,
                         [psA[1], self.COLS, gu2], [ym])
                if g % 2 == 1:
                    yield

        def streamS2(blk):
            hs = blk % 2
            s3 = blk % 3
            is_ctx = blk < NCB
            rt = ropet[blk % 2]
            if not is_ctx:
                t0 = (blk - NCB) * 128
                self.dma(SP, rt.t[:], self.rope_d[t0:t0 + 128, :], rt, w=[rt])
            proj(psQ, hs, 2304, 512)
            yield
            self.act(wg.t[:], psQ.t[:], AF.Tanh, [psQ], [wg], scale=0.5)
            yield
            self.stt(DVE, sgb2[s3].t[:], wg.t[:], 1.0, psQ.t[:], ALU.add, ALU.mult, [wg, psQ], [sgb2[s3]])
            yield
            proj(psQ, hs, 1536, 512)
            yield
            if is_ctx:
                q3 = psQ.t[:, 0:512].rearrange("p (h d) -> p h d", h=8)
                self.cp(ACT, qz.t[:, 0::2, 0:64], q3[:, 0::2, :], [psQ], [qz])
                self.cp(ACT, qz.t[:, 1::2, 64:128], q3[:, 1::2, :], [psQ], [qz])
                yield
            else:
                src = psQ.t[:, 0:512].rearrange("p (h d) -> p h d", h=8)
                d1 = r1.t[:, 0:512].rearrange("p (h d) -> p h d", h=8)
                d2 = r2.t[:, 0:512].rearrange("p (h d) -> p h d", h=8)
                self.tt(DVE, d1, src, rt.t[:, 0:64].unsqueeze(1).broadcast_to([128, 8, 64]), ALU.mult, [psQ, rt], [r1])
                yield
                self.tt(DVE, d2[:, :, 0:32], src[:, :, 32:64], rt.t[:, 64:96].unsqueeze(1).broadcast_to([128, 8, 32]), ALU.mult,
                        [psQ, rt], [r2])
                self.tt(DVE, d2[:, :, 32:64], src[:, :, 0:32], rt.t[:, 96:128].unsqueeze(1).broadcast_to([128, 8, 32]), ALU.mult,
                        [psQ, rt], [r2])
                yield
                self.tt(POOL, qz.t[:, 0::2, 0:64], d1[:, 0::2, :], d2[:, 0::2, :], ALU.add, [r1, r2], [qz])
                self.tt(POOL, qz.t[:, 1::2, 64:128], d1[:, 1::2, :], d2[:, 1::2, :], ALU.add, [r1, r2], [qz])
                yield
            proj(psQ, hs, 2048, 256)
            yield
            self.cp(ACT, Vb.t[:, blk, :, 0:64], psQ.t[:, 128:256].rearrange("p (h d) -> p h d", h=2), [psQ], [Vbb[blk]])
            k3 = psQ.t[:, 0:128].rearrange("p (h d) -> p h d", h=2)
            if is_ctx:
                for dup in range(2):
                    self.cp(ACT, kdup.t[:, :, dup, :], k3, [psQ], [kdup])
                yield
            else:
                e1 = r1.t[:, 512:640].rearrange("p (h d) -> p h d", h=2)
                e2 = r2.t[:, 512:640].rearrange("p (h d) -> p h d", h=2)
                self.tt(DVE, e1, k3, rt.t[:, 0:64].unsqueeze(1).broadcast_to([128, 2, 64]), ALU.mult, [psQ, rt], [r1])
                self.tt(DVE, e2[:, :, 0:32], k3[:, :, 32:64], rt.t[:, 64:96].unsqueeze(1).broadcast_to([128, 2, 32]), ALU.mult,
                        [psQ, rt], [r2])
                self.tt(DVE, e2[:, :, 32:64], k3[:, :, 0:32], rt.t[:, 96:128].unsqueeze(1).broadcast_to([128, 2, 32]), ALU.mult,
                        [psQ, rt], [r2])
                yield
                for dup in range(2):
                    self.tt(POOL, kdup.t[:, :, dup, :], e1, e2, ALU.add, [r1, r2], [kdup])
                yield
            for h in range(8):
                self.tr(psTb.t[:, h, :], qz.t[:, h, :], self.identb.t[:], [qz, self.identb], [psTb])
            self.cp(ACT, qT[s3].t[:], psTb.t[:], [psTb], [qT[s3]])
            yield
            for kh in range(2):
                self.tr(psTb.t[:, kh, :], kdup.t[:, kh, :, :].rearrange("p a d -> p (a d)"), self.identb.t[:],
                        [kdup, self.identb], [psTb])
            self.cp(ACT, KT.t[:, :, blk, :], psTb.t[:, 0:2, :], [psTb], [KTb[blk]])
            yield

        def streamB(blk):
            s3 = blk % 3
            is_ctx = blk < NCB
            ym = ymix[s3]
            xt = xblk[blk % 4]
            if is_ctx:
                kbs = [(0, None), (1, None)]
            else:
                kbs = [(0, None), (1, None)]
                if blk - 1 >= NCB:
                    kbs.append((blk - 1, self.mbprev))
                kbs.append((blk, None))
                if blk + 1 < NBLK:
                    kbs.append((blk + 1, self.mbnext))
            nk = len(kbs)
            for kh in range(2):
                for ki, (kb, mb) in enumerate(kbs):
                    pss = psS[(kh * 5 + ki) % 2]
                    if mb is not None:
                        self.mmx(pss.t[:], self.identb.t[:], mb.t[:].rearrange("p h q -> p (h q)"), True, False,
                                 [self.identb, mb], [pss])
                    for hl in range(4):
                        h = 4 * kh + hl
                        if mb is not None:
                            self.mmx(pss.t[:, hl * 128:(hl + 1) * 128], KT.t[:, kh, kb, :], qT[s3].t[:, h, :], False, hl == 3,
                                     [KTb[kb], qT[s3]], [pss])
                        else:
                            self.mm(pss.t[:, hl * 128:(hl + 1) * 128], KT.t[:, kh, kb, :], qT[s3].t[:, h, :],
                                    True, True, [KTb[kb], qT[s3]], [pss])
                    yield
                    pt = PT[kh][ki]
                    self.act(pt.t[:], pss.t[:], AF.Exp, [pss], [pt], scale=0.125)
                    yield
                for hl in range(4):
                    for ki, (kb, mb) in enumerate(kbs):
                        self.mm(psV.t[:, hl, 0:65], PT[kh][ki].t[:, hl * 128:(hl + 1) * 128], Vb.t[:, kb, kh, 0:65],
                                ki == 0, ki == nk - 1, [PT[kh][ki], Vbb[kb]], [psV])
                    if hl % 2 == 1:
                        yield
                self.tt(DVE, den.t[:, 4 * kh:4 * kh + 4], psV.t[:, :, 64], esink.t[:, 4 * kh:4 * kh + 4], ALU.add,
                        [psV, esink], [den])
                self.P.op(DVE, lambda e, kh=kh: e.reciprocal(out=rden.t[:, 4 * kh:4 * kh + 4], in_=den.t[:, 4 * kh:4 * kh + 4]),
                          [den], [rden])
                yv = ybt.t[:, kh * 256:(kh + 1) * 256].rearrange("p (h d) -> p h d", h=4)
                self.stt(DVE, yv, psV.t[:, :, 0:64], 0.5, rden.t[:, 4 * kh:4 * kh + 4].unsqueeze(2).broadcast_to([128, 4, 64]),
                         ALU.mult, ALU.mult, [psV, rden], [ybt])
                yield
            self.tt(POOL, ym.t[:, 512:1024], ybt.t[:], sgb2[s3].t[:], ALU.mult, [ybt, sgb2[s3]], [ym])
            yield
            for c in range(8):
                self.tr(psTb.t[:, c, :], ym.t[:, c * 128:(c + 1) * 128], self.identb.t[:], [ym, self.identb], [psTb])
            self.cp(ACT, ymixT.t[:], psTb.t[:], [psTb], [ymixT])
            yield
            g = self.gate[1]
            for n in range(2):
                for kc in range(8):
                    self.mm(psY.t[:], ymixT.t[:, kc, :], W_out.t[:, kc, n * 512:(n + 1) * 512], kc == 0, kc == 7,
                            [ymixT, W_out], [psY])
                yield
                if is_ctx:
                    self.tt(DVE, ytmp.t[:, n * 512:(n + 1) * 512], psY.t[:], g.t[:, n * 512:(n + 1) * 512], ALU.mult, [psY, g], [ytmp])
                else:
                    self.tt(DVE, xt.t[:, n * 512:(n + 1) * 512], psY.t[:], xt.t[:, n * 512:(n + 1) * 512], ALU.add, [psY, xt], [xt])
                yield
            if is_ctx:
                self.tt(POOL, xt.t[:], ytmp.t[:], xt.t[:], ALU.add, [ytmp, xt], [xt])
                yield
            self.act(ytmp.t[:], xt.t[:], AF.Square, [xt], [ytmp, ssq_next], accum=ssq_next.t[:, blk:blk + 1])
            self.dma(SP, self.x_dst(l, blk), xt.t[:], xt, r=[xt])
            yield

        class _Dry:
            def op(self, *a, **k):
                pass

            def dma(self, *a, **k):
                pass

        def count_steps(gen_fn, blk):
            real = self.P
            self.P = _Dry()
            try:
                n = sum(1 for _ in gen_fn(blk))
            finally:
                self.P = real
            return n + 1

        def run_streams(items):
            st_ = [[g_, 0, float(n_)] for g_, n_ in items]
            while st_:
                st_.sort(key=lambda e_: (e_[1] + 1) / e_[2])
                e_ = st_[0]
                try:
                    next(e_[0])
                    e_[1] += 1
                except StopIteration:
                    st_.remove(e_)

        nsteps = {}

        def item(fn, blk):
            key = (fn.__name__, blk < NCB, blk == NCB, blk == NBLK - 1)
            if key not in nsteps:
                nsteps[key] = count_steps(fn, blk)
            return (fn(blk), nsteps[key])

        for t in range(NBLK + 3):
            items = []
            if t - 3 >= 0:
                items.append(item(streamB, t - 3))
            if 0 <= t - 1 < NBLK:
                items.append(item(streamS1, t - 1))
                items.append(item(streamS2, t - 1))
            if t < NBLK:
                items.append(item(streamN, t))
            run_streams(items)
            if t == NCB + 2:
                for kc in range(8):
                    self.tt(POOL if kc % 2 else DVE, W_out.t[:, kc, :], W_out.t[:, kc, :], self.gate[0].t[:], ALU.mult,
                            [W_out, self.gate[0]], [W_out])
            if self.ck("E%d" % t):
                return


    def odd_layer(self, st, l):
        i = l // 2
        need_ctx = l < DEPTH - 1
        last = (l == DEPTH - 1) and self.do_final
        NT = 256
        NSB = L // NT
        W_in = self.sb(st, "cWin", [128, 8, 2 * D], BF16)
        W_a = self.sb(st, "cWa", [128, 2, 4, 2, 256], BF16)
        W_i = self.sb(st, "cWi", [128, 2, 4, 2, 256], BF16)
        W_out = self.sb(st, "cWout", [128, 8, D], BF16)
        wsrc = self.c_w_in[i].rearrange("(kc p) n -> p kc n", p=128)
        Wxr, Wgg = Buf("cWin_xr"), Buf("cWin_g")
        for (c0, bb) in ((0, Wxr), (D, Wgg)):
            for kc in range(0, 8, 4):
                self.dma(POOL, W_in.t[:, kc:kc + 4, c0:c0 + D], wsrc[:, kc:kc + 4, c0:c0 + D], bb, w=[bb])
        for d in range(2):
            self.dma(POOL, W_a.t[:, d], self.c_w_a[i, d].rearrange("h (ic p) j -> p h ic j", p=128), W_a, w=[W_a])
            self.dma(POOL, W_i.t[:, d], self.c_w_i[i, d].rearrange("h (ic p) j -> p h ic j", p=128), W_i, w=[W_i])
        wosrc = self.c_w_out[i].rearrange("(kc p) n -> p kc n", p=128)
        for kc in range(0, 8, 2):
            self.dma(POOL, W_out.t[:, kc:kc + 2, :], wosrc[:, kc:kc + 2, :], W_out, w=[W_out])
        coefh = self.sb(st, "coefh", [128, 2, 8])
        hba = self.sb(st, "hba", [128, 2, 8])
        hbi = self.sb(st, "hbi", [128, 2, 8])
        for d in range(2):
            lam0 = CM["lam%d_%d" % (i, d)]
            self.act(coefh.t[:, d, :], self.COLS.t[:, lam0:lam0 + 8], AF.Exp, [self.COLS], [coefh], scale=-1.0)
            self.act(coefh.t[:, d, :], coefh.t[:, d, :], AF.Ln, [coefh], [coefh], bias=1.0)
            self.ts(DVE, coefh.t[:, d, :], coefh.t[:, d, :], -4.0, None, ALU.mult, None, [coefh], [coefh])
            b0 = CM["ba%d_%d" % (i, d)]
            self.ts(DVE, hba.t[:, d, :], self.COLS.t[:, b0:b0 + 8], 0.5, None, ALU.mult, None, [self.COLS], [hba])
            b0 = CM["bi%d_%d" % (i, d)]
            self.ts(DVE, hbi.t[:, d, :], self.COLS.t[:, b0:b0 + 8], 0.5, None, ALU.mult, None, [self.COLS], [hbi])
        cw0 = [CM["cw%d_%d" % (i, j)] for j in range(4)]
        cb0 = CM["cb%d" % i]
        DG = self.sb(st, "DG", [128, 4, 8, 128], BF16)
        for j in range(4):
            for c in range(8):
                self.ts(DVE, DG.t[:, j, c, :], self.identf.t[:], self.COLS.t[:, cw0[j] + c:cw0[j] + c + 1], None, ALU.mult, None,
                        [self.identf, self.COLS], [DG])
        if last:
            fg = self.sb(st, "fg", [128, D])
            self.dma(SP, fg.t[:], self.rows_d[:, RM["fg"]:RM["fg"] + D], fg, w=[fg])

        xb = [self.sb(st, "xb%d" % k, [128, D]) for k in range(4)]
        xbi = [0]
        xn = self.sb(st, "cxn", [128, D], BF16)
        hTr = [self.sb(st, "chT%d" % k, [128, 8, NT], BF16) for k in range(2)]
        zr = [self.sbc(st, "z0", [128, 8, NT], F32, 8), self.sbc(st, "z1", [128, 8, NT], F32, 8)]
        zbr = [self.sbc(st, "zb0", [128, 8, NT], BF16, 8), self.sbc(st, "zb1", [128, 8, NT], BF16, 8)]
        sgr = [self.sbc(st, "sg0", [128, 8, NT], BF16, 8)]
        gth_ = self.sb(st, "gth", [128, NT])
        gth = [gth_, gth_]
        A = self.sbc(st, "A", [128, 8, NT], F32, 8)
        Wq = self.sbc(st, "Wq", [128, 8, NT], F32, 8)
        TI = self.sbc(st, "TI", [128, 8, NT], F32, 8)
        thr = [self.sb(st, "thr%d" % k, [128, NT]) for k in range(2)]
        S0r = [self.sbc(st, "S00", [128, 8, NT], F32, 8)]
        S1 = self.sbc(st, "S1", [128, 8, NT], F32, 8)
        carry = [self.sbc(st, "carry%d" % k, [128, 8], F32, 8) for k in range(2)]
        ytmp = self.sb(st, "cytmp", [128, D])
        ssq1 = self.sb(st, "ssq1", [128, 2])
        v1 = self.sb(st, "v1", [128, 2])
        r1_ = self.sb(st, "r1", [128, 2])
        hs1 = self.sb(st, "hs1", [128, 2])
        psT = [self.ps(st, "cpsT%d" % k, [128, 8, 128], BF16) for k in range(2)]
        psP = [self.ps(st, "cpsP%d" % k, [128, 512]) for k in range(3)]
        psG = [self.ps(st, "cpsG%d" % k, [128, 512]) for k in range(3)]
        rot = [0, 0]

        def nextP():
            rot[0] += 1
            return psP[rot[0] % 3]

        def nextG():
            rot[1] += 1
            return psG[rot[1] % 3]

        ssq_next = self.SSQ[(l + 1) % 2]
        xbase = self.X[l % 2] if l != self.layers[0] else self.dbg_xin
        ZSb = [Buf("ZS%d" % k) for k in range(NSB)]
        SFb = [Buf("SF%d" % k) for k in range(NSB)]
        SGb = [Buf("SG%d" % k) for k in range(NSB)]
        ZBb = [Buf("ZB%d" % k) for k in range(NSB)]
        wscaled = [False]

        def scale_wout():
            for kc in range(8):
                self.tt(POOL if kc % 2 else DVE, W_out.t[:, kc, :], W_out.t[:, kc, :], self.gate[0].t[:], ALU.mult,
                        [W_out, self.gate[0]], [W_out])
            wscaled[0] = True

        def next_xb():
            xbi[0] += 1
            return xb[xbi[0] % 4]

        def x_load(blk):
            xt = next_xb()
            self.dma(SP, xt.t[:], xbase[blk * 128:(blk + 1) * 128, :], xt, w=[xt])
            return xt

        def g_load_norm(blk0, nb, is_ctx, hT, xts):
            i_ = 1 if is_ctx else 0
            for tb in range(nb):
                xt = xts[tb]
                self.ts(DVE, xn.t[:], xt.t[:], self.rstd.t[:, blk0 + tb:blk0 + tb + 1], None, ALU.mult, None,
                        [xt, self.rstd], [xn])
                yield
                pT = psT[tb % 2]
                for c in range(8):
                    self.tr(pT.t[:, c, :], xn.t[:, c * 128:(c + 1) * 128], self.identb.t[:], [xn, self.identb], [pT])
                yield
                for c in range(8):
                    self.ts(DVE, hT.t[:, c, tb * 128:(tb + 1) * 128], pT.t[:, c, :], self.gs[i_].t[:, c:c + 1],
                            self.sh[i_].t[:, c:c + 1], ALU.mult, ALU.add, [pT, self.gs[i_], self.sh[i_]], [hT])
                    if c % 4 == 3:
                        yield

        def g_project(nt, hT, xr_t, want_g, sg):
            for c in range(8):
                p = nextP()
                for kc in range(8):
                    self.mm(p.t[:, 0:nt], W_in.t[:, kc, c * 128:(c + 1) * 128], hT.t[:, kc, 0:nt], kc == 0, kc == 7,
                            [Wxr, hT], [p])
                yield
                self.cp(DVE, xr_t.t[:, c, 2:2 + nt], p.t[:, 0:nt], [p], [xr_t.bs[c]])
                yield
            if want_g:
                for c in range(8):
                    p = nextP()
                    for kc in range(8):
                        self.mm(p.t[:, 0:nt], W_in.t[:, kc, D + c * 128:D + (c + 1) * 128], hT.t[:, kc, 0:nt], kc == 0, kc == 7,
                                [Wgg, hT], [p])
                    yield
                    w_ = gth[c % 2]
                    self.act(w_.t[:, 0:nt], p.t[:, 0:nt], AF.Tanh, [p], [w_], scale=0.5)
                    yield
                    self.stt(DVE, sg.t[:, c, 0:nt], w_.t[:, 0:nt], 1.0, p.t[:, 0:nt], ALU.add, ALU.mult, [w_, p], [sg.bs[c]])
                    yield

        def g_conv(nt, xr_t, z, zb):
            for c in range(8):
                pz = nextG()
                for j in range(4):
                    self.mm(pz.t[:, 0:nt], DG.t[:, j, c, :], xr_t.t[:, c, j:j + nt], j == 0, j == 3, [DG, xr_t.bs[c]], [pz])
                yield
                self.act(zb.t[:, c, 0:nt], pz.t[:, 0:nt], AF.Identity, [pz, self.COLS], [zb.bs[c]],
                         bias=self.COLS.t[:, cb0 + c:cb0 + c + 1])
                yield
                self.ts(DVE, z.t[:, c, 0:nt], pz.t[:, 0:nt], self.COLS.t[:, cb0 + c:cb0 + c + 1], None, ALU.add, None,
                        [pz, self.COLS], [z.bs[c]])
                yield

        def g_gates_scan(nt, d, use_carry, z, zb, Sout, a2_eng, mid=None):
            for jc in range(8):
                hh = jc // 2
                jl = jc % 2
                pa = nextG()
                for ic in range(2):
                    self.mm(pa.t[:, 0:nt], W_a.t[:, d, hh, ic, jl * 128:(jl + 1) * 128], zb.t[:, 2 * hh + ic, 0:nt],
                            ic == 0, ic == 1, [W_a, zb.bs[2 * hh + ic]], [pa])
                pi_ = nextG()
                for ic in range(2):
                    self.mm(pi_.t[:, 0:nt], W_i.t[:, d, hh, ic, jl * 128:(jl + 1) * 128], zb.t[:, 2 * hh + ic, 0:nt],
                            ic == 0, ic == 1, [W_i, zb.bs[2 * hh + ic]], [pi_])
                yield
                t_ = thr[jc % 2]
                self.act(t_.t[:, 0:nt], pa.t[:, 0:nt], AF.Tanh, [pa, hba], [t_], scale=0.5, bias=hba.t[:, d, jc:jc + 1])
                self.act(A.t[:, jc, 0:nt], t_.t[:, 0:nt], AF.Exp, [t_, coefh], [A.bs[jc]], scale=coefh.t[:, d, jc:jc + 1],
                         bias=coefh.t[:, d, jc:jc + 1])
                yield
                self.act(TI.t[:, jc, 0:nt], pi_.t[:, 0:nt], AF.Tanh, [pi_, hbi], [TI.bs[jc]], scale=0.5,
                         bias=hbi.t[:, d, jc:jc + 1])
                if a2_eng == ACT:
                    self.act(Wq.t[:, jc, 0:nt], A.t[:, jc, 0:nt], AF.Square, [A.bs[jc]], [Wq.bs[jc]])
                else:
                    self.tt(POOL, Wq.t[:, jc, 0:nt], A.t[:, jc, 0:nt], A.t[:, jc, 0:nt], ALU.mult, [A.bs[jc]], [Wq.bs[jc]])
                yield
            if mid is not None:
                yield from mid()
            for jc in range(8):
                self.act(Wq.t[:, jc, 0:nt], Wq.t[:, jc, 0:nt], AF.Sqrt, [Wq.bs[jc]], [Wq.bs[jc]], scale=-0.25, bias=0.25)
                if jc % 2 == 1:
                    yield
            for jc in range(8):
                self.tt(POOL, Wq.t[:, jc, 0:nt], Wq.t[:, jc, 0:nt], z.t[:, jc, 0:nt], ALU.mult, [Wq.bs[jc], z.bs[jc]], [Wq.bs[jc]])
                if jc % 2 == 1:
                    yield
            for jc in range(8):
                self.stt(DVE, TI.t[:, jc, 0:nt], TI.t[:, jc, 0:nt], 1.0, Wq.t[:, jc, 0:nt], ALU.add, ALU.mult,
                         [TI.bs[jc], Wq.bs[jc]], [TI.bs[jc]])
                if use_carry:
                    ini = carry[d].t[:, jc:jc + 1]
                    ideps = [carry[d].bs[jc]]
                else:
                    ini = 0.0
                    ideps = []
                if d == 0:
                    o_, a_, b_ = Sout.t[:, jc, 0:nt], A.t[:, jc, 0:nt], TI.t[:, jc, 0:nt]
                else:
                    o_, a_, b_ = Sout.t[:, jc, 0:nt][:, ::-1], A.t[:, jc, 0:nt][:, ::-1], TI.t[:, jc, 0:nt][:, ::-1]
                self.P.op(DVE, lambda e, o_=o_, a_=a_, b_=b_, ini=ini: e.tensor_tensor_scan(
                    out=o_, data0=a_, data1=b_, initial=ini, op0=ALU.mult, op1=ALU.add),
                    [A.bs[jc], TI.bs[jc]] + ideps, [Sout.bs[jc]], dur=0.2 + nt * (0.0022 if d == 0 else 0.0045))
                col = nt - 1 if d == 0 else 0
                self.cp(DVE, carry[d].t[:, jc:jc + 1], Sout.t[:, jc, col:col + 1], [Sout.bs[jc]], [carry[d].bs[jc]])
                yield

        def g_combine(nt, S0, sg, ymT):
            for c in range(0, 8, 2):
                self.tt(POOL, S0.t[:, c:c + 2, 0:nt], S0.t[:, c:c + 2, 0:nt], S1.t[:, c:c + 2, 0:nt], ALU.add,
                        S0.bs[c:c + 2] + S1.bs[c:c + 2], S0.bs[c:c + 2])
            yield
            self.stt(DVE, ymT.t[:, :, 0:nt], S0.t[:, :, 0:nt], 0.5, sg.t[:, :, 0:nt], ALU.mult, ALU.mult, S0.bs + sg.bs, [ymT])
            yield

        def g_out(tb, blk, gate_t, xt, ymT, final):
            for n in range(2):
                py = nextP()
                for kc in range(8):
                    self.mm(py.t[:], ymT.t[:, kc, tb * 128:(tb + 1) * 128], W_out.t[:, kc, n * 512:(n + 1) * 512],
                            kc == 0, kc == 7, [ymT, W_out], [py])
                yield
                if gate_t is None:
                    self.tt(DVE, xt.t[:, n * 512:(n + 1) * 512], py.t[:], xt.t[:, n * 512:(n + 1) * 512], ALU.add, [py, xt], [xt])
                else:
                    self.tt(DVE, ytmp.t[:, n * 512:(n + 1) * 512], py.t[:], gate_t.t[:, n * 512:(n + 1) * 512], ALU.mult,
                            [py, gate_t], [ytmp])
                yield
            if gate_t is not None:
                self.tt(POOL, xt.t[:], ytmp.t[:], xt.t[:], ALU.add, [ytmp, xt], [xt])
                yield
            if final:
                self.act(ytmp.t[:], xt.t[:], AF.Square, [xt], [ytmp, ssq1], accum=ssq1.t[:, tb:tb + 1])
            else:
                self.act(ytmp.t[:], xt.t[:], AF.Square, [xt], [ytmp, ssq_next], accum=ssq_next.t[:, blk:blk + 1])
                self.dma(SP, self.x_dst(l, blk), xt.t[:], xt, r=[xt])
            yield

        def g_final(blk0, xts):
            self.ts(DVE, v1.t[:], ssq1.t[:], 1.0 / D, EPS, ALU.mult, ALU.add, [ssq1], [v1])
            self.act(hs1.t[:], v1.t[:], AF.Sqrt, [v1], [hs1])
            yield
            self.P.op(DVE, lambda e: e.reciprocal(out=r1_.t[:], in_=hs1.t[:]), [hs1], [r1_])
            self.memset(POOL, ssq1.t[:], 0.0, [ssq1])
            for tb, xt in enumerate(xts):
                self.stt(DVE, xt.t[:], xt.t[:], r1_.t[:, tb:tb + 1], fg.t[:], ALU.mult, ALU.mult, [xt, r1_, fg], [xt])
                blk = blk0 + tb
                self.dma(SP, self.out_d[(blk - NCB) * 128:(blk - NCB + 1) * 128, :], xt.t[:], xt, r=[xt])
                yield

        def run_streams(gens):
            active = list(gens)
            while active:
                for g_ in list(active):
                    try:
                        next(g_)
                    except StopIteration:
                        active.remove(g_)

        def seq(*gens):
            for g_ in gens:
                yield from g_

        z = zr[0]
        sg0 = sgr[0]
        S0 = S0r[0]
        with ExitStack() as p1:
            XR = [self.sbc(p1, "XR%d" % k, [128, 8, NT + 3], BF16, 8) for k in range(3)]
            nt = LC
            self.memset(POOL, XR[0].t[:, :, 0:2], 0.0, XR[0].bs)
            self.memset(POOL, XR[0].t[:, :, 2 + nt:3 + nt], 0.0, XR[0].bs)
            xts = [x_load(tb) for tb in range(NCB)]
            run_streams([seq(g_load_norm(0, NCB, True, hTr[0], xts), g_project(nt, hTr[0], XR[0], need_ctx, sg0),
                             g_conv(nt, XR[0], z, zbr[0]))])
            run_streams([g_gates_scan(nt, 0, False, z, zbr[0], S0, POOL)])
            run_streams([g_gates_scan(nt, 1, False, z, zbr[0], S1, POOL)])
            if need_ctx:
                run_streams([g_combine(nt, S0, sg0, hTr[1])])
                for tb in range(NCB):
                    run_streams([g_out(tb, tb, self.gate[1], xts[tb], hTr[1], False)])
            scale_wout()

            pre = {}

            def NS(it):
                yield from g_load_norm(NCB + it * 2, 2, False, hTr[it % 2], pre.pop(it))

            def PJ(it):
                cur = XR[it % 3]
                prv = XR[(it - 1) % 3]
                hT = hTr[it % 2]
                yield from g_project(NT, hT, cur, True, sg0)
                sgd = self.SG[:, it * NT:(it + 1) * NT].rearrange("(c p) t -> p c t", p=128)
                self.dma(SP, sgd, sg0.t[:], sg0.bs[0], r=sg0.bs, w=[SGb[it]])
                if it == 0:
                    self.memset(POOL, cur.t[:, :, 0:2], 0.0, cur.bs)
                else:
                    self.cp(POOL, cur.t[:, :, 0:2], prv.t[:, :, NT:NT + 2], prv.bs, cur.bs)
                    self.cp(POOL, prv.t[:, :, NT + 2:NT + 3], cur.t[:, :, 2:3], cur.bs, prv.bs)
                if it == NSB - 1:
                    self.memset(POOL, cur.t[:, :, NT + 2:NT + 3], 0.0, cur.bs)
                yield

            def FWD(sb_):
                xr_t = XR[sb_ % 3]
                z = zr[sb_ % 2]
                zb = zbr[sb_ % 2]
                yield from g_conv(NT, xr_t, z, zb)
                zd = self.ZS[:, sb_ * NT:(sb_ + 1) * NT].rearrange("(c p) t -> p c t", p=128)
                self.dma(SP, zd, z.t[:], z.bs[0], r=z.bs, w=[ZSb[sb_]])
                zbd = self.ZB[:, sb_ * NT:(sb_ + 1) * NT].rearrange("(c p) t -> p c t", p=128)
                self.dma(SP, zbd, zb.t[:], zb.bs[0], r=zb.bs, w=[ZBb[sb_]])
                yield
                yield from g_gates_scan(NT, 0, True, z, zb, S0, POOL)
                sfd = self.SF[:, sb_ * NT:(sb_ + 1) * NT].rearrange("(c p) t -> p c t", p=128)
                self.dma(SP, sfd, S0.t[:], S0.bs[0], r=S0.bs, w=[SFb[sb_]])
                yield

            pre[0] = [x_load(NCB), x_load(NCB + 1)]
            self.run_prop([(NS(0), 8)])
            for it in range(NSB + 2):
                if it + 1 < NSB:
                    pre[it + 1] = [x_load(NCB + (it + 1) * 2), x_load(NCB + (it + 1) * 2 + 1)]
                items = []
                if it - 2 >= 0:
                    items.append((FWD(it - 2), 66))
                if it < NSB:
                    items.append((PJ(it), 41))
                if it + 1 < NSB:
                    items.append((NS(it + 1), 8))
                self.run_prop(items)
            self.P.barrier()

        with ExitStack() as p2:
            sgr.append(self.sbc(p2, "sg1", [128, 8, NT], BF16, 8))
            S0r.append(self.sbc(p2, "S01", [128, 8, NT], F32, 8))

            def loads(sb_):
                k = sb_ % 2
                zd = self.ZS[:, sb_ * NT:(sb_ + 1) * NT].rearrange("(c p) t -> p c t", p=128)
                self.dma(SP, zr[k].t[:], zd, zr[k].bs[0], r=[ZSb[sb_]], w=zr[k].bs)
                sfd = self.SF[:, sb_ * NT:(sb_ + 1) * NT].rearrange("(c p) t -> p c t", p=128)
                self.dma(SP, S0r[k].t[:], sfd, S0r[k].bs[0], r=[SFb[sb_]], w=S0r[k].bs)
                sgd = self.SG[:, sb_ * NT:(sb_ + 1) * NT].rearrange("(c p) t -> p c t", p=128)
                self.dma(SP, sgr[k].t[:], sgd, sgr[k].bs[0], r=[SGb[sb_]], w=sgr[k].bs)
                zbd = self.ZB[:, sb_ * NT:(sb_ + 1) * NT].rearrange("(c p) t -> p c t", p=128)
                self.dma(SP, zbr[k].t[:], zbd, zbr[k].bs[0], r=[ZBb[sb_]], w=zbr[k].bs)

            def G(sb_):
                k = sb_ % 2
                if sb_ - 1 >= 0:
                    loads(sb_ - 1)
                yield
                yield from g_gates_scan(NT, 1, True, zr[k], zbr[k], S1, POOL)
                yield from g_combine(NT, S0r[k], sgr[k], hTr[k])

            def O(sb_):
                k = sb_ % 2
                blk0 = NCB + sb_ * 2
                xts = [x_load(blk0 + tb) for tb in range(2)]
                yield
                for tb in range(2):
                    yield from g_out(tb, blk0 + tb, None, xts[tb], hTr[k], last)
                if last:
                    yield from g_final(blk0, xts)

            self.memset(POOL, ssq1.t[:], 0.0, [ssq1])
            loads(NSB - 1)
            for sb_ in range(NSB - 1, -2, -1):
                items = []
                if sb_ >= 0:
                    g_ = G(sb_)
                    for _ in range(25):
                        next(g_)
                    items.append((g_, 22))
                if sb_ + 1 < NSB:
                    items.append((O(sb_ + 1), 15))
                self.run_prop(items)
            self.P.barrier()


_CACHE = {}


def make_in_maps(inp, n_cores=8):
    f = lambda a: np.ascontiguousarray(np.asarray(a, np.float32))
    rows = np.zeros((128, RM.n), np.float32)
    for l in range(DEPTH):
        rows[:, RM["bg%d" % l]:RM["bg%d" % l] + D] = f(inp["b_mod"])[l, 2 * D:3 * D][None, :]
    for i in range(2):
        rows[:, RM["lng%d" % i]:RM["lng%d" % i] + 512] = f(inp["a_ln_g"])[i][None, :]
        rows[:, RM["lnb%d" % i]:RM["lnb%d" % i] + 512] = f(inp["a_ln_b"])[i][None, :]
        rows[:, RM["sink%d" % i]:RM["sink%d" % i] + 8] = f(inp["b_sink"])[i][None, :]
    rows[:, RM["fg"]:RM["fg"] + D] = f(inp["final_g"])[None, :]
    rope = rope_table()
    a_w_sT = np.ascontiguousarray(np.transpose(f(inp["a_w_s"]), (0, 3, 1, 2)))
    shared = {
        "rows": rows, "rope": rope, "w_mod": f(inp["w_mod"]), "ab_w_in": f(inp["ab_w_in"]),
        "ab_w_out": f(inp["ab_w_out"]), "a_w_sT": a_w_sT, "c_w_in": f(inp["c_w_in"]),
        "c_w_a": f(inp["c_w_a"]), "c_w_i": f(inp["c_w_i"]), "c_w_out": f(inp["c_w_out"]),
    }
    maps = []
    for b in range(n_cores):
        cols = np.zeros((128, CM.n), np.float32)
        for l in range(DEPTH):
            cols[:, CM["ng%d" % l]:CM["ng%d" % l] + 8] = col8(inp["norm_g"][l])
            cols[:, CM["bsh%d" % l]:CM["bsh%d" % l] + 8] = col8(inp["b_mod"][l, 0:D])
            cols[:, CM["bsc%d" % l]:CM["bsc%d" % l] + 8] = col8(inp["b_mod"][l, D:2 * D])
        cols[:, CM["c"]:CM["c"] + 8] = col8(inp["c"][b])
        cols[:, CM["cctx"]:CM["cctx"] + 8] = col8(inp["c_ctx"])
        for i in range(2):
            for j in range(4):
                cols[:, CM["cw%d_%d" % (i, j)]:CM["cw%d_%d" % (i, j)] + 8] = col8(inp["c_conv_w"][i, j])
            cols[:, CM["cb%d" % i]:CM["cb%d" % i] + 8] = col8(inp["c_conv_b"][i])
            for d in range(2):
                cols[:, CM["ba%d_%d" % (i, d)]:CM["ba%d_%d" % (i, d)] + 8] = col8(inp["c_b_a"][i, d])
                cols[:, CM["bi%d_%d" % (i, d)]:CM["bi%d_%d" % (i, d)] + 8] = col8(inp["c_b_i"][i, d])
                cols[:, CM["lam%d_%d" % (i, d)]:CM["lam%d_%d" % (i, d)] + 8] = col8(inp["c_lam"][i, d])
            cols[:, CM["bs%d" % i]:CM["bs%d" % i] + 4] = np.asarray(inp["a_b_s"][i], np.float32).T
        m = dict(shared)
        m["cols"] = cols
        m["x"] = f(inp["x"][b])
        m["ctx"] = f(inp["ctx"][b])
        maps.append(m)
    return maps


def kernel(**inputs):
    if "nc" not in _CACHE:
        _CACHE["nc"] = Builder().build()
    nc = _CACHE["nc"]
    maps = make_in_maps(inputs, 8)
    res = run_bass_kernel_spmd(nc, maps, core_ids=list(range(8)))
    out = np.stack([np.asarray(r["out"], np.float32) for r in res.results], axis=0)
    return out
```
